# Optimizing a Trainium2 kernel written in Bass

```python
import jax, jax.numpy as jnp
from jax import lax
import numpy as np

D_MODEL = 1024
BATCH = 16
SEQ = 256
DEPTH = 4
DEC_BATCH = 2
DEC_SEQ = 1024
PAST_LEN = 512

GRID_W = 64
HEAD_DIM = 64
BRANCH_W = D_MODEL // 2
GLA_HEADS = 4
GLA_DV = BRANCH_W // GLA_HEADS
GLA_DK = GLA_DV // 2
GLA_LOWRANK = 16
GLA_TAU = 16.0
GLA_CHUNK = 32
SWA_Q_HEADS = BRANCH_W // HEAD_DIM
SWA_KV_HEADS = 2
SWA_GROUP = SWA_Q_HEADS // SWA_KV_HEADS
SWA_WINDOW = 128
SWA_BLOCK = 128
NA_HEADS = BRANCH_W // HEAD_DIM
NA_ROWS = 8
NA_COLS = 16
MLP_HIDDEN = 4 * D_MODEL
ROPE_BASE = 10000.0
EPS = 1e-6
Q_BLOCK = 128

SPLIT_SIZES = (
    GLA_HEADS * GLA_DK, GLA_HEADS * GLA_DK, GLA_HEADS * GLA_DV, GLA_HEADS * GLA_DV,
    GLA_LOWRANK, GLA_LOWRANK,
    SWA_Q_HEADS * HEAD_DIM, SWA_KV_HEADS * HEAD_DIM, SWA_KV_HEADS * HEAD_DIM,
    NA_HEADS * HEAD_DIM, NA_HEADS * HEAD_DIM, NA_HEADS * HEAD_DIM,
    D_MODEL, D_MODEL, D_MODEL,
)
IN_W = sum(SPLIT_SIZES)

kernel_name = "hybrid_prefix_diffusion_trunk_step"


def rms_norm(x, g):
    xf = x.astype(jnp.float32)
    y = xf * lax.rsqrt(jnp.mean(xf * xf, axis=-1, keepdims=True) + EPS)
    return (y * g.astype(jnp.float32)).astype(x.dtype)


def split_proj(z):
    idx = np.cumsum(np.array(SPLIT_SIZES))[:-1].tolist()
    return jnp.split(z, idx, axis=-1)


def axial_rope(x):
    n = x.shape[1]
    t = jnp.arange(n)
    row = (t // GRID_W).astype(jnp.float32)
    col = (t % GRID_W).astype(jnp.float32)
    half = x.shape[-1] // 2
    nf = half // 2
    inv_freq = ROPE_BASE ** (-jnp.arange(nf, dtype=jnp.float32) / nf)

    def rot(xh, pos):
        ang = pos[:, None] * inv_freq[None, :]
        cos = jnp.cos(ang)[None, :, None, :].astype(x.dtype)
        sin = jnp.sin(ang)[None, :, None, :].astype(x.dtype)
        x1, x2 = xh[..., :nf], xh[..., nf:]
        return jnp.concatenate([x1 * cos - x2 * sin, x1 * sin + x2 * cos], axis=-1)

    return jnp.concatenate([rot(x[..., :half], row), rot(x[..., half:], col)], axis=-1)


def softmax_attend(q, parts, sink):
    scale = q.shape[-1] ** -0.5
    scores = []
    for k, _, bias in parts:
        s = jnp.einsum('bqhgd,bkhd->bhgqk', q, k).astype(jnp.float32) * scale
        if bias is not None:
            s = s + bias
        scores.append(s)
    if sink is not None:
        scores.append(jnp.broadcast_to(sink.astype(jnp.float32)[None, :, :, None, None], scores[0].shape[:-1] + (1,)))
    p = jax.nn.softmax(jnp.concatenate(scores, axis=-1), axis=-1)
    out = None
    off = 0
    for k, v, _ in parts:
        tk = k.shape[1]
        o = jnp.einsum('bhgqk,bkhd->bqhgd', p[..., off:off + tk].astype(v.dtype), v)
        out = o if out is None else out + o
        off += tk
    return out


def dense_ctx_attention(q, k, v, sink):
    b, s = q.shape[:2]
    nb = s // Q_BLOCK
    qb = q.reshape((b, nb, Q_BLOCK) + q.shape[2:]).swapaxes(0, 1)
    out = lax.map(lambda qi: softmax_attend(qi, [(k, v, None)], sink), qb)
    return out.swapaxes(0, 1).reshape(q.shape)


def windowed_attention(q, k, v, k_ctx, v_ctx, sink):
    b, n = q.shape[:2]
    nb = n // SWA_BLOCK
    span = SWA_BLOCK + 2 * SWA_WINDOW
    pad = ((0, 0), (SWA_WINDOW, SWA_WINDOW), (0, 0), (0, 0))
    kp = jnp.pad(k, pad)
    vp = jnp.pad(v, pad)
    i = jnp.arange(SWA_BLOCK)[:, None]
    j = jnp.arange(span)[None, :]
    rel = j - i
    band = (rel >= 0) & (rel <= 2 * SWA_WINDOW)

    def blk(args):
        qi, nidx = args
        start = nidx * SWA_BLOCK
        ki = lax.dynamic_slice_in_dim(kp, start, span, axis=1)
        vi = lax.dynamic_slice_in_dim(vp, start, span, axis=1)
        kpos = start - SWA_WINDOW + j
        valid = band & (kpos >= 0) & (kpos < n)
        bias = jnp.where(valid, 0.0, -jnp.inf).astype(jnp.float32)
        return softmax_attend(qi, [(ki, vi, bias), (k_ctx, v_ctx, None)], sink)

    qb = q.reshape((b, nb, SWA_BLOCK) + q.shape[2:]).swapaxes(0, 1)
    out = lax.map(blk, (qb, jnp.arange(nb)))
    return out.swapaxes(0, 1).reshape(q.shape)


def neighbourhood_attention(q, k, v, k_ctx, v_ctx, rpb):
    b, n, h, dh = q.shape
    rows = n // GRID_W
    wr = min(NA_ROWS, rows)
    qg = q.reshape(b, rows, GRID_W, h, 1, dh).swapaxes(0, 1)
    kg = k.reshape(b, rows, GRID_W, h, dh)
    vg = v.reshape(b, rows, GRID_W, h, dh)
    cq = jnp.arange(GRID_W)[:, None]
    ck = jnp.arange(GRID_W)[None, :]
    cs = jnp.clip(cq - NA_COLS // 2, 0, GRID_W - NA_COLS)
    col_ok = (ck >= cs) & (ck < cs + NA_COLS)
    dc_idx = jnp.clip(ck - cq + NA_COLS - 1, 0, 2 * NA_COLS - 2)

    def row_blk(args):
        qi, r = args
        rs = jnp.clip(r - NA_ROWS // 2, 0, rows - wr)
        ki = lax.dynamic_slice_in_dim(kg, rs, wr, axis=1).reshape(b, wr * GRID_W, h, dh)
        vi = lax.dynamic_slice_in_dim(vg, rs, wr, axis=1).reshape(b, wr * GRID_W, h, dh)
        dr_idx = rs + jnp.arange(wr) - r + NA_ROWS - 1
        bias = rpb[:, dr_idx[None, :, None], dc_idx[:, None, :]].astype(jnp.float32)
        bias = jnp.where(col_ok[:, None, :], bias, -jnp.inf).reshape(h, GRID_W, wr * GRID_W)
        return softmax_attend(qi, [(ki, vi, bias[:, None]), (k_ctx, v_ctx, None)], None)

    out = lax.map(row_blk, (qg, jnp.arange(rows)))
    return out.swapaxes(0, 1).reshape(b, n, h, dh)


def gla_chunked(q, k, v, log_a, s0):
    b, n, h, dk = q.shape
    dv = v.shape[-1]
    c = GLA_CHUNK
    nc = n // c
    f32 = jnp.float32
    qc = q.astype(f32).reshape(b, nc, c, h, dk)
    kc = k.astype(f32).reshape(b, nc, c, h, dk)
    vc = v.astype(f32).reshape(b, nc, c, h, dv)
    cum = jnp.cumsum(log_a.astype(f32).reshape(b, nc, c, h, dk), axis=2)
    tri = jnp.tril(jnp.ones((c, c), dtype=bool))
    diff = cum[:, :, :, None] - cum[:, :, None, :]
    decay = jnp.exp(jnp.where(tri[None, None, :, :, None, None], diff, -jnp.inf))
    attn = jnp.einsum('bnthd,bnshd,bntshd->bnhts', qc, kc, decay)
    o_intra = jnp.einsum('bnhts,bnshv->bnthv', attn, vc)
    last = cum[:, :, -1]
    q_in = qc * jnp.exp(cum)
    k_in = kc * jnp.exp(last[:, :, None] - cum)
    a_last = jnp.exp(last)

    def step(state, xs):
        qi, ki, vi, ai = xs
        o = jnp.einsum('bthk,bhkv->bthv', qi, state)
        state = state * ai[..., None] + jnp.einsum('bthk,bthv->bhkv', ki, vi)
        return state, o

    xs = (q_in.swapaxes(0, 1), k_in.swapaxes(0, 1), vc.swapaxes(0, 1), a_last.swapaxes(0, 1))
    s_fin, o_inter = lax.scan(step, s0.astype(f32), xs)
    o = o_intra + o_inter.swapaxes(0, 1)
    return o.reshape(b, n, h, dv).astype(v.dtype), s_fin


def mixer(h, l, w, ctx):
    b, n, _ = h.shape
    (aq, ak, av, ar, alf, alb, bq, bk, bv, cq, ck, cv, ga, gb, gc) = split_proj(h @ w['w_in'][l])

    qa = aq.reshape(b, n, GLA_HEADS, GLA_DK) * (GLA_DK ** -0.5)
    ka = ak.reshape(b, n, GLA_HEADS, GLA_DK)
    va = av.reshape(b, n, GLA_HEADS, GLA_DV)
    log_f = (jax.nn.log_sigmoid((alf @ w['w_a2_f'][l] + w['b_a_f'][l]).astype(jnp.float32)) / GLA_TAU).reshape(b, n, GLA_HEADS, GLA_DK)
    log_b = (jax.nn.log_sigmoid((alb @ w['w_a2_b'][l] + w['b_a_b'][l]).astype(jnp.float32)) / GLA_TAU).reshape(b, n, GLA_HEADS, GLA_DK)
    if ctx is None:
        s0 = jnp.zeros((b, 2, GLA_HEADS, GLA_DK, GLA_DV), jnp.float32)
    else:
        s0 = ctx[0]
    o_f, s_f = gla_chunked(qa, ka, va, log_f, s0[:, 0])
    o_b, s_b = gla_chunked(jnp.flip(qa, 1), jnp.flip(ka, 1), jnp.flip(va, 1), jnp.flip(log_b, 1), s0[:, 1])
    o_a = rms_norm(o_f + jnp.flip(o_b, 1), w['gla_onorm'][l]).reshape(b, n, GLA_HEADS * GLA_DV) * jax.nn.silu(ar)

    qb = rms_norm(bq.reshape(b, n, SWA_Q_HEADS, HEAD_DIM), w['qn_swa'][l])
    kb = rms_norm(bk.reshape(b, n, SWA_KV_HEADS, HEAD_DIM), w['kn_swa'][l])
    vb = bv.reshape(b, n, SWA_KV_HEADS, HEAD_DIM)
    sink = w['sink_swa'][l].reshape(SWA_KV_HEADS, SWA_GROUP)
    if ctx is None:
        o_b_attn = dense_ctx_attention(qb.reshape(b, n, SWA_KV_HEADS, SWA_GROUP, HEAD_DIM), kb, vb, sink)
    else:
        qr = axial_rope(qb).reshape(b, n, SWA_KV_HEADS, SWA_GROUP, HEAD_DIM)
        o_b_attn = windowed_attention(qr, axial_rope(kb), vb, ctx[1], ctx[2], sink)

    qn = rms_norm(cq.reshape(b, n, NA_HEADS, HEAD_DIM), w['qn_na'][l])
    kn = rms_norm(ck.reshape(b, n, NA_HEADS, HEAD_DIM), w['kn_na'][l])
    vn = cv.reshape(b, n, NA_HEADS, HEAD_DIM)
    if ctx is None:
        o_c = dense_ctx_attention(qn[:, :, :, None, :], kn, vn, None)[:, :, :, 0]
    else:
        o_c = neighbourhood_attention(qn, kn, vn, ctx[3], ctx[4], w['rpb_na'][l])

    ya = o_a @ w['w_pa'][l]
    yb = o_b_attn.reshape(b, n, SWA_Q_HEADS * HEAD_DIM) @ w['w_pb'][l]
    yc = o_c.reshape(b, n, NA_HEADS * HEAD_DIM) @ w['w_pc'][l]
    merged = jax.nn.sigmoid(ga) * ya + jax.nn.sigmoid(gb) * yb + jax.nn.sigmoid(gc) * yc
    out = merged @ w['w_o'][l]
    if ctx is None:
        return out, (jnp.stack([s_f, s_b], axis=1), kb, vb, kn, vn)
    return out, None


def trunk_layer(x, cond, l, w, ctx):
    mod = jax.nn.silu(cond) @ w['w_mod'][l] + w['b_mod'][l]
    sh1, sc1, g1, sh2, sc2, g2 = jnp.split(mod[:, None, :], 6, axis=-1)
    h = rms_norm(x, w['norm1'][l]) * (1 + sc1) + sh1
    mix, new_ctx = mixer(h, l, w, ctx)
    x = x + g1 * mix
    h = rms_norm(x, w['norm2'][l]) * (1 + sc2) + sh2
    x = x + g2 * (jnp.square(jax.nn.relu(h @ w['w_fc1'][l])) @ w['w_fc2'][l])
    return x, new_ctx


def setup_inputs(seed: int = 0) -> dict:
    key = jax.random.key(seed)
    ks = jax.random.split(key, 32)

    def nrm(k, shape, scale):
        return jax.random.normal(k, shape, jnp.float32) * scale

    d = D_MODEL
    return {
        "x_prompt": nrm(ks[0], (BATCH, SEQ, d), 1.0),
        "x_sample": nrm(ks[1], (DEC_BATCH, DEC_SEQ, d), 1.0),
        "state_gla": nrm(ks[2], (DEC_BATCH, DEPTH, 2, GLA_HEADS, GLA_DK, GLA_DV), 0.5),
        "cache_swa_k": nrm(ks[3], (DEC_BATCH, DEPTH, PAST_LEN, SWA_KV_HEADS, HEAD_DIM), 1.0),
        "cache_swa_v": nrm(ks[4], (DEC_BATCH, DEPTH, PAST_LEN, SWA_KV_HEADS, HEAD_DIM), 1.0),
        "cache_na_k": nrm(ks[5], (DEC_BATCH, DEPTH, PAST_LEN, NA_HEADS, HEAD_DIM), 1.0),
        "cache_na_v": nrm(ks[6], (DEC_BATCH, DEPTH, PAST_LEN, NA_HEADS, HEAD_DIM), 1.0),
        "c": nrm(ks[7], (DEC_BATCH, d), 1.0),
        "c_ctx": nrm(ks[8], (d,), 1.0),
        "w_mod": nrm(ks[9], (DEPTH, d, 6 * d), d ** -0.5),
        "b_mod": nrm(ks[10], (DEPTH, 6 * d), 0.02),
        "norm1": 1.0 + nrm(ks[11], (DEPTH, d), 0.05),
        "norm2": 1.0 + nrm(ks[12], (DEPTH, d), 0.05),
        "w_in": nrm(ks[13], (DEPTH, d, IN_W), d ** -0.5),
        "w_a2_f": nrm(ks[14], (DEPTH, GLA_LOWRANK, GLA_HEADS * GLA_DK), GLA_LOWRANK ** -0.5),
        "b_a_f": nrm(ks[15], (DEPTH, GLA_HEADS * GLA_DK), 0.1),
        "w_a2_b": nrm(ks[16], (DEPTH, GLA_LOWRANK, GLA_HEADS * GLA_DK), GLA_LOWRANK ** -0.5),
        "b_a_b": nrm(ks[17], (DEPTH, GLA_HEADS * GLA_DK), 0.1),
        "gla_onorm": 1.0 + nrm(ks[18], (DEPTH, GLA_DV), 0.05),
        "qn_swa": 1.0 + nrm(ks[19], (DEPTH, HEAD_DIM), 0.05),
        "kn_swa": 1.0 + nrm(ks[20], (DEPTH, HEAD_DIM), 0.05),
        "sink_swa": nrm(ks[21], (DEPTH, SWA_Q_HEADS), 0.5),
        "qn_na": 1.0 + nrm(ks[22], (DEPTH, HEAD_DIM), 0.05),
        "kn_na": 1.0 + nrm(ks[23], (DEPTH, HEAD_DIM), 0.05),
        "rpb_na": nrm(ks[24], (DEPTH, NA_HEADS, 2 * NA_ROWS - 1, 2 * NA_COLS - 1), 0.5),
        "w_pa": nrm(ks[25], (DEPTH, GLA_HEADS * GLA_DV, d), (GLA_HEADS * GLA_DV) ** -0.5),
        "w_pb": nrm(ks[26], (DEPTH, SWA_Q_HEADS * HEAD_DIM, d), (SWA_Q_HEADS * HEAD_DIM) ** -0.5),
        "w_pc": nrm(ks[27], (DEPTH, NA_HEADS * HEAD_DIM, d), (NA_HEADS * HEAD_DIM) ** -0.5),
        "w_o": nrm(ks[28], (DEPTH, d, d), d ** -0.5),
        "w_fc1": nrm(ks[29], (DEPTH, d, MLP_HIDDEN), d ** -0.5),
        "w_fc2": nrm(ks[30], (DEPTH, MLP_HIDDEN, d), MLP_HIDDEN ** -0.5),
    }


def reference(x_prompt, x_sample, state_gla, cache_swa_k, cache_swa_v, cache_na_k, cache_na_v, c,
              c_ctx, w_mod, b_mod, norm1, norm2, w_in, w_a2_f, b_a_f, w_a2_b, b_a_b, gla_onorm,
              qn_swa, kn_swa, sink_swa, qn_na, kn_na, rpb_na, w_pa, w_pb, w_pc, w_o, w_fc1, w_fc2):
    w = dict(w_mod=w_mod, b_mod=b_mod, norm1=norm1, norm2=norm2, w_in=w_in, w_a2_f=w_a2_f, b_a_f=b_a_f,
             w_a2_b=w_a2_b, b_a_b=b_a_b, gla_onorm=gla_onorm, qn_swa=qn_swa, kn_swa=kn_swa,
             sink_swa=sink_swa, qn_na=qn_na, kn_na=kn_na, rpb_na=rpb_na, w_pa=w_pa, w_pb=w_pb,
             w_pc=w_pc, w_o=w_o, w_fc1=w_fc1, w_fc2=w_fc2)

    xp = x_prompt
    st_l, bk_l, bv_l, nk_l, nv_l = [], [], [], [], []
    for l in range(DEPTH):
        xp, (st, bk, bv, nk, nv) = trunk_layer(xp, c_ctx[None, :], l, w, None)
        st_l.append(st)
        bk_l.append(bk)
        bv_l.append(bv)
        nk_l.append(nk)
        nv_l.append(nv)

    xs = x_sample
    for l in range(DEPTH):
        ctx = (state_gla[:, l], cache_swa_k[:, l], cache_swa_v[:, l], cache_na_k[:, l], cache_na_v[:, l])
        xs, _ = trunk_layer(xs, c, l, w, ctx)

    new_state_gla = jnp.stack(st_l, axis=1)
    new_swa_k = jnp.stack(bk_l, axis=1)
    new_swa_v = jnp.stack(bv_l, axis=1)
    new_na_k = jnp.stack(nk_l, axis=1)
    new_na_v = jnp.stack(nv_l, axis=1)
    return (xp, xs, new_state_gla, new_swa_k, new_swa_v, new_na_k, new_na_v)
```

```python
import os
import numpy as np
import ml_dtypes
import concourse.bass as bass
import concourse.mybir as mybir
from concourse.bass_utils import run_bass_kernel_spmd

F32 = mybir.dt.float32
BF16 = mybir.dt.bfloat16
AF = mybir.ActivationFunctionType
ALU = mybir.AluOpType

D = 1024
DEPTH = 4
NCORES = 8
TP = 512
TS = 1024
TT = TP + TS
IN_W = 6944
NEG = -30000.0
PA_STOP = float(os.environ.get('PA_STOP', '99'))
EPS = 1e-6

O_AQ, O_AK, O_AV, O_AR, O_ALF, O_ALB = 0, 256, 512, 1024, 1536, 1552
O_BQ, O_BK, O_BV = 1568, 2080, 2208
O_CQ, O_CK, O_CV = 2336, 2848, 3360
O_GA, O_GB, O_GC = 3872, 4896, 5920


class Buf:
    __slots__ = ("lw", "rd", "name")

    def __init__(self, name=""):
        self.lw = None
        self.rd = []
        self.name = name


class Op:
    __slots__ = ("eng", "fn", "deps", "dma", "ticket", "semkey", "nsig", "idx", "cost", "tag", "st", "fi")


class Prog:
    ENGS = ("pe", "act", "dve", "pool", "sp")

    def __init__(self, nc):
        self.nc = nc
        self.ops = []
        self.dyn = {}
        self.dyn_spec = {}

    def op(self, eng, fn, reads=(), writes=(), dma=False, cost=500.0):
        o = Op()
        o.tag = getattr(self, "tag", "")
        o.cost = cost
        o.eng = eng
        o.fn = fn
        o.dma = dma
        o.idx = len(self.ops)
        deps = set()
        for b in reads:
            if b.lw is not None:
                deps.add(b.lw)
        for b in writes:
            if b.lw is not None:
                deps.add(b.lw)
            deps.update(b.rd)
        o.deps = deps
        self.ops.append(o)
        for b in reads:
            b.rd.append(o.idx)
        for b in writes:
            b.lw = o.idx
            b.rd = []
        return o


    def schedule(self, window=40):
        ops = self.ops
        n = len(ops)
        left = [len(o.deps) for o in ops]
        users = [[] for _ in ops]
        for o in ops:
            for dd in o.deps:
                users[dd].append(o.idx)
        ready = [0.0] * n
        fin = [0.0] * n
        done = [False] * n
        pend = {e: [o.idx for o in ops if o.eng == e] for e in self.ENGS}
        head = {e: 0 for e in self.ENGS}
        free = {e: 0.0 for e in self.ENGS}
        order = {e: [] for e in self.ENGS}
        pipe = 0.0
        remaining = n
        glob = []
        while remaining:
            best = None
            for e in self.ENGS:
                lst = pend[e]
                i = head[e]
                while i < len(lst) and done[lst[i]]:
                    i += 1
                head[e] = i
                cnt = 0
                fe = free[e]
                while i < len(lst) and cnt < window:
                    idx = lst[i]
                    i += 1
                    if done[idx]:
                        continue
                    cnt += 1
                    if left[idx] == 0:
                        st = ready[idx] if ready[idx] > fe else fe
                        if best is None or (st, idx) < best[0]:
                            best = ((st, idx), e, idx)
                        if st <= fe:
                            break
            assert best is not None, "scheduler deadlock"
            (st, _), e, idx = best
            o = ops[idx]
            if o.dma:
                issue = 1000.0 if e == "pool" else 100.0
                free[e] = st + issue
                p0 = max(pipe, st + issue)
                pipe = p0 + o.cost
                f = pipe + 2000.0
            else:
                free[e] = st + o.cost
                f = st + o.cost + 150.0
            fin[idx] = f
            o.st = st
            o.fi = f
            done[idx] = True
            order[e].append(idx)
            glob.append(idx)
            remaining -= 1
            for u in users[idx]:
                left[u] -= 1
                if ready[u] < f:
                    ready[u] = f
        self.order = order
        self.est_ns = max(fin) if fin else 0.0

    def emit(self, final_wait_eng="sp"):
        nc = self.nc
        ops = self.ops
        KD = {"sp": 14, "pool": 10, "act": 4}
        needed = [False] * len(ops)
        for o in ops:
            for d in o.deps:
                a = ops[d]
                if a.eng == o.eng and o.eng == "pe" and not a.dma and not o.dma:
                    continue
                needed[d] = True
        cnt = {e: 0 for e in self.ENGS}
        dcnt = {e: 0 for e in KD}
        order = getattr(self, "order", None)
        if order is None:
            order = {e: [o.idx for o in ops if o.eng == e] for e in self.ENGS}
        seq = [ops[i] for e in self.ENGS for i in order[e]]
        for o in seq:
            if o.dma:
                n = dcnt[o.eng]
                dcnt[o.eng] += 1
                k = n % KD[o.eng]
                o.semkey = (o.eng, k)
                o.ticket = 16 * (n // KD[o.eng] + 1)
            else:
                o.semkey = o.eng
                if needed[o.idx]:
                    cnt[o.eng] += 1
                    o.ticket = cnt[o.eng]
                else:
                    o.ticket = None
        sems = {}
        import contextlib
        with contextlib.ExitStack() as st:
            for e in self.ENGS:
                sems[e] = st.enter_context(nc.semaphore("s_" + e))
            for e, k in KD.items():
                for i in range(k):
                    sems[(e, i)] = st.enter_context(nc.semaphore("d_%s%d" % (e, i)))
            block = st.enter_context(nc.Block())
            per_eng = {e: [ops[i] for i in order[e]] for e in self.ENGS}

            def run(eng_name, eng):
                seen = {}
                if eng_name == "sp":
                    for key, (ap, lo, hi) in getattr(self, "dyn_spec", {}).items():
                        reg = eng.alloc_register("dyn_" + key)
                        eng.reg_load(reg, ap)
                        self.dyn[key] = eng.snap(reg, min_val=lo, max_val=hi)
                for o in per_eng[eng_name]:
                    waits = {}
                    for d in o.deps:
                        a = ops[d]
                        if a.eng == o.eng and o.eng == "pe" and not a.dma and not o.dma:
                            continue
                        if a.ticket is None:
                            continue
                        if waits.get(a.semkey, 0) < a.ticket:
                            waits[a.semkey] = a.ticket
                    if o.dma:
                        prev = o.ticket - 16
                        if prev > 0 and waits.get(o.semkey, 0) < prev:
                            waits[o.semkey] = prev
                    for key, val in waits.items():
                        if seen.get(key, 0) < val:
                            eng.wait_ge(sems[key], val)
                            seen[key] = val
                    ins = o.fn(eng)
                    if o.dma:
                        ins.then_inc(sems[o.semkey], 16)
                    elif o.ticket is not None:
                        ins.then_inc(sems[o.semkey], 1)
                if eng_name in KD:
                    n = dcnt[eng_name]
                    for k in range(min(n, KD[eng_name])):
                        tot = 16 * ((n - 1 - k) // KD[eng_name] + 1)
                        if seen.get((eng_name, k), 0) < tot:
                            eng.wait_ge(sems[(eng_name, k)], tot)

            @block.tensor
            def _(e):
                run("pe", e)

            @block.scalar
            def _(e):
                run("act", e)

            @block.vector
            def _(e):
                run("dve", e)

            @block.gpsimd
            def _(e):
                run("pool", e)

            @block.sync
            def _(e):
                run("sp", e)


def make_consts():
    c = {}
    bf = ml_dtypes.bfloat16
    c["ident_bf"] = np.eye(128, dtype=np.float32).astype(bf)
    c["ident_f"] = np.eye(128, dtype=np.float32)
    bd = np.zeros((128, 128), np.float32)
    bd[:64, :64] = 1.0
    bd[64:, 64:] = 1.0
    c["ones_bd"] = bd.astype(bf)
    c["ones_all"] = np.ones((128, 128), np.float32).astype(bf)
    s = np.arange(128)[:, None]
    t = np.arange(128)[None, :]
    c["trif"] = np.where(s <= t, -1.0 / 16.0, 0.0).astype(np.float32).astype(bf)
    c["trib"] = np.where(s >= t, -1.0 / 16.0, 0.0).astype(np.float32).astype(bf)
    c["maskf"] = np.where(s <= t, 1.0, 0.0).astype(np.float32).astype(bf)
    c["maskb"] = np.where(s >= t, 1.0, 0.0).astype(np.float32).astype(bf)
    c["band_next"] = np.where(t <= s, 0.0, NEG).astype(np.float32).astype(bf)
    c["band_prev"] = np.where(s <= t, 0.0, NEG).astype(np.float32).astype(bf)
    nf = 16
    inv_freq = (10000.0 ** (-np.arange(nf, dtype=np.float32) / nf)).astype(np.float32)
    tt = np.arange(TS)
    row = (tt // 64).astype(np.float32)
    col = (tt % 64).astype(np.float32)
    ang = np.zeros((64, TS), np.float32)
    for d in range(64):
        pos = row if d < 32 else col
        ang[d] = pos * inv_freq[d % 16]
    cos = np.cos(ang).astype(np.float32)
    sin = np.sin(ang).astype(np.float32)
    c["rope_cos"] = np.concatenate([cos, cos], 0)
    c["rope_sin"] = np.concatenate([sin, sin], 0)
    Pm = np.zeros((128, 128), np.float32)
    for d in range(128):
        if (d % 32) < 16:
            Pm[d, d + 16] = -1.0
        else:
            Pm[d, d - 16] = 1.0
    c["rope_pT"] = np.ascontiguousarray(Pm.T).astype(bf)
    J = np.zeros((64, 64), np.float32)
    for i in range(64):
        J[i, 63 - i] = 1.0
    JJ = np.zeros((128, 128), np.float32)
    JJ[:64, :64] = J
    JJ[64:, 64:] = J
    c["jj"] = JJ.astype(bf)
    cq = np.arange(64)[None, :]
    ckp = np.arange(64)[:, None]
    ck = 63 - ckp
    cs = np.clip(cq - 8, 0, 48)
    ok = (ck >= cs) & (ck < cs + 16)
    nm_ = np.where(ok, 0.0, NEG).astype(np.float32)
    c["na_mask"] = np.concatenate([nm_, nm_], 0)
    return c


CONST_SPECS = [
    ("ident_bf", [128, 128], BF16), ("ident_f", [128, 128], F32), ("ones_bd", [128, 128], BF16),
    ("ones_all", [128, 128], BF16), ("trif", [128, 128], BF16), ("trib", [128, 128], BF16),
    ("maskf", [128, 128], BF16), ("maskb", [128, 128], BF16), ("band_next", [128, 128], BF16),
    ("band_prev", [128, 128], BF16), ("rope_cos", [128, TS], F32), ("rope_sin", [128, TS], F32),
    ("rope_pT", [128, 128], BF16), ("jj", [128, 128], BF16), ("na_mask", [128, 64], F32),
]


GRAN = 512


class V:
    __slots__ = ("ap", "bufs", "excl")

    def __init__(self, ap, bufs, excl=False):
        self.ap = ap
        self.bufs = bufs
        self.excl = excl


def DR(ap):
    return V(ap, [])


class Arena:
    def __init__(self, nc, nbytes):
        self.nc = nc
        self.nbytes = nbytes
        self.t = nc.alloc_sbuf_tensor("arena", [128, nbytes // 4], F32)
        self.g = [Buf("g%d" % i) for i in range((nbytes + GRAN - 1) // GRAN)]
        self.top = 0

    def bufs(self, lo, hi):
        return self.g[lo // GRAN:(hi - 1) // GRAN + 1]

    def alloc(self, shape, dtype, align=GRAN):
        esz = 4 if dtype == F32 else 2
        n = 1
        for d in shape[1:]:
            n *= d
        nb = (n * esz + 3) // 4 * 4
        off = (self.top + align - 1) // align * align
        assert off + nb <= self.nbytes, ("arena overflow", off, nb, self.nbytes)
        self.top = off + nb
        return Tile(self, off, shape, dtype)

    def mark(self):
        return self.top

    def release(self, m):
        self.top = m


class Tile:
    def __init__(self, arena, off, shape, dtype):
        self.arena = arena
        self.off = off
        self.shape = tuple(shape)
        self.dt = dtype
        self.esz = 4 if dtype == F32 else 2
        n = 1
        for d in shape[1:]:
            n *= d
        w0 = off // 4
        w1 = w0 + (n * self.esz + 3) // 4
        base = arena.t[:, w0:w1]
        if dtype != F32:
            base = base.bitcast(dtype)
        if len(shape) > 2:
            names = ["d%d" % i for i in range(len(shape) - 1)]
            pat = "p (" + " ".join(names) + ") -> p " + " ".join(names)
            base = base.rearrange(pat, **{nm: shape[i + 1] for i, nm in enumerate(names[:-1])})
        self.ap = base[0:shape[0]]
        st = []
        acc = 1
        for d in reversed(shape[1:]):
            st.append(acc)
            acc *= d
        self.strides = list(reversed(st))

    def __getitem__(self, key):
        if not isinstance(key, tuple):
            key = (key,)
        ap = self.ap[key]
        fk = list(key[1:]) + [slice(None)] * (len(self.shape) - len(key))
        dims = []
        for k, d, s in zip(fk, self.shape[1:], self.strides):
            if isinstance(k, slice):
                a, b, stp = k.indices(d)
                cnt = max(0, (b - a + stp - 1) // stp)
                dims.append((a, cnt, stp, s, d))
            else:
                dims.append((k, 1, 1, s, d))
        gset = {}
        esz = self.esz
        off = self.off
        arena = self.arena

        def rec(i, base):
            a, cnt, stp, st, d = dims[i]
            inner_full = all(dd[1] == dd[4] and dd[2] == 1 for dd in dims[i + 1:])
            if stp == 1 and inner_full:
                lo = base + a * st
                hi = base + (a + cnt) * st
                for b_ in arena.bufs(off + lo * esz, off + hi * esz):
                    gset[id(b_)] = b_
                return
            if i == len(dims) - 1:
                for j in range(cnt):
                    lo = base + (a + j * stp) * st
                    for b_ in arena.bufs(off + lo * esz, off + (lo + 1) * esz):
                        gset[id(b_)] = b_
                return
            for j in range(cnt):
                rec(i + 1, base + (a + j * stp) * st)

        rec(0, 0)
        return V(ap, list(gset.values()))

    def full(self):
        return self[tuple(slice(None) for _ in self.shape)]


class Bank:
    def __init__(self, nc, i):
        self.t = nc.alloc_psum_tensor("pb%d" % i, [128, 512], F32)
        self.buf = Buf("pb%d" % i)
        self.ap = self.t[:, :]

    def __getitem__(self, key):
        return V(self.ap[key], [self.buf], True)

    def v(self, ap):
        return V(ap, [self.buf], True)

    def bf(self):
        return self.ap.bitcast(BF16)


class K:
    def __init__(self, nc):
        self.nc = nc
        self.P = Prog(nc)
        self.A = Arena(nc, 205 * 1024)
        self.banks = [Bank(nc, i) for i in range(8)]
        self.bi = 0
        self.flip = 0
        self.held = []
        self.fence = V(None, [Buf('fence')])

    def bank(self):
        while True:
            b = self.banks[self.bi]
            self.bi = (self.bi + 1) % 8
            if b not in self.held:
                return b

    def hold(self):
        b = self.bank()
        self.held.append(b)
        return b

    def unhold(self, b):
        self.held.remove(b)

    def _rw(self, reads, writes):
        r, w = [], []
        for v in reads:
            if isinstance(v, V):
                (w if v.excl else r).extend(v.bufs)
        for v in writes:
            w.extend(v.bufs)
        return r, w

    def op(self, eng, fn, reads, writes, dma=False):
        r, w = self._rw(reads, writes)
        try:
            shp = writes[0].ap.shape
            nfree = 1
            for x in shp[1:]:
                nfree *= x
            npart = shp[0]
        except Exception:
            nfree, npart = 512, 128
        if dma:
            try:
                esz = 4 if reads[0].ap.dtype == F32 else 2
            except Exception:
                esz = 4
            cost = npart * nfree * esz / 180.0
        elif eng == "pe":
            cost = max(64.0, nfree) * 0.46 + (70.0 if nfree < 256 else 15.0)
        elif eng == "act":
            cost = 230.0 + nfree * 0.75
        elif eng == "dve":
            cost = 120.0 + nfree * 0.95
        else:
            cost = 250.0 + nfree * 1.9
        return self.P.op(eng, fn, r, w, dma, cost)

    def mm(self, out, lhsT, rhs, start=True, stop=True):
        self.op("pe", lambda e: e.matmul(out.ap, lhsT=lhsT.ap, rhs=rhs.ap, start=start, stop=stop,
                                         skip_group_check=True), [lhsT, rhs], [out])

    def tr(self, out, in_, ident):
        self.op("pe", lambda e: e.transpose(out.ap, in_.ap, ident.ap), [in_, ident], [out])

    def act(self, out, in_, func, bias=None, scale=None):
        kw = {}
        rd = [in_]
        if bias is not None:
            kw["bias"] = bias.ap if isinstance(bias, V) else bias
            rd.append(bias)
        if scale is not None:
            kw["scale"] = scale.ap if isinstance(scale, V) else scale
            rd.append(scale)
        self.op("act", lambda e: e.activation(out.ap, in_.ap, func, **kw), rd, [out])

    def tt(self, out, a, b, op, eng="dve"):
        self.op(eng, lambda e: e.tensor_tensor(out.ap, a.ap, b.ap, op), [a, b], [out])

    def ts(self, out, a, s1, op0, s2=None, op1=None, eng="dve"):
        rd = [a, s1, s2]
        s1a = s1.ap if isinstance(s1, V) else s1
        s2a = s2.ap if isinstance(s2, V) else s2
        if op1 is None:
            self.op(eng, lambda e: e.tensor_scalar(out.ap, a.ap, s1a, None, op0), rd, [out])
        else:
            self.op(eng, lambda e: e.tensor_scalar(out.ap, a.ap, s1a, s2a, op0, op1), rd, [out])

    def stt(self, out, in0, scalar, in1, op0, op1, eng="dve"):
        sa = scalar.ap if isinstance(scalar, V) else scalar
        self.op(eng, lambda e: e.scalar_tensor_tensor(out.ap, in0.ap, sa, in1.ap, op0, op1),
                [in0, scalar, in1], [out])

    def cp(self, out, in_, eng="dve"):
        if eng == "act":
            self.op("act", lambda e: e.copy(out.ap, in_.ap), [in_], [out])
        else:
            self.op(eng, lambda e: e.tensor_copy(out.ap, in_.ap), [in_], [out])

    def cpx(self, out, in_):
        self.flip ^= 1
        self.cp(out, in_, "act" if self.flip else "dve")

    def recip(self, out, in_):
        self.op("dve", lambda e: e.reciprocal(out.ap, in_.ap), [in_], [out])

    def rstd(self, out, ss, n):
        self.act(out, ss, AF.Ln, bias=EPS, scale=1.0 / n)
        self.act(out, out, AF.Exp, scale=-0.5)

    def recip_act(self, out, in_):
        self.act(out, in_, AF.Ln)
        self.act(out, out, AF.Exp, scale=-1.0)

    def memset(self, out, val, eng="dve"):
        self.op(eng, lambda e: e.memset(out.ap, val), [], [out])

    def dma_dyn(self, out, src_tile, key, width, track):
        P = self.P

        def fn(e):
            return e.dma_start(out=out.ap, in_=src_tile.ap[:, :, bass.ds(P.dyn[key], width)])
        self.op("sp", fn, [track], [out], dma=True)

    def allgather(self, out, in_, groups):
        o_ = self.op("pool", lambda e: e.collective_compute("AllGather", ALU.bypass, replica_groups=groups,
                                                            ins=[in_.ap.opt()], outs=[out.ap.opt()]),
                     [in_], [out, self.fence])
        o_.cost = 50000.0

    def dma(self, q, out, in_, **kw):
        rd = [in_, self.fence] if q == "pool" else [in_]
        self.op(q, lambda e: e.dma_start(out=out.ap, in_=in_.ap, **kw), rd, [out], dma=True)


def build_program(depth=DEPTH, do_sample=True, dbg_names=(), stop=None):
    nc = bass.Bass("TRN2", target_bir_lowering=False)

    def din(name, shape, dt=F32):
        return nc.dram_tensor(name, shape, dt, kind="ExternalInput").ap()

    def dout(name, shape):
        return nc.dram_tensor(name, shape, F32, kind="ExternalOutput").ap()

    d = {}
    d["xp"] = din("xp", [TP, D])
    d["xs"] = din("xs", [TS, D])
    d["cond"] = din("cond", [16, 128])
    d["rk"] = din("rk", [1, 2], mybir.dt.int32)
    d["st_gla"] = din("st_gla", [DEPTH, 2, 4, 64, 128])
    d["cswk"] = din("cswk", [DEPTH, 512, 128])
    d["cswv"] = din("cswv", [DEPTH, 512, 128])
    d["cnak"] = din("cnak", [DEPTH, 512, 512])
    d["cnav"] = din("cnav", [DEPTH, 512, 512])
    d["w_mod"] = din("w_mod", [DEPTH, D, 6 * D])
    d["b_mod"] = din("b_mod", [DEPTH, 48, 128])
    d["norm1"] = din("norm1", [DEPTH, 8, 128])
    d["norm2"] = din("norm2", [DEPTH, 8, 128])
    d["w_in"] = din("w_in", [DEPTH, D, IN_W])
    d["w_a2_f"] = din("w_a2_f", [DEPTH, 16, 256])
    d["b_a_f"] = din("b_a_f", [DEPTH, 256])
    d["w_a2_b"] = din("w_a2_b", [DEPTH, 16, 256])
    d["b_a_b"] = din("b_a_b", [DEPTH, 256])
    d["gla_onorm"] = din("gla_onorm", [DEPTH, 128])
    for nm in ("qn_swa", "kn_swa", "qn_na", "kn_na"):
        d[nm] = din(nm, [DEPTH, 64])
    d["sink_swa"] = din("sink_swa", [DEPTH, 8])
    d["rpb_pad"] = din("rpb_pad", [DEPTH, 8, 15, 127])
    for nm in ("w_pa", "w_pb", "w_pc"):
        d[nm] = din(nm, [DEPTH, 512, D])
    d["w_o"] = din("w_o", [DEPTH, D, D])
    d["w_fc1"] = din("w_fc1", [DEPTH, D, 4 * D])
    d["w_fc2"] = din("w_fc2", [DEPTH, 4 * D, D])
    for nm, shp, dt in CONST_SPECS:
        d[nm] = din(nm, shp, dt)
    o = {}
    o["yp"] = dout("yp", [TP, D])
    o["ysq"] = dout("ysq", [256, D])
    o["o_gla"] = dout("o_gla", [2, DEPTH, 2, 4, 64, 128])
    o["o_swk"] = dout("o_swk", [2, DEPTH, 256, 128])
    o["o_swv"] = dout("o_swv", [2, DEPTH, 256, 128])
    o["o_nak"] = dout("o_nak", [2, DEPTH, 256, 512])
    o["o_nav"] = dout("o_nav", [2, DEPTH, 256, 512])

    k = K(nc)
    A = k.A
    k.P.dyn_spec["cabs"] = (d["rk"][0:1, 0:1], TP, TP + 768)
    k.P.dyn_spec["crel"] = (d["rk"][0:1, 1:2], 0, 768)
    xq_in = nc.dram_tensor("xq_in", [1024, 256], F32)
    xq_all = nc.dram_tensor("xq_all", [4096, 256], F32)
    XQI = V(xq_in.ap(), [Buf("xq_in")])
    XQA = V(xq_all.ap(), [Buf("xq_all")])
    X = A.alloc([128, 8, TT], F32)
    H = A.alloc([128, 8, TT], BF16)
    MG = A.alloc([128, 8, TP], BF16)
    MGQ = A.alloc([128, 8, 256], BF16)
    HQ = A.alloc([128, 8, 256], BF16)
    WS = [A.alloc([128, 8, 512], BF16) for _ in range(4)]
    wsi = [0]

    def wslot():
        w = WS[wsi[0]]
        wsi[0] = (wsi[0] + 1) % len(WS)
        return w

    C = {}
    for nm, shp, dt in CONST_SPECS:
        C[nm] = A.alloc(shp, dt, align=64 if shp[1] <= 128 else GRAN)
        k.dma("sp", C[nm].full(), DR(d[nm]))
    SCT = A.alloc([128, 8, 2], BF16, align=64)
    PVt = A.alloc([128, DEPTH, 64], F32)
    MODVL = [A.alloc([128, 48, 2], F32, align=64) for _ in range(2)]
    SCLL = [A.alloc([128, 2, 2, 8], F32, align=64) for _ in range(2)]
    cur = [0]
    GVL = [A.alloc([128, 8], F32, align=64) for _ in range(2)]
    ESKL = [A.alloc([128, 8], F32, align=64) for _ in range(2)]
    ALR = A.alloc([64, TS], BF16)
    WA2L = [A.alloc([64, 512], BF16) for _ in range(2)]

    class _Cur:
        def __init__(self, lst):
            self.lst = lst

        def __getitem__(self, key):
            return self.lst[cur[0]][key]

        def full(self):
            return self.lst[cur[0]].full()

    GV = _Cur(GVL)
    ESK = _Cur(ESKL)
    WA2 = _Cur(WA2L)
    k.memset(ALR.full(), 1.0)

    dbg = {}

    def dbg_out(name, v, shape):
        if name in dbg_names:
            ap = dout("dbg_" + name, shape)
            k.dma("sp", DR(ap), v)

    m0 = A.mark()
    for l in range(DEPTH):
        stg = A.alloc([64, 128], F32)
        k.dma("sp", stg[0:48, :], DR(d["b_mod"][l]))
        k.dma("sp", stg[48:56, :], DR(d["norm1"][l]))
        k.dma("sp", stg[56:64, :], DR(d["norm2"][l]))
        pb = k.bank()
        k.tr(pb[:, 0:64], stg.full(), C["ident_f"][0:64, 0:64])
        k.cp(PVt[:, l, :], pb[:, 0:64])
    stg = A.alloc([16, 128], F32)
    k.dma("sp", stg.full(), DR(d["cond"]))
    pb = k.bank()
    k.tr(pb[:, 0:16], stg.full(), C["ident_f"][0:16, 0:16])
    for ci in range(2):
        k.act(SCT[:, :, ci], pb[:, ci * 8:(ci + 1) * 8], AF.Silu)
    A.release(m0)

    def load_tokens(src, n, col0):
        m = A.mark()
        st2 = [A.alloc([128, D], F32) for _ in range(2)]
        for tix in range(n // 128):
            s_ = st2[tix % 2]
            k.dma("sp", s_.full(), DR(src[tix * 128:(tix + 1) * 128, :]))
            for q4 in range(2):
                pb = k.bank()
                for j in range(4):
                    oc = q4 * 4 + j
                    k.tr(pb[:, j * 128:(j + 1) * 128], s_[:, oc * 128:(oc + 1) * 128], C["ident_f"].full())
                c0 = col0 + tix * 128
                k.cpx(X[:, q4 * 4:q4 * 4 + 4, c0:c0 + 128],
                      pb.v(pb.ap.rearrange("p (a b) -> p a b", a=4)))
        A.release(m)

    load_tokens(d["xp"], TP, 0)
    load_tokens(d["xs"], TS, TP)

    def wload(slot_view, src_ap):
        k.dma("pool", slot_view, DR(src_ap))

    wc_scr = {}
    wc_valid = set()

    def wtile(key, slot_view, loader):
        if os.environ.get("NO_WCACHE") == "1":
            loader()
            return
        shp = list(slot_view.ap.shape)
        if key not in wc_scr:
            t = nc.dram_tensor("wc%d" % len(wc_scr), shp, BF16)
            wc_scr[key] = V(t.ap(), [Buf("wc")])
        scr = wc_scr[key]
        if key in wc_valid:
            k.dma("sp", slot_view, scr)
        else:
            loader()
            k.dma("sp", scr, slot_view)
            wc_valid.add(key)

    def wrows(w2d, c0, n):
        return w2d[:, c0:c0 + n].rearrange("(kc p) n -> p kc n", p=128)

    GCI = [0, 1, 1]

    def mod_steps(l):
        MODV = MODVL[l % 2]
        SCL = SCLL[l % 2]
        st = {}

        def step(j):
            if j == 0:
                st["mb"] = k.hold()
            mb = st["mb"]
            ws = wslot()
            wload(ws.full(), wrows(d["w_mod"][l], j * 512, 512))
            for jj in range(4):
                ch = j * 4 + jj
                for kc in range(8):
                    k.mm(mb[:, ch * 2:ch * 2 + 2], ws[:, kc, jj * 128:(jj + 1) * 128], SCT[:, kc, :],
                         start=(kc == 0), stop=(kc == 7))

        def fin():
            mb = st["mb"]
            bm = PVt[:, l, 0:48]
            k.tt(MODV.full(), mb.v(mb.ap[:, 0:96].rearrange("p (a b) -> p a b", b=2)),
                 V(bm.ap.unsqueeze(2).to_broadcast([128, 48, 2]), bm.bufs), ALU.add)
            k.unhold(mb)
            for n in range(2):
                for ci in range(2):
                    sc = MODV[:, 8 + 24 * n:16 + 24 * n, ci]
                    k.ts(SCL[:, n, ci, :], sc, 1.0, ALU.add)
                    k.tt(SCL[:, n, ci, :], SCL[:, n, ci, :], PVt[:, l, 48 + 8 * n:56 + 8 * n], ALU.mult)

        return [(lambda j=j: step(j)) for j in range(12)] + [fin]

    def norm_group(l, n, g):
        ci = GCI[g]
        tok = slice(g * 512, (g + 1) * 512)
        m = A.mark()
        sq = [A.alloc([128, 512], BF16) for _ in range(2)]
        rs = A.alloc([128, 512], F32)
        tmp = [A.alloc([128, 512], F32) for _ in range(2)]
        sb = k.bank()
        for oc in range(8):
            s_ = sq[oc % 2]
            k.act(s_.full(), X[:, oc, tok], AF.Square)
            k.mm(sb[:, :], C["ones_all"].full(), s_.full(), start=(oc == 0), stop=(oc == 7))
        k.rstd(rs.full(), sb[:, :], 1024.0)
        for oc in range(8):
            t_ = tmp[oc % 2]
            k.stt(t_.full(), X[:, oc, tok], SCLL[cur[0]][:, n, ci, oc:oc + 1], rs.full(), ALU.mult, ALU.mult)
            k.act(H[:, oc, tok], t_.full(), AF.Identity, bias=MODVL[cur[0]][:, 24 * n + oc, ci:ci + 1])
        A.release(m)

    def proj_fm(ws, c0, mcols, tok, pb=None, col_off=0):
        if pb is None:
            pb = k.bank()
        if isinstance(tok, tuple):
            hf, n = tok
        else:
            hf, n = (lambda kc: H[:, kc, tok]), 512
        for kc in range(8):
            k.mm(pb[0:mcols, col_off:col_off + n], ws[:, kc, c0:c0 + mcols], hf(kc),
                 start=(kc == 0), stop=(kc == 7))
        return pb

    def proj_tm(ws, c0, ncols, t0, pb=None):
        if pb is None:
            pb = k.bank()
        for kc in range(8):
            k.mm(pb[:, 0:ncols], H[:, kc, t0:t0 + 128], ws[:, kc, c0:c0 + ncols],
                 start=(kc == 0), stop=(kc == 7))
        return pb

    nh_flip = [0]

    def normhead(pb, out_f32, out_bf, gain, out_pair=None):
        m = A.mark()
        nh_flip[0] ^= 1
        if nh_flip[0]:
            A.alloc([128, 512 + 256 + 512], F32)
        qf = A.alloc([128, 512], F32)
        sq = A.alloc([128, 512], BF16)
        rs = A.alloc([128, 512], F32)
        k.cp(qf.full(), pb[:, :], "dve")
        k.act(sq.full(), pb[:, :], AF.Square)
        sb = k.bank()
        k.mm(sb[:, :], C["ones_bd"].full(), sq.full())
        k.rstd(rs.full(), sb[:, :], 64.0)
        if out_pair is not None:
            for hf in range(2):
                rw = slice(hf * 64, hf * 64 + 64)
                k.stt(out_pair[hf], qf[rw, :], V(gain.ap[rw], gain.bufs), rs[rw, :], ALU.mult, ALU.mult)
        elif out_f32 is not None:
            k.stt(out_f32, qf.full(), gain, rs.full(), ALU.mult, ALU.mult)
            if out_bf is not None:
                k.cp(out_bf, out_f32, "act")
        else:
            k.stt(out_bf, qf.full(), gain, rs.full(), ALU.mult, ALU.mult)
        A.release(m)

    def layer_small(l):
        GV, ESK, WA2 = GVL[l % 2], ESKL[l % 2], WA2L[l % 2]
        for j, (nm, f) in enumerate((("qn_swa", 0.125), ("kn_swa", 1.0), ("qn_na", 0.125), ("kn_na", 1.0))):
            src = d[nm][l].rearrange("(p o) -> p o", o=1)
            k.dma("sp", GV[0:64, j:j + 1], DR(src))
            k.dma("sp", GV[64:128, j:j + 1], DR(src))
        k.dma("sp", GV[:, 4:5], DR(d["gla_onorm"][l].rearrange("(p o) -> p o", o=1)))
        k.ts(GV[:, 0:1], GV[:, 0:1], 0.125, ALU.mult)
        k.ts(GV[:, 2:3], GV[:, 2:3], 0.125, ALU.mult)
        k.dma("sp", ESK.full(), DR(d["sink_swa"][l].partition_broadcast(128)))
        k.act(ESK.full(), ESK.full(), AF.Exp)
        k.memset(WA2.full(), 0.0)
        k.dma("pool", WA2[0:16, 0:256], DR(d["w_a2_f"][l]))
        k.dma("pool", WA2[16:17, 0:256], DR(d["b_a_f"][l].rearrange("(o n) -> o n", o=1)))
        k.dma("pool", WA2[32:48, 256:512], DR(d["w_a2_b"][l]))
        k.dma("pool", WA2[48:49, 256:512], DR(d["b_a_b"][l].rearrange("(o n) -> o n", o=1)))

    def dense_attn(Q, Kt, kmap, VA, vmap, O, sink):
        m = A.mark()
        PT = [A.alloc([128, 512], BF16) for _ in range(3)]
        RC = [A.alloc([64, 256], F32) for _ in range(2)]
        it = 0
        ob = None
        for s in range(2):
            for h in range(8):
                rows = slice((h % 2) * 64, (h % 2) * 64 + 64)
                sb = k.bank()
                for kt in range(2):
                    k0 = s * 256 + kt * 128
                    k.mm(sb[:, kt * 256:(kt + 1) * 256], Kt[:, kmap(h), k0:k0 + 128],
                         Q[:, h, s * 256:(s + 1) * 256])
                pt = PT[it % 3]
                k.act(pt.full(), sb[:, :], AF.Exp)
                if it % 2 == 0:
                    ob = k.bank()
                c0 = (it % 2) * 256
                for kt in range(2):
                    k.mm(ob[:, c0:c0 + 256], VA[:, s * 2 + kt, vmap(h), :], pt[:, kt * 256:(kt + 1) * 256],
                         start=(kt == 0), stop=(kt == 1))
                rc = RC[it % 2]
                if sink is not None:
                    k.act(rc.full(), ob[64:128, c0:c0 + 256], AF.Ln, bias=ESK[0:64, h:h + 1])
                else:
                    k.act(rc.full(), ob[64:128, c0:c0 + 256], AF.Ln)
                k.act(rc.full(), rc.full(), AF.Exp, scale=-1.0)
                k.tt(O[rows, h // 2, s * 256:(s + 1) * 256], ob[0:64, c0:c0 + 256], rc.full(), ALU.mult)
                it += 1
        A.release(m)

    def merge(l, wp, gcol, groups, first):
        m = A.mark()
        SG = [A.alloc([128, 512], F32) for _ in range(2)]
        TM = [A.alloc([128, 512], F32) for _ in range(2)]
        for half in range(2):
            wsp = wslot()
            wtile(("mp", gcol, half), wsp[:, 0:4, :], lambda wsp=wsp, half=half: wload(
                wsp[:, 0:4, :], wp[:, half * 512:(half + 1) * 512].rearrange("(h p) n -> p h n", p=128)))
            wsg = wslot()
            wtile(("mg", gcol, half), wsg.full(), lambda wsg=wsg, half=half: wload(
                wsg.full(), wrows(d["w_in"][l], gcol + half * 512, 512)))
            it = 0
            for g in groups:
                n = g["n"]
                for j in range(4):
                    oc = half * 4 + j
                    gb = proj_fm(wsg, j * 128, 128, (g["h"], n))
                    sg = SG[it % 2]
                    k.act(sg[:, 0:n], gb[:, 0:n], AF.Sigmoid)
                    yb = k.bank()
                    for kk in range(4):
                        k.mm(yb[:, 0:n], wsp[:, kk, j * 128:(j + 1) * 128], g["o"](kk), start=(kk == 0), stop=(kk == 3))
                    mgv = g["mg"](oc)
                    if first:
                        k.tt(mgv, yb[:, 0:n], sg[:, 0:n], ALU.mult)
                    else:
                        tm = TM[it % 2]
                        k.tt(tm[:, 0:n], yb[:, 0:n], sg[:, 0:n], ALU.mult)
                        k.tt(mgv, mgv, tm[:, 0:n], ALU.add)
                    it += 1
        A.release(m)

    ctx = dict(nc=nc, k=k, A=A, d=d, o=o, X=X, H=H, MG=MG, MGQ=MGQ, HQ=HQ, XQI=XQI, XQA=XQA, C=C, wslot=wslot, wload=wload, wrows=wrows,
               wtile=wtile, wc_valid=wc_valid, proj_fm=proj_fm, proj_tm=proj_tm, normhead=normhead, dense_attn=dense_attn, merge=merge,
               GV=GV, ESK=ESK, ALR=ALR, WA2=WA2, MODVL=MODVL, SCLL=SCLL, cur=cur, dbg_out=dbg_out, depth=depth,
               do_sample=do_sample, stop=stop, mod_steps=mod_steps, norm_group=norm_group, layer_small=layer_small)
    build_layers(ctx)
    if os.environ.get('NO_SCHED') != '1':
        k.P.schedule()
    k.P.emit()
    return nc


def build_layers(c):
    k, A, d, o, X, H, MG, C = c["k"], c["A"], c["d"], c["o"], c["X"], c["H"], c["MG"], c["C"]
    MGQ, HQ, XQI, XQA = c["MGQ"], c["HQ"], c["XQI"], c["XQA"]

    def grp_P(O):
        return dict(h=lambda kc: H[:, kc, 0:512], o=lambda kk: O[:, kk, 0:512], mg=lambda oc: MG[:, oc, 0:512], n=512)

    def merge_quarter(l, wp, gcol, O, first):
        mq = A.mark()
        OQ = A.alloc([128, 4, 256], BF16)
        k.dma_dyn(OQ.full(), O, "crel", 256, O.full())
        merge(l, wp, gcol, [dict(h=lambda kc: HQ[:, kc, :], o=lambda kk: OQ[:, kk, :], mg=lambda oc: MGQ[:, oc, :], n=256)], first)
        A.release(mq)
    wslot, wload, wrows = c["wslot"], c["wload"], c["wrows"]
    wtile, wc_valid = c["wtile"], c["wc_valid"]
    proj_fm, proj_tm, normhead, dense_attn, merge = c["proj_fm"], c["proj_tm"], c["normhead"], c["dense_attn"], c["merge"]
    GV, ESK, ALR, WA2, MODVL, cur = c["GV"], c["ESK"], c["ALR"], c["WA2"], c["MODVL"], c["cur"]
    depth = c["depth"]
    dbg_out = c["dbg_out"]

    def phaseA(l, t0, nchunk, seqs, is_prompt):
        k.P.tag = "A.%s" % ("P" if is_prompt else "S")
        ngrp = nchunk // 4
        n = nchunk * 128
        m = A.mark()
        QP = A.alloc([128, 4, n], BF16)
        KP = A.alloc([128, 4, n], BF16)
        VT = A.alloc([128, nchunk, 512], BF16)
        SIN = A.alloc([128, nchunk, 2, 2, 128], BF16)
        MB = A.alloc([128, nchunk, 2, 128], F32)
        ALAST = A.alloc([128, 4, nchunk], F32, align=64)
        R = A.alloc([128, 2, 2, 128], F32)
        w1 = wslot()
        wtile("a1", w1.full(), lambda: wload(w1.full(), wrows(d["w_in"][l], O_AQ, 512)))
        w4 = wslot()
        wload(w4[:, :, 0:16], wrows(d["w_in"][l], O_ALF, 16))
        wload(w4[:, :, 32:48], wrows(d["w_in"][l], O_ALB, 16))
        w2 = wslot()
        wtile("a2", w2.full(), lambda: wload(w2.full(), wrows(d["w_in"][l], O_AV, 512)))
        while pending:
            pending.pop(0)()

        def seq_of(cidx):
            for si, (a, b) in enumerate(seqs):
                if a <= cidx < b:
                    return si, a, b
            raise AssertionError

        def init_state(si, dr):
            if is_prompt:
                k.memset(R[:, dr, :, :], 0.0)
            else:
                src = d["st_gla"][l, dr].rearrange("h k v -> (h k) v").rearrange("(pr q) v -> q pr v", q=128)
                k.dma("sp", R[:, dr, :, :], DR(src))

        def store_state(si, dr, a_c):
            if not is_prompt:
                return
            mm_ = A.mark()
            sf = A.alloc([128, 2, 128], F32)
            for pr in range(2):
                k.ts(sf[:, pr, :], R[:, dr, pr, :], ALAST[:, dr * 2 + pr, a_c:a_c + 1], ALU.mult)
            dst = o["o_gla"][si, l, dr].rearrange("h k v -> (h k) v").rearrange("(pr q) v -> q pr v", q=128)
            k.dma("sp", DR(dst), sf.full())
            A.release(mm_)

        def scan_step(cidx, dr, msrc):
            si, a, b = seq_of(cidx)
            first = (cidx == a) if dr == 0 else (cidx == b - 1)
            prev = cidx - 1 if dr == 0 else cidx + 1
            if first:
                init_state(si, dr)
            for pr in range(2):
                for hf in range(2):
                    rows = slice(hf * 64, hf * 64 + 64)
                    if first:
                        k.cp(SIN[rows, cidx, dr, pr, :], R[rows, dr, pr, :])
                        k.tt(R[rows, dr, pr, :], R[rows, dr, pr, :], msrc(pr, hf), ALU.add)
                    else:
                        al = ALAST[rows, dr * 2 + pr, prev:prev + 1]
                        k.ts(SIN[rows, cidx, dr, pr, :], R[rows, dr, pr, :], al, ALU.mult)
                        k.stt(R[rows, dr, pr, :], R[rows, dr, pr, :], al, msrc(pr, hf), ALU.mult, ALU.add)
            last = (cidx == b - 1) if dr == 0 else (cidx == a)
            if last:
                store_state(si, dr, cidx)

        m1 = A.mark()
        QK32 = A.alloc([128, 4, 512], F32)
        EP = A.alloc([128, 4, 256], F32)
        EN = A.alloc([128, 4, 256], F32)
        E1 = A.alloc([128, 512], F32)
        LA = A.alloc([128, 512], F32)
        LAH = A.alloc([128, 512], BF16)
        LAL = A.alloc([128, 512], BF16)
        KT = [A.alloc([128, 4, 128], BF16) for _ in range(2)]
        for gi in range(ngrp):
            htok = slice(t0 + gi * 512, t0 + gi * 512 + 512)
            pb = k.bank()
            for kc in range(8):
                k.mm(pb[0:64, :], w4[:, kc, 0:64], H[:, kc, htok], start=(kc == 0), stop=(kc == 7))
            k.cp(ALR[0:16, gi * 512:gi * 512 + 512], pb[0:16, :], "act")
            k.cp(ALR[32:48, gi * 512:gi * 512 + 512], pb[32:48, :], "dve")
            for j in range(4):
                pq = proj_fm(w1, j * 128, 128, htok)
                k.cpx(QK32[:, j, :], pq[:, :])
            if PA_STOP <= 1:
                continue
            for hf2 in range(2):
                cb = [k.hold(), k.hold()]
                for c2 in range(2):
                    cl = gi * 4 + hf2 * 2 + c2
                    tk = cl * 128
                    zb = k.bank()
                    k.mm(zb[:, :], ALR[0:64, tk:tk + 128], WA2.full())
                    if PA_STOP <= 1.2:
                        k.cp(E1.full(), zb[:, :])
                        continue
                    k.act(E1.full(), zb[:, :], AF.Exp, scale=-1.0)
                    k.act(LA.full(), E1.full(), AF.Ln, bias=1.0)
                    if PA_STOP <= 1.4:
                        continue
                    k.cp(LAH.full(), LA.full(), "act")
                    k.tt(LAL.full(), LA.full(), LAH.full(), ALU.subtract)
                    for dr in range(2):
                        tri = C["trif"] if dr == 0 else C["trib"]
                        for cc in range(2):
                            osl = cb[dr][:, cc * 256 + c2 * 128:cc * 256 + c2 * 128 + 128]
                            k.mm(osl, LAH[:, dr * 256 + cc * 128:dr * 256 + cc * 128 + 128], tri.full(), True, False)
                            k.mm(osl, LAL[:, dr * 256 + cc * 128:dr * 256 + cc * 128 + 128], tri.full(), False, True)
                if PA_STOP <= 1.6:
                    k.unhold(cb[0])
                    k.unhold(cb[1])
                    continue
                for dr in range(2):
                    cv = cb[dr].v(cb[dr].ap.rearrange("p (a b) -> p a b", a=2))
                    k.act(EP[:, dr * 2:dr * 2 + 2, :], cv, AF.Exp)
                    k.act(EN[:, dr * 2:dr * 2 + 2, :], cv, AF.Exp, scale=-1.0)
                k.unhold(cb[0])
                k.unhold(cb[1])
                if PA_STOP <= 2:
                    continue
                cbase = gi * 4 + hf2 * 2
                k.cp(ALAST[:, 0:2, cbase:cbase + 2], EP[:, 0:2, 127:256:128])
                k.cp(ALAST[:, 2:4, cbase:cbase + 2], EP[:, 2:4, 0:256:128])
                tcol = slice(gi * 512 + hf2 * 256, gi * 512 + hf2 * 256 + 256)
                hcol = slice(hf2 * 256, hf2 * 256 + 256)
                for dc in range(4):
                    cc = dc % 2
                    k.stt(QP[:, dc, tcol], QK32[:, cc, hcol], 0.125, EP[:, dc, :], ALU.mult, ALU.mult)
                    k.tt(KP[:, dc, tcol], QK32[:, 2 + cc, hcol], EN[:, dc, :], ALU.mult)
                if PA_STOP <= 3:
                    continue
                for c2 in range(2):
                    cl = gi * 4 + hf2 * 2 + c2
                    ctok = slice(cl * 128, cl * 128 + 128)
                    tb = k.bank()
                    tbv = tb.bf()
                    for dc in range(4):
                        k.tr(tb.v(tbv[:, dc * 128:(dc + 1) * 128]), KP[:, dc, ctok], C["ident_bf"].full())
                    kt = KT[cl % 2]
                    k.cp(kt.full(), tb.v(tbv[:, 0:512].rearrange("p (a b) -> p a b", a=4)), "act")
                    vb = proj_tm(w2, 0, 512, t0 + cl * 128)
                    k.cp(VT[:, cl, :], vb[:, :], "dve")
                    for dr in range(2):
                        mb = k.bank()
                        for pr in range(2):
                            k.mm(mb[:, pr * 256:(pr + 1) * 256], kt[:, dr * 2 + pr, :], VT[:, cl, pr * 256:(pr + 1) * 256])
                        mv = mb.ap.rearrange("p (a b) -> p a b", a=2)
                        if dr == 0:
                            scan_step(cl, 0, lambda pr, hf, mv=mv, mb=mb: mb.v(
                                mv[hf * 64:hf * 64 + 64, pr, hf * 128:hf * 128 + 128]))
                        else:
                            for hf in range(2):
                                k.cp(MB[hf * 64:hf * 64 + 64, cl, :, :],
                                     mb.v(mv[hf * 64:hf * 64 + 64, :, hf * 128:hf * 128 + 128]), "act")
        A.release(m1)
        if PA_STOP <= 4:
            A.release(m)
            return
        for cl in reversed(range(nchunk)):
            scan_step(cl, 1, lambda pr, hf, cl=cl: MB[hf * 64:hf * 64 + 64, cl, pr, :])
        if PA_STOP <= 5:
            A.release(m)
            return
        OA = A.alloc([128, 4, n], BF16)
        m3 = A.mark()
        w3 = wslot()
        wtile("a3", w3.full(), lambda: wload(w3.full(), wrows(d["w_in"][l], O_AR, 512)))
        SIL = A.alloc([128, 4, 512], BF16)
        ATM = [A.alloc([128, 4, 128], BF16) for _ in range(2)]
        OFs = [A.alloc([128, 512], F32) for _ in range(2)]
        SQs = [A.alloc([128, 512], BF16) for _ in range(2)]
        RSs = [A.alloc([128, 512], F32) for _ in range(2)]
        for gi in range(ngrp):
            htok = slice(t0 + gi * 512, t0 + gi * 512 + 512)
            for hd in range(4):
                pb = proj_fm(w3, hd * 128, 128, htok)
                k.act(SIL[:, hd, :], pb[:, :], AF.Silu)
            ob = [k.hold() for _ in range(4)]
            for c4 in range(4):
                cl = gi * 4 + c4
                ctok = slice(cl * 128, cl * 128 + 128)
                for dr in range(2):
                    mk = C["maskf"] if dr == 0 else C["maskb"]
                    mkf = mk.full()
                    for hf in range(2):
                        ab = k.bank()
                        rows = slice(hf * 64, hf * 64 + 64)
                        for pr in range(2):
                            k.mm(ab[:, pr * 128:(pr + 1) * 128], KP[rows, dr * 2 + pr, ctok], QP[rows, dr * 2 + pr, ctok])
                        k.tt(ATM[dr][:, hf:4:2, :], ab.v(ab.ap[:, 0:256].rearrange("p (a b) -> p a b", a=2)),
                             V(mkf.ap.unsqueeze(1).to_broadcast([128, 2, 128]), mkf.bufs), ALU.mult)
                for hd in range(4):
                    rows = slice((hd % 2) * 64, (hd % 2) * 64 + 64)
                    col = slice(c4 * 128, c4 * 128 + 128)
                    k.mm(ob[hd][:, col], VT[:, cl, hd * 128:(hd + 1) * 128], ATM[0][:, hd, :], True, False)
                    k.mm(ob[hd][:, col], VT[:, cl, hd * 128:(hd + 1) * 128], ATM[1][:, hd, :], False, False)
                    k.mm(ob[hd][:, col], SIN[rows, cl, 0, hd // 2, :], QP[rows, hd // 2, ctok], False, False)
                    k.mm(ob[hd][:, col], SIN[rows, cl, 1, hd // 2, :], QP[rows, 2 + hd // 2, ctok], False, True)
            gtok = slice(gi * 512, gi * 512 + 512)
            for hd in range(4):
                OF, SQ, RS = OFs[hd % 2], SQs[hd % 2], RSs[hd % 2]
                k.cp(OF.full(), ob[hd][:, :], "dve")
                k.act(SQ.full(), ob[hd][:, :], AF.Square)
                k.unhold(ob[hd])
                sb = k.bank()
                k.mm(sb[:, :], C["ones_all"].full(), SQ.full())
                k.rstd(RS.full(), sb[:, :], 128.0)
                k.tt(OF.full(), OF.full(), RS.full(), ALU.mult)
                k.stt(OA[:, hd, gtok], OF.full(), GV[:, 4:5], SIL[:, hd, :], ALU.mult, ALU.mult)
        A.release(m3)
        if PA_STOP <= 6:
            A.release(m)
            return
        if is_prompt:
            merge(l, d["w_pa"][l], O_GA, [grp_P(OA)], True)
        else:
            merge_quarter(l, d["w_pa"][l], O_GA, OA, True)
        A.release(m)

    def phaseB_P(l):
        k.P.tag = "B.P"
        tok = slice(0, 512)
        m = A.mark()
        QN = A.alloc([128, 8, 512], BF16)
        k.memset(QN.full(), 0.0)
        KN = A.alloc([128, 2, 512], BF16)
        KNF = A.alloc([128, 2, 512], F32)
        VA = A.alloc([128, 4, 2, 128], BF16)
        OB = A.alloc([128, 4, 512], BF16)
        KO = [A.alloc([128, 128], F32) for _ in range(2)]
        w1 = wslot()
        wtile("b1", w1.full(), lambda: wload(w1.full(), wrows(d["w_in"][l], O_BQ, 512)))
        w2 = wslot()

        def ld_b2():
            for kv in range(2):
                for dup in range(2):
                    wload(w2[:, :, kv * 128 + dup * 64:kv * 128 + dup * 64 + 64], wrows(d["w_in"][l], O_BK + kv * 64, 64))
            wload(w2[:, :, 256:384], wrows(d["w_in"][l], O_BV, 128))
        wtile("b2", w2[:, :, 0:384], ld_b2)
        for cc in range(4):
            pb = proj_fm(w1, cc * 128, 128, tok)
            normhead(pb, None, None, GV[:, 0:1], out_pair=(QN[0:64, 2 * cc, :], QN[64:128, 2 * cc + 1, :]))
        for kv in range(2):
            pb = proj_fm(w2, kv * 128, 128, tok)
            normhead(pb, KNF[:, kv, :], KN[:, kv, :], GV[:, 1:2])
        k.memset(VA[:, :, :, 64:128], 1.0)
        for tix in range(4):
            ttok = slice(tix * 128, tix * 128 + 128)
            pb = k.bank()
            for kv in range(2):
                k.tr(pb[:, kv * 128:(kv + 1) * 128], KNF[:, kv, ttok], C["ident_f"].full())
            ko = KO[0]
            kov = ko.full()
            k.cp(V(kov.ap.rearrange("p (a b) -> p a b", a=2), kov.bufs),
                 pb.v(pb.ap[:, 0:256].rearrange("p (a b) -> p a b", a=2)[:, :, 0:64]), "act")
            k.dma("sp", DR(o["o_swk"][tix // 2, l, (tix % 2) * 128:(tix % 2) * 128 + 128, :]), kov)
            pv = proj_tm(w2, 256, 128, tix * 128)
            vo = KO[1]
            k.cp(vo.full(), pv[:, 0:128], "dve")
            k.dma("sp", DR(o["o_swv"][tix // 2, l, (tix % 2) * 128:(tix % 2) * 128 + 128, :]), vo.full())
            k.cp(VA[:, tix, :, 0:64], pv.v(pv.ap[:, 0:128].rearrange("p (a b) -> p a b", a=2)), "act")
        dense_attn(QN, KN, lambda h: h // 4, VA, lambda h: h // 4, OB, True)
        merge(l, d["w_pb"][l], O_GB, [grp_P(OB)], False)
        A.release(m)

    def phaseC_P(l):
        k.P.tag = "C.P"
        tok = slice(0, 512)
        m = A.mark()
        QN = A.alloc([128, 8, 512], BF16)
        k.memset(QN.full(), 0.0)
        KN = A.alloc([128, 4, 512], BF16)
        KNF = A.alloc([128, 4, 512], F32)
        VA = A.alloc([128, 4, 8, 128], BF16)
        OC = A.alloc([128, 4, 512], BF16)
        KO = [A.alloc([128, 512], F32) for _ in range(2)]
        wqk = []
        wv = []
        for hh in range(2):
            wa = wslot()
            wtile(("c1", hh), wa.full(), lambda wa=wa, hh=hh: (
                wload(wa[:, :, 0:256], wrows(d["w_in"][l], O_CQ + hh * 256, 256)),
                wload(wa[:, :, 256:512], wrows(d["w_in"][l], O_CK + hh * 256, 256))))
            wqk.append(wa)
            for c2 in range(2):
                cc = hh * 2 + c2
                pb = proj_fm(wa, c2 * 128, 128, tok)
                normhead(pb, None, None, GV[:, 2:3], out_pair=(QN[0:64, 2 * cc, :], QN[64:128, 2 * cc + 1, :]))
                pb = proj_fm(wa, 256 + c2 * 128, 128, tok)
                normhead(pb, KNF[:, cc, :], KN[:, cc, :], GV[:, 3:4])
        for hh in range(2):
            wb = wslot()
            wtile(("c3", hh), wb[:, :, 0:256], lambda wb=wb, hh=hh: wload(
                wb[:, :, 0:256], wrows(d["w_in"][l], O_CV + hh * 256, 256)))
            wv.append(wb)
        k.memset(VA[:, :, :, 64:128], 1.0)
        for tix in range(4):
            ttok = slice(tix * 128, tix * 128 + 128)
            pb = k.bank()
            for cc in range(4):
                k.tr(pb[:, cc * 128:(cc + 1) * 128], KNF[:, cc, ttok], C["ident_f"].full())
            k.cp(KO[0].full(), pb[:, :], "act")
            k.dma("sp", DR(o["o_nak"][tix // 2, l, (tix % 2) * 128:(tix % 2) * 128 + 128, :]), KO[0].full())
            pv = k.bank()
            for hh in range(2):
                for kc in range(8):
                    k.mm(pv[:, hh * 256:(hh + 1) * 256], H[:, kc, tix * 128:tix * 128 + 128], wv[hh][:, kc, 0:256],
                         start=(kc == 0), stop=(kc == 7))
            k.cp(KO[1].full(), pv[:, :], "dve")
            k.dma("sp", DR(o["o_nav"][tix // 2, l, (tix % 2) * 128:(tix % 2) * 128 + 128, :]), KO[1].full())
            k.cp(VA[:, tix, :, 0:64], pv.v(pv.ap.rearrange("p (a b) -> p a b", a=8)), "act")
        dense_attn(QN, KN, lambda h: h // 2, VA, lambda h: h, OC, None)
        merge(l, d["w_pc"][l], O_GC, [grp_P(OC)], False)
        A.release(m)

    rp_flip = [0]

    def rope_apply(qf, qfb, out_bf, tsl):
        mr = A.mark()
        rp_flip[0] ^= 1
        if rp_flip[0]:
            A.alloc([128, 1024], F32)
        t1 = A.alloc([128, 512], F32)
        t2 = A.alloc([128, 512], F32)
        pr = k.bank()
        k.mm(pr[:, :], C["rope_pT"].full(), qfb)
        k.tt(t1.full(), qf, C["rope_cos"][:, tsl], ALU.mult)
        k.tt(t2.full(), pr[:, :], C["rope_sin"][:, tsl], ALU.mult)
        if isinstance(out_bf, tuple):
            k.tt(out_bf[0], t1[0:64, :], t2[0:64, :], ALU.add, eng="pool")
            k.tt(out_bf[1], t1[64:128, :], t2[64:128, :], ALU.add, eng="pool")
        else:
            k.tt(out_bf, t1.full(), t2.full(), ALU.add, eng="pool")
        A.release(mr)

    def phaseB_S(l, extra=()):
        k.P.tag = "B.S"
        m = A.mark()
        QN = A.alloc([128, 8, TS], BF16)
        k.memset(QN.full(), 0.0, eng="pool")
        KN = A.alloc([128, 2, TS], BF16)
        VA = A.alloc([128, 8, 2, 128], BF16)
        KC = A.alloc([128, 2, 512], BF16)
        VC = A.alloc([128, 4, 2, 128], BF16)
        OB = A.alloc([128, 4, TS], BF16)
        w1 = wslot()
        wtile("b1", w1.full(), lambda: wload(w1.full(), wrows(d["w_in"][l], O_BQ, 512)))
        w2 = wslot()

        def ld_b2():
            for kv in range(2):
                for dup in range(2):
                    wload(w2[:, :, kv * 128 + dup * 64:kv * 128 + dup * 64 + 64], wrows(d["w_in"][l], O_BK + kv * 64, 64))
            wload(w2[:, :, 256:384], wrows(d["w_in"][l], O_BV, 128))
        wtile("b2", w2[:, :, 0:384], ld_b2)
        m2 = A.mark()
        QFs = [A.alloc([128, 512], F32) for _ in range(2)]
        QFBs = [A.alloc([128, 512], BF16) for _ in range(2)]
        qi = 0
        for g in range(2):
            htok = slice(TP + g * 512, TP + g * 512 + 512)
            tsl = slice(g * 512, g * 512 + 512)
            for cc in range(4):
                QF, QFB = QFs[qi % 2], QFBs[qi % 2]
                qi += 1
                pb = proj_fm(w1, cc * 128, 128, htok)
                normhead(pb, QF.full(), QFB.full(), GV[:, 0:1])
                rope_apply(QF.full(), QFB.full(), (QN[0:64, 2 * cc, tsl], QN[64:128, 2 * cc + 1, tsl]), tsl)
            for kv in range(2):
                QF, QFB = QFs[qi % 2], QFBs[qi % 2]
                qi += 1
                pb = proj_fm(w2, kv * 128, 128, htok)
                normhead(pb, QF.full(), QFB.full(), GV[:, 1:2])
                rope_apply(QF.full(), QFB.full(), KN[:, kv, tsl], tsl)
        A.release(m2)
        k.memset(VA[:, :, :, 64:128], 1.0, eng="pool")
        k.memset(VC[:, :, :, 64:128], 1.0, eng="pool")
        for tix in range(8):
            pv = proj_tm(w2, 256, 128, TP + tix * 128)
            k.cpx(VA[:, tix, :, 0:64], pv.v(pv.ap[:, 0:128].rearrange("p (a b) -> p a b", a=2)))
        m2 = A.mark()
        ST = [A.alloc([128, 256], F32) for _ in range(2)]
        SV = [A.alloc([128, 128], F32) for _ in range(2)]
        for i in range(4):
            st = ST[i % 2]
            src = d["cswk"][l, i * 128:(i + 1) * 128, :].rearrange("p (a b) -> p a b", a=2)
            stv = st.full()
            for dup in range(2):
                k.dma("sp", V(stv.ap.rearrange("p (a c b) -> p a c b", a=2, c=2)[:, :, dup, :], stv.bufs), DR(src))
            pb = k.bank()
            for kv in range(2):
                k.tr(pb[:, kv * 128:(kv + 1) * 128], st[:, kv * 128:(kv + 1) * 128], C["ident_f"].full())
            k.cpx(KC[:, :, i * 128:(i + 1) * 128], pb.v(pb.ap[:, 0:256].rearrange("p (a b) -> p a b", a=2)))
            sv = SV[i % 2]
            k.dma("sp", sv.full(), DR(d["cswv"][l, i * 128:(i + 1) * 128, :]))
            svv = sv.full()
            k.cpx(VC[:, i, :, 0:64], V(svv.ap.rearrange("p (a b) -> p a b", a=2), svv.bufs))
        A.release(m2)
        m2 = A.mark()
        PT = [A.alloc([128, 512], BF16) for _ in range(3)]
        RC = [A.alloc([64, 512], F32) for _ in range(2)]
        it = 0
        for h in range(8):
            if extra:
                tg_ = k.P.tag
                k.P.tag = "pre"
                extra.pop(0)()
                k.P.tag = tg_
            rows = slice((h % 2) * 64, (h % 2) * 64 + 64)
            kv = h // 4
            ob = [k.hold(), k.hold()]
            for qh in range(2):
                qsl = slice(qh * 512, qh * 512 + 512)
                for i in range(4):
                    sa = k.bank()
                    k.mm(sa[:, :], KC[:, kv, i * 128:(i + 1) * 128], QN[:, h, qsl])
                    pt = PT[it % 3]
                    it += 1
                    k.act(pt.full(), sa[:, :], AF.Exp)
                    k.mm(ob[qh][:, :], VC[:, i, kv, :], pt.full(), i == 0, False)
            for kt in range(8):
                qb0, qb1 = max(0, kt - 1), min(7, kt + 1)
                nq = (qb1 - qb0 + 1) * 128
                q0 = qb0 * 128
                sbk = k.bank()
                k.mm(sbk[:, 0:nq], KN[:, kv, kt * 128:(kt + 1) * 128], QN[:, h, q0:q0 + nq], True, False)
                for qb in range(qb0, qb1 + 1):
                    if qb == kt:
                        continue
                    msk = C["band_next"] if qb == kt + 1 else C["band_prev"]
                    k.mm(sbk[:, (qb - qb0) * 128:(qb - qb0 + 1) * 128], C["ident_bf"].full(), msk.full(), False, False)
                pt = PT[it % 3]
                it += 1
                k.act(pt[:, 0:nq], sbk[:, 0:nq], AF.Exp)
                segs = []
                a_ = q0
                while a_ < q0 + nq:
                    b_ = min(q0 + nq, (a_ // 512 + 1) * 512)
                    segs.append((a_, b_))
                    a_ = b_
                for (a_, b_) in segs:
                    qh = a_ // 512
                    k.mm(ob[qh][:, a_ - qh * 512:b_ - qh * 512], VA[:, kt, kv, :], pt[:, a_ - q0:b_ - q0], False, False)
            for qh in range(2):
                rc = RC[qh]
                k.act(rc.full(), ob[qh][64:128, :], AF.Ln, bias=ESK[0:64, h:h + 1])
                k.act(rc.full(), rc.full(), AF.Exp, scale=-1.0)
                k.tt(OB[rows, h // 2, qh * 512:(qh + 1) * 512], ob[qh][0:64, :], rc.full(), ALU.mult)
                k.unhold(ob[qh])
        A.release(m2)
        merge_quarter(l, d["w_pb"][l], O_GB, OB, False)
        A.release(m)

    def na_runs(mt):
        runs = []
        if mt <= 3:
            runs.append((0, 4, False))
        lo, hi = max(5, 2 * mt - 3), min(11, 2 * mt + 5)
        if lo <= hi:
            if lo <= 7 and hi >= 8:
                runs.append((lo, 7, True))
                runs.append((8, hi, True))
            else:
                runs.append((lo, hi, True))
        if mt >= 4:
            runs.append((12, 15, False))
        return runs

    def phaseC_S(l, extra=()):
        k.P.tag = "C.S"
        m = A.mark()
        OC = A.alloc([128, 4, TS], BF16)
        for hh in range(2):
            mh = A.mark()
            QN = A.alloc([128, 4, TS], BF16)
            k.memset(QN.full(), 0.0, eng="pool")
            KN = A.alloc([128, 2, TS], BF16)
            VA = A.alloc([128, 8, 4, 128], BF16)
            KC = A.alloc([128, 2, 512], BF16)
            VC = A.alloc([128, 4, 4, 128], BF16)
            w1 = wslot()
            wtile(("c1", hh), w1.full(), lambda w1=w1, hh=hh: (
                wload(w1[:, :, 0:256], wrows(d["w_in"][l], O_CQ + hh * 256, 256)),
                wload(w1[:, :, 256:512], wrows(d["w_in"][l], O_CK + hh * 256, 256))))
            w3 = wslot()
            wtile(("c3", hh), w3[:, :, 0:256], lambda w3=w3, hh=hh: wload(
                w3[:, :, 0:256], wrows(d["w_in"][l], O_CV + hh * 256, 256)))
            for g in range(2):
                htok = slice(TP + g * 512, TP + g * 512 + 512)
                tsl = slice(g * 512, g * 512 + 512)
                for cc in range(2):
                    pb = proj_fm(w1, cc * 128, 128, htok)
                    normhead(pb, None, None, GV[:, 2:3], out_pair=(QN[0:64, 2 * cc, tsl], QN[64:128, 2 * cc + 1, tsl]))
                    pb = proj_fm(w1, 256 + cc * 128, 128, htok)
                    normhead(pb, None, KN[:, cc, tsl], GV[:, 3:4])
            k.memset(VA[:, :, :, 64:128], 1.0, eng="pool")
            k.memset(VC[:, :, :, 64:128], 1.0, eng="pool")
            for tix in range(8):
                pv = proj_tm(w3, 0, 256, TP + tix * 128)
                k.cpx(VA[:, tix, :, 0:64], pv.v(pv.ap[:, 0:256].rearrange("p (a b) -> p a b", a=4)))
            m2 = A.mark()
            ST = [A.alloc([128, 256], F32) for _ in range(2)]
            SV = [A.alloc([128, 256], F32) for _ in range(2)]
            for i in range(4):
                st = ST[i % 2]
                k.dma("sp", st.full(), DR(d["cnak"][l, i * 128:(i + 1) * 128, hh * 256:hh * 256 + 256]))
                pb = k.bank()
                for cc in range(2):
                    k.tr(pb[:, cc * 128:(cc + 1) * 128], st[:, cc * 128:(cc + 1) * 128], C["ident_f"].full())
                k.cpx(KC[:, :, i * 128:(i + 1) * 128], pb.v(pb.ap[:, 0:256].rearrange("p (a b) -> p a b", a=2)))
                sv = SV[i % 2]
                k.dma("sp", sv.full(), DR(d["cnav"][l, i * 128:(i + 1) * 128, hh * 256:hh * 256 + 256]))
                svv = sv.full()
                k.cpx(VC[:, i, :, 0:64], V(svv.ap.rearrange("p (a b) -> p a b", a=4), svv.bufs))
            A.release(m2)
            m2 = A.mark()
            STG = A.alloc([128, 16, 64], F32)
            SFUL = A.alloc([128, 16, 64], BF16)
            SINT = A.alloc([128, 16, 64], BF16)
            PT = [A.alloc([128, 512], BF16) for _ in range(2)]
            RC = A.alloc([64, 512], F32)
            it = 0
            for hl in range(4):
                if extra:
                    tg_ = k.P.tag
                    k.P.tag = "pre"
                    extra.pop(0)()
                    k.P.tag = tg_
                h = hh * 4 + hl
                rows = slice((hl % 2) * 64, (hl % 2) * 64 + 64)
                cc = hl // 2
                base = d["rpb_pad"][l, h]
                hk = bass.AP(base.tensor, base.offset, [[1, 64], [127, 15], [1, 64]])
                k.memset(STG.full(), 0.0)
                k.dma("sp", STG[0:64, 0:15, :], DR(hk))
                k.dma("sp", STG[64:128, 1:16, :], DR(hk))
                nmk = C["na_mask"].full()
                k.tt(SFUL.full(), STG.full(), V(nmk.ap.unsqueeze(1).to_broadcast([128, 16, 64]), nmk.bufs), ALU.add)
                k.cp(SINT.full(), SFUL.full(), "act")
                k.memset(SINT[0:64, 0:4, :], NEG, eng="pool")
                k.memset(SINT[0:64, 12:16, :], NEG, eng="pool")
                k.memset(SINT[64:128, 0:5, :], NEG, eng="pool")
                k.memset(SINT[64:128, 13:16, :], NEG, eng="pool")
                ob = [k.hold(), k.hold()]
                for qh in range(2):
                    qsl = slice(qh * 512, qh * 512 + 512)
                    for i in range(4):
                        sb = k.bank()
                        k.mm(sb[:, :], KC[:, cc, i * 128:(i + 1) * 128], QN[:, hl, qsl])
                        pt = PT[it % 2]
                        it += 1
                        k.act(pt.full(), sb[:, :], AF.Exp)
                        k.mm(ob[qh][:, :], VC[:, i, hl, :], pt.full(), i == 0, False)
                for mt in range(8):
                    for (r0, r1, interior) in na_runs(mt):
                        nq = (r1 - r0 + 1) * 64
                        q0 = r0 * 64
                        qh = q0 // 512
                        b0 = 7 + r0 - 2 * mt
                        strip = SINT if interior else SFUL
                        sb = k.bank()
                        k.mm(sb[:, 0:nq], KN[:, cc, mt * 128:(mt + 1) * 128], QN[:, hl, q0:q0 + nq], True, False)
                        sv_ = strip[:, b0:b0 + (r1 - r0 + 1), :]
                        k.mm(sb[:, 0:nq], C["jj"].full(), V(sv_.ap.rearrange("p a b -> p (a b)"), sv_.bufs), False, True)
                        pt = PT[it % 2]
                        it += 1
                        k.act(pt[:, 0:nq], sb[:, 0:nq], AF.Exp)
                        k.mm(ob[qh][:, q0 - qh * 512:q0 - qh * 512 + nq], VA[:, mt, hl, :], pt[:, 0:nq], False, False)
                for qh in range(2):
                    k.recip_act(RC.full(), ob[qh][64:128, :])
                    k.tt(OC[rows, h // 2, qh * 512:(qh + 1) * 512], ob[qh][0:64, :], RC.full(), ALU.mult)
                    k.unhold(ob[qh])
            A.release(m2)
            A.release(mh)
        merge_quarter(l, d["w_pc"][l], O_GC, OC, False)
        A.release(m)

    def wo_groups(l, groups):
        k.P.tag = "wo"
        for half in range(2):
            ws = wslot()
            wtile(("wo", half), ws.full(), lambda ws=ws, half=half: wload(ws.full(), wrows(d["w_o"][l], half * 512, 512)))
            for g in groups:
                n = g["n"]
                for j in range(4):
                    oc = half * 4 + j
                    pb = k.bank()
                    for kc in range(8):
                        k.mm(pb[:, 0:n], ws[:, kc, j * 128:(j + 1) * 128], g["mg"](kc), start=(kc == 0), stop=(kc == 7))
                    xv = g["x"](oc)
                    k.stt(xv, pb[:, 0:n], MODVL[cur[0]][:, 16 + oc, g["ci"]:g["ci"] + 1], xv, ALU.mult, ALU.add)

    def norm_q(l, XQ, HQ2):
        mn = A.mark()
        sq = [A.alloc([128, 256], BF16) for _ in range(2)]
        rs = A.alloc([128, 256], F32)
        tmp = [A.alloc([128, 256], F32) for _ in range(2)]
        sb = k.bank()
        for oc in range(8):
            s_ = sq[oc % 2]
            k.act(s_.full(), XQ[:, oc, :], AF.Square)
            k.mm(sb[:, 0:256], C["ones_all"].full(), s_.full(), start=(oc == 0), stop=(oc == 7))
        k.rstd(rs.full(), sb[:, 0:256], 1024.0)
        for oc in range(8):
            t_ = tmp[oc % 2]
            k.stt(t_.full(), XQ[:, oc, :], c["SCLL"][cur[0]][:, 1, 1, oc:oc + 1], rs.full(), ALU.mult, ALU.mult)
            k.act(HQ2[:, oc, :], t_.full(), AF.Identity, bias=MODVL[cur[0]][:, 24 + oc, 1:2])
        A.release(mn)

    pending = []

    def tail(l, extra):
        m = A.mark()
        XQ = A.alloc([128, 8, 256], F32)
        HQ2 = A.alloc([128, 8, 256], BF16)
        k.dma_dyn(XQ.full(), X, "cabs", 256, X[:, :, TP:TT])
        wo_groups(l, [dict(mg=lambda kc: MGQ[:, kc, :], x=lambda oc: XQ[:, oc, :], n=256, ci=1)])
        k.P.tag = "mlp"
        c["norm_group"](l, 1, 0)
        norm_q(l, XQ, HQ2)
        G = [dict(h=lambda kc: H[:, kc, 0:512], u=slice(0, 512), x=lambda oc: X[:, oc, 0:512], n=512, ci=0),
             dict(h=lambda kc: HQ2[:, kc, :], u=slice(512, 768), x=lambda oc: XQ[:, oc, :], n=256, ci=1)]
        U = A.alloc([128, 16, 768], BF16)
        RL = [A.alloc([128, 512], BF16) for _ in range(2)]
        ri = 0
        for hh in range(2):
            for t4 in range(4):
                ws = wslot()
                wload(ws.full(), wrows(d["w_fc1"][l], hh * 2048 + t4 * 512, 512))
                for g in G:
                    n = g["n"]
                    for j in range(4):
                        pb = proj_fm(ws, j * 128, 128, (g["h"], n))
                        rl = RL[ri % 2]
                        ri += 1
                        uv = U[:, t4 * 4 + j, g["u"]]
                        if j % 2 == 0:
                            k.act(rl[:, 0:n], pb[:, 0:n], AF.Relu)
                            k.tt(uv, rl[:, 0:n], rl[:, 0:n], ALU.mult)
                        else:
                            k.ts(rl[:, 0:n], pb[:, 0:n], 0.0, ALU.max)
                            k.tt(uv, rl[:, 0:n], rl[:, 0:n], ALU.mult, eng="pool")
                if extra:
                    extra.pop(0)()
            for oh in range(2):
                wsa = wslot()
                wsb = wslot()
                r0 = hh * 2048
                wload(wsa.full(), d["w_fc2"][l][r0:r0 + 1024, oh * 512:oh * 512 + 512].rearrange("(kc p) n -> p kc n", p=128))
                wload(wsb.full(), d["w_fc2"][l][r0 + 1024:r0 + 2048, oh * 512:oh * 512 + 512].rearrange("(kc p) n -> p kc n", p=128))
                for g in G:
                    n = g["n"]
                    for j in range(4):
                        oc = oh * 4 + j
                        pb = k.bank()
                        for kk in range(16):
                            wsx = wsa if kk < 8 else wsb
                            k.mm(pb[:, 0:n], wsx[:, kk % 8, j * 128:(j + 1) * 128], U[:, kk, g["u"]], start=(kk == 0), stop=(kk == 15))
                        xv = g["x"](oc)
                        k.stt(xv, pb[:, 0:n], MODVL[cur[0]][:, 40 + oc, g["ci"]:g["ci"] + 1], xv, ALU.mult, ALU.add)
                if extra:
                    extra.pop(0)()
        while extra:
            extra.pop(0)()
        if l + 1 < depth:
            k.dma("sp", V(XQI.ap.rearrange("(oc p) t -> p oc t", p=128), XQI.bufs), XQ.full())

            def gather():
                tg = k.P.tag
                k.P.tag = "gather"
                k.allgather(XQA, XQI, [[0, 1, 2, 3], [4, 5, 6, 7]])
                for r in range(4):
                    k.dma("sp", X[:, :, TP + r * 256:TP + (r + 1) * 256],
                          V(XQA.ap[r * 1024:(r + 1) * 1024, :].rearrange("(oc p) t -> p oc t", p=128), XQA.bufs))
                k.P.tag = tg
            pending.append(gather)
        else:
            st2 = [A.alloc([128, D], F32) for _ in range(2)]
            for tix in range(2):
                s_ = st2[tix]
                for q4 in range(2):
                    pb = k.bank()
                    for j in range(4):
                        oc = q4 * 4 + j
                        k.tr(pb[:, j * 128:(j + 1) * 128], XQ[:, oc, tix * 128:(tix + 1) * 128], C["ident_f"].full())
                    k.cpx(s_[:, q4 * 512:(q4 + 1) * 512], pb[:, :])
                k.dma("sp", DR(o["ysq"][tix * 128:(tix + 1) * 128, :]), s_.full())
        A.release(m)

    def store_tokens(dst, ntok, col0):
        m = A.mark()
        st2 = [A.alloc([128, D], F32) for _ in range(2)]
        for tix in range(ntok // 128):
            s_ = st2[tix % 2]
            c0 = col0 + tix * 128
            for q4 in range(2):
                pb = k.bank()
                for j in range(4):
                    oc = q4 * 4 + j
                    k.tr(pb[:, j * 128:(j + 1) * 128], X[:, oc, c0:c0 + 128], C["ident_f"].full())
                k.cpx(s_[:, q4 * 512:(q4 + 1) * 512], pb[:, :])
            k.dma("sp", DR(dst[tix * 128:(tix + 1) * 128, :]), s_.full())
        A.release(m)

    stop = c["stop"]
    for l in range(depth):
        if stop == "load":
            break
        cur[0] = l % 2
        k.P.tag = "pre"
        if l == 0:
            c["layer_small"](0)
            for f_ in c["mod_steps"](0):
                f_()
        c["norm_group"](l, 0, 0)
        wc_valid.clear()
        ex = []
        if l + 1 < depth:
            ex = [lambda l=l: c["layer_small"](l + 1)] + c["mod_steps"](l + 1)
        phaseA(l, 0, 4, [(0, 2), (2, 4)], True)
        k.P.tag = "pre"
        c["norm_group"](l, 0, 1)
        c["norm_group"](l, 0, 2)
        k.dma_dyn(HQ.full(), H, "cabs", 256, H[:, :, TP:TT])
        phaseA(l, TP, 8, [(0, 8)], False)
        phaseB_S(l, ex)
        phaseC_S(l, ex)
        phaseB_P(l)
        phaseC_P(l)
        wo_groups(l, [dict(mg=lambda kc: MG[:, kc, 0:512], x=lambda oc: X[:, oc, 0:512], n=512, ci=0)])
        tail(l, ex)
    store_tokens(o["yp"], TP, 0)


_CACHE = {}


def make_in_maps(inp):
    consts = make_consts()
    f = lambda a: np.ascontiguousarray(np.asarray(a), dtype=np.float32)
    shared = {}
    for nm in ("w_mod", "w_in", "w_a2_f", "b_a_f", "w_a2_b", "b_a_b", "gla_onorm", "qn_swa", "kn_swa",
               "qn_na", "kn_na", "sink_swa", "w_pa", "w_pb", "w_pc", "w_o", "w_fc1", "w_fc2"):
        shared[nm] = f(inp[nm])
    shared["b_mod"] = f(inp["b_mod"]).reshape(DEPTH, 48, 128)
    shared["norm1"] = f(inp["norm1"]).reshape(DEPTH, 8, 128)
    shared["norm2"] = f(inp["norm2"]).reshape(DEPTH, 8, 128)
    rp = f(inp["rpb_na"])[:, :, ::-1, ::-1]
    pad = np.zeros((DEPTH, 8, 15, 127), np.float32)
    pad[..., 48:79] = rp
    shared["rpb_pad"] = pad
    shared.update({nm: consts[nm] for nm, _, _ in CONST_SPECS})
    xp = f(inp["x_prompt"])
    xs = f(inp["x_sample"])
    maps = []
    for ci in range(NCORES):
        b = ci // 4
        m = dict(shared)
        m["xp"] = np.ascontiguousarray(xp[2 * ci:2 * ci + 2].reshape(TP, D))
        m["xs"] = np.ascontiguousarray(xs[b])
        cond = np.stack([f(inp["c_ctx"]), f(inp["c"])[b]], 0).reshape(16, 128)
        m["cond"] = np.ascontiguousarray(cond)
        m["rk"] = np.array([[TP + (ci % 4) * 256, (ci % 4) * 256]], np.int32)
        m["st_gla"] = np.ascontiguousarray(f(inp["state_gla"])[b])
        m["cswk"] = np.ascontiguousarray(f(inp["cache_swa_k"])[b].reshape(DEPTH, 512, 128))
        m["cswv"] = np.ascontiguousarray(f(inp["cache_swa_v"])[b].reshape(DEPTH, 512, 128))
        m["cnak"] = np.ascontiguousarray(f(inp["cache_na_k"])[b].reshape(DEPTH, 512, 512))
        m["cnav"] = np.ascontiguousarray(f(inp["cache_na_v"])[b].reshape(DEPTH, 512, 512))
        maps.append(m)
    return maps


def assemble(results):
    yp = np.stack([r["yp"].reshape(2, 256, D) for r in results], 0).reshape(16, 256, D)
    ys = np.stack([np.concatenate([results[4 * b + r]["ysq"] for r in range(4)], 0) for b in range(2)], 0)
    gla = np.concatenate([r["o_gla"] for r in results], 0)
    swk = np.concatenate([r["o_swk"].reshape(2, DEPTH, 256, 2, 64) for r in results], 0)
    swv = np.concatenate([r["o_swv"].reshape(2, DEPTH, 256, 2, 64) for r in results], 0)
    nak = np.concatenate([r["o_nak"].reshape(2, DEPTH, 256, 8, 64) for r in results], 0)
    nav = np.concatenate([r["o_nav"].reshape(2, DEPTH, 256, 8, 64) for r in results], 0)
    return tuple(np.ascontiguousarray(a, dtype=np.float32) for a in (yp, ys, gla, swk, swv, nak, nav))


def kernel(**inputs):
    if "nc" not in _CACHE:
        _CACHE["nc"] = build_program()
    nc = _CACHE["nc"]
    maps = make_in_maps(inputs)
    res = run_bass_kernel_spmd(nc, maps, core_ids=list(range(NCORES)))
    return assemble(res.results)
```

```python
import os
import numpy as np
import ml_dtypes
import concourse.bass as bass
import concourse.mybir as mybir
from concourse.bass_utils import run_bass_kernel_spmd

F32 = mybir.dt.float32
BF16 = mybir.dt.bfloat16
AF = mybir.ActivationFunctionType
ALU = mybir.AluOpType

D = 1024
DEPTH = 4
NCORES = 8
TP = 512
TS = 1024
TT = TP + TS
IN_W = 6944
NEG = -30000.0
PA_STOP = float(os.environ.get('PA_STOP', '99'))
EPS = 1e-6

O_AQ, O_AK, O_AV, O_AR, O_ALF, O_ALB = 0, 256, 512, 1024, 1536, 1552
O_BQ, O_BK, O_BV = 1568, 2080, 2208
O_CQ, O_CK, O_CV = 2336, 2848, 3360
O_GA, O_GB, O_GC = 3872, 4896, 5920


class Buf:
    __slots__ = ("lw", "rd", "name")

    def __init__(self, name=""):
        self.lw = None
        self.rd = []
        self.name = name


class Op:
    __slots__ = ("eng", "fn", "deps", "dma", "ticket", "semkey", "nsig", "idx", "cost", "tag", "st", "fi")


class Prog:
    ENGS = ("pe", "act", "dve", "pool", "sp")

    def __init__(self, nc):
        self.nc = nc
        self.ops = []
        self.dyn = {}
        self.dyn_spec = {}

    def op(self, eng, fn, reads=(), writes=(), dma=False, cost=500.0):
        o = Op()
        o.tag = getattr(self, "tag", "")
        o.cost = cost
        o.eng = eng
        o.fn = fn
        o.dma = dma
        o.idx = len(self.ops)
        deps = set()
        for b in reads:
            if b.lw is not None:
                deps.add(b.lw)
        for b in writes:
            if b.lw is not None:
                deps.add(b.lw)
            deps.update(b.rd)
        o.deps = deps
        self.ops.append(o)
        for b in reads:
            b.rd.append(o.idx)
        for b in writes:
            b.lw = o.idx
            b.rd = []
        return o


    def schedule(self, window=40):
        ops = self.ops
        n = len(ops)
        left = [len(o.deps) for o in ops]
        users = [[] for _ in ops]
        for o in ops:
            for dd in o.deps:
                users[dd].append(o.idx)
        ready = [0.0] * n
        fin = [0.0] * n
        done = [False] * n
        pend = {e: [o.idx for o in ops if o.eng == e] for e in self.ENGS}
        head = {e: 0 for e in self.ENGS}
        free = {e: 0.0 for e in self.ENGS}
        order = {e: [] for e in self.ENGS}
        pipe = 0.0
        remaining = n
        glob = []
        while remaining:
            best = None
            for e in self.ENGS:
                lst = pend[e]
                i = head[e]
                while i < len(lst) and done[lst[i]]:
                    i += 1
                head[e] = i
                cnt = 0
                fe = free[e]
                while i < len(lst) and cnt < window:
                    idx = lst[i]
                    i += 1
                    if done[idx]:
                        continue
                    cnt += 1
                    if left[idx] == 0:
                        st = ready[idx] if ready[idx] > fe else fe
                        if best is None or (st, idx) < best[0]:
                            best = ((st, idx), e, idx)
                        if st <= fe:
                            break
            assert best is not None, "scheduler deadlock"
            (st, _), e, idx = best
            o = ops[idx]
            if o.dma:
                issue = 1000.0 if e == "pool" else 100.0
                free[e] = st + issue
                p0 = max(pipe, st + issue)
                pipe = p0 + o.cost
                f = pipe + 2000.0
            else:
                free[e] = st + o.cost
                f = st + o.cost + 150.0
            fin[idx] = f
            o.st = st
            o.fi = f
            done[idx] = True
            order[e].append(idx)
            glob.append(idx)
            remaining -= 1
            for u in users[idx]:
                left[u] -= 1
                if ready[u] < f:
                    ready[u] = f
        self.order = order
        self.est_ns = max(fin) if fin else 0.0

    def emit(self, final_wait_eng="sp"):
        nc = self.nc
        ops = self.ops
        KD = {"sp": 14, "pool": 10, "act": 4}
        needed = [False] * len(ops)
        for o in ops:
            for d in o.deps:
                a = ops[d]
                if a.eng == o.eng and o.eng == "pe" and not a.dma and not o.dma:
                    continue
                needed[d] = True
        cnt = {e: 0 for e in self.ENGS}
        dcnt = {e: 0 for e in KD}
        order = getattr(self, "order", None)
        if order is None:
            order = {e: [o.idx for o in ops if o.eng == e] for e in self.ENGS}
        seq = [ops[i] for e in self.ENGS for i in order[e]]
        for o in seq:
            if o.dma:
                n = dcnt[o.eng]
                dcnt[o.eng] += 1
                k = n % KD[o.eng]
                o.semkey = (o.eng, k)
                o.ticket = 16 * (n // KD[o.eng] + 1)
            else:
                o.semkey = o.eng
                if needed[o.idx]:
                    cnt[o.eng] += 1
                    o.ticket = cnt[o.eng]
                else:
                    o.ticket = None
        sems = {}
        import contextlib
        with contextlib.ExitStack() as st:
            for e in self.ENGS:
                sems[e] = st.enter_context(nc.semaphore("s_" + e))
            for e, k in KD.items():
                for i in range(k):
                    sems[(e, i)] = st.enter_context(nc.semaphore("d_%s%d" % (e, i)))
            block = st.enter_context(nc.Block())
            per_eng = {e: [ops[i] for i in order[e]] for e in self.ENGS}

            def run(eng_name, eng):
                seen = {}
                if eng_name == "sp":
                    for key, (ap, lo, hi) in getattr(self, "dyn_spec", {}).items():
                        reg = eng.alloc_register("dyn_" + key)
                        eng.reg_load(reg, ap)
                        self.dyn[key] = eng.snap(reg, min_val=lo, max_val=hi)
                for o in per_eng[eng_name]:
                    waits = {}
                    for d in o.deps:
                        a = ops[d]
                        if a.eng == o.eng and o.eng == "pe" and not a.dma and not o.dma:
                            continue
                        if a.ticket is None:
                            continue
                        if waits.get(a.semkey, 0) < a.ticket:
                            waits[a.semkey] = a.ticket
                    if o.dma:
                        prev = o.ticket - 16
                        if prev > 0 and waits.get(o.semkey, 0) < prev:
                            waits[o.semkey] = prev
                    for key, val in waits.items():
                        if seen.get(key, 0) < val:
                            eng.wait_ge(sems[key], val)
                            seen[key] = val
                    ins = o.fn(eng)
                    if o.dma:
                        ins.then_inc(sems[o.semkey], 16)
                    elif o.ticket is not None:
                        ins.then_inc(sems[o.semkey], 1)
                if eng_name in KD:
                    n = dcnt[eng_name]
                    for k in range(min(n, KD[eng_name])):
                        tot = 16 * ((n - 1 - k) // KD[eng_name] + 1)
                        if seen.get((eng_name, k), 0) < tot:
                            eng.wait_ge(sems[(eng_name, k)], tot)

            @block.tensor
            def _(e):
                run("pe", e)

            @block.scalar
            def _(e):
                run("act", e)

            @block.vector
            def _(e):
                run("dve", e)

            @block.gpsimd
            def _(e):
                run("pool", e)

            @block.sync
            def _(e):
                run("sp", e)


def make_consts():
    c = {}
    bf = ml_dtypes.bfloat16
    c["ident_bf"] = np.eye(128, dtype=np.float32).astype(bf)
    c["ident_f"] = np.eye(128, dtype=np.float32)
    bd = np.zeros((128, 128), np.float32)
    bd[:64, :64] = 1.0
    bd[64:, 64:] = 1.0
    c["ones_bd"] = bd.astype(bf)
    c["ones_all"] = np.ones((128, 128), np.float32).astype(bf)
    s = np.arange(128)[:, None]
    t = np.arange(128)[None, :]
    c["trif"] = np.where(s <= t, -1.0 / 16.0, 0.0).astype(np.float32).astype(bf)
    c["trib"] = np.where(s >= t, -1.0 / 16.0, 0.0).astype(np.float32).astype(bf)
    c["maskf"] = np.where(s <= t, 1.0, 0.0).astype(np.float32).astype(bf)
    c["maskb"] = np.where(s >= t, 1.0, 0.0).astype(np.float32).astype(bf)
    c["band_next"] = np.where(t <= s, 0.0, NEG).astype(np.float32).astype(bf)
    c["band_prev"] = np.where(s <= t, 0.0, NEG).astype(np.float32).astype(bf)
    nf = 16
    inv_freq = (10000.0 ** (-np.arange(nf, dtype=np.float32) / nf)).astype(np.float32)
    tt = np.arange(TS)
    row = (tt // 64).astype(np.float32)
    col = (tt % 64).astype(np.float32)
    ang = np.zeros((64, TS), np.float32)
    for d in range(64):
        pos = row if d < 32 else col
        ang[d] = pos * inv_freq[d % 16]
    cos = np.cos(ang).astype(np.float32)
    sin = np.sin(ang).astype(np.float32)
    c["rope_cos"] = np.concatenate([cos, cos], 0)
    c["rope_sin"] = np.concatenate([sin, sin], 0)
    Pm = np.zeros((128, 128), np.float32)
    for d in range(128):
        if (d % 32) < 16:
            Pm[d, d + 16] = -1.0
        else:
            Pm[d, d - 16] = 1.0
    c["rope_pT"] = np.ascontiguousarray(Pm.T).astype(bf)
    J = np.zeros((64, 64), np.float32)
    for i in range(64):
        J[i, 63 - i] = 1.0
    JJ = np.zeros((128, 128), np.float32)
    JJ[:64, :64] = J
    JJ[64:, 64:] = J
    c["jj"] = JJ.astype(bf)
    cq = np.arange(64)[None, :]
    ckp = np.arange(64)[:, None]
    ck = 63 - ckp
    cs = np.clip(cq - 8, 0, 48)
    ok = (ck >= cs) & (ck < cs + 16)
    nm_ = np.where(ok, 0.0, NEG).astype(np.float32)
    c["na_mask"] = np.concatenate([nm_, nm_], 0)
    return c


CONST_SPECS = [
    ("ident_bf", [128, 128], BF16), ("ident_f", [128, 128], F32), ("ones_bd", [128, 128], BF16),
    ("ones_all", [128, 128], BF16), ("trif", [128, 128], BF16), ("trib", [128, 128], BF16),
    ("maskf", [128, 128], BF16), ("maskb", [128, 128], BF16), ("band_next", [128, 128], BF16),
    ("band_prev", [128, 128], BF16), ("rope_cos", [128, TS], F32), ("rope_sin", [128, TS], F32),
    ("rope_pT", [128, 128], BF16), ("jj", [128, 128], BF16), ("na_mask", [128, 64], F32),
]


GRAN = 512


class V:
    __slots__ = ("ap", "bufs", "excl")

    def __init__(self, ap, bufs, excl=False):
        self.ap = ap
        self.bufs = bufs
        self.excl = excl


def DR(ap):
    return V(ap, [])


class Arena:
    def __init__(self, nc, nbytes):
        self.nc = nc
        self.nbytes = nbytes
        self.t = nc.alloc_sbuf_tensor("arena", [128, nbytes // 4], F32)
        self.g = [Buf("g%d" % i) for i in range((nbytes + GRAN - 1) // GRAN)]
        self.top = 0

    def bufs(self, lo, hi):
        return self.g[lo // GRAN:(hi - 1) // GRAN + 1]

    def alloc(self, shape, dtype, align=GRAN):
        esz = 4 if dtype == F32 else 2
        n = 1
        for d in shape[1:]:
            n *= d
        nb = (n * esz + 3) // 4 * 4
        off = (self.top + align - 1) // align * align
        assert off + nb <= self.nbytes, ("arena overflow", off, nb, self.nbytes)
        self.top = off + nb
        return Tile(self, off, shape, dtype)

    def mark(self):
        return self.top

    def release(self, m):
        self.top = m


class Tile:
    def __init__(self, arena, off, shape, dtype):
        self.arena = arena
        self.off = off
        self.shape = tuple(shape)
        self.dt = dtype
        self.esz = 4 if dtype == F32 else 2
        n = 1
        for d in shape[1:]:
            n *= d
        w0 = off // 4
        w1 = w0 + (n * self.esz + 3) // 4
        base = arena.t[:, w0:w1]
        if dtype != F32:
            base = base.bitcast(dtype)
        if len(shape) > 2:
            names = ["d%d" % i for i in range(len(shape) - 1)]
            pat = "p (" + " ".join(names) + ") -> p " + " ".join(names)
            base = base.rearrange(pat, **{nm: shape[i + 1] for i, nm in enumerate(names[:-1])})
        self.ap = base[0:shape[0]]
        st = []
        acc = 1
        for d in reversed(shape[1:]):
            st.append(acc)
            acc *= d
        self.strides = list(reversed(st))

    def __getitem__(self, key):
        if not isinstance(key, tuple):
            key = (key,)
        ap = self.ap[key]
        fk = list(key[1:]) + [slice(None)] * (len(self.shape) - len(key))
        dims = []
        for k, d, s in zip(fk, self.shape[1:], self.strides):
            if isinstance(k, slice):
                a, b, stp = k.indices(d)
                cnt = max(0, (b - a + stp - 1) // stp)
                dims.append((a, cnt, stp, s, d))
            else:
                dims.append((k, 1, 1, s, d))
        gset = {}
        esz = self.esz
        off = self.off
        arena = self.arena

        def rec(i, base):
            a, cnt, stp, st, d = dims[i]
            inner_full = all(dd[1] == dd[4] and dd[2] == 1 for dd in dims[i + 1:])
            if stp == 1 and inner_full:
                lo = base + a * st
                hi = base + (a + cnt) * st
                for b_ in arena.bufs(off + lo * esz, off + hi * esz):
                    gset[id(b_)] = b_
                return
            if i == len(dims) - 1:
                for j in range(cnt):
                    lo = base + (a + j * stp) * st
                    for b_ in arena.bufs(off + lo * esz, off + (lo + 1) * esz):
                        gset[id(b_)] = b_
                return
            for j in range(cnt):
                rec(i + 1, base + (a + j * stp) * st)

        rec(0, 0)
        return V(ap, list(gset.values()))

    def full(self):
        return self[tuple(slice(None) for _ in self.shape)]


class Bank:
    def __init__(self, nc, i):
        self.t = nc.alloc_psum_tensor("pb%d" % i, [128, 512], F32)
        self.buf = Buf("pb%d" % i)
        self.ap = self.t[:, :]

    def __getitem__(self, key):
        return V(self.ap[key], [self.buf], True)

    def v(self, ap):
        return V(ap, [self.buf], True)

    def bf(self):
        return self.ap.bitcast(BF16)


class K:
    def __init__(self, nc):
        self.nc = nc
        self.P = Prog(nc)
        self.A = Arena(nc, 205 * 1024)
        self.banks = [Bank(nc, i) for i in range(8)]
        self.bi = 0
        self.flip = 0
        self.held = []
        self.fence = V(None, [Buf('fence')])

    def bank(self):
        while True:
            b = self.banks[self.bi]
            self.bi = (self.bi + 1) % 8
            if b not in self.held:
                return b

    def hold(self):
        b = self.bank()
        self.held.append(b)
        return b

    def unhold(self, b):
        self.held.remove(b)

    def _rw(self, reads, writes):
        r, w = [], []
        for v in reads:
            if isinstance(v, V):
                (w if v.excl else r).extend(v.bufs)
        for v in writes:
            w.extend(v.bufs)
        return r, w

    def op(self, eng, fn, reads, writes, dma=False):
        r, w = self._rw(reads, writes)
        try:
            shp = writes[0].ap.shape
            nfree = 1
            for x in shp[1:]:
                nfree *= x
            npart = shp[0]
        except Exception:
            nfree, npart = 512, 128
        if dma:
            try:
                esz = 4 if reads[0].ap.dtype == F32 else 2
            except Exception:
                esz = 4
            cost = npart * nfree * esz / 180.0
        elif eng == "pe":
            cost = max(64.0, nfree) * 0.46 + (70.0 if nfree < 256 else 15.0)
        elif eng == "act":
            cost = 230.0 + nfree * 0.75
        elif eng == "dve":
            cost = 120.0 + nfree * 0.95
        else:
            cost = 250.0 + nfree * 1.9
        return self.P.op(eng, fn, r, w, dma, cost)

    def mm(self, out, lhsT, rhs, start=True, stop=True):
        self.op("pe", lambda e: e.matmul(out.ap, lhsT=lhsT.ap, rhs=rhs.ap, start=start, stop=stop,
                                         skip_group_check=True), [lhsT, rhs], [out])

    def tr(self, out, in_, ident):
        self.op("pe", lambda e: e.transpose(out.ap, in_.ap, ident.ap), [in_, ident], [out])

    def act(self, out, in_, func, bias=None, scale=None):
        kw = {}
        rd = [in_]
        if bias is not None:
            kw["bias"] = bias.ap if isinstance(bias, V) else bias
            rd.append(bias)
        if scale is not None:
            kw["scale"] = scale.ap if isinstance(scale, V) else scale
            rd.append(scale)
        self.op("act", lambda e: e.activation(out.ap, in_.ap, func, **kw), rd, [out])

    def tt(self, out, a, b, op, eng="dve"):
        self.op(eng, lambda e: e.tensor_tensor(out.ap, a.ap, b.ap, op), [a, b], [out])

    def ts(self, out, a, s1, op0, s2=None, op1=None, eng="dve"):
        rd = [a, s1, s2]
        s1a = s1.ap if isinstance(s1, V) else s1
        s2a = s2.ap if isinstance(s2, V) else s2
        if op1 is None:
            self.op(eng, lambda e: e.tensor_scalar(out.ap, a.ap, s1a, None, op0), rd, [out])
        else:
            self.op(eng, lambda e: e.tensor_scalar(out.ap, a.ap, s1a, s2a, op0, op1), rd, [out])

    def stt(self, out, in0, scalar, in1, op0, op1, eng="dve"):
        sa = scalar.ap if isinstance(scalar, V) else scalar
        self.op(eng, lambda e: e.scalar_tensor_tensor(out.ap, in0.ap, sa, in1.ap, op0, op1),
                [in0, scalar, in1], [out])

    def cp(self, out, in_, eng="dve"):
        if eng == "act":
            self.op("act", lambda e: e.copy(out.ap, in_.ap), [in_], [out])
        else:
            self.op(eng, lambda e: e.tensor_copy(out.ap, in_.ap), [in_], [out])

    def cpx(self, out, in_):
        self.flip ^= 1
        self.cp(out, in_, "act" if self.flip else "dve")

    def recip(self, out, in_):
        self.op("dve", lambda e: e.reciprocal(out.ap, in_.ap), [in_], [out])

    def rstd(self, out, ss, n):
        self.act(out, ss, AF.Ln, bias=EPS, scale=1.0 / n)
        self.act(out, out, AF.Exp, scale=-0.5)

    def recip_act(self, out, in_):
        self.act(out, in_, AF.Ln)
        self.act(out, out, AF.Exp, scale=-1.0)

    def memset(self, out, val, eng="dve"):
        self.op(eng, lambda e: e.memset(out.ap, val), [], [out])

    def dma_dyn(self, out, src_tile, key, width, track):
        P = self.P

        def fn(e):
            return e.dma_start(out=out.ap, in_=src_tile.ap[:, :, bass.ds(P.dyn[key], width)])
        self.op("sp", fn, [track], [out], dma=True)

    def allgather(self, out, in_, groups):
        o_ = self.op("pool", lambda e: e.collective_compute("AllGather", ALU.bypass, replica_groups=groups,
                                                            ins=[in_.ap.opt()], outs=[out.ap.opt()]),
                     [in_], [out, self.fence])
        o_.cost = 50000.0

    def dma(self, q, out, in_, **kw):
        rd = [in_, self.fence] if q == "pool" else [in_]
        self.op(q, lambda e: e.dma_start(out=out.ap, in_=in_.ap, **kw), rd, [out], dma=True)


def build_program(depth=DEPTH, do_sample=True, dbg_names=(), stop=None):
    nc = bass.Bass("TRN2", target_bir_lowering=False)

    def din(name, shape, dt=F32):
        return nc.dram_tensor(name, shape, dt, kind="ExternalInput").ap()

    def dout(name, shape):
        return nc.dram_tensor(name, shape, F32, kind="ExternalOutput").ap()

    d = {}
    d["xp"] = din("xp", [TP, D])
    d["xs"] = din("xs", [TS, D])
    d["cond"] = din("cond", [16, 128])
    d["rk"] = din("rk", [1, 2], mybir.dt.int32)
    d["st_gla"] = din("st_gla", [DEPTH, 2, 4, 64, 128])
    d["cswk"] = din("cswk", [DEPTH, 512, 128])
    d["cswv"] = din("cswv", [DEPTH, 512, 128])
    d["cnak"] = din("cnak", [DEPTH, 512, 512])
    d["cnav"] = din("cnav", [DEPTH, 512, 512])
    d["w_mod"] = din("w_mod", [DEPTH, D, 6 * D])
    d["b_mod"] = din("b_mod", [DEPTH, 48, 128])
    d["norm1"] = din("norm1", [DEPTH, 8, 128])
    d["norm2"] = din("norm2", [DEPTH, 8, 128])
    d["w_in"] = din("w_in", [DEPTH, D, IN_W])
    d["w_a2_f"] = din("w_a2_f", [DEPTH, 16, 256])
    d["b_a_f"] = din("b_a_f", [DEPTH, 256])
    d["w_a2_b"] = din("w_a2_b", [DEPTH, 16, 256])
    d["b_a_b"] = din("b_a_b", [DEPTH, 256])
    d["gla_onorm"] = din("gla_onorm", [DEPTH, 128])
    for nm in ("qn_swa", "kn_swa", "qn_na", "kn_na"):
        d[nm] = din(nm, [DEPTH, 64])
    d["sink_swa"] = din("sink_swa", [DEPTH, 8])
    d["rpb_pad"] = din("rpb_pad", [DEPTH, 8, 15, 127])
    for nm in ("w_pa", "w_pb", "w_pc"):
        d[nm] = din(nm, [DEPTH, 512, D])
    d["w_o"] = din("w_o", [DEPTH, D, D])
    d["w_fc1"] = din("w_fc1", [DEPTH, D, 4 * D])
    d["w_fc2"] = din("w_fc2", [DEPTH, 4 * D, D])
    for nm, shp, dt in CONST_SPECS:
        d[nm] = din(nm, shp, dt)
    o = {}
    o["yp"] = dout("yp", [TP, D])
    o["ysq"] = dout("ysq", [256, D])
    o["o_gla"] = dout("o_gla", [2, DEPTH, 2, 4, 64, 128])
    o["o_swk"] = dout("o_swk", [2, DEPTH, 256, 128])
    o["o_swv"] = dout("o_swv", [2, DEPTH, 256, 128])
    o["o_nak"] = dout("o_nak", [2, DEPTH, 256, 512])
    o["o_nav"] = dout("o_nav", [2, DEPTH, 256, 512])

    k = K(nc)
    A = k.A
    k.P.dyn_spec["cabs"] = (d["rk"][0:1, 0:1], TP, TP + 768)
    k.P.dyn_spec["crel"] = (d["rk"][0:1, 1:2], 0, 768)
    xq_in = nc.dram_tensor("xq_in", [1024, 256], F32)
    xq_all = nc.dram_tensor("xq_all", [4096, 256], F32)
    XQI = V(xq_in.ap(), [Buf("xq_in")])
    XQA = V(xq_all.ap(), [Buf("xq_all")])
    X = A.alloc([128, 8, TT], F32)
    H = A.alloc([128, 8, TT], BF16)
    MG = A.alloc([128, 8, TP], BF16)
    MGQ = A.alloc([128, 8, 256], BF16)
    HQ = A.alloc([128, 8, 256], BF16)
    WS = [A.alloc([128, 8, 512], BF16) for _ in range(4)]
    wsi = [0]

    def wslot():
        w = WS[wsi[0]]
        wsi[0] = (wsi[0] + 1) % len(WS)
        return w

    C = {}
    for nm, shp, dt in CONST_SPECS:
        C[nm] = A.alloc(shp, dt, align=64 if shp[1] <= 128 else GRAN)
        k.dma("sp", C[nm].full(), DR(d[nm]))
    SCT = A.alloc([128, 8, 2], BF16, align=64)
    PVt = A.alloc([128, DEPTH, 64], F32)
    MODVL = [A.alloc([128, 48, 2], F32, align=64) for _ in range(2)]
    SCLL = [A.alloc([128, 2, 2, 8], F32, align=64) for _ in range(2)]
    cur = [0]
    GVL = [A.alloc([128, 8], F32, align=64) for _ in range(2)]
    ESKL = [A.alloc([128, 8], F32, align=64) for _ in range(2)]
    ALR = A.alloc([64, TS], BF16)
    WA2L = [A.alloc([64, 512], BF16) for _ in range(2)]

    class _Cur:
        def __init__(self, lst):
            self.lst = lst

        def __getitem__(self, key):
            return self.lst[cur[0]][key]

        def full(self):
            return self.lst[cur[0]].full()

    GV = _Cur(GVL)
    ESK = _Cur(ESKL)
    WA2 = _Cur(WA2L)
    k.memset(ALR.full(), 1.0)

    dbg = {}

    def dbg_out(name, v, shape):
        if name in dbg_names:
            ap = dout("dbg_" + name, shape)
            k.dma("sp", DR(ap), v)

    m0 = A.mark()
    for l in range(DEPTH):
        stg = A.alloc([64, 128], F32)
        k.dma("sp", stg[0:48, :], DR(d["b_mod"][l]))
        k.dma("sp", stg[48:56, :], DR(d["norm1"][l]))
        k.dma("sp", stg[56:64, :], DR(d["norm2"][l]))
        pb = k.bank()
        k.tr(pb[:, 0:64], stg.full(), C["ident_f"][0:64, 0:64])
        k.cp(PVt[:, l, :], pb[:, 0:64])
    stg = A.alloc([16, 128], F32)
    k.dma("sp", stg.full(), DR(d["cond"]))
    pb = k.bank()
    k.tr(pb[:, 0:16], stg.full(), C["ident_f"][0:16, 0:16])
    for ci in range(2):
        k.act(SCT[:, :, ci], pb[:, ci * 8:(ci + 1) * 8], AF.Silu)
    A.release(m0)

    def load_tokens(src, n, col0):
        m = A.mark()
        st2 = [A.alloc([128, D], F32) for _ in range(2)]
        for tix in range(n // 128):
            s_ = st2[tix % 2]
            k.dma("sp", s_.full(), DR(src[tix * 128:(tix + 1) * 128, :]))
            for q4 in range(2):
                pb = k.bank()
                for j in range(4):
                    oc = q4 * 4 + j
                    k.tr(pb[:, j * 128:(j + 1) * 128], s_[:, oc * 128:(oc + 1) * 128], C["ident_f"].full())
                c0 = col0 + tix * 128
                k.cpx(X[:, q4 * 4:q4 * 4 + 4, c0:c0 + 128],
                      pb.v(pb.ap.rearrange("p (a b) -> p a b", a=4)))
        A.release(m)

    load_tokens(d["xp"], TP, 0)
    load_tokens(d["xs"], TS, TP)

    def wload(slot_view, src_ap):
        k.dma("pool", slot_view, DR(src_ap))

    wc_scr = {}
    wc_valid = set()

    def wtile(key, slot_view, loader):
        if os.environ.get("NO_WCACHE") == "1":
            loader()
            return
        shp = list(slot_view.ap.shape)
        if key not in wc_scr:
            t = nc.dram_tensor("wc%d" % len(wc_scr), shp, BF16)
            wc_scr[key] = V(t.ap(), [Buf("wc")])
        scr = wc_scr[key]
        if key in wc_valid:
            k.dma("sp", slot_view, scr)
        else:
            loader()
            k.dma("sp", scr, slot_view)
            wc_valid.add(key)

    def wrows(w2d, c0, n):
        return w2d[:, c0:c0 + n].rearrange("(kc p) n -> p kc n", p=128)

    GCI = [0, 1, 1]

    def mod_steps(l):
        MODV = MODVL[l % 2]
        SCL = SCLL[l % 2]
        st = {}

        def step(j):
            if j == 0:
                st["mb"] = k.hold()
            mb = st["mb"]
            ws = wslot()
            wload(ws.full(), wrows(d["w_mod"][l], j * 512, 512))
            for jj in range(4):
                ch = j * 4 + jj
                for kc in range(8):
                    k.mm(mb[:, ch * 2:ch * 2 + 2], ws[:, kc, jj * 128:(jj + 1) * 128], SCT[:, kc, :],
                         start=(kc == 0), stop=(kc == 7))

        def fin():
            mb = st["mb"]
            bm = PVt[:, l, 0:48]
            k.tt(MODV.full(), mb.v(mb.ap[:, 0:96].rearrange("p (a b) -> p a b", b=2)),
                 V(bm.ap.unsqueeze(2).to_broadcast([128, 48, 2]), bm.bufs), ALU.add)
            k.unhold(mb)
            for n in range(2):
                for ci in range(2):
                    sc = MODV[:, 8 + 24 * n:16 + 24 * n, ci]
                    k.ts(SCL[:, n, ci, :], sc, 1.0, ALU.add)
                    k.tt(SCL[:, n, ci, :], SCL[:, n, ci, :], PVt[:, l, 48 + 8 * n:56 + 8 * n], ALU.mult)

        return [(lambda j=j: step(j)) for j in range(12)] + [fin]

    def norm_group(l, n, g):
        ci = GCI[g]
        tok = slice(g * 512, (g + 1) * 512)
        m = A.mark()
        sq = [A.alloc([128, 512], BF16) for _ in range(2)]
        rs = A.alloc([128, 512], F32)
        tmp = [A.alloc([128, 512], F32) for _ in range(2)]
        sb = k.bank()
        for oc in range(8):
            s_ = sq[oc % 2]
            k.act(s_.full(), X[:, oc, tok], AF.Square)
            k.mm(sb[:, :], C["ones_all"].full(), s_.full(), start=(oc == 0), stop=(oc == 7))
        k.rstd(rs.full(), sb[:, :], 1024.0)
        for oc in range(8):
            t_ = tmp[oc % 2]
            k.stt(t_.full(), X[:, oc, tok], SCLL[cur[0]][:, n, ci, oc:oc + 1], rs.full(), ALU.mult, ALU.mult)
            k.act(H[:, oc, tok], t_.full(), AF.Identity, bias=MODVL[cur[0]][:, 24 * n + oc, ci:ci + 1])
        A.release(m)

    def proj_fm(ws, c0, mcols, tok, pb=None, col_off=0):
        if pb is None:
            pb = k.bank()
        if isinstance(tok, tuple):
            hf, n = tok
        else:
            hf, n = (lambda kc: H[:, kc, tok]), 512
        for kc in range(8):
            k.mm(pb[0:mcols, col_off:col_off + n], ws[:, kc, c0:c0 + mcols], hf(kc),
                 start=(kc == 0), stop=(kc == 7))
        return pb

    def proj_tm(ws, c0, ncols, t0, pb=None):
        if pb is None:
            pb = k.bank()
        for kc in range(8):
            k.mm(pb[:, 0:ncols], H[:, kc, t0:t0 + 128], ws[:, kc, c0:c0 + ncols],
                 start=(kc == 0), stop=(kc == 7))
        return pb

    nh_flip = [0]

    def normhead(pb, out_f32, out_bf, gain, out_pair=None):
        m = A.mark()
        nh_flip[0] ^= 1
        if nh_flip[0]:
            A.alloc([128, 512 + 256 + 512], F32)
        qf = A.alloc([128, 512], F32)
        sq = A.alloc([128, 512], BF16)
        rs = A.alloc([128, 512], F32)
        k.cp(qf.full(), pb[:, :], "dve")
        k.act(sq.full(), pb[:, :], AF.Square)
        sb = k.bank()
        k.mm(sb[:, :], C["ones_bd"].full(), sq.full())
        k.rstd(rs.full(), sb[:, :], 64.0)
        if out_pair is not None:
            for hf in range(2):
                rw = slice(hf * 64, hf * 64 + 64)
                k.stt(out_pair[hf], qf[rw, :], V(gain.ap[rw], gain.bufs), rs[rw, :], ALU.mult, ALU.mult)
        elif out_f32 is not None:
            k.stt(out_f32, qf.full(), gain, rs.full(), ALU.mult, ALU.mult)
            if out_bf is not None:
                k.cp(out_bf, out_f32, "act")
        else:
            k.stt(out_bf, qf.full(), gain, rs.full(), ALU.mult, ALU.mult)
        A.release(m)

    def layer_small(l):
        GV, ESK, WA2 = GVL[l % 2], ESKL[l % 2], WA2L[l % 2]
        for j, (nm, f) in enumerate((("qn_swa", 0.125), ("kn_swa", 1.0), ("qn_na", 0.125), ("kn_na", 1.0))):
            src = d[nm][l].rearrange("(p o) -> p o", o=1)
            k.dma("sp", GV[0:64, j:j + 1], DR(src))
            k.dma("sp", GV[64:128, j:j + 1], DR(src))
        k.dma("sp", GV[:, 4:5], DR(d["gla_onorm"][l].rearrange("(p o) -> p o", o=1)))
        k.ts(GV[:, 0:1], GV[:, 0:1], 0.125, ALU.mult)
        k.ts(GV[:, 2:3], GV[:, 2:3], 0.125, ALU.mult)
        k.dma("sp", ESK.full(), DR(d["sink_swa"][l].partition_broadcast(128)))
        k.act(ESK.full(), ESK.full(), AF.Exp)
        k.memset(WA2.full(), 0.0)
        k.dma("pool", WA2[0:16, 0:256], DR(d["w_a2_f"][l]))
        k.dma("pool", WA2[16:17, 0:256], DR(d["b_a_f"][l].rearrange("(o n) -> o n", o=1)))
        k.dma("pool", WA2[32:48, 256:512], DR(d["w_a2_b"][l]))
        k.dma("pool", WA2[48:49, 256:512], DR(d["b_a_b"][l].rearrange("(o n) -> o n", o=1)))

    def dense_attn(Q, Kt, kmap, VA, vmap, O, sink):
        m = A.mark()
        PT = [A.alloc([128, 512], BF16) for _ in range(3)]
        RC = [A.alloc([64, 256], F32) for _ in range(2)]
        it = 0
        ob = None
        for s in range(2):
            for h in range(8):
                rows = slice((h % 2) * 64, (h % 2) * 64 + 64)
                sb = k.bank()
                for kt in range(2):
                    k0 = s * 256 + kt * 128
                    k.mm(sb[:, kt * 256:(kt + 1) * 256], Kt[:, kmap(h), k0:k0 + 128],
                         Q[:, h, s * 256:(s + 1) * 256])
                pt = PT[it % 3]
                k.act(pt.full(), sb[:, :], AF.Exp)
                if it % 2 == 0:
                    ob = k.bank()
                c0 = (it % 2) * 256
                for kt in range(2):
                    k.mm(ob[:, c0:c0 + 256], VA[:, s * 2 + kt, vmap(h), :], pt[:, kt * 256:(kt + 1) * 256],
                         start=(kt == 0), stop=(kt == 1))
                rc = RC[it % 2]
                if sink is not None:
                    k.act(rc.full(), ob[64:128, c0:c0 + 256], AF.Ln, bias=ESK[0:64, h:h + 1])
                else:
                    k.act(rc.full(), ob[64:128, c0:c0 + 256], AF.Ln)
                k.act(rc.full(), rc.full(), AF.Exp, scale=-1.0)
                k.tt(O[rows, h // 2, s * 256:(s + 1) * 256], ob[0:64, c0:c0 + 256], rc.full(), ALU.mult)
                it += 1
        A.release(m)

    def merge(l, wp, gcol, groups, first):
        m = A.mark()
        SG = [A.alloc([128, 512], F32) for _ in range(2)]
        TM = [A.alloc([128, 512], F32) for _ in range(2)]
        for half in range(2):
            wsp = wslot()
            wtile(("mp", gcol, half), wsp[:, 0:4, :], lambda wsp=wsp, half=half: wload(
                wsp[:, 0:4, :], wp[:, half * 512:(half + 1) * 512].rearrange("(h p) n -> p h n", p=128)))
            wsg = wslot()
            wtile(("mg", gcol, half), wsg.full(), lambda wsg=wsg, half=half: wload(
                wsg.full(), wrows(d["w_in"][l], gcol + half * 512, 512)))
            it = 0
            for g in groups:
                n = g["n"]
                for j in range(4):
                    oc = half * 4 + j
                    gb = proj_fm(wsg, j * 128, 128, (g["h"], n))
                    sg = SG[it % 2]
                    k.act(sg[:, 0:n], gb[:, 0:n], AF.Sigmoid)
                    yb = k.bank()
                    for kk in range(4):
                        k.mm(yb[:, 0:n], wsp[:, kk, j * 128:(j + 1) * 128], g["o"](kk), start=(kk == 0), stop=(kk == 3))
                    mgv = g["mg"](oc)
                    if first:
                        k.tt(mgv, yb[:, 0:n], sg[:, 0:n], ALU.mult)
                    else:
                        tm = TM[it % 2]
                        k.tt(tm[:, 0:n], yb[:, 0:n], sg[:, 0:n], ALU.mult)
                        k.tt(mgv, mgv, tm[:, 0:n], ALU.add)
                    it += 1
        A.release(m)

    ctx = dict(nc=nc, k=k, A=A, d=d, o=o, X=X, H=H, MG=MG, MGQ=MGQ, HQ=HQ, XQI=XQI, XQA=XQA, C=C, wslot=wslot, wload=wload, wrows=wrows,
               wtile=wtile, wc_valid=wc_valid, proj_fm=proj_fm, proj_tm=proj_tm, normhead=normhead, dense_attn=dense_attn, merge=merge,
               GV=GV, ESK=ESK, ALR=ALR, WA2=WA2, MODVL=MODVL, SCLL=SCLL, cur=cur, dbg_out=dbg_out, depth=depth,
               do_sample=do_sample, stop=stop, mod_steps=mod_steps, norm_group=norm_group, layer_small=layer_small)
    build_layers(ctx)
    if os.environ.get('NO_SCHED') != '1':
        k.P.schedule()
    k.P.emit()
    return nc


def build_layers(c):
    k, A, d, o, X, H, MG, C = c["k"], c["A"], c["d"], c["o"], c["X"], c["H"], c["MG"], c["C"]
    MGQ, HQ, XQI, XQA = c["MGQ"], c["HQ"], c["XQI"], c["XQA"]

    def grp_P(O):
        return dict(h=lambda kc: H[:, kc, 0:512], o=lambda kk: O[:, kk, 0:512], mg=lambda oc: MG[:, oc, 0:512], n=512)

    def merge_quarter(l, wp, gcol, O, first):
        mq = A.mark()
        OQ = A.alloc([128, 4, 256], BF16)
        k.dma_dyn(OQ.full(), O, "crel", 256, O.full())
        merge(l, wp, gcol, [dict(h=lambda kc: HQ[:, kc, :], o=lambda kk: OQ[:, kk, :], mg=lambda oc: MGQ[:, oc, :], n=256)], first)
        A.release(mq)
    wslot, wload, wrows = c["wslot"], c["wload"], c["wrows"]
    wtile, wc_valid = c["wtile"], c["wc_valid"]
    proj_fm, proj_tm, normhead, dense_attn, merge = c["proj_fm"], c["proj_tm"], c["normhead"], c["dense_attn"], c["merge"]
    GV, ESK, ALR, WA2, MODVL, cur = c["GV"], c["ESK"], c["ALR"], c["WA2"], c["MODVL"], c["cur"]
    depth = c["depth"]
    dbg_out = c["dbg_out"]

    def phaseA(l, t0, nchunk, seqs, is_prompt):
        k.P.tag = "A.%s" % ("P" if is_prompt else "S")
        ngrp = nchunk // 4
        n = nchunk * 128
        m = A.mark()
        QP = A.alloc([128, 4, n], BF16)
        KP = A.alloc([128, 4, n], BF16)
        VT = A.alloc([128, nchunk, 512], BF16)
        SIN = A.alloc([128, nchunk, 2, 2, 128], BF16)
        MB = A.alloc([128, nchunk, 2, 128], F32)
        ALAST = A.alloc([128, 4, nchunk], F32, align=64)
        R = A.alloc([128, 2, 2, 128], F32)
        w1 = wslot()
        wtile("a1", w1.full(), lambda: wload(w1.full(), wrows(d["w_in"][l], O_AQ, 512)))
        w4 = wslot()
        wload(w4[:, :, 0:16], wrows(d["w_in"][l], O_ALF, 16))
        wload(w4[:, :, 32:48], wrows(d["w_in"][l], O_ALB, 16))
        w2 = wslot()
        wtile("a2", w2.full(), lambda: wload(w2.full(), wrows(d["w_in"][l], O_AV, 512)))
        while pending:
            pending.pop(0)()

        def seq_of(cidx):
            for si, (a, b) in enumerate(seqs):
                if a <= cidx < b:
                    return si, a, b
            raise AssertionError

        def init_state(si, dr):
            if is_prompt:
                k.memset(R[:, dr, :, :], 0.0)
            else:
                src = d["st_gla"][l, dr].rearrange("h k v -> (h k) v").rearrange("(pr q) v -> q pr v", q=128)
                k.dma("sp", R[:, dr, :, :], DR(src))

        def store_state(si, dr, a_c):
            if not is_prompt:
                return
            mm_ = A.mark()
            sf = A.alloc([128, 2, 128], F32)
            for pr in range(2):
                k.ts(sf[:, pr, :], R[:, dr, pr, :], ALAST[:, dr * 2 + pr, a_c:a_c + 1], ALU.mult)
            dst = o["o_gla"][si, l, dr].rearrange("h k v -> (h k) v").rearrange("(pr q) v -> q pr v", q=128)
            k.dma("sp", DR(dst), sf.full())
            A.release(mm_)

        def scan_step(cidx, dr, msrc):
            si, a, b = seq_of(cidx)
            first = (cidx == a) if dr == 0 else (cidx == b - 1)
            prev = cidx - 1 if dr == 0 else cidx + 1
            if first:
                init_state(si, dr)
            for pr in range(2):
                for hf in range(2):
                    rows = slice(hf * 64, hf * 64 + 64)
                    if first:
                        k.cp(SIN[rows, cidx, dr, pr, :], R[rows, dr, pr, :])
                        k.tt(R[rows, dr, pr, :], R[rows, dr, pr, :], msrc(pr, hf), ALU.add)
                    else:
                        al = ALAST[rows, dr * 2 + pr, prev:prev + 1]
                        k.ts(SIN[rows, cidx, dr, pr, :], R[rows, dr, pr, :], al, ALU.mult)
                        k.stt(R[rows, dr, pr, :], R[rows, dr, pr, :], al, msrc(pr, hf), ALU.mult, ALU.add)
            last = (cidx == b - 1) if dr == 0 else (cidx == a)
            if last:
                store_state(si, dr, cidx)

        m1 = A.mark()
        QK32 = A.alloc([128, 4, 512], F32)
        EP = A.alloc([128, 4, 256], F32)
        EN = A.alloc([128, 4, 256], F32)
        E1 = A.alloc([128, 512], F32)
        LA = A.alloc([128, 512], F32)
        LAH = A.alloc([128, 512], BF16)
        LAL = A.alloc([128, 512], BF16)
        KT = [A.alloc([128, 4, 128], BF16) for _ in range(2)]
        for gi in range(ngrp):
            htok = slice(t0 + gi * 512, t0 + gi * 512 + 512)
            pb = k.bank()
            for kc in range(8):
                k.mm(pb[0:64, :], w4[:, kc, 0:64], H[:, kc, htok], start=(kc == 0), stop=(kc == 7))
            k.cp(ALR[0:16, gi * 512:gi * 512 + 512], pb[0:16, :], "act")
            k.cp(ALR[32:48, gi * 512:gi * 512 + 512], pb[32:48, :], "dve")
            for j in range(4):
                pq = proj_fm(w1, j * 128, 128, htok)
                k.cpx(QK32[:, j, :], pq[:, :])
            if PA_STOP <= 1:
                continue
            for hf2 in range(2):
                cb = [k.hold(), k.hold()]
                for c2 in range(2):
                    cl = gi * 4 + hf2 * 2 + c2
                    tk = cl * 128
                    zb = k.bank()
                    k.mm(zb[:, :], ALR[0:64, tk:tk + 128], WA2.full())
                    if PA_STOP <= 1.2:
                        k.cp(E1.full(), zb[:, :])
                        continue
                    k.act(E1.full(), zb[:, :], AF.Exp, scale=-1.0)
                    k.act(LA.full(), E1.full(), AF.Ln, bias=1.0)
                    if PA_STOP <= 1.4:
                        continue
                    k.cp(LAH.full(), LA.full(), "act")
                    k.tt(LAL.full(), LA.full(), LAH.full(), ALU.subtract)
                    for dr in range(2):
                        tri = C["trif"] if dr == 0 else C["trib"]
                        for cc in range(2):
                            osl = cb[dr][:, cc * 256 + c2 * 128:cc * 256 + c2 * 128 + 128]
                            k.mm(osl, LAH[:, dr * 256 + cc * 128:dr * 256 + cc * 128 + 128], tri.full(), True, False)
                            k.mm(osl, LAL[:, dr * 256 + cc * 128:dr * 256 + cc * 128 + 128], tri.full(), False, True)
                if PA_STOP <= 1.6:
                    k.unhold(cb[0])
                    k.unhold(cb[1])
                    continue
                for dr in range(2):
                    cv = cb[dr].v(cb[dr].ap.rearrange("p (a b) -> p a b", a=2))
                    k.act(EP[:, dr * 2:dr * 2 + 2, :], cv, AF.Exp)
                    k.act(EN[:, dr * 2:dr * 2 + 2, :], cv, AF.Exp, scale=-1.0)
                k.unhold(cb[0])
                k.unhold(cb[1])
                if PA_STOP <= 2:
                    continue
                cbase = gi * 4 + hf2 * 2
                k.cp(ALAST[:, 0:2, cbase:cbase + 2], EP[:, 0:2, 127:256:128])
                k.cp(ALAST[:, 2:4, cbase:cbase + 2], EP[:, 2:4, 0:256:128])
                tcol = slice(gi * 512 + hf2 * 256, gi * 512 + hf2 * 256 + 256)
                hcol = slice(hf2 * 256, hf2 * 256 + 256)
                for dc in range(4):
                    cc = dc % 2
                    k.stt(QP[:, dc, tcol], QK32[:, cc, hcol], 0.125, EP[:, dc, :], ALU.mult, ALU.mult)
                    k.tt(KP[:, dc, tcol], QK32[:, 2 + cc, hcol], EN[:, dc, :], ALU.mult)
                if PA_STOP <= 3:
                    continue
                for c2 in range(2):
                    cl = gi * 4 + hf2 * 2 + c2
                    ctok = slice(cl * 128, cl * 128 + 128)
                    tb = k.bank()
                    tbv = tb.bf()
                    for dc in range(4):
                        k.tr(tb.v(tbv[:, dc * 128:(dc + 1) * 128]), KP[:, dc, ctok], C["ident_bf"].full())
                    kt = KT[cl % 2]
                    k.cp(kt.full(), tb.v(tbv[:, 0:512].rearrange("p (a b) -> p a b", a=4)), "act")
                    vb = proj_tm(w2, 0, 512, t0 + cl * 128)
                    k.cp(VT[:, cl, :], vb[:, :], "dve")
                    for dr in range(2):
                        mb = k.bank()
                        for pr in range(2):
                            k.mm(mb[:, pr * 256:(pr + 1) * 256], kt[:, dr * 2 + pr, :], VT[:, cl, pr * 256:(pr + 1) * 256])
                        mv = mb.ap.rearrange("p (a b) -> p a b", a=2)
                        if dr == 0:
                            scan_step(cl, 0, lambda pr, hf, mv=mv, mb=mb: mb.v(
                                mv[hf * 64:hf * 64 + 64, pr, hf * 128:hf * 128 + 128]))
                        else:
                            for hf in range(2):
                                k.cp(MB[hf * 64:hf * 64 + 64, cl, :, :],
                                     mb.v(mv[hf * 64:hf * 64 + 64, :, hf * 128:hf * 128 + 128]), "act")
        A.release(m1)
        if PA_STOP <= 4:
            A.release(m)
            return
        for cl in reversed(range(nchunk)):
            scan_step(cl, 1, lambda pr, hf, cl=cl: MB[hf * 64:hf * 64 + 64, cl, pr, :])
        if PA_STOP <= 5:
            A.release(m)
            return
        OA = A.alloc([128, 4, n], BF16)
        m3 = A.mark()
        w3 = wslot()
        wtile("a3", w3.full(), lambda: wload(w3.full(), wrows(d["w_in"][l], O_AR, 512)))
        SIL = A.alloc([128, 4, 512], BF16)
        ATM = [A.alloc([128, 4, 128], BF16) for _ in range(2)]
        OFs = [A.alloc([128, 512], F32) for _ in range(2)]
        SQs = [A.alloc([128, 512], BF16) for _ in range(2)]
        RSs = [A.alloc([128, 512], F32) for _ in range(2)]
        for gi in range(ngrp):
            htok = slice(t0 + gi * 512, t0 + gi * 512 + 512)
            for hd in range(4):
                pb = proj_fm(w3, hd * 128, 128, htok)
                k.act(SIL[:, hd, :], pb[:, :], AF.Silu)
            ob = [k.hold() for _ in range(4)]
            for c4 in range(4):
                cl = gi * 4 + c4
                ctok = slice(cl * 128, cl * 128 + 128)
                for dr in range(2):
                    mk = C["maskf"] if dr == 0 else C["maskb"]
                    mkf = mk.full()
                    for hf in range(2):
                        ab = k.bank()
                        rows = slice(hf * 64, hf * 64 + 64)
                        for pr in range(2):
                            k.mm(ab[:, pr * 128:(pr + 1) * 128], KP[rows, dr * 2 + pr, ctok], QP[rows, dr * 2 + pr, ctok])
                        k.tt(ATM[dr][:, hf:4:2, :], ab.v(ab.ap[:, 0:256].rearrange("p (a b) -> p a b", a=2)),
                             V(mkf.ap.unsqueeze(1).to_broadcast([128, 2, 128]), mkf.bufs), ALU.mult)
                for hd in range(4):
                    rows = slice((hd % 2) * 64, (hd % 2) * 64 + 64)
                    col = slice(c4 * 128, c4 * 128 + 128)
                    k.mm(ob[hd][:, col], VT[:, cl, hd * 128:(hd + 1) * 128], ATM[0][:, hd, :], True, False)
                    k.mm(ob[hd][:, col], VT[:, cl, hd * 128:(hd + 1) * 128], ATM[1][:, hd, :], False, False)
                    k.mm(ob[hd][:, col], SIN[rows, cl, 0, hd // 2, :], QP[rows, hd // 2, ctok], False, False)
                    k.mm(ob[hd][:, col], SIN[rows, cl, 1, hd // 2, :], QP[rows, 2 + hd // 2, ctok], False, True)
            gtok = slice(gi * 512, gi * 512 + 512)
            for hd in range(4):
                OF, SQ, RS = OFs[hd % 2], SQs[hd % 2], RSs[hd % 2]
                k.cp(OF.full(), ob[hd][:, :], "dve")
                k.act(SQ.full(), ob[hd][:, :], AF.Square)
                k.unhold(ob[hd])
                sb = k.bank()
                k.mm(sb[:, :], C["ones_all"].full(), SQ.full())
                k.rstd(RS.full(), sb[:, :], 128.0)
                k.tt(OF.full(), OF.full(), RS.full(), ALU.mult)
                k.stt(OA[:, hd, gtok], OF.full(), GV[:, 4:5], SIL[:, hd, :], ALU.mult, ALU.mult)
        A.release(m3)
        if PA_STOP <= 6:
            A.release(m)
            return
        if is_prompt:
            merge(l, d["w_pa"][l], O_GA, [grp_P(OA)], True)
        else:
            merge_quarter(l, d["w_pa"][l], O_GA, OA, True)
        A.release(m)

    def phaseB_P(l):
        k.P.tag = "B.P"
        tok = slice(0, 512)
        m = A.mark()
        QN = A.alloc([128, 8, 512], BF16)
        k.memset(QN.full(), 0.0)
        KN = A.alloc([128, 2, 512], BF16)
        KNF = A.alloc([128, 2, 512], F32)
        VA = A.alloc([128, 4, 2, 128], BF16)
        OB = A.alloc([128, 4, 512], BF16)
        KO = [A.alloc([128, 128], F32) for _ in range(2)]
        w1 = wslot()
        wtile("b1", w1.full(), lambda: wload(w1.full(), wrows(d["w_in"][l], O_BQ, 512)))
        w2 = wslot()

        def ld_b2():
            for kv in range(2):
                for dup in range(2):
                    wload(w2[:, :, kv * 128 + dup * 64:kv * 128 + dup * 64 + 64], wrows(d["w_in"][l], O_BK + kv * 64, 64))
            wload(w2[:, :, 256:384], wrows(d["w_in"][l], O_BV, 128))
        wtile("b2", w2[:, :, 0:384], ld_b2)
        for cc in range(4):
            pb = proj_fm(w1, cc * 128, 128, tok)
            normhead(pb, None, None, GV[:, 0:1], out_pair=(QN[0:64, 2 * cc, :], QN[64:128, 2 * cc + 1, :]))
        for kv in range(2):
            pb = proj_fm(w2, kv * 128, 128, tok)
            normhead(pb, KNF[:, kv, :], KN[:, kv, :], GV[:, 1:2])
        k.memset(VA[:, :, :, 64:128], 1.0)
        for tix in range(4):
            ttok = slice(tix * 128, tix * 128 + 128)
            pb = k.bank()
            for kv in range(2):
                k.tr(pb[:, kv * 128:(kv + 1) * 128], KNF[:, kv, ttok], C["ident_f"].full())
            ko = KO[0]
            kov = ko.full()
            k.cp(V(kov.ap.rearrange("p (a b) -> p a b", a=2), kov.bufs),
                 pb.v(pb.ap[:, 0:256].rearrange("p (a b) -> p a b", a=2)[:, :, 0:64]), "act")
            k.dma("sp", DR(o["o_swk"][tix // 2, l, (tix % 2) * 128:(tix % 2) * 128 + 128, :]), kov)
            pv = proj_tm(w2, 256, 128, tix * 128)
            vo = KO[1]
            k.cp(vo.full(), pv[:, 0:128], "dve")
            k.dma("sp", DR(o["o_swv"][tix // 2, l, (tix % 2) * 128:(tix % 2) * 128 + 128, :]), vo.full())
            k.cp(VA[:, tix, :, 0:64], pv.v(pv.ap[:, 0:128].rearrange("p (a b) -> p a b", a=2)), "act")
        dense_attn(QN, KN, lambda h: h // 4, VA, lambda h: h // 4, OB, True)
        merge(l, d["w_pb"][l], O_GB, [grp_P(OB)], False)
        A.release(m)

    def phaseC_P(l):
        k.P.tag = "C.P"
        tok = slice(0, 512)
        m = A.mark()
        QN = A.alloc([128, 8, 512], BF16)
        k.memset(QN.full(), 0.0)
        KN = A.alloc([128, 4, 512], BF16)
        KNF = A.alloc([128, 4, 512], F32)
        VA = A.alloc([128, 4, 8, 128], BF16)
        OC = A.alloc([128, 4, 512], BF16)
        KO = [A.alloc([128, 512], F32) for _ in range(2)]
        wqk = []
        wv = []
        for hh in range(2):
            wa = wslot()
            wtile(("c1", hh), wa.full(), lambda wa=wa, hh=hh: (
                wload(wa[:, :, 0:256], wrows(d["w_in"][l], O_CQ + hh * 256, 256)),
                wload(wa[:, :, 256:512], wrows(d["w_in"][l], O_CK + hh * 256, 256))))
            wqk.append(wa)
            for c2 in range(2):
                cc = hh * 2 + c2
                pb = proj_fm(wa, c2 * 128, 128, tok)
                normhead(pb, None, None, GV[:, 2:3], out_pair=(QN[0:64, 2 * cc, :], QN[64:128, 2 * cc + 1, :]))
                pb = proj_fm(wa, 256 + c2 * 128, 128, tok)
                normhead(pb, KNF[:, cc, :], KN[:, cc, :], GV[:, 3:4])
        for hh in range(2):
            wb = wslot()
            wtile(("c3", hh), wb[:, :, 0:256], lambda wb=wb, hh=hh: wload(
                wb[:, :, 0:256], wrows(d["w_in"][l], O_CV + hh * 256, 256)))
            wv.append(wb)
        k.memset(VA[:, :, :, 64:128], 1.0)
        for tix in range(4):
            ttok = slice(tix * 128, tix * 128 + 128)
            pb = k.bank()
            for cc in range(4):
                k.tr(pb[:, cc * 128:(cc + 1) * 128], KNF[:, cc, ttok], C["ident_f"].full())
            k.cp(KO[0].full(), pb[:, :], "act")
            k.dma("sp", DR(o["o_nak"][tix // 2, l, (tix % 2) * 128:(tix % 2) * 128 + 128, :]), KO[0].full())
            pv = k.bank()
            for hh in range(2):
                for kc in range(8):
                    k.mm(pv[:, hh * 256:(hh + 1) * 256], H[:, kc, tix * 128:tix * 128 + 128], wv[hh][:, kc, 0:256],
                         start=(kc == 0), stop=(kc == 7))
            k.cp(KO[1].full(), pv[:, :], "dve")
            k.dma("sp", DR(o["o_nav"][tix // 2, l, (tix % 2) * 128:(tix % 2) * 128 + 128, :]), KO[1].full())
            k.cp(VA[:, tix, :, 0:64], pv.v(pv.ap.rearrange("p (a b) -> p a b", a=8)), "act")
        dense_attn(QN, KN, lambda h: h // 2, VA, lambda h: h, OC, None)
        merge(l, d["w_pc"][l], O_GC, [grp_P(OC)], False)
        A.release(m)

    rp_flip = [0]

    def rope_apply(qf, qfb, out_bf, tsl):
        mr = A.mark()
        rp_flip[0] ^= 1
        if rp_flip[0]:
            A.alloc([128, 1024], F32)
        t1 = A.alloc([128, 512], F32)
        t2 = A.alloc([128, 512], F32)
        pr = k.bank()
        k.mm(pr[:, :], C["rope_pT"].full(), qfb)
        k.tt(t1.full(), qf, C["rope_cos"][:, tsl], ALU.mult)
        k.tt(t2.full(), pr[:, :], C["rope_sin"][:, tsl], ALU.mult)
        if isinstance(out_bf, tuple):
            k.tt(out_bf[0], t1[0:64, :], t2[0:64, :], ALU.add, eng="pool")
            k.tt(out_bf[1], t1[64:128, :], t2[64:128, :], ALU.add, eng="pool")
        else:
            k.tt(out_bf, t1.full(), t2.full(), ALU.add, eng="pool")
        A.release(mr)

    def phaseB_S(l, extra=()):
        k.P.tag = "B.S"
        m = A.mark()
        QN = A.alloc([128, 8, TS], BF16)
        k.memset(QN.full(), 0.0, eng="pool")
        KN = A.alloc([128, 2, TS], BF16)
        VA = A.alloc([128, 8, 2, 128], BF16)
        KC = A.alloc([128, 2, 512], BF16)
        VC = A.alloc([128, 4, 2, 128], BF16)
        OB = A.alloc([128, 4, TS], BF16)
        w1 = wslot()
        wtile("b1", w1.full(), lambda: wload(w1.full(), wrows(d["w_in"][l], O_BQ, 512)))
        w2 = wslot()

        def ld_b2():
            for kv in range(2):
                for dup in range(2):
                    wload(w2[:, :, kv * 128 + dup * 64:kv * 128 + dup * 64 + 64], wrows(d["w_in"][l], O_BK + kv * 64, 64))
            wload(w2[:, :, 256:384], wrows(d["w_in"][l], O_BV, 128))
        wtile("b2", w2[:, :, 0:384], ld_b2)
        m2 = A.mark()
        QFs = [A.alloc([128, 512], F32) for _ in range(2)]
        QFBs = [A.alloc([128, 512], BF16) for _ in range(2)]
        qi = 0
        for g in range(2):
            htok = slice(TP + g * 512, TP + g * 512 + 512)
            tsl = slice(g * 512, g * 512 + 512)
            for cc in range(4):
                QF, QFB = QFs[qi % 2], QFBs[qi % 2]
                qi += 1
                pb = proj_fm(w1, cc * 128, 128, htok)
                normhead(pb, QF.full(), QFB.full(), GV[:, 0:1])
                rope_apply(QF.full(), QFB.full(), (QN[0:64, 2 * cc, tsl], QN[64:128, 2 * cc + 1, tsl]), tsl)
            for kv in range(2):
                QF, QFB = QFs[qi % 2], QFBs[qi % 2]
                qi += 1
                pb = proj_fm(w2, kv * 128, 128, htok)
                normhead(pb, QF.full(), QFB.full(), GV[:, 1:2])
                rope_apply(QF.full(), QFB.full(), KN[:, kv, tsl], tsl)
        A.release(m2)
        k.memset(VA[:, :, :, 64:128], 1.0, eng="pool")
        k.memset(VC[:, :, :, 64:128], 1.0, eng="pool")
        for tix in range(8):
            pv = proj_tm(w2, 256, 128, TP + tix * 128)
            k.cpx(VA[:, tix, :, 0:64], pv.v(pv.ap[:, 0:128].rearrange("p (a b) -> p a b", a=2)))
        m2 = A.mark()
        ST = [A.alloc([128, 256], F32) for _ in range(2)]
        SV = [A.alloc([128, 128], F32) for _ in range(2)]
        for i in range(4):
            st = ST[i % 2]
            src = d["cswk"][l, i * 128:(i + 1) * 128, :].rearrange("p (a b) -> p a b", a=2)
            stv = st.full()
            for dup in range(2):
                k.dma("sp", V(stv.ap.rearrange("p (a c b) -> p a c b", a=2, c=2)[:, :, dup, :], stv.bufs), DR(src))
            pb = k.bank()
            for kv in range(2):
                k.tr(pb[:, kv * 128:(kv + 1) * 128], st[:, kv * 128:(kv + 1) * 128], C["ident_f"].full())
            k.cpx(KC[:, :, i * 128:(i + 1) * 128], pb.v(pb.ap[:, 0:256].rearrange("p (a b) -> p a b", a=2)))
            sv = SV[i % 2]
            k.dma("sp", sv.full(), DR(d["cswv"][l, i * 128:(i + 1) * 128, :]))
            svv = sv.full()
            k.cpx(VC[:, i, :, 0:64], V(svv.ap.rearrange("p (a b) -> p a b", a=2), svv.bufs))
        A.release(m2)
        m2 = A.mark()
        PT = [A.alloc([128, 512], BF16) for _ in range(3)]
        RC = [A.alloc([64, 512], F32) for _ in range(2)]
        it = 0
        for h in range(8):
            if extra:
                tg_ = k.P.tag
                k.P.tag = "pre"
                extra.pop(0)()
                k.P.tag = tg_
            rows = slice((h % 2) * 64, (h % 2) * 64 + 64)
            kv = h // 4
            ob = [k.hold(), k.hold()]
            for qh in range(2):
                qsl = slice(qh * 512, qh * 512 + 512)
                for i in range(4):
                    sa = k.bank()
                    k.mm(sa[:, :], KC[:, kv, i * 128:(i + 1) * 128], QN[:, h, qsl])
                    pt = PT[it % 3]
                    it += 1
                    k.act(pt.full(), sa[:, :], AF.Exp)
                    k.mm(ob[qh][:, :], VC[:, i, kv, :], pt.full(), i == 0, False)
            for kt in range(8):
                qb0, qb1 = max(0, kt - 1), min(7, kt + 1)
                nq = (qb1 - qb0 + 1) * 128
                q0 = qb0 * 128
                sbk = k.bank()
                k.mm(sbk[:, 0:nq], KN[:, kv, kt * 128:(kt + 1) * 128], QN[:, h, q0:q0 + nq], True, False)
                for qb in range(qb0, qb1 + 1):
                    if qb == kt:
                        continue
                    msk = C["band_next"] if qb == kt + 1 else C["band_prev"]
                    k.mm(sbk[:, (qb - qb0) * 128:(qb - qb0 + 1) * 128], C["ident_bf"].full(), msk.full(), False, False)
                pt = PT[it % 3]
                it += 1
                k.act(pt[:, 0:nq], sbk[:, 0:nq], AF.Exp)
                segs = []
                a_ = q0
                while a_ < q0 + nq:
                    b_ = min(q0 + nq, (a_ // 512 + 1) * 512)
                    segs.append((a_, b_))
                    a_ = b_
                for (a_, b_) in segs:
                    qh = a_ // 512
                    k.mm(ob[qh][:, a_ - qh * 512:b_ - qh * 512], VA[:, kt, kv, :], pt[:, a_ - q0:b_ - q0], False, False)
            for qh in range(2):
                rc = RC[qh]
                k.act(rc.full(), ob[qh][64:128, :], AF.Ln, bias=ESK[0:64, h:h + 1])
                k.act(rc.full(), rc.full(), AF.Exp, scale=-1.0)
                k.tt(OB[rows, h // 2, qh * 512:(qh + 1) * 512], ob[qh][0:64, :], rc.full(), ALU.mult)
                k.unhold(ob[qh])
        A.release(m2)
        merge_quarter(l, d["w_pb"][l], O_GB, OB, False)
        A.release(m)

    def na_runs(mt):
        runs = []
        if mt <= 3:
            runs.append((0, 4, False))
        lo, hi = max(5, 2 * mt - 3), min(11, 2 * mt + 5)
        if lo <= hi:
            if lo <= 7 and hi >= 8:
                runs.append((lo, 7, True))
                runs.append((8, hi, True))
            else:
                runs.append((lo, hi, True))
        if mt >= 4:
            runs.append((12, 15, False))
        return runs

    def phaseC_S(l, extra=()):
        k.P.tag = "C.S"
        m = A.mark()
        OC = A.alloc([128, 4, TS], BF16)
        for hh in range(2):
            mh = A.mark()
            QN = A.alloc([128, 4, TS], BF16)
            k.memset(QN.full(), 0.0, eng="pool")
            KN = A.alloc([128, 2, TS], BF16)
            VA = A.alloc([128, 8, 4, 128], BF16)
            KC = A.alloc([128, 2, 512], BF16)
            VC = A.alloc([128, 4, 4, 128], BF16)
            w1 = wslot()
            wtile(("c1", hh), w1.full(), lambda w1=w1, hh=hh: (
                wload(w1[:, :, 0:256], wrows(d["w_in"][l], O_CQ + hh * 256, 256)),
                wload(w1[:, :, 256:512], wrows(d["w_in"][l], O_CK + hh * 256, 256))))
            w3 = wslot()
            wtile(("c3", hh), w3[:, :, 0:256], lambda w3=w3, hh=hh: wload(
                w3[:, :, 0:256], wrows(d["w_in"][l], O_CV + hh * 256, 256)))
            for g in range(2):
                htok = slice(TP + g * 512, TP + g * 512 + 512)
                tsl = slice(g * 512, g * 512 + 512)
                for cc in range(2):
                    pb = proj_fm(w1, cc * 128, 128, htok)
                    normhead(pb, None, None, GV[:, 2:3], out_pair=(QN[0:64, 2 * cc, tsl], QN[64:128, 2 * cc + 1, tsl]))
                    pb = proj_fm(w1, 256 + cc * 128, 128, htok)
                    normhead(pb, None, KN[:, cc, tsl], GV[:, 3:4])
            k.memset(VA[:, :, :, 64:128], 1.0, eng="pool")
            k.memset(VC[:, :, :, 64:128], 1.0, eng="pool")
            for tix in range(8):
                pv = proj_tm(w3, 0, 256, TP + tix * 128)
                k.cpx(VA[:, tix, :, 0:64], pv.v(pv.ap[:, 0:256].rearrange("p (a b) -> p a b", a=4)))
            m2 = A.mark()
            ST = [A.alloc([128, 256], F32) for _ in range(2)]
            SV = [A.alloc([128, 256], F32) for _ in range(2)]
            for i in range(4):
                st = ST[i % 2]
                k.dma("sp", st.full(), DR(d["cnak"][l, i * 128:(i + 1) * 128, hh * 256:hh * 256 + 256]))
                pb = k.bank()
                for cc in range(2):
                    k.tr(pb[:, cc * 128:(cc + 1) * 128], st[:, cc * 128:(cc + 1) * 128], C["ident_f"].full())
                k.cpx(KC[:, :, i * 128:(i + 1) * 128], pb.v(pb.ap[:, 0:256].rearrange("p (a b) -> p a b", a=2)))
                sv = SV[i % 2]
                k.dma("sp", sv.full(), DR(d["cnav"][l, i * 128:(i + 1) * 128, hh * 256:hh * 256 + 256]))
                svv = sv.full()
                k.cpx(VC[:, i, :, 0:64], V(svv.ap.rearrange("p (a b) -> p a b", a=4), svv.bufs))
            A.release(m2)
            m2 = A.mark()
            STG = A.alloc([128, 16, 64], F32)
            SFUL = A.alloc([128, 16, 64], BF16)
            SINT = A.alloc([128, 16, 64], BF16)
            PT = [A.alloc([128, 512], BF16) for _ in range(2)]
            RC = A.alloc([64, 512], F32)
            it = 0
            for hl in range(4):
                if extra:
                    tg_ = k.P.tag
                    k.P.tag = "pre"
                    extra.pop(0)()
                    k.P.tag = tg_
                h = hh * 4 + hl
                rows = slice((hl % 2) * 64, (hl % 2) * 64 + 64)
                cc = hl // 2
                base = d["rpb_pad"][l, h]
                hk = bass.AP(base.tensor, base.offset, [[1, 64], [127, 15], [1, 64]])
                k.memset(STG.full(), 0.0)
                k.dma("sp", STG[0:64, 0:15, :], DR(hk))
                k.dma("sp", STG[64:128, 1:16, :], DR(hk))
                nmk = C["na_mask"].full()
                k.tt(SFUL.full(), STG.full(), V(nmk.ap.unsqueeze(1).to_broadcast([128, 16, 64]), nmk.bufs), ALU.add)
                k.cp(SINT.full(), SFUL.full(), "act")
                k.memset(SINT[0:64, 0:4, :], NEG, eng="pool")
                k.memset(SINT[0:64, 12:16, :], NEG, eng="pool")
                k.memset(SINT[64:128, 0:5, :], NEG, eng="pool")
                k.memset(SINT[64:128, 13:16, :], NEG, eng="pool")
                ob = [k.hold(), k.hold()]
                for qh in range(2):
                    qsl = slice(qh * 512, qh * 512 + 512)
                    for i in range(4):
                        sb = k.bank()
                        k.mm(sb[:, :], KC[:, cc, i * 128:(i + 1) * 128], QN[:, hl, qsl])
                        pt = PT[it % 2]
                        it += 1
                        k.act(pt.full(), sb[:, :], AF.Exp)
                        k.mm(ob[qh][:, :], VC[:, i, hl, :], pt.full(), i == 0, False)
                for mt in range(8):
                    for (r0, r1, interior) in na_runs(mt):
                        nq = (r1 - r0 + 1) * 64
                        q0 = r0 * 64
                        qh = q0 // 512
                        b0 = 7 + r0 - 2 * mt
                        strip = SINT if interior else SFUL
                        sb = k.bank()
                        k.mm(sb[:, 0:nq], KN[:, cc, mt * 128:(mt + 1) * 128], QN[:, hl, q0:q0 + nq], True, False)
                        sv_ = strip[:, b0:b0 + (r1 - r0 + 1), :]
                        k.mm(sb[:, 0:nq], C["jj"].full(), V(sv_.ap.rearrange("p a b -> p (a b)"), sv_.bufs), False, True)
                        pt = PT[it % 2]
                        it += 1
                        k.act(pt[:, 0:nq], sb[:, 0:nq], AF.Exp)
                        k.mm(ob[qh][:, q0 - qh * 512:q0 - qh * 512 + nq], VA[:, mt, hl, :], pt[:, 0:nq], False, False)
                for qh in range(2):
                    k.recip_act(RC.full(), ob[qh][64:128, :])
                    k.tt(OC[rows, h // 2, qh * 512:(qh + 1) * 512], ob[qh][0:64, :], RC.full(), ALU.mult)
                    k.unhold(ob[qh])
            A.release(m2)
            A.release(mh)
        merge_quarter(l, d["w_pc"][l], O_GC, OC, False)
        A.release(m)

    def wo_groups(l, groups):
        k.P.tag = "wo"
        for half in range(2):
            ws = wslot()
            wtile(("wo", half), ws.full(), lambda ws=ws, half=half: wload(ws.full(), wrows(d["w_o"][l], half * 512, 512)))
            for g in groups:
                n = g["n"]
                for j in range(4):
                    oc = half * 4 + j
                    pb = k.bank()
                    for kc in range(8):
                        k.mm(pb[:, 0:n], ws[:, kc, j * 128:(j + 1) * 128], g["mg"](kc), start=(kc == 0), stop=(kc == 7))
                    xv = g["x"](oc)
                    k.stt(xv, pb[:, 0:n], MODVL[cur[0]][:, 16 + oc, g["ci"]:g["ci"] + 1], xv, ALU.mult, ALU.add)

    def norm_q(l, XQ, HQ2):
        mn = A.mark()
        sq = [A.alloc([128, 256], BF16) for _ in range(2)]
        rs = A.alloc([128, 256], F32)
        tmp = [A.alloc([128, 256], F32) for _ in range(2)]
        sb = k.bank()
        for oc in range(8):
            s_ = sq[oc % 2]
            k.act(s_.full(), XQ[:, oc, :], AF.Square)
            k.mm(sb[:, 0:256], C["ones_all"].full(), s_.full(), start=(oc == 0), stop=(oc == 7))
        k.rstd(rs.full(), sb[:, 0:256], 1024.0)
        for oc in range(8):
            t_ = tmp[oc % 2]
            k.stt(t_.full(), XQ[:, oc, :], c["SCLL"][cur[0]][:, 1, 1, oc:oc + 1], rs.full(), ALU.mult, ALU.mult)
            k.act(HQ2[:, oc, :], t_.full(), AF.Identity, bias=MODVL[cur[0]][:, 24 + oc, 1:2])
        A.release(mn)

    pending = []

    def tail(l, extra):
        m = A.mark()
        XQ = A.alloc([128, 8, 256], F32)
        HQ2 = A.alloc([128, 8, 256], BF16)
        k.dma_dyn(XQ.full(), X, "cabs", 256, X[:, :, TP:TT])
        wo_groups(l, [dict(mg=lambda kc: MGQ[:, kc, :], x=lambda oc: XQ[:, oc, :], n=256, ci=1)])
        k.P.tag = "mlp"
        c["norm_group"](l, 1, 0)
        norm_q(l, XQ, HQ2)
        G = [dict(h=lambda kc: H[:, kc, 0:512], u=slice(0, 512), x=lambda oc: X[:, oc, 0:512], n=512, ci=0),
             dict(h=lambda kc: HQ2[:, kc, :], u=slice(512, 768), x=lambda oc: XQ[:, oc, :], n=256, ci=1)]
        U = A.alloc([128, 16, 768], BF16)
        RL = [A.alloc([128, 512], BF16) for _ in range(2)]
        ri = 0
        for hh in range(2):
            for t4 in range(4):
                ws = wslot()
                wload(ws.full(), wrows(d["w_fc1"][l], hh * 2048 + t4 * 512, 512))
                for g in G:
                    n = g["n"]
                    for j in range(4):
                        pb = proj_fm(ws, j * 128, 128, (g["h"], n))
                        rl = RL[ri % 2]
                        ri += 1
                        uv = U[:, t4 * 4 + j, g["u"]]
                        if j % 2 == 0:
                            k.act(rl[:, 0:n], pb[:, 0:n], AF.Relu)
                            k.tt(uv, rl[:, 0:n], rl[:, 0:n], ALU.mult)
                        else:
                            k.ts(rl[:, 0:n], pb[:, 0:n], 0.0, ALU.max)
                            k.tt(uv, rl[:, 0:n], rl[:, 0:n], ALU.mult, eng="pool")
                if extra:
                    extra.pop(0)()
            for oh in range(2):
                wsa = wslot()
                wsb = wslot()
                r0 = hh * 2048
                wload(wsa.full(), d["w_fc2"][l][r0:r0 + 1024, oh * 512:oh * 512 + 512].rearrange("(kc p) n -> p kc n", p=128))
                wload(wsb.full(), d["w_fc2"][l][r0 + 1024:r0 + 2048, oh * 512:oh * 512 + 512].rearrange("(kc p) n -> p kc n", p=128))
                for g in G:
                    n = g["n"]
                    for j in range(4):
                        oc = oh * 4 + j
                        pb = k.bank()
                        for kk in range(16):
                            wsx = wsa if kk < 8 else wsb
                            k.mm(pb[:, 0:n], wsx[:, kk % 8, j * 128:(j + 1) * 128], U[:, kk, g["u"]], start=(kk == 0), stop=(kk == 15))
                        xv = g["x"](oc)
                        k.stt(xv, pb[:, 0:n], MODVL[cur[0]][:, 40 + oc, g["ci"]:g["ci"] + 1], xv, ALU.mult, ALU.add)
                if extra:
                    extra.pop(0)()
        while extra:
            extra.pop(0)()
        if l + 1 < depth:
            k.dma("sp", V(XQI.ap.rearrange("(oc p) t -> p oc t", p=128), XQI.bufs), XQ.full())

            def gather():
                tg = k.P.tag
                k.P.tag = "gather"
                k.allgather(XQA, XQI, [[0, 1, 2, 3], [4, 5, 6, 7]])
                for r in range(4):
                    k.dma("sp", X[:, :, TP + r * 256:TP + (r + 1) * 256],
                          V(XQA.ap[r * 1024:(r + 1) * 1024, :].rearrange("(oc p) t -> p oc t", p=128), XQA.bufs))
                k.P.tag = tg
            pending.append(gather)
        else:
            st2 = [A.alloc([128, D], F32) for _ in range(2)]
            for tix in range(2):
                s_ = st2[tix]
                for q4 in range(2):
                    pb = k.bank()
                    for j in range(4):
                        oc = q4 * 4 + j
                        k.tr(pb[:, j * 128:(j + 1) * 128], XQ[:, oc, tix * 128:(tix + 1) * 128], C["ident_f"].full())
                    k.cpx(s_[:, q4 * 512:(q4 + 1) * 512], pb[:, :])
                k.dma("sp", DR(o["ysq"][tix * 128:(tix + 1) * 128, :]), s_.full())
        A.release(m)

    def store_tokens(dst, ntok, col0):
        m = A.mark()
        st2 = [A.alloc([128, D], F32) for _ in range(2)]
        for tix in range(ntok // 128):
            s_ = st2[tix % 2]
            c0 = col0 + tix * 128
            for q4 in range(2):
                pb = k.bank()
                for j in range(4):
                    oc = q4 * 4 + j
                    k.tr(pb[:, j * 128:(j + 1) * 128], X[:, oc, c0:c0 + 128], C["ident_f"].full())
                k.cpx(s_[:, q4 * 512:(q4 + 1) * 512], pb[:, :])
            k.dma("sp", DR(dst[tix * 128:(tix + 1) * 128, :]), s_.full())
        A.release(m)

    stop = c["stop"]
    for l in range(depth):
        if stop == "load":
            break
        cur[0] = l % 2
        k.P.tag = "pre"
        if l == 0:
            c["layer_small"](0)
            for f_ in c["mod_steps"](0):
                f_()
        c["norm_group"](l, 0, 0)
        wc_valid.clear()
        phaseA(l, 0, 4, [(0, 2), (2, 4)], True)
        phaseB_P(l)
        phaseC_P(l)
        wo_groups(l, [dict(mg=lambda kc: MG[:, kc, 0:512], x=lambda oc: X[:, oc, 0:512], n=512, ci=0)])
        k.P.tag = "pre"
        c["norm_group"](l, 0, 1)
        c["norm_group"](l, 0, 2)
        k.dma_dyn(HQ.full(), H, "cabs", 256, H[:, :, TP:TT])
        ex = []
        if l + 1 < depth:
            ex = [lambda l=l: c["layer_small"](l + 1)] + c["mod_steps"](l + 1)
        phaseA(l, TP, 8, [(0, 8)], False)
        phaseB_S(l, ex)
        phaseC_S(l, ex)
        tail(l, ex)
    store_tokens(o["yp"], TP, 0)


_CACHE = {}


def make_in_maps(inp):
    consts = make_consts()
    f = lambda a: np.ascontiguousarray(np.asarray(a), dtype=np.float32)
    shared = {}
    for nm in ("w_mod", "w_in", "w_a2_f", "b_a_f", "w_a2_b", "b_a_b", "gla_onorm", "qn_swa", "kn_swa",
               "qn_na", "kn_na", "sink_swa", "w_pa", "w_pb", "w_pc", "w_o", "w_fc1", "w_fc2"):
        shared[nm] = f(inp[nm])
    shared["b_mod"] = f(inp["b_mod"]).reshape(DEPTH, 48, 128)
    shared["norm1"] = f(inp["norm1"]).reshape(DEPTH, 8, 128)
    shared["norm2"] = f(inp["norm2"]).reshape(DEPTH, 8, 128)
    rp = f(inp["rpb_na"])[:, :, ::-1, ::-1]
    pad = np.zeros((DEPTH, 8, 15, 127), np.float32)
    pad[..., 48:79] = rp
    shared["rpb_pad"] = pad
    shared.update({nm: consts[nm] for nm, _, _ in CONST_SPECS})
    xp = f(inp["x_prompt"])
    xs = f(inp["x_sample"])
    maps = []
    for ci in range(NCORES):
        b = ci // 4
        m = dict(shared)
        m["xp"] = np.ascontiguousarray(xp[2 * ci:2 * ci + 2].reshape(TP, D))
        m["xs"] = np.ascontiguousarray(xs[b])
        cond = np.stack([f(inp["c_ctx"]), f(inp["c"])[b]], 0).reshape(16, 128)
        m["cond"] = np.ascontiguousarray(cond)
        m["rk"] = np.array([[TP + (ci % 4) * 256, (ci % 4) * 256]], np.int32)
        m["st_gla"] = np.ascontiguousarray(f(inp["state_gla"])[b])
        m["cswk"] = np.ascontiguousarray(f(inp["cache_swa_k"])[b].reshape(DEPTH, 512, 128))
        m["cswv"] = np.ascontiguousarray(f(inp["cache_swa_v"])[b].reshape(DEPTH, 512, 128))
        m["cnak"] = np.ascontiguousarray(f(inp["cache_na_k"])[b].reshape(DEPTH, 512, 512))
        m["cnav"] = np.ascontiguousarray(f(inp["cache_na_v"])[b].reshape(DEPTH, 512, 512))
        maps.append(m)
    return maps


def assemble(results):
    yp = np.stack([r["yp"].reshape(2, 256, D) for r in results], 0).reshape(16, 256, D)
    ys = np.stack([np.concatenate([results[4 * b + r]["ysq"] for r in range(4)], 0) for b in range(2)], 0)
    gla = np.concatenate([r["o_gla"] for r in results], 0)
    swk = np.concatenate([r["o_swk"].reshape(2, DEPTH, 256, 2, 64) for r in results], 0)
    swv = np.concatenate([r["o_swv"].reshape(2, DEPTH, 256, 2, 64) for r in results], 0)
    nak = np.concatenate([r["o_nak"].reshape(2, DEPTH, 256, 8, 64) for r in results], 0)
    nav = np.concatenate([r["o_nav"].reshape(2, DEPTH, 256, 8, 64) for r in results], 0)
    return tuple(np.ascontiguousarray(a, dtype=np.float32) for a in (yp, ys, gla, swk, swv, nak, nav))


def kernel(**inputs):
    if "nc" not in _CACHE:
        _CACHE["nc"] = build_program()
    nc = _CACHE["nc"]
    maps = make_in_maps(inputs)
    res = run_bass_kernel_spmd(nc, maps, core_ids=list(range(NCORES)))
    return assemble(res.results)
```

```python
import os
import numpy as np
import ml_dtypes
import concourse.bass as bass
import concourse.mybir as mybir
from concourse.bass_utils import run_bass_kernel_spmd

F32 = mybir.dt.float32
BF16 = mybir.dt.bfloat16
AF = mybir.ActivationFunctionType
ALU = mybir.AluOpType

D = 1024
DEPTH = 4
NCORES = 8
TP = 512
TS = 1024
TT = TP + TS
IN_W = 6944
NEG = -30000.0
PA_STOP = float(os.environ.get('PA_STOP', '99'))
EPS = 1e-6

O_AQ, O_AK, O_AV, O_AR, O_ALF, O_ALB = 0, 256, 512, 1024, 1536, 1552
O_BQ, O_BK, O_BV = 1568, 2080, 2208
O_CQ, O_CK, O_CV = 2336, 2848, 3360
O_GA, O_GB, O_GC = 3872, 4896, 5920


class Buf:
    __slots__ = ("lw", "rd", "name")

    def __init__(self, name=""):
        self.lw = None
        self.rd = []
        self.name = name


class Op:
    __slots__ = ("eng", "fn", "deps", "dma", "ticket", "semkey", "nsig", "idx", "cost", "tag", "st", "fi")


class Prog:
    ENGS = ("pe", "act", "dve", "pool", "sp")

    def __init__(self, nc):
        self.nc = nc
        self.ops = []
        self.dyn = {}
        self.dyn_spec = {}

    def op(self, eng, fn, reads=(), writes=(), dma=False, cost=500.0):
        o = Op()
        o.tag = getattr(self, "tag", "")
        o.cost = cost
        o.eng = eng
        o.fn = fn
        o.dma = dma
        o.idx = len(self.ops)
        deps = set()
        for b in reads:
            if b.lw is not None:
                deps.add(b.lw)
        for b in writes:
            if b.lw is not None:
                deps.add(b.lw)
            deps.update(b.rd)
        o.deps = deps
        self.ops.append(o)
        for b in reads:
            b.rd.append(o.idx)
        for b in writes:
            b.lw = o.idx
            b.rd = []
        return o


    def schedule(self, window=40):
        ops = self.ops
        n = len(ops)
        left = [len(o.deps) for o in ops]
        users = [[] for _ in ops]
        for o in ops:
            for dd in o.deps:
                users[dd].append(o.idx)
        ready = [0.0] * n
        fin = [0.0] * n
        done = [False] * n
        cpl = [0.0] * n
        use_cp = os.environ.get("NO_CP") != "1"
        for o in reversed(ops):
            tail_ = 0.0
            for u in users[o.idx]:
                if cpl[u] > tail_:
                    tail_ = cpl[u]
            cpl[o.idx] = tail_ + o.cost + (2000.0 if o.dma else 150.0)
        pend = {e: [o.idx for o in ops if o.eng == e] for e in self.ENGS}
        head = {e: 0 for e in self.ENGS}
        free = {e: 0.0 for e in self.ENGS}
        order = {e: [] for e in self.ENGS}
        pipe = 0.0
        remaining = n
        glob = []
        while remaining:
            best = None
            for e in self.ENGS:
                lst = pend[e]
                i = head[e]
                while i < len(lst) and done[lst[i]]:
                    i += 1
                head[e] = i
                cnt = 0
                fe = free[e]
                while i < len(lst) and cnt < window:
                    idx = lst[i]
                    i += 1
                    if done[idx]:
                        continue
                    cnt += 1
                    if left[idx] == 0:
                        st = ready[idx] if ready[idx] > fe else fe
                        key = (st, -cpl[idx], idx) if use_cp else (st, 0.0, idx)
                        if best is None or key < best[0]:
                            best = (key, e, idx)
                        if st <= fe and not use_cp:
                            break
            assert best is not None, "scheduler deadlock"
            (st, _, _), e, idx = best
            o = ops[idx]
            if o.dma:
                issue = 1000.0 if e == "pool" else 100.0
                free[e] = st + issue
                p0 = max(pipe, st + issue)
                pipe = p0 + o.cost
                f = pipe + 2000.0
            else:
                free[e] = st + o.cost
                f = st + o.cost + 150.0
            fin[idx] = f
            o.st = st
            o.fi = f
            done[idx] = True
            order[e].append(idx)
            glob.append(idx)
            remaining -= 1
            for u in users[idx]:
                left[u] -= 1
                if ready[u] < f:
                    ready[u] = f
        self.order = order
        self.est_ns = max(fin) if fin else 0.0

    def emit(self, final_wait_eng="sp"):
        nc = self.nc
        ops = self.ops
        KD = {"sp": 14, "pool": 10, "act": 4}
        needed = [False] * len(ops)
        for o in ops:
            for d in o.deps:
                a = ops[d]
                if a.eng == o.eng and o.eng == "pe" and not a.dma and not o.dma:
                    continue
                needed[d] = True
        cnt = {e: 0 for e in self.ENGS}
        dcnt = {e: 0 for e in KD}
        order = getattr(self, "order", None)
        if order is None:
            order = {e: [o.idx for o in ops if o.eng == e] for e in self.ENGS}
        seq = [ops[i] for e in self.ENGS for i in order[e]]
        for o in seq:
            if o.dma:
                n = dcnt[o.eng]
                dcnt[o.eng] += 1
                k = n % KD[o.eng]
                o.semkey = (o.eng, k)
                o.ticket = 16 * (n // KD[o.eng] + 1)
            else:
                o.semkey = o.eng
                if needed[o.idx]:
                    cnt[o.eng] += 1
                    o.ticket = cnt[o.eng]
                else:
                    o.ticket = None
        sems = {}
        import contextlib
        with contextlib.ExitStack() as st:
            for e in self.ENGS:
                sems[e] = st.enter_context(nc.semaphore("s_" + e))
            for e, k in KD.items():
                for i in range(k):
                    sems[(e, i)] = st.enter_context(nc.semaphore("d_%s%d" % (e, i)))
            block = st.enter_context(nc.Block())
            per_eng = {e: [ops[i] for i in order[e]] for e in self.ENGS}

            def run(eng_name, eng):
                seen = {}
                if eng_name == "sp":
                    for key, (ap, lo, hi) in getattr(self, "dyn_spec", {}).items():
                        reg = eng.alloc_register("dyn_" + key)
                        eng.reg_load(reg, ap)
                        self.dyn[key] = eng.snap(reg, min_val=lo, max_val=hi)
                for o in per_eng[eng_name]:
                    waits = {}
                    for d in o.deps:
                        a = ops[d]
                        if a.eng == o.eng and o.eng == "pe" and not a.dma and not o.dma:
                            continue
                        if a.ticket is None:
                            continue
                        if waits.get(a.semkey, 0) < a.ticket:
                            waits[a.semkey] = a.ticket
                    if o.dma:
                        prev = o.ticket - 16
                        if prev > 0 and waits.get(o.semkey, 0) < prev:
                            waits[o.semkey] = prev
                    for key, val in waits.items():
                        if seen.get(key, 0) < val:
                            eng.wait_ge(sems[key], val)
                            seen[key] = val
                    ins = o.fn(eng)
                    if o.dma:
                        ins.then_inc(sems[o.semkey], 16)
                    elif o.ticket is not None:
                        ins.then_inc(sems[o.semkey], 1)
                if eng_name in KD:
                    n = dcnt[eng_name]
                    for k in range(min(n, KD[eng_name])):
                        tot = 16 * ((n - 1 - k) // KD[eng_name] + 1)
                        if seen.get((eng_name, k), 0) < tot:
                            eng.wait_ge(sems[(eng_name, k)], tot)

            @block.tensor
            def _(e):
                run("pe", e)

            @block.scalar
            def _(e):
                run("act", e)

            @block.vector
            def _(e):
                run("dve", e)

            @block.gpsimd
            def _(e):
                run("pool", e)

            @block.sync
            def _(e):
                run("sp", e)


def make_consts():
    c = {}
    bf = ml_dtypes.bfloat16
    c["ident_bf"] = np.eye(128, dtype=np.float32).astype(bf)
    c["ident_f"] = np.eye(128, dtype=np.float32)
    bd = np.zeros((128, 128), np.float32)
    bd[:64, :64] = 1.0
    bd[64:, 64:] = 1.0
    c["ones_bd"] = bd.astype(bf)
    c["ones_all"] = np.ones((128, 128), np.float32).astype(bf)
    s = np.arange(128)[:, None]
    t = np.arange(128)[None, :]
    c["trif"] = np.where(s <= t, -1.0 / 16.0, 0.0).astype(np.float32).astype(bf)
    c["trib"] = np.where(s >= t, -1.0 / 16.0, 0.0).astype(np.float32).astype(bf)
    c["maskf"] = np.where(s <= t, 1.0, 0.0).astype(np.float32).astype(bf)
    c["maskb"] = np.where(s >= t, 1.0, 0.0).astype(np.float32).astype(bf)
    c["band_next"] = np.where(t <= s, 0.0, NEG).astype(np.float32).astype(bf)
    c["band_prev"] = np.where(s <= t, 0.0, NEG).astype(np.float32).astype(bf)
    nf = 16
    inv_freq = (10000.0 ** (-np.arange(nf, dtype=np.float32) / nf)).astype(np.float32)
    tt = np.arange(TS)
    row = (tt // 64).astype(np.float32)
    col = (tt % 64).astype(np.float32)
    ang = np.zeros((64, TS), np.float32)
    for d in range(64):
        pos = row if d < 32 else col
        ang[d] = pos * inv_freq[d % 16]
    cos = np.cos(ang).astype(np.float32)
    sin = np.sin(ang).astype(np.float32)
    c["rope_cos"] = np.concatenate([cos, cos], 0)
    c["rope_sin"] = np.concatenate([sin, sin], 0)
    Pm = np.zeros((128, 128), np.float32)
    for d in range(128):
        if (d % 32) < 16:
            Pm[d, d + 16] = -1.0
        else:
            Pm[d, d - 16] = 1.0
    c["rope_pT"] = np.ascontiguousarray(Pm.T).astype(bf)
    J = np.zeros((64, 64), np.float32)
    for i in range(64):
        J[i, 63 - i] = 1.0
    JJ = np.zeros((128, 128), np.float32)
    JJ[:64, :64] = J
    JJ[64:, 64:] = J
    c["jj"] = JJ.astype(bf)
    cq = np.arange(64)[None, :]
    ckp = np.arange(64)[:, None]
    ck = 63 - ckp
    cs = np.clip(cq - 8, 0, 48)
    ok = (ck >= cs) & (ck < cs + 16)
    nm_ = np.where(ok, 0.0, NEG).astype(np.float32)
    c["na_mask"] = np.concatenate([nm_, nm_], 0)
    return c


CONST_SPECS = [
    ("ident_bf", [128, 128], BF16), ("ident_f", [128, 128], F32), ("ones_bd", [128, 128], BF16),
    ("ones_all", [128, 128], BF16), ("trif", [128, 128], BF16), ("trib", [128, 128], BF16),
    ("maskf", [128, 128], BF16), ("maskb", [128, 128], BF16), ("band_next", [128, 128], BF16),
    ("band_prev", [128, 128], BF16), ("rope_cos", [128, TS], F32), ("rope_sin", [128, TS], F32),
    ("rope_pT", [128, 128], BF16), ("jj", [128, 128], BF16), ("na_mask", [128, 64], F32),
]


GRAN = 512


class V:
    __slots__ = ("ap", "bufs", "excl")

    def __init__(self, ap, bufs, excl=False):
        self.ap = ap
        self.bufs = bufs
        self.excl = excl


def DR(ap):
    return V(ap, [])


class Arena:
    def __init__(self, nc, nbytes):
        self.nc = nc
        self.nbytes = nbytes
        self.t = nc.alloc_sbuf_tensor("arena", [128, nbytes // 4], F32)
        self.g = [Buf("g%d" % i) for i in range((nbytes + GRAN - 1) // GRAN)]
        self.top = 0

    def bufs(self, lo, hi):
        return self.g[lo // GRAN:(hi - 1) // GRAN + 1]

    def alloc(self, shape, dtype, align=GRAN):
        esz = 4 if dtype == F32 else 2
        n = 1
        for d in shape[1:]:
            n *= d
        nb = (n * esz + 3) // 4 * 4
        off = (self.top + align - 1) // align * align
        assert off + nb <= self.nbytes, ("arena overflow", off, nb, self.nbytes)
        self.top = off + nb
        return Tile(self, off, shape, dtype)

    def mark(self):
        return self.top

    def release(self, m):
        self.top = m


class Tile:
    def __init__(self, arena, off, shape, dtype):
        self.arena = arena
        self.off = off
        self.shape = tuple(shape)
        self.dt = dtype
        self.esz = 4 if dtype == F32 else 2
        n = 1
        for d in shape[1:]:
            n *= d
        w0 = off // 4
        w1 = w0 + (n * self.esz + 3) // 4
        base = arena.t[:, w0:w1]
        if dtype != F32:
            base = base.bitcast(dtype)
        if len(shape) > 2:
            names = ["d%d" % i for i in range(len(shape) - 1)]
            pat = "p (" + " ".join(names) + ") -> p " + " ".join(names)
            base = base.rearrange(pat, **{nm: shape[i + 1] for i, nm in enumerate(names[:-1])})
        self.ap = base[0:shape[0]]
        st = []
        acc = 1
        for d in reversed(shape[1:]):
            st.append(acc)
            acc *= d
        self.strides = list(reversed(st))

    def __getitem__(self, key):
        if not isinstance(key, tuple):
            key = (key,)
        ap = self.ap[key]
        fk = list(key[1:]) + [slice(None)] * (len(self.shape) - len(key))
        dims = []
        for k, d, s in zip(fk, self.shape[1:], self.strides):
            if isinstance(k, slice):
                a, b, stp = k.indices(d)
                cnt = max(0, (b - a + stp - 1) // stp)
                dims.append((a, cnt, stp, s, d))
            else:
                dims.append((k, 1, 1, s, d))
        gset = {}
        esz = self.esz
        off = self.off
        arena = self.arena

        def rec(i, base):
            a, cnt, stp, st, d = dims[i]
            inner_full = all(dd[1] == dd[4] and dd[2] == 1 for dd in dims[i + 1:])
            if stp == 1 and inner_full:
                lo = base + a * st
                hi = base + (a + cnt) * st
                for b_ in arena.bufs(off + lo * esz, off + hi * esz):
                    gset[id(b_)] = b_
                return
            if i == len(dims) - 1:
                for j in range(cnt):
                    lo = base + (a + j * stp) * st
                    for b_ in arena.bufs(off + lo * esz, off + (lo + 1) * esz):
                        gset[id(b_)] = b_
                return
            for j in range(cnt):
                rec(i + 1, base + (a + j * stp) * st)

        rec(0, 0)
        return V(ap, list(gset.values()))

    def full(self):
        return self[tuple(slice(None) for _ in self.shape)]


class Bank:
    def __init__(self, nc, i):
        self.t = nc.alloc_psum_tensor("pb%d" % i, [128, 512], F32)
        self.buf = Buf("pb%d" % i)
        self.ap = self.t[:, :]

    def __getitem__(self, key):
        return V(self.ap[key], [self.buf], True)

    def v(self, ap):
        return V(ap, [self.buf], True)

    def bf(self):
        return self.ap.bitcast(BF16)


class K:
    def __init__(self, nc):
        self.nc = nc
        self.P = Prog(nc)
        self.A = Arena(nc, 205 * 1024)
        self.banks = [Bank(nc, i) for i in range(8)]
        self.bi = 0
        self.flip = 0
        self.held = []
        self.fence = V(None, [Buf('fence')])

    def bank(self):
        while True:
            b = self.banks[self.bi]
            self.bi = (self.bi + 1) % 8
            if b not in self.held:
                return b

    def hold(self):
        b = self.bank()
        self.held.append(b)
        return b

    def unhold(self, b):
        self.held.remove(b)

    def _rw(self, reads, writes):
        r, w = [], []
        for v in reads:
            if isinstance(v, V):
                (w if v.excl else r).extend(v.bufs)
        for v in writes:
            w.extend(v.bufs)
        return r, w

    def op(self, eng, fn, reads, writes, dma=False):
        r, w = self._rw(reads, writes)
        try:
            shp = writes[0].ap.shape
            nfree = 1
            for x in shp[1:]:
                nfree *= x
            npart = shp[0]
        except Exception:
            nfree, npart = 512, 128
        if dma:
            try:
                esz = 4 if reads[0].ap.dtype == F32 else 2
            except Exception:
                esz = 4
            cost = npart * nfree * esz / 180.0
        elif eng == "pe":
            cost = max(64.0, nfree) * 0.46 + (70.0 if nfree < 256 else 15.0)
        elif eng == "act":
            cost = 230.0 + nfree * 0.75
        elif eng == "dve":
            cost = 120.0 + nfree * 0.95
        else:
            cost = 250.0 + nfree * 1.9
        return self.P.op(eng, fn, r, w, dma, cost)

    def mm(self, out, lhsT, rhs, start=True, stop=True):
        self.op("pe", lambda e: e.matmul(out.ap, lhsT=lhsT.ap, rhs=rhs.ap, start=start, stop=stop,
                                         skip_group_check=True), [lhsT, rhs], [out])

    def tr(self, out, in_, ident):
        self.op("pe", lambda e: e.transpose(out.ap, in_.ap, ident.ap), [in_, ident], [out])

    def act(self, out, in_, func, bias=None, scale=None):
        kw = {}
        rd = [in_]
        if bias is not None:
            kw["bias"] = bias.ap if isinstance(bias, V) else bias
            rd.append(bias)
        if scale is not None:
            kw["scale"] = scale.ap if isinstance(scale, V) else scale
            rd.append(scale)
        self.op("act", lambda e: e.activation(out.ap, in_.ap, func, **kw), rd, [out])

    def tt(self, out, a, b, op, eng="dve"):
        self.op(eng, lambda e: e.tensor_tensor(out.ap, a.ap, b.ap, op), [a, b], [out])

    def ts(self, out, a, s1, op0, s2=None, op1=None, eng="dve"):
        rd = [a, s1, s2]
        s1a = s1.ap if isinstance(s1, V) else s1
        s2a = s2.ap if isinstance(s2, V) else s2
        if op1 is None:
            self.op(eng, lambda e: e.tensor_scalar(out.ap, a.ap, s1a, None, op0), rd, [out])
        else:
            self.op(eng, lambda e: e.tensor_scalar(out.ap, a.ap, s1a, s2a, op0, op1), rd, [out])

    def stt(self, out, in0, scalar, in1, op0, op1, eng="dve"):
        sa = scalar.ap if isinstance(scalar, V) else scalar
        self.op(eng, lambda e: e.scalar_tensor_tensor(out.ap, in0.ap, sa, in1.ap, op0, op1),
                [in0, scalar, in1], [out])

    def cp(self, out, in_, eng="dve"):
        if eng == "act":
            self.op("act", lambda e: e.copy(out.ap, in_.ap), [in_], [out])
        else:
            self.op(eng, lambda e: e.tensor_copy(out.ap, in_.ap), [in_], [out])

    def cpx(self, out, in_):
        self.flip ^= 1
        self.cp(out, in_, "act" if self.flip else "dve")

    def recip(self, out, in_):
        self.op("dve", lambda e: e.reciprocal(out.ap, in_.ap), [in_], [out])

    def rstd(self, out, ss, n):
        self.act(out, ss, AF.Ln, bias=EPS, scale=1.0 / n)
        self.act(out, out, AF.Exp, scale=-0.5)

    def recip_act(self, out, in_):
        self.act(out, in_, AF.Ln)
        self.act(out, out, AF.Exp, scale=-1.0)

    def memset(self, out, val, eng="dve"):
        self.op(eng, lambda e: e.memset(out.ap, val), [], [out])

    def dma_dyn(self, out, src_tile, key, width, track):
        P = self.P

        def fn(e):
            return e.dma_start(out=out.ap, in_=src_tile.ap[:, :, bass.ds(P.dyn[key], width)])
        self.op("sp", fn, [track], [out], dma=True)

    def allgather(self, out, in_, groups):
        o_ = self.op("pool", lambda e: e.collective_compute("AllGather", ALU.bypass, replica_groups=groups,
                                                            ins=[in_.ap.opt()], outs=[out.ap.opt()]),
                     [in_], [out, self.fence])
        o_.cost = 50000.0

    def dma(self, q, out, in_, **kw):
        rd = [in_, self.fence] if q == "pool" else [in_]
        self.op(q, lambda e: e.dma_start(out=out.ap, in_=in_.ap, **kw), rd, [out], dma=True)


def build_program(depth=DEPTH, do_sample=True, dbg_names=(), stop=None):
    nc = bass.Bass("TRN2", target_bir_lowering=False)

    def din(name, shape, dt=F32):
        return nc.dram_tensor(name, shape, dt, kind="ExternalInput").ap()

    def dout(name, shape):
        return nc.dram_tensor(name, shape, F32, kind="ExternalOutput").ap()

    d = {}
    d["xp"] = din("xp", [TP, D])
    d["xs"] = din("xs", [TS, D])
    d["cond"] = din("cond", [16, 128])
    d["rk"] = din("rk", [1, 2], mybir.dt.int32)
    d["st_gla"] = din("st_gla", [DEPTH, 2, 4, 64, 128])
    d["cswk"] = din("cswk", [DEPTH, 512, 128])
    d["cswv"] = din("cswv", [DEPTH, 512, 128])
    d["cnak"] = din("cnak", [DEPTH, 512, 512])
    d["cnav"] = din("cnav", [DEPTH, 512, 512])
    d["w_mod"] = din("w_mod", [DEPTH, D, 6 * D])
    d["b_mod"] = din("b_mod", [DEPTH, 48, 128])
    d["norm1"] = din("norm1", [DEPTH, 8, 128])
    d["norm2"] = din("norm2", [DEPTH, 8, 128])
    d["w_in"] = din("w_in", [DEPTH, D, IN_W])
    d["w_a2_f"] = din("w_a2_f", [DEPTH, 16, 256])
    d["b_a_f"] = din("b_a_f", [DEPTH, 256])
    d["w_a2_b"] = din("w_a2_b", [DEPTH, 16, 256])
    d["b_a_b"] = din("b_a_b", [DEPTH, 256])
    d["gla_onorm"] = din("gla_onorm", [DEPTH, 128])
    for nm in ("qn_swa", "kn_swa", "qn_na", "kn_na"):
        d[nm] = din(nm, [DEPTH, 64])
    d["sink_swa"] = din("sink_swa", [DEPTH, 8])
    d["rpb_pad"] = din("rpb_pad", [DEPTH, 8, 15, 127])
    for nm in ("w_pa", "w_pb", "w_pc"):
        d[nm] = din(nm, [DEPTH, 512, D])
    d["w_o"] = din("w_o", [DEPTH, D, D])
    d["w_fc1"] = din("w_fc1", [DEPTH, D, 4 * D])
    d["w_fc2"] = din("w_fc2", [DEPTH, 4 * D, D])
    for nm, shp, dt in CONST_SPECS:
        d[nm] = din(nm, shp, dt)
    o = {}
    o["yp"] = dout("yp", [TP, D])
    o["ysq"] = dout("ysq", [256, D])
    o["o_gla"] = dout("o_gla", [2, DEPTH, 2, 4, 64, 128])
    o["o_swk"] = dout("o_swk", [2, DEPTH, 256, 128])
    o["o_swv"] = dout("o_swv", [2, DEPTH, 256, 128])
    o["o_nak"] = dout("o_nak", [2, DEPTH, 256, 512])
    o["o_nav"] = dout("o_nav", [2, DEPTH, 256, 512])

    k = K(nc)
    A = k.A
    k.P.dyn_spec["cabs"] = (d["rk"][0:1, 0:1], TP, TP + 768)
    k.P.dyn_spec["crel"] = (d["rk"][0:1, 1:2], 0, 768)
    xq_in = nc.dram_tensor("xq_in", [1024, 256], F32)
    xq_all = nc.dram_tensor("xq_all", [4096, 256], F32)
    XQI = V(xq_in.ap(), [Buf("xq_in")])
    XQA = V(xq_all.ap(), [Buf("xq_all")])
    X = A.alloc([128, 8, TT], F32)
    H = A.alloc([128, 8, TT], BF16)
    MG = A.alloc([128, 8, TP], BF16)
    MGQ = A.alloc([128, 8, 256], BF16)
    HQ = A.alloc([128, 8, 256], BF16)
    WS = [A.alloc([128, 8, 512], BF16) for _ in range(4)]
    wsi = [0]

    def wslot():
        w = WS[wsi[0]]
        wsi[0] = (wsi[0] + 1) % len(WS)
        return w

    C = {}
    for nm, shp, dt in CONST_SPECS:
        C[nm] = A.alloc(shp, dt, align=64 if shp[1] <= 128 else GRAN)
        k.dma("sp", C[nm].full(), DR(d[nm]))
    SCT = A.alloc([128, 8, 2], BF16, align=64)
    PVt = A.alloc([128, DEPTH, 64], F32)
    MODVL = [A.alloc([128, 48, 2], F32, align=64) for _ in range(2)]
    SCLL = [A.alloc([128, 2, 2, 8], F32, align=64) for _ in range(2)]
    cur = [0]
    GVL = [A.alloc([128, 8], F32, align=64) for _ in range(2)]
    ESKL = [A.alloc([128, 8], F32, align=64) for _ in range(2)]
    ALR = A.alloc([64, TS], BF16)
    WA2L = [A.alloc([64, 512], BF16) for _ in range(2)]

    class _Cur:
        def __init__(self, lst):
            self.lst = lst

        def __getitem__(self, key):
            return self.lst[cur[0]][key]

        def full(self):
            return self.lst[cur[0]].full()

    GV = _Cur(GVL)
    ESK = _Cur(ESKL)
    WA2 = _Cur(WA2L)
    k.memset(ALR.full(), 1.0)

    dbg = {}

    def dbg_out(name, v, shape):
        if name in dbg_names:
            ap = dout("dbg_" + name, shape)
            k.dma("sp", DR(ap), v)

    m0 = A.mark()
    for l in range(DEPTH):
        stg = A.alloc([64, 128], F32)
        k.dma("sp", stg[0:48, :], DR(d["b_mod"][l]))
        k.dma("sp", stg[48:56, :], DR(d["norm1"][l]))
        k.dma("sp", stg[56:64, :], DR(d["norm2"][l]))
        pb = k.bank()
        k.tr(pb[:, 0:64], stg.full(), C["ident_f"][0:64, 0:64])
        k.cp(PVt[:, l, :], pb[:, 0:64])
    stg = A.alloc([16, 128], F32)
    k.dma("sp", stg.full(), DR(d["cond"]))
    pb = k.bank()
    k.tr(pb[:, 0:16], stg.full(), C["ident_f"][0:16, 0:16])
    for ci in range(2):
        k.act(SCT[:, :, ci], pb[:, ci * 8:(ci + 1) * 8], AF.Silu)
    A.release(m0)

    def load_tokens(src, n, col0):
        m = A.mark()
        st2 = [A.alloc([128, D], F32) for _ in range(2)]
        for tix in range(n // 128):
            s_ = st2[tix % 2]
            k.dma("sp", s_.full(), DR(src[tix * 128:(tix + 1) * 128, :]))
            for q4 in range(2):
                pb = k.bank()
                for j in range(4):
                    oc = q4 * 4 + j
                    k.tr(pb[:, j * 128:(j + 1) * 128], s_[:, oc * 128:(oc + 1) * 128], C["ident_f"].full())
                c0 = col0 + tix * 128
                k.cpx(X[:, q4 * 4:q4 * 4 + 4, c0:c0 + 128],
                      pb.v(pb.ap.rearrange("p (a b) -> p a b", a=4)))
        A.release(m)

    load_tokens(d["xp"], TP, 0)
    load_tokens(d["xs"], TS, TP)

    def wload(slot_view, src_ap):
        k.dma("pool", slot_view, DR(src_ap))

    wc_scr = {}
    wc_valid = set()

    def wtile(key, slot_view, loader):
        if os.environ.get("NO_WCACHE") == "1":
            loader()
            return
        shp = list(slot_view.ap.shape)
        if key not in wc_scr:
            t = nc.dram_tensor("wc%d" % len(wc_scr), shp, BF16)
            wc_scr[key] = V(t.ap(), [Buf("wc")])
        scr = wc_scr[key]
        if key in wc_valid:
            k.dma("sp", slot_view, scr)
        else:
            loader()
            k.dma("sp", scr, slot_view)
            wc_valid.add(key)

    def wrows(w2d, c0, n):
        return w2d[:, c0:c0 + n].rearrange("(kc p) n -> p kc n", p=128)

    GCI = [0, 1, 1]

    def mod_steps(l):
        MODV = MODVL[l % 2]
        SCL = SCLL[l % 2]
        st = {}

        def step(j):
            if j == 0:
                st["mb"] = k.hold()
            mb = st["mb"]
            ws = wslot()
            wload(ws.full(), wrows(d["w_mod"][l], j * 512, 512))
            for jj in range(4):
                ch = j * 4 + jj
                for kc in range(8):
                    k.mm(mb[:, ch * 2:ch * 2 + 2], ws[:, kc, jj * 128:(jj + 1) * 128], SCT[:, kc, :],
                         start=(kc == 0), stop=(kc == 7))

        def fin():
            mb = st["mb"]
            bm = PVt[:, l, 0:48]
            k.tt(MODV.full(), mb.v(mb.ap[:, 0:96].rearrange("p (a b) -> p a b", b=2)),
                 V(bm.ap.unsqueeze(2).to_broadcast([128, 48, 2]), bm.bufs), ALU.add)
            k.unhold(mb)
            for n in range(2):
                for ci in range(2):
                    sc = MODV[:, 8 + 24 * n:16 + 24 * n, ci]
                    k.ts(SCL[:, n, ci, :], sc, 1.0, ALU.add)
                    k.tt(SCL[:, n, ci, :], SCL[:, n, ci, :], PVt[:, l, 48 + 8 * n:56 + 8 * n], ALU.mult)

        return [(lambda j=j: step(j)) for j in range(12)] + [fin]

    def norm_group(l, n, g):
        ci = GCI[g]
        tok = slice(g * 512, (g + 1) * 512)
        m = A.mark()
        sq = [A.alloc([128, 512], BF16) for _ in range(2)]
        rs = A.alloc([128, 512], F32)
        tmp = [A.alloc([128, 512], F32) for _ in range(2)]
        sb = k.bank()
        for oc in range(8):
            s_ = sq[oc % 2]
            k.act(s_.full(), X[:, oc, tok], AF.Square)
            k.mm(sb[:, :], C["ones_all"].full(), s_.full(), start=(oc == 0), stop=(oc == 7))
        k.rstd(rs.full(), sb[:, :], 1024.0)
        for oc in range(8):
            t_ = tmp[oc % 2]
            k.stt(t_.full(), X[:, oc, tok], SCLL[cur[0]][:, n, ci, oc:oc + 1], rs.full(), ALU.mult, ALU.mult)
            k.act(H[:, oc, tok], t_.full(), AF.Identity, bias=MODVL[cur[0]][:, 24 * n + oc, ci:ci + 1])
        A.release(m)

    def proj_fm(ws, c0, mcols, tok, pb=None, col_off=0):
        if pb is None:
            pb = k.bank()
        if isinstance(tok, tuple):
            hf, n = tok
        else:
            hf, n = (lambda kc: H[:, kc, tok]), 512
        for kc in range(8):
            k.mm(pb[0:mcols, col_off:col_off + n], ws[:, kc, c0:c0 + mcols], hf(kc),
                 start=(kc == 0), stop=(kc == 7))
        return pb

    def proj_tm(ws, c0, ncols, t0, pb=None):
        if pb is None:
            pb = k.bank()
        for kc in range(8):
            k.mm(pb[:, 0:ncols], H[:, kc, t0:t0 + 128], ws[:, kc, c0:c0 + ncols],
                 start=(kc == 0), stop=(kc == 7))
        return pb

    nh_flip = [0]

    def normhead(pb, out_f32, out_bf, gain, out_pair=None):
        m = A.mark()
        nh_flip[0] ^= 1
        if nh_flip[0]:
            A.alloc([128, 512 + 256 + 512], F32)
        qf = A.alloc([128, 512], F32)
        sq = A.alloc([128, 512], BF16)
        rs = A.alloc([128, 512], F32)
        k.cp(qf.full(), pb[:, :], "dve")
        k.act(sq.full(), pb[:, :], AF.Square)
        sb = k.bank()
        k.mm(sb[:, :], C["ones_bd"].full(), sq.full())
        k.rstd(rs.full(), sb[:, :], 64.0)
        if out_pair is not None:
            for hf in range(2):
                rw = slice(hf * 64, hf * 64 + 64)
                k.stt(out_pair[hf], qf[rw, :], V(gain.ap[rw], gain.bufs), rs[rw, :], ALU.mult, ALU.mult)
        elif out_f32 is not None:
            k.stt(out_f32, qf.full(), gain, rs.full(), ALU.mult, ALU.mult)
            if out_bf is not None:
                k.cp(out_bf, out_f32, "act")
        else:
            k.stt(out_bf, qf.full(), gain, rs.full(), ALU.mult, ALU.mult)
        A.release(m)

    def layer_small(l):
        GV, ESK, WA2 = GVL[l % 2], ESKL[l % 2], WA2L[l % 2]
        for j, (nm, f) in enumerate((("qn_swa", 0.125), ("kn_swa", 1.0), ("qn_na", 0.125), ("kn_na", 1.0))):
            src = d[nm][l].rearrange("(p o) -> p o", o=1)
            k.dma("sp", GV[0:64, j:j + 1], DR(src))
            k.dma("sp", GV[64:128, j:j + 1], DR(src))
        k.dma("sp", GV[:, 4:5], DR(d["gla_onorm"][l].rearrange("(p o) -> p o", o=1)))
        k.ts(GV[:, 0:1], GV[:, 0:1], 0.125, ALU.mult)
        k.ts(GV[:, 2:3], GV[:, 2:3], 0.125, ALU.mult)
        k.dma("sp", ESK.full(), DR(d["sink_swa"][l].partition_broadcast(128)))
        k.act(ESK.full(), ESK.full(), AF.Exp)
        k.memset(WA2.full(), 0.0)
        k.dma("pool", WA2[0:16, 0:256], DR(d["w_a2_f"][l]))
        k.dma("pool", WA2[16:17, 0:256], DR(d["b_a_f"][l].rearrange("(o n) -> o n", o=1)))
        k.dma("pool", WA2[32:48, 256:512], DR(d["w_a2_b"][l]))
        k.dma("pool", WA2[48:49, 256:512], DR(d["b_a_b"][l].rearrange("(o n) -> o n", o=1)))

    def dense_attn(Q, Kt, kmap, VA, vmap, O, sink):
        m = A.mark()
        PT = [A.alloc([128, 512], BF16) for _ in range(3)]
        RC = [A.alloc([64, 256], F32) for _ in range(2)]
        it = 0
        ob = None
        for s in range(2):
            for h in range(8):
                rows = slice((h % 2) * 64, (h % 2) * 64 + 64)
                sb = k.bank()
                for kt in range(2):
                    k0 = s * 256 + kt * 128
                    k.mm(sb[:, kt * 256:(kt + 1) * 256], Kt[:, kmap(h), k0:k0 + 128],
                         Q[:, h, s * 256:(s + 1) * 256])
                pt = PT[it % 3]
                k.act(pt.full(), sb[:, :], AF.Exp)
                if it % 2 == 0:
                    ob = k.bank()
                c0 = (it % 2) * 256
                for kt in range(2):
                    k.mm(ob[:, c0:c0 + 256], VA[:, s * 2 + kt, vmap(h), :], pt[:, kt * 256:(kt + 1) * 256],
                         start=(kt == 0), stop=(kt == 1))
                rc = RC[it % 2]
                if sink is not None:
                    k.act(rc.full(), ob[64:128, c0:c0 + 256], AF.Ln, bias=ESK[0:64, h:h + 1])
                else:
                    k.act(rc.full(), ob[64:128, c0:c0 + 256], AF.Ln)
                k.act(rc.full(), rc.full(), AF.Exp, scale=-1.0)
                k.tt(O[rows, h // 2, s * 256:(s + 1) * 256], ob[0:64, c0:c0 + 256], rc.full(), ALU.mult)
                it += 1
        A.release(m)

    def merge(l, wp, gcol, groups, first):
        m = A.mark()
        SG = [A.alloc([128, 512], F32) for _ in range(2)]
        TM = [A.alloc([128, 512], F32) for _ in range(2)]
        for half in range(2):
            wsp = wslot()
            wtile(("mp", gcol, half), wsp[:, 0:4, :], lambda wsp=wsp, half=half: wload(
                wsp[:, 0:4, :], wp[:, half * 512:(half + 1) * 512].rearrange("(h p) n -> p h n", p=128)))
            wsg = wslot()
            wtile(("mg", gcol, half), wsg.full(), lambda wsg=wsg, half=half: wload(
                wsg.full(), wrows(d["w_in"][l], gcol + half * 512, 512)))
            it = 0
            for g in groups:
                n = g["n"]
                for j in range(4):
                    oc = half * 4 + j
                    gb = proj_fm(wsg, j * 128, 128, (g["h"], n))
                    sg = SG[it % 2]
                    k.act(sg[:, 0:n], gb[:, 0:n], AF.Sigmoid)
                    yb = k.bank()
                    for kk in range(4):
                        k.mm(yb[:, 0:n], wsp[:, kk, j * 128:(j + 1) * 128], g["o"](kk), start=(kk == 0), stop=(kk == 3))
                    mgv = g["mg"](oc)
                    if first:
                        k.tt(mgv, yb[:, 0:n], sg[:, 0:n], ALU.mult)
                    else:
                        tm = TM[it % 2]
                        k.tt(tm[:, 0:n], yb[:, 0:n], sg[:, 0:n], ALU.mult)
                        k.tt(mgv, mgv, tm[:, 0:n], ALU.add)
                    it += 1
        A.release(m)

    ctx = dict(nc=nc, k=k, A=A, d=d, o=o, X=X, H=H, MG=MG, MGQ=MGQ, HQ=HQ, XQI=XQI, XQA=XQA, C=C, wslot=wslot, wload=wload, wrows=wrows,
               wtile=wtile, wc_valid=wc_valid, proj_fm=proj_fm, proj_tm=proj_tm, normhead=normhead, dense_attn=dense_attn, merge=merge,
               GV=GV, ESK=ESK, ALR=ALR, WA2=WA2, MODVL=MODVL, SCLL=SCLL, cur=cur, dbg_out=dbg_out, depth=depth,
               do_sample=do_sample, stop=stop, mod_steps=mod_steps, norm_group=norm_group, layer_small=layer_small)
    build_layers(ctx)
    if os.environ.get('NO_SCHED') != '1':
        k.P.schedule()
    k.P.emit()
    return nc


def build_layers(c):
    k, A, d, o, X, H, MG, C = c["k"], c["A"], c["d"], c["o"], c["X"], c["H"], c["MG"], c["C"]
    MGQ, HQ, XQI, XQA = c["MGQ"], c["HQ"], c["XQI"], c["XQA"]

    def grp_P(O):
        return dict(h=lambda kc: H[:, kc, 0:512], o=lambda kk: O[:, kk, 0:512], mg=lambda oc: MG[:, oc, 0:512], n=512)

    def merge_quarter(l, wp, gcol, O, first):
        mq = A.mark()
        OQ = A.alloc([128, 4, 256], BF16)
        k.dma_dyn(OQ.full(), O, "crel", 256, O.full())
        merge(l, wp, gcol, [dict(h=lambda kc: HQ[:, kc, :], o=lambda kk: OQ[:, kk, :], mg=lambda oc: MGQ[:, oc, :], n=256)], first)
        A.release(mq)
    wslot, wload, wrows = c["wslot"], c["wload"], c["wrows"]
    wtile, wc_valid = c["wtile"], c["wc_valid"]
    proj_fm, proj_tm, normhead, dense_attn, merge = c["proj_fm"], c["proj_tm"], c["normhead"], c["dense_attn"], c["merge"]
    GV, ESK, ALR, WA2, MODVL, cur = c["GV"], c["ESK"], c["ALR"], c["WA2"], c["MODVL"], c["cur"]
    depth = c["depth"]
    dbg_out = c["dbg_out"]

    def phaseA(l, t0, nchunk, seqs, is_prompt):
        k.P.tag = "A.%s" % ("P" if is_prompt else "S")
        ngrp = nchunk // 4
        n = nchunk * 128
        m = A.mark()
        QP = A.alloc([128, 4, n], BF16)
        KP = A.alloc([128, 4, n], BF16)
        VT = A.alloc([128, nchunk, 512], BF16)
        SIN = A.alloc([128, nchunk, 2, 2, 128], BF16)
        MB = A.alloc([128, nchunk, 2, 128], F32)
        ALAST = A.alloc([128, 4, nchunk], F32, align=64)
        R = A.alloc([128, 2, 2, 128], F32)
        w1 = wslot()
        wtile("a1", w1.full(), lambda: wload(w1.full(), wrows(d["w_in"][l], O_AQ, 512)))
        w4 = wslot()
        wload(w4[:, :, 0:16], wrows(d["w_in"][l], O_ALF, 16))
        wload(w4[:, :, 32:48], wrows(d["w_in"][l], O_ALB, 16))
        w2 = wslot()
        wtile("a2", w2.full(), lambda: wload(w2.full(), wrows(d["w_in"][l], O_AV, 512)))
        while pending:
            pending.pop(0)()

        def seq_of(cidx):
            for si, (a, b) in enumerate(seqs):
                if a <= cidx < b:
                    return si, a, b
            raise AssertionError

        def init_state(si, dr):
            if is_prompt:
                k.memset(R[:, dr, :, :], 0.0)
            else:
                src = d["st_gla"][l, dr].rearrange("h k v -> (h k) v").rearrange("(pr q) v -> q pr v", q=128)
                k.dma("sp", R[:, dr, :, :], DR(src))

        def store_state(si, dr, a_c):
            if not is_prompt:
                return
            mm_ = A.mark()
            sf = A.alloc([128, 2, 128], F32)
            for pr in range(2):
                k.ts(sf[:, pr, :], R[:, dr, pr, :], ALAST[:, dr * 2 + pr, a_c:a_c + 1], ALU.mult)
            dst = o["o_gla"][si, l, dr].rearrange("h k v -> (h k) v").rearrange("(pr q) v -> q pr v", q=128)
            k.dma("sp", DR(dst), sf.full())
            A.release(mm_)

        def scan_step(cidx, dr, msrc):
            si, a, b = seq_of(cidx)
            first = (cidx == a) if dr == 0 else (cidx == b - 1)
            prev = cidx - 1 if dr == 0 else cidx + 1
            if first:
                init_state(si, dr)
            for pr in range(2):
                for hf in range(2):
                    rows = slice(hf * 64, hf * 64 + 64)
                    if first:
                        k.cp(SIN[rows, cidx, dr, pr, :], R[rows, dr, pr, :])
                        k.tt(R[rows, dr, pr, :], R[rows, dr, pr, :], msrc(pr, hf), ALU.add)
                    else:
                        al = ALAST[rows, dr * 2 + pr, prev:prev + 1]
                        k.ts(SIN[rows, cidx, dr, pr, :], R[rows, dr, pr, :], al, ALU.mult)
                        k.stt(R[rows, dr, pr, :], R[rows, dr, pr, :], al, msrc(pr, hf), ALU.mult, ALU.add)
            last = (cidx == b - 1) if dr == 0 else (cidx == a)
            if last:
                store_state(si, dr, cidx)

        m1 = A.mark()
        QK32 = A.alloc([128, 4, 512], F32)
        EP = A.alloc([128, 4, 256], F32)
        EN = A.alloc([128, 4, 256], F32)
        E1 = A.alloc([128, 512], F32)
        LA = A.alloc([128, 512], F32)
        LAH = A.alloc([128, 512], BF16)
        LAL = A.alloc([128, 512], BF16)
        KT = [A.alloc([128, 4, 128], BF16) for _ in range(2)]
        for gi in range(ngrp):
            htok = slice(t0 + gi * 512, t0 + gi * 512 + 512)
            pb = k.bank()
            for kc in range(8):
                k.mm(pb[0:64, :], w4[:, kc, 0:64], H[:, kc, htok], start=(kc == 0), stop=(kc == 7))
            k.cp(ALR[0:16, gi * 512:gi * 512 + 512], pb[0:16, :], "act")
            k.cp(ALR[32:48, gi * 512:gi * 512 + 512], pb[32:48, :], "dve")
            for j in range(4):
                pq = proj_fm(w1, j * 128, 128, htok)
                k.cpx(QK32[:, j, :], pq[:, :])
            if PA_STOP <= 1:
                continue
            for hf2 in range(2):
                cb = [k.hold(), k.hold()]
                for c2 in range(2):
                    cl = gi * 4 + hf2 * 2 + c2
                    tk = cl * 128
                    zb = k.bank()
                    k.mm(zb[:, :], ALR[0:64, tk:tk + 128], WA2.full())
                    if PA_STOP <= 1.2:
                        k.cp(E1.full(), zb[:, :])
                        continue
                    k.act(E1.full(), zb[:, :], AF.Exp, scale=-1.0)
                    k.act(LA.full(), E1.full(), AF.Ln, bias=1.0)
                    if PA_STOP <= 1.4:
                        continue
                    k.cp(LAH.full(), LA.full(), "act")
                    k.tt(LAL.full(), LA.full(), LAH.full(), ALU.subtract)
                    for dr in range(2):
                        tri = C["trif"] if dr == 0 else C["trib"]
                        for cc in range(2):
                            osl = cb[dr][:, cc * 256 + c2 * 128:cc * 256 + c2 * 128 + 128]
                            k.mm(osl, LAH[:, dr * 256 + cc * 128:dr * 256 + cc * 128 + 128], tri.full(), True, False)
                            k.mm(osl, LAL[:, dr * 256 + cc * 128:dr * 256 + cc * 128 + 128], tri.full(), False, True)
                if PA_STOP <= 1.6:
                    k.unhold(cb[0])
                    k.unhold(cb[1])
                    continue
                for dr in range(2):
                    cv = cb[dr].v(cb[dr].ap.rearrange("p (a b) -> p a b", a=2))
                    k.act(EP[:, dr * 2:dr * 2 + 2, :], cv, AF.Exp)
                    k.act(EN[:, dr * 2:dr * 2 + 2, :], cv, AF.Exp, scale=-1.0)
                k.unhold(cb[0])
                k.unhold(cb[1])
                if PA_STOP <= 2:
                    continue
                cbase = gi * 4 + hf2 * 2
                k.cp(ALAST[:, 0:2, cbase:cbase + 2], EP[:, 0:2, 127:256:128])
                k.cp(ALAST[:, 2:4, cbase:cbase + 2], EP[:, 2:4, 0:256:128])
                tcol = slice(gi * 512 + hf2 * 256, gi * 512 + hf2 * 256 + 256)
                hcol = slice(hf2 * 256, hf2 * 256 + 256)
                for dc in range(4):
                    cc = dc % 2
                    k.stt(QP[:, dc, tcol], QK32[:, cc, hcol], 0.125, EP[:, dc, :], ALU.mult, ALU.mult)
                    k.tt(KP[:, dc, tcol], QK32[:, 2 + cc, hcol], EN[:, dc, :], ALU.mult)
                if PA_STOP <= 3:
                    continue
                for c2 in range(2):
                    cl = gi * 4 + hf2 * 2 + c2
                    ctok = slice(cl * 128, cl * 128 + 128)
                    tb = k.bank()
                    tbv = tb.bf()
                    for dc in range(4):
                        k.tr(tb.v(tbv[:, dc * 128:(dc + 1) * 128]), KP[:, dc, ctok], C["ident_bf"].full())
                    kt = KT[cl % 2]
                    k.cp(kt.full(), tb.v(tbv[:, 0:512].rearrange("p (a b) -> p a b", a=4)), "act")
                    vb = proj_tm(w2, 0, 512, t0 + cl * 128)
                    k.cp(VT[:, cl, :], vb[:, :], "dve")
                    for dr in range(2):
                        mb = k.bank()
                        for pr in range(2):
                            k.mm(mb[:, pr * 256:(pr + 1) * 256], kt[:, dr * 2 + pr, :], VT[:, cl, pr * 256:(pr + 1) * 256])
                        mv = mb.ap.rearrange("p (a b) -> p a b", a=2)
                        if dr == 0:
                            scan_step(cl, 0, lambda pr, hf, mv=mv, mb=mb: mb.v(
                                mv[hf * 64:hf * 64 + 64, pr, hf * 128:hf * 128 + 128]))
                        else:
                            for hf in range(2):
                                k.cp(MB[hf * 64:hf * 64 + 64, cl, :, :],
                                     mb.v(mv[hf * 64:hf * 64 + 64, :, hf * 128:hf * 128 + 128]), "act")
        A.release(m1)
        if PA_STOP <= 4:
            A.release(m)
            return
        for cl in reversed(range(nchunk)):
            scan_step(cl, 1, lambda pr, hf, cl=cl: MB[hf * 64:hf * 64 + 64, cl, pr, :])
        if PA_STOP <= 5:
            A.release(m)
            return
        OA = A.alloc([128, 4, n], BF16)
        m3 = A.mark()
        w3 = wslot()
        wtile("a3", w3.full(), lambda: wload(w3.full(), wrows(d["w_in"][l], O_AR, 512)))
        SIL = A.alloc([128, 4, 512], BF16)
        ATM = [A.alloc([128, 4, 128], BF16) for _ in range(2)]
        OFs = [A.alloc([128, 512], F32) for _ in range(2)]
        SQs = [A.alloc([128, 512], BF16) for _ in range(2)]
        RSs = [A.alloc([128, 512], F32) for _ in range(2)]
        for gi in range(ngrp):
            htok = slice(t0 + gi * 512, t0 + gi * 512 + 512)
            for hd in range(4):
                pb = proj_fm(w3, hd * 128, 128, htok)
                k.act(SIL[:, hd, :], pb[:, :], AF.Silu)
            ob = [k.hold() for _ in range(4)]
            for c4 in range(4):
                cl = gi * 4 + c4
                ctok = slice(cl * 128, cl * 128 + 128)
                for dr in range(2):
                    mk = C["maskf"] if dr == 0 else C["maskb"]
                    mkf = mk.full()
                    for hf in range(2):
                        ab = k.bank()
                        rows = slice(hf * 64, hf * 64 + 64)
                        for pr in range(2):
                            k.mm(ab[:, pr * 128:(pr + 1) * 128], KP[rows, dr * 2 + pr, ctok], QP[rows, dr * 2 + pr, ctok])
                        k.tt(ATM[dr][:, hf:4:2, :], ab.v(ab.ap[:, 0:256].rearrange("p (a b) -> p a b", a=2)),
                             V(mkf.ap.unsqueeze(1).to_broadcast([128, 2, 128]), mkf.bufs), ALU.mult)
                for hd in range(4):
                    rows = slice((hd % 2) * 64, (hd % 2) * 64 + 64)
                    col = slice(c4 * 128, c4 * 128 + 128)
                    k.mm(ob[hd][:, col], VT[:, cl, hd * 128:(hd + 1) * 128], ATM[0][:, hd, :], True, False)
                    k.mm(ob[hd][:, col], VT[:, cl, hd * 128:(hd + 1) * 128], ATM[1][:, hd, :], False, False)
                    k.mm(ob[hd][:, col], SIN[rows, cl, 0, hd // 2, :], QP[rows, hd // 2, ctok], False, False)
                    k.mm(ob[hd][:, col], SIN[rows, cl, 1, hd // 2, :], QP[rows, 2 + hd // 2, ctok], False, True)
            gtok = slice(gi * 512, gi * 512 + 512)
            for hd in range(4):
                OF, SQ, RS = OFs[hd % 2], SQs[hd % 2], RSs[hd % 2]
                k.cp(OF.full(), ob[hd][:, :], "dve")
                k.act(SQ.full(), ob[hd][:, :], AF.Square)
                k.unhold(ob[hd])
                sb = k.bank()
                k.mm(sb[:, :], C["ones_all"].full(), SQ.full())
                k.rstd(RS.full(), sb[:, :], 128.0)
                k.tt(OF.full(), OF.full(), RS.full(), ALU.mult)
                k.stt(OA[:, hd, gtok], OF.full(), GV[:, 4:5], SIL[:, hd, :], ALU.mult, ALU.mult)
        A.release(m3)
        if PA_STOP <= 6:
            A.release(m)
            return
        if is_prompt:
            merge(l, d["w_pa"][l], O_GA, [grp_P(OA)], True)
        else:
            merge_quarter(l, d["w_pa"][l], O_GA, OA, True)
        A.release(m)

    def phaseB_P(l):
        k.P.tag = "B.P"
        tok = slice(0, 512)
        m = A.mark()
        QN = A.alloc([128, 8, 512], BF16)
        k.memset(QN.full(), 0.0)
        KN = A.alloc([128, 2, 512], BF16)
        KNF = A.alloc([128, 2, 512], F32)
        VA = A.alloc([128, 4, 2, 128], BF16)
        OB = A.alloc([128, 4, 512], BF16)
        KO = [A.alloc([128, 128], F32) for _ in range(2)]
        w1 = wslot()
        wtile("b1", w1.full(), lambda: wload(w1.full(), wrows(d["w_in"][l], O_BQ, 512)))
        w2 = wslot()

        def ld_b2():
            for kv in range(2):
                for dup in range(2):
                    wload(w2[:, :, kv * 128 + dup * 64:kv * 128 + dup * 64 + 64], wrows(d["w_in"][l], O_BK + kv * 64, 64))
            wload(w2[:, :, 256:384], wrows(d["w_in"][l], O_BV, 128))
        wtile("b2", w2[:, :, 0:384], ld_b2)
        for cc in range(4):
            pb = proj_fm(w1, cc * 128, 128, tok)
            normhead(pb, None, None, GV[:, 0:1], out_pair=(QN[0:64, 2 * cc, :], QN[64:128, 2 * cc + 1, :]))
        for kv in range(2):
            pb = proj_fm(w2, kv * 128, 128, tok)
            normhead(pb, KNF[:, kv, :], KN[:, kv, :], GV[:, 1:2])
        k.memset(VA[:, :, :, 64:128], 1.0)
        for tix in range(4):
            ttok = slice(tix * 128, tix * 128 + 128)
            pb = k.bank()
            for kv in range(2):
                k.tr(pb[:, kv * 128:(kv + 1) * 128], KNF[:, kv, ttok], C["ident_f"].full())
            ko = KO[0]
            kov = ko.full()
            k.cp(V(kov.ap.rearrange("p (a b) -> p a b", a=2), kov.bufs),
                 pb.v(pb.ap[:, 0:256].rearrange("p (a b) -> p a b", a=2)[:, :, 0:64]), "act")
            k.dma("sp", DR(o["o_swk"][tix // 2, l, (tix % 2) * 128:(tix % 2) * 128 + 128, :]), kov)
            pv = proj_tm(w2, 256, 128, tix * 128)
            vo = KO[1]
            k.cp(vo.full(), pv[:, 0:128], "dve")
            k.dma("sp", DR(o["o_swv"][tix // 2, l, (tix % 2) * 128:(tix % 2) * 128 + 128, :]), vo.full())
            k.cp(VA[:, tix, :, 0:64], pv.v(pv.ap[:, 0:128].rearrange("p (a b) -> p a b", a=2)), "act")
        dense_attn(QN, KN, lambda h: h // 4, VA, lambda h: h // 4, OB, True)
        merge(l, d["w_pb"][l], O_GB, [grp_P(OB)], False)
        A.release(m)

    def phaseC_P(l):
        k.P.tag = "C.P"
        tok = slice(0, 512)
        m = A.mark()
        QN = A.alloc([128, 8, 512], BF16)
        k.memset(QN.full(), 0.0)
        KN = A.alloc([128, 4, 512], BF16)
        KNF = A.alloc([128, 4, 512], F32)
        VA = A.alloc([128, 4, 8, 128], BF16)
        OC = A.alloc([128, 4, 512], BF16)
        KO = [A.alloc([128, 512], F32) for _ in range(2)]
        wqk = []
        wv = []
        for hh in range(2):
            wa = wslot()
            wtile(("c1", hh), wa.full(), lambda wa=wa, hh=hh: (
                wload(wa[:, :, 0:256], wrows(d["w_in"][l], O_CQ + hh * 256, 256)),
                wload(wa[:, :, 256:512], wrows(d["w_in"][l], O_CK + hh * 256, 256))))
            wqk.append(wa)
            for c2 in range(2):
                cc = hh * 2 + c2
                pb = proj_fm(wa, c2 * 128, 128, tok)
                normhead(pb, None, None, GV[:, 2:3], out_pair=(QN[0:64, 2 * cc, :], QN[64:128, 2 * cc + 1, :]))
                pb = proj_fm(wa, 256 + c2 * 128, 128, tok)
                normhead(pb, KNF[:, cc, :], KN[:, cc, :], GV[:, 3:4])
        for hh in range(2):
            wb = wslot()
            wtile(("c3", hh), wb[:, :, 0:256], lambda wb=wb, hh=hh: wload(
                wb[:, :, 0:256], wrows(d["w_in"][l], O_CV + hh * 256, 256)))
            wv.append(wb)
        k.memset(VA[:, :, :, 64:128], 1.0)
        for tix in range(4):
            ttok = slice(tix * 128, tix * 128 + 128)
            pb = k.bank()
            for cc in range(4):
                k.tr(pb[:, cc * 128:(cc + 1) * 128], KNF[:, cc, ttok], C["ident_f"].full())
            k.cp(KO[0].full(), pb[:, :], "act")
            k.dma("sp", DR(o["o_nak"][tix // 2, l, (tix % 2) * 128:(tix % 2) * 128 + 128, :]), KO[0].full())
            pv = k.bank()
            for hh in range(2):
                for kc in range(8):
                    k.mm(pv[:, hh * 256:(hh + 1) * 256], H[:, kc, tix * 128:tix * 128 + 128], wv[hh][:, kc, 0:256],
                         start=(kc == 0), stop=(kc == 7))
            k.cp(KO[1].full(), pv[:, :], "dve")
            k.dma("sp", DR(o["o_nav"][tix // 2, l, (tix % 2) * 128:(tix % 2) * 128 + 128, :]), KO[1].full())
            k.cp(VA[:, tix, :, 0:64], pv.v(pv.ap.rearrange("p (a b) -> p a b", a=8)), "act")
        dense_attn(QN, KN, lambda h: h // 2, VA, lambda h: h, OC, None)
        merge(l, d["w_pc"][l], O_GC, [grp_P(OC)], False)
        A.release(m)

    rp_flip = [0]

    def rope_apply(qf, qfb, out_bf, tsl):
        mr = A.mark()
        rp_flip[0] ^= 1
        if rp_flip[0]:
            A.alloc([128, 1024], F32)
        t1 = A.alloc([128, 512], F32)
        t2 = A.alloc([128, 512], F32)
        pr = k.bank()
        k.mm(pr[:, :], C["rope_pT"].full(), qfb)
        k.tt(t1.full(), qf, C["rope_cos"][:, tsl], ALU.mult)
        k.tt(t2.full(), pr[:, :], C["rope_sin"][:, tsl], ALU.mult)
        if isinstance(out_bf, tuple):
            k.tt(out_bf[0], t1[0:64, :], t2[0:64, :], ALU.add, eng="pool")
            k.tt(out_bf[1], t1[64:128, :], t2[64:128, :], ALU.add, eng="pool")
        else:
            k.tt(out_bf, t1.full(), t2.full(), ALU.add, eng="pool")
        A.release(mr)

    def phaseB_S(l):
        k.P.tag = "B.S"
        m = A.mark()
        QN = A.alloc([128, 8, TS], BF16)
        k.memset(QN.full(), 0.0, eng="pool")
        KN = A.alloc([128, 2, TS], BF16)
        VA = A.alloc([128, 8, 2, 128], BF16)
        KC = A.alloc([128, 2, 512], BF16)
        VC = A.alloc([128, 4, 2, 128], BF16)
        OB = A.alloc([128, 4, TS], BF16)
        w1 = wslot()
        wtile("b1", w1.full(), lambda: wload(w1.full(), wrows(d["w_in"][l], O_BQ, 512)))
        w2 = wslot()

        def ld_b2():
            for kv in range(2):
                for dup in range(2):
                    wload(w2[:, :, kv * 128 + dup * 64:kv * 128 + dup * 64 + 64], wrows(d["w_in"][l], O_BK + kv * 64, 64))
            wload(w2[:, :, 256:384], wrows(d["w_in"][l], O_BV, 128))
        wtile("b2", w2[:, :, 0:384], ld_b2)
        m2 = A.mark()
        QFs = [A.alloc([128, 512], F32) for _ in range(2)]
        QFBs = [A.alloc([128, 512], BF16) for _ in range(2)]
        qi = 0
        for g in range(2):
            htok = slice(TP + g * 512, TP + g * 512 + 512)
            tsl = slice(g * 512, g * 512 + 512)
            for cc in range(4):
                QF, QFB = QFs[qi % 2], QFBs[qi % 2]
                qi += 1
                pb = proj_fm(w1, cc * 128, 128, htok)
                normhead(pb, QF.full(), QFB.full(), GV[:, 0:1])
                rope_apply(QF.full(), QFB.full(), (QN[0:64, 2 * cc, tsl], QN[64:128, 2 * cc + 1, tsl]), tsl)
            for kv in range(2):
                QF, QFB = QFs[qi % 2], QFBs[qi % 2]
                qi += 1
                pb = proj_fm(w2, kv * 128, 128, htok)
                normhead(pb, QF.full(), QFB.full(), GV[:, 1:2])
                rope_apply(QF.full(), QFB.full(), KN[:, kv, tsl], tsl)
        A.release(m2)
        k.memset(VA[:, :, :, 64:128], 1.0, eng="pool")
        k.memset(VC[:, :, :, 64:128], 1.0, eng="pool")
        for tix in range(8):
            pv = proj_tm(w2, 256, 128, TP + tix * 128)
            k.cpx(VA[:, tix, :, 0:64], pv.v(pv.ap[:, 0:128].rearrange("p (a b) -> p a b", a=2)))
        m2 = A.mark()
        ST = [A.alloc([128, 256], F32) for _ in range(2)]
        SV = [A.alloc([128, 128], F32) for _ in range(2)]
        for i in range(4):
            st = ST[i % 2]
            src = d["cswk"][l, i * 128:(i + 1) * 128, :].rearrange("p (a b) -> p a b", a=2)
            stv = st.full()
            for dup in range(2):
                k.dma("sp", V(stv.ap.rearrange("p (a c b) -> p a c b", a=2, c=2)[:, :, dup, :], stv.bufs), DR(src))
            pb = k.bank()
            for kv in range(2):
                k.tr(pb[:, kv * 128:(kv + 1) * 128], st[:, kv * 128:(kv + 1) * 128], C["ident_f"].full())
            k.cpx(KC[:, :, i * 128:(i + 1) * 128], pb.v(pb.ap[:, 0:256].rearrange("p (a b) -> p a b", a=2)))
            sv = SV[i % 2]
            k.dma("sp", sv.full(), DR(d["cswv"][l, i * 128:(i + 1) * 128, :]))
            svv = sv.full()
            k.cpx(VC[:, i, :, 0:64], V(svv.ap.rearrange("p (a b) -> p a b", a=2), svv.bufs))
        A.release(m2)
        m2 = A.mark()
        PT = [A.alloc([128, 512], BF16) for _ in range(3)]
        RC = [A.alloc([64, 512], F32) for _ in range(2)]
        it = 0
        for h in range(8):
            rows = slice((h % 2) * 64, (h % 2) * 64 + 64)
            kv = h // 4
            ob = [k.hold(), k.hold()]
            for qh in range(2):
                qsl = slice(qh * 512, qh * 512 + 512)
                for i in range(4):
                    sa = k.bank()
                    k.mm(sa[:, :], KC[:, kv, i * 128:(i + 1) * 128], QN[:, h, qsl])
                    pt = PT[it % 3]
                    it += 1
                    k.act(pt.full(), sa[:, :], AF.Exp)
                    k.mm(ob[qh][:, :], VC[:, i, kv, :], pt.full(), i == 0, False)
            for kt in range(8):
                qb0, qb1 = max(0, kt - 1), min(7, kt + 1)
                nq = (qb1 - qb0 + 1) * 128
                q0 = qb0 * 128
                sbk = k.bank()
                k.mm(sbk[:, 0:nq], KN[:, kv, kt * 128:(kt + 1) * 128], QN[:, h, q0:q0 + nq], True, False)
                for qb in range(qb0, qb1 + 1):
                    if qb == kt:
                        continue
                    msk = C["band_next"] if qb == kt + 1 else C["band_prev"]
                    k.mm(sbk[:, (qb - qb0) * 128:(qb - qb0 + 1) * 128], C["ident_bf"].full(), msk.full(), False, False)
                pt = PT[it % 3]
                it += 1
                k.act(pt[:, 0:nq], sbk[:, 0:nq], AF.Exp)
                segs = []
                a_ = q0
                while a_ < q0 + nq:
                    b_ = min(q0 + nq, (a_ // 512 + 1) * 512)
                    segs.append((a_, b_))
                    a_ = b_
                for (a_, b_) in segs:
                    qh = a_ // 512
                    k.mm(ob[qh][:, a_ - qh * 512:b_ - qh * 512], VA[:, kt, kv, :], pt[:, a_ - q0:b_ - q0], False, False)
            for qh in range(2):
                rc = RC[qh]
                k.act(rc.full(), ob[qh][64:128, :], AF.Ln, bias=ESK[0:64, h:h + 1])
                k.act(rc.full(), rc.full(), AF.Exp, scale=-1.0)
                k.tt(OB[rows, h // 2, qh * 512:(qh + 1) * 512], ob[qh][0:64, :], rc.full(), ALU.mult)
                k.unhold(ob[qh])
        A.release(m2)
        merge_quarter(l, d["w_pb"][l], O_GB, OB, False)
        A.release(m)

    def na_runs(mt):
        runs = []
        if mt <= 3:
            runs.append((0, 4, False))
        lo, hi = max(5, 2 * mt - 3), min(11, 2 * mt + 5)
        if lo <= hi:
            if lo <= 7 and hi >= 8:
                runs.append((lo, 7, True))
                runs.append((8, hi, True))
            else:
                runs.append((lo, hi, True))
        if mt >= 4:
            runs.append((12, 15, False))
        return runs

    def phaseC_S(l):
        k.P.tag = "C.S"
        m = A.mark()
        OC = A.alloc([128, 4, TS], BF16)
        for hh in range(2):
            mh = A.mark()
            QN = A.alloc([128, 4, TS], BF16)
            k.memset(QN.full(), 0.0, eng="pool")
            KN = A.alloc([128, 2, TS], BF16)
            VA = A.alloc([128, 8, 4, 128], BF16)
            KC = A.alloc([128, 2, 512], BF16)
            VC = A.alloc([128, 4, 4, 128], BF16)
            w1 = wslot()
            wtile(("c1", hh), w1.full(), lambda w1=w1, hh=hh: (
                wload(w1[:, :, 0:256], wrows(d["w_in"][l], O_CQ + hh * 256, 256)),
                wload(w1[:, :, 256:512], wrows(d["w_in"][l], O_CK + hh * 256, 256))))
            w3 = wslot()
            wtile(("c3", hh), w3[:, :, 0:256], lambda w3=w3, hh=hh: wload(
                w3[:, :, 0:256], wrows(d["w_in"][l], O_CV + hh * 256, 256)))
            for g in range(2):
                htok = slice(TP + g * 512, TP + g * 512 + 512)
                tsl = slice(g * 512, g * 512 + 512)
                for cc in range(2):
                    pb = proj_fm(w1, cc * 128, 128, htok)
                    normhead(pb, None, None, GV[:, 2:3], out_pair=(QN[0:64, 2 * cc, tsl], QN[64:128, 2 * cc + 1, tsl]))
                    pb = proj_fm(w1, 256 + cc * 128, 128, htok)
                    normhead(pb, None, KN[:, cc, tsl], GV[:, 3:4])
            k.memset(VA[:, :, :, 64:128], 1.0, eng="pool")
            k.memset(VC[:, :, :, 64:128], 1.0, eng="pool")
            for tix in range(8):
                pv = proj_tm(w3, 0, 256, TP + tix * 128)
                k.cpx(VA[:, tix, :, 0:64], pv.v(pv.ap[:, 0:256].rearrange("p (a b) -> p a b", a=4)))
            m2 = A.mark()
            ST = [A.alloc([128, 256], F32) for _ in range(2)]
            SV = [A.alloc([128, 256], F32) for _ in range(2)]
            for i in range(4):
                st = ST[i % 2]
                k.dma("sp", st.full(), DR(d["cnak"][l, i * 128:(i + 1) * 128, hh * 256:hh * 256 + 256]))
                pb = k.bank()
                for cc in range(2):
                    k.tr(pb[:, cc * 128:(cc + 1) * 128], st[:, cc * 128:(cc + 1) * 128], C["ident_f"].full())
                k.cpx(KC[:, :, i * 128:(i + 1) * 128], pb.v(pb.ap[:, 0:256].rearrange("p (a b) -> p a b", a=2)))
                sv = SV[i % 2]
                k.dma("sp", sv.full(), DR(d["cnav"][l, i * 128:(i + 1) * 128, hh * 256:hh * 256 + 256]))
                svv = sv.full()
                k.cpx(VC[:, i, :, 0:64], V(svv.ap.rearrange("p (a b) -> p a b", a=4), svv.bufs))
            A.release(m2)
            m2 = A.mark()
            STG = A.alloc([128, 16, 64], F32)
            SFUL = A.alloc([128, 16, 64], BF16)
            SINT = A.alloc([128, 16, 64], BF16)
            PT = [A.alloc([128, 512], BF16) for _ in range(2)]
            RC = A.alloc([64, 512], F32)
            it = 0
            for hl in range(4):
                h = hh * 4 + hl
                rows = slice((hl % 2) * 64, (hl % 2) * 64 + 64)
                cc = hl // 2
                base = d["rpb_pad"][l, h]
                hk = bass.AP(base.tensor, base.offset, [[1, 64], [127, 15], [1, 64]])
                k.memset(STG.full(), 0.0)
                k.dma("sp", STG[0:64, 0:15, :], DR(hk))
                k.dma("sp", STG[64:128, 1:16, :], DR(hk))
                nmk = C["na_mask"].full()
                k.tt(SFUL.full(), STG.full(), V(nmk.ap.unsqueeze(1).to_broadcast([128, 16, 64]), nmk.bufs), ALU.add)
                k.cp(SINT.full(), SFUL.full(), "act")
                k.memset(SINT[0:64, 0:4, :], NEG, eng="pool")
                k.memset(SINT[0:64, 12:16, :], NEG, eng="pool")
                k.memset(SINT[64:128, 0:5, :], NEG, eng="pool")
                k.memset(SINT[64:128, 13:16, :], NEG, eng="pool")
                ob = [k.hold(), k.hold()]
                for qh in range(2):
                    qsl = slice(qh * 512, qh * 512 + 512)
                    for i in range(4):
                        sb = k.bank()
                        k.mm(sb[:, :], KC[:, cc, i * 128:(i + 1) * 128], QN[:, hl, qsl])
                        pt = PT[it % 2]
                        it += 1
                        k.act(pt.full(), sb[:, :], AF.Exp)
                        k.mm(ob[qh][:, :], VC[:, i, hl, :], pt.full(), i == 0, False)
                for mt in range(8):
                    for (r0, r1, interior) in na_runs(mt):
                        nq = (r1 - r0 + 1) * 64
                        q0 = r0 * 64
                        qh = q0 // 512
                        b0 = 7 + r0 - 2 * mt
                        strip = SINT if interior else SFUL
                        sb = k.bank()
                        k.mm(sb[:, 0:nq], KN[:, cc, mt * 128:(mt + 1) * 128], QN[:, hl, q0:q0 + nq], True, False)
                        sv_ = strip[:, b0:b0 + (r1 - r0 + 1), :]
                        k.mm(sb[:, 0:nq], C["jj"].full(), V(sv_.ap.rearrange("p a b -> p (a b)"), sv_.bufs), False, True)
                        pt = PT[it % 2]
                        it += 1
                        k.act(pt[:, 0:nq], sb[:, 0:nq], AF.Exp)
                        k.mm(ob[qh][:, q0 - qh * 512:q0 - qh * 512 + nq], VA[:, mt, hl, :], pt[:, 0:nq], False, False)
                for qh in range(2):
                    k.recip_act(RC.full(), ob[qh][64:128, :])
                    k.tt(OC[rows, h // 2, qh * 512:(qh + 1) * 512], ob[qh][0:64, :], RC.full(), ALU.mult)
                    k.unhold(ob[qh])
            A.release(m2)
            A.release(mh)
        merge_quarter(l, d["w_pc"][l], O_GC, OC, False)
        A.release(m)

    def wo_groups(l, groups):
        k.P.tag = "wo"
        for half in range(2):
            ws = wslot()
            wtile(("wo", half), ws.full(), lambda ws=ws, half=half: wload(ws.full(), wrows(d["w_o"][l], half * 512, 512)))
            for g in groups:
                n = g["n"]
                for j in range(4):
                    oc = half * 4 + j
                    pb = k.bank()
                    for kc in range(8):
                        k.mm(pb[:, 0:n], ws[:, kc, j * 128:(j + 1) * 128], g["mg"](kc), start=(kc == 0), stop=(kc == 7))
                    xv = g["x"](oc)
                    k.stt(xv, pb[:, 0:n], MODVL[cur[0]][:, 16 + oc, g["ci"]:g["ci"] + 1], xv, ALU.mult, ALU.add)

    def norm_q(l, XQ, HQ2):
        mn = A.mark()
        sq = [A.alloc([128, 256], BF16) for _ in range(2)]
        rs = A.alloc([128, 256], F32)
        tmp = [A.alloc([128, 256], F32) for _ in range(2)]
        sb = k.bank()
        for oc in range(8):
            s_ = sq[oc % 2]
            k.act(s_.full(), XQ[:, oc, :], AF.Square)
            k.mm(sb[:, 0:256], C["ones_all"].full(), s_.full(), start=(oc == 0), stop=(oc == 7))
        k.rstd(rs.full(), sb[:, 0:256], 1024.0)
        for oc in range(8):
            t_ = tmp[oc % 2]
            k.stt(t_.full(), XQ[:, oc, :], c["SCLL"][cur[0]][:, 1, 1, oc:oc + 1], rs.full(), ALU.mult, ALU.mult)
            k.act(HQ2[:, oc, :], t_.full(), AF.Identity, bias=MODVL[cur[0]][:, 24 + oc, 1:2])
        A.release(mn)

    pending = []

    def tail(l, extra):
        m = A.mark()
        XQ = A.alloc([128, 8, 256], F32)
        HQ2 = A.alloc([128, 8, 256], BF16)
        k.dma_dyn(XQ.full(), X, "cabs", 256, X[:, :, TP:TT])
        wo_groups(l, [dict(mg=lambda kc: MGQ[:, kc, :], x=lambda oc: XQ[:, oc, :], n=256, ci=1)])
        k.P.tag = "mlp"
        c["norm_group"](l, 1, 0)
        norm_q(l, XQ, HQ2)
        G = [dict(h=lambda kc: H[:, kc, 0:512], u=slice(0, 512), x=lambda oc: X[:, oc, 0:512], n=512, ci=0),
             dict(h=lambda kc: HQ2[:, kc, :], u=slice(512, 768), x=lambda oc: XQ[:, oc, :], n=256, ci=1)]
        U = A.alloc([128, 16, 768], BF16)
        RL = [A.alloc([128, 512], BF16) for _ in range(2)]
        ri = 0
        for hh in range(2):
            for t4 in range(4):
                ws = wslot()
                wload(ws.full(), wrows(d["w_fc1"][l], hh * 2048 + t4 * 512, 512))
                for g in G:
                    n = g["n"]
                    for j in range(4):
                        pb = proj_fm(ws, j * 128, 128, (g["h"], n))
                        rl = RL[ri % 2]
                        ri += 1
                        uv = U[:, t4 * 4 + j, g["u"]]
                        if j % 2 == 0:
                            k.act(rl[:, 0:n], pb[:, 0:n], AF.Relu)
                            k.tt(uv, rl[:, 0:n], rl[:, 0:n], ALU.mult)
                        else:
                            k.ts(rl[:, 0:n], pb[:, 0:n], 0.0, ALU.max)
                            k.tt(uv, rl[:, 0:n], rl[:, 0:n], ALU.mult, eng="pool")
                if extra:
                    extra.pop(0)()
            for oh in range(2):
                wsa = wslot()
                wsb = wslot()
                r0 = hh * 2048
                wload(wsa.full(), d["w_fc2"][l][r0:r0 + 1024, oh * 512:oh * 512 + 512].rearrange("(kc p) n -> p kc n", p=128))
                wload(wsb.full(), d["w_fc2"][l][r0 + 1024:r0 + 2048, oh * 512:oh * 512 + 512].rearrange("(kc p) n -> p kc n", p=128))
                for g in G:
                    n = g["n"]
                    for j in range(4):
                        oc = oh * 4 + j
                        pb = k.bank()
                        for kk in range(16):
                            wsx = wsa if kk < 8 else wsb
                            k.mm(pb[:, 0:n], wsx[:, kk % 8, j * 128:(j + 1) * 128], U[:, kk, g["u"]], start=(kk == 0), stop=(kk == 15))
                        xv = g["x"](oc)
                        k.stt(xv, pb[:, 0:n], MODVL[cur[0]][:, 40 + oc, g["ci"]:g["ci"] + 1], xv, ALU.mult, ALU.add)
                if extra:
                    extra.pop(0)()
        while extra:
            extra.pop(0)()
        if l + 1 < depth:
            k.dma("sp", V(XQI.ap.rearrange("(oc p) t -> p oc t", p=128), XQI.bufs), XQ.full())

            def gather():
                tg = k.P.tag
                k.P.tag = "gather"
                k.allgather(XQA, XQI, [[0, 1, 2, 3], [4, 5, 6, 7]])
                for r in range(4):
                    k.dma("sp", X[:, :, TP + r * 256:TP + (r + 1) * 256],
                          V(XQA.ap[r * 1024:(r + 1) * 1024, :].rearrange("(oc p) t -> p oc t", p=128), XQA.bufs))
                k.P.tag = tg
            pending.append(gather)
        else:
            st2 = [A.alloc([128, D], F32) for _ in range(2)]
            for tix in range(2):
                s_ = st2[tix]
                for q4 in range(2):
                    pb = k.bank()
                    for j in range(4):
                        oc = q4 * 4 + j
                        k.tr(pb[:, j * 128:(j + 1) * 128], XQ[:, oc, tix * 128:(tix + 1) * 128], C["ident_f"].full())
                    k.cpx(s_[:, q4 * 512:(q4 + 1) * 512], pb[:, :])
                k.dma("sp", DR(o["ysq"][tix * 128:(tix + 1) * 128, :]), s_.full())
        A.release(m)

    def store_tokens(dst, ntok, col0):
        m = A.mark()
        st2 = [A.alloc([128, D], F32) for _ in range(2)]
        for tix in range(ntok // 128):
            s_ = st2[tix % 2]
            c0 = col0 + tix * 128
            for q4 in range(2):
                pb = k.bank()
                for j in range(4):
                    oc = q4 * 4 + j
                    k.tr(pb[:, j * 128:(j + 1) * 128], X[:, oc, c0:c0 + 128], C["ident_f"].full())
                k.cpx(s_[:, q4 * 512:(q4 + 1) * 512], pb[:, :])
            k.dma("sp", DR(dst[tix * 128:(tix + 1) * 128, :]), s_.full())
        A.release(m)

    stop = c["stop"]
    for l in range(depth):
        if stop == "load":
            break
        cur[0] = l % 2
        k.P.tag = "pre"
        if l == 0:
            c["layer_small"](0)
            for f_ in c["mod_steps"](0):
                f_()
        c["norm_group"](l, 0, 0)
        wc_valid.clear()
        phaseA(l, 0, 4, [(0, 2), (2, 4)], True)
        phaseB_P(l)
        phaseC_P(l)
        wo_groups(l, [dict(mg=lambda kc: MG[:, kc, 0:512], x=lambda oc: X[:, oc, 0:512], n=512, ci=0)])
        k.P.tag = "pre"
        c["norm_group"](l, 0, 1)
        c["norm_group"](l, 0, 2)
        k.dma_dyn(HQ.full(), H, "cabs", 256, H[:, :, TP:TT])
        phaseA(l, TP, 8, [(0, 8)], False)
        phaseB_S(l)
        phaseC_S(l)
        ex = []
        if l + 1 < depth:
            ex = [lambda l=l: c["layer_small"](l + 1)] + c["mod_steps"](l + 1)
        tail(l, ex)
    store_tokens(o["yp"], TP, 0)


_CACHE = {}


def make_in_maps(inp):
    consts = make_consts()
    f = lambda a: np.ascontiguousarray(np.asarray(a), dtype=np.float32)
    shared = {}
    for nm in ("w_mod", "w_in", "w_a2_f", "b_a_f", "w_a2_b", "b_a_b", "gla_onorm", "qn_swa", "kn_swa",
               "qn_na", "kn_na", "sink_swa", "w_pa", "w_pb", "w_pc", "w_o", "w_fc1", "w_fc2"):
        shared[nm] = f(inp[nm])
    shared["b_mod"] = f(inp["b_mod"]).reshape(DEPTH, 48, 128)
    shared["norm1"] = f(inp["norm1"]).reshape(DEPTH, 8, 128)
    shared["norm2"] = f(inp["norm2"]).reshape(DEPTH, 8, 128)
    rp = f(inp["rpb_na"])[:, :, ::-1, ::-1]
    pad = np.zeros((DEPTH, 8, 15, 127), np.float32)
    pad[..., 48:79] = rp
    shared["rpb_pad"] = pad
    shared.update({nm: consts[nm] for nm, _, _ in CONST_SPECS})
    xp = f(inp["x_prompt"])
    xs = f(inp["x_sample"])
    maps = []
    for ci in range(NCORES):
        b = ci // 4
        m = dict(shared)
        m["xp"] = np.ascontiguousarray(xp[2 * ci:2 * ci + 2].reshape(TP, D))
        m["xs"] = np.ascontiguousarray(xs[b])
        cond = np.stack([f(inp["c_ctx"]), f(inp["c"])[b]], 0).reshape(16, 128)
        m["cond"] = np.ascontiguousarray(cond)
        m["rk"] = np.array([[TP + (ci % 4) * 256, (ci % 4) * 256]], np.int32)
        m["st_gla"] = np.ascontiguousarray(f(inp["state_gla"])[b])
        m["cswk"] = np.ascontiguousarray(f(inp["cache_swa_k"])[b].reshape(DEPTH, 512, 128))
        m["cswv"] = np.ascontiguousarray(f(inp["cache_swa_v"])[b].reshape(DEPTH, 512, 128))
        m["cnak"] = np.ascontiguousarray(f(inp["cache_na_k"])[b].reshape(DEPTH, 512, 512))
        m["cnav"] = np.ascontiguousarray(f(inp["cache_na_v"])[b].reshape(DEPTH, 512, 512))
        maps.append(m)
    return maps


def assemble(results):
    yp = np.stack([r["yp"].reshape(2, 256, D) for r in results], 0).reshape(16, 256, D)
    ys = np.stack([np.concatenate([results[4 * b + r]["ysq"] for r in range(4)], 0) for b in range(2)], 0)
    gla = np.concatenate([r["o_gla"] for r in results], 0)
    swk = np.concatenate([r["o_swk"].reshape(2, DEPTH, 256, 2, 64) for r in results], 0)
    swv = np.concatenate([r["o_swv"].reshape(2, DEPTH, 256, 2, 64) for r in results], 0)
    nak = np.concatenate([r["o_nak"].reshape(2, DEPTH, 256, 8, 64) for r in results], 0)
    nav = np.concatenate([r["o_nav"].reshape(2, DEPTH, 256, 8, 64) for r in results], 0)
    return tuple(np.ascontiguousarray(a, dtype=np.float32) for a in (yp, ys, gla, swk, swv, nak, nav))


def kernel(**inputs):
    if "nc" not in _CACHE:
        _CACHE["nc"] = build_program()
    nc = _CACHE["nc"]
    maps = make_in_maps(inputs)
    res = run_bass_kernel_spmd(nc, maps, core_ids=list(range(NCORES)))
    return assemble(res.results)
```

```python
import os
import numpy as np
import ml_dtypes
import concourse.bass as bass
import concourse.mybir as mybir
from concourse.bass_utils import run_bass_kernel_spmd

F32 = mybir.dt.float32
BF16 = mybir.dt.bfloat16
AF = mybir.ActivationFunctionType
ALU = mybir.AluOpType

D = 1024
DEPTH = 4
NCORES = 8
TP = 512
TS = 1024
TT = TP + TS
IN_W = 6944
NEG = -30000.0
PA_STOP = float(os.environ.get('PA_STOP', '99'))
EPS = 1e-6

O_AQ, O_AK, O_AV, O_AR, O_ALF, O_ALB = 0, 256, 512, 1024, 1536, 1552
O_BQ, O_BK, O_BV = 1568, 2080, 2208
O_CQ, O_CK, O_CV = 2336, 2848, 3360
O_GA, O_GB, O_GC = 3872, 4896, 5920


class Buf:
    __slots__ = ("lw", "rd", "name")

    def __init__(self, name=""):
        self.lw = None
        self.rd = []
        self.name = name


class Op:
    __slots__ = ("eng", "fn", "deps", "dma", "ticket", "semkey", "nsig", "idx", "cost", "tag", "st", "fi", "aset")


class Prog:
    ENGS = ("pe", "act", "dve", "pool", "sp")

    def __init__(self, nc):
        self.nc = nc
        self.ops = []
        self.dyn = {}
        self.dyn_spec = {}

    def op(self, eng, fn, reads=(), writes=(), dma=False, cost=500.0):
        o = Op()
        o.aset = None
        o.tag = getattr(self, "tag", "")
        o.cost = cost
        o.eng = eng
        o.fn = fn
        o.dma = dma
        o.idx = len(self.ops)
        deps = set()
        for b in reads:
            if b.lw is not None:
                deps.add(b.lw)
        for b in writes:
            if b.lw is not None:
                deps.add(b.lw)
            deps.update(b.rd)
        o.deps = deps
        self.ops.append(o)
        for b in reads:
            b.rd.append(o.idx)
        for b in writes:
            b.lw = o.idx
            b.rd = []
        return o


    def schedule(self, window=40):
        ops = self.ops
        n = len(ops)
        left = [len(o.deps) for o in ops]
        users = [[] for _ in ops]
        for o in ops:
            for dd in o.deps:
                users[dd].append(o.idx)
        ready = [0.0] * n
        fin = [0.0] * n
        done = [False] * n
        cpl = [0.0] * n
        use_cp = os.environ.get("NO_CP") != "1"
        for o in reversed(ops):
            tail_ = 0.0
            for u in users[o.idx]:
                if cpl[u] > tail_:
                    tail_ = cpl[u]
            cpl[o.idx] = tail_ + o.cost + (2000.0 if o.dma else 150.0)
        pend = {e: [o.idx for o in ops if o.eng == e] for e in self.ENGS}
        head = {e: 0 for e in self.ENGS}
        free = {e: 0.0 for e in self.ENGS}
        order = {e: [] for e in self.ENGS}
        pipe = 0.0
        remaining = n
        glob = []
        last_aset = [None]
        TSW = 1283.0 if os.environ.get("NO_TSW") != "1" else 0.0
        while remaining:
            best = None
            for e in self.ENGS:
                lst = pend[e]
                i = head[e]
                while i < len(lst) and done[lst[i]]:
                    i += 1
                head[e] = i
                cnt = 0
                fe = free[e]
                while i < len(lst) and cnt < window:
                    idx = lst[i]
                    i += 1
                    if done[idx]:
                        continue
                    cnt += 1
                    if left[idx] == 0:
                        st = ready[idx] if ready[idx] > fe else fe
                        if e == "act" and ops[idx].aset is not None and ops[idx].aset != last_aset[0]:
                            st += TSW
                        key = (st, -cpl[idx], idx) if use_cp else (st, 0.0, idx)
                        if best is None or key < best[0]:
                            best = (key, e, idx)
                        if st <= fe and not use_cp:
                            break
            assert best is not None, "scheduler deadlock"
            (st, _, _), e, idx = best
            o = ops[idx]
            if o.dma:
                issue = 1000.0 if e == "pool" else 100.0
                free[e] = st + issue
                p0 = max(pipe, st + issue)
                pipe = p0 + o.cost
                f = pipe + 2000.0
            else:
                if e == "act" and o.aset is not None:
                    last_aset[0] = o.aset
                free[e] = st + o.cost
                f = st + o.cost + 150.0
            fin[idx] = f
            o.st = st
            o.fi = f
            done[idx] = True
            order[e].append(idx)
            glob.append(idx)
            remaining -= 1
            for u in users[idx]:
                left[u] -= 1
                if ready[u] < f:
                    ready[u] = f
        self.order = order
        self.est_ns = max(fin) if fin else 0.0

    def emit(self, final_wait_eng="sp"):
        nc = self.nc
        ops = self.ops
        KD = {"sp": 14, "pool": 10, "act": 4}
        needed = [False] * len(ops)
        for o in ops:
            for d in o.deps:
                a = ops[d]
                if a.eng == o.eng and o.eng == "pe" and not a.dma and not o.dma:
                    continue
                needed[d] = True
        cnt = {e: 0 for e in self.ENGS}
        dcnt = {e: 0 for e in KD}
        order = getattr(self, "order", None)
        if order is None:
            order = {e: [o.idx for o in ops if o.eng == e] for e in self.ENGS}
        seq = [ops[i] for e in self.ENGS for i in order[e]]
        for o in seq:
            if o.dma:
                n = dcnt[o.eng]
                dcnt[o.eng] += 1
                k = n % KD[o.eng]
                o.semkey = (o.eng, k)
                o.ticket = 16 * (n // KD[o.eng] + 1)
            else:
                o.semkey = o.eng
                if needed[o.idx]:
                    cnt[o.eng] += 1
                    o.ticket = cnt[o.eng]
                else:
                    o.ticket = None
        sems = {}
        import contextlib
        with contextlib.ExitStack() as st:
            for e in self.ENGS:
                sems[e] = st.enter_context(nc.semaphore("s_" + e))
            for e, k in KD.items():
                for i in range(k):
                    sems[(e, i)] = st.enter_context(nc.semaphore("d_%s%d" % (e, i)))
            block = st.enter_context(nc.Block())
            per_eng = {e: [ops[i] for i in order[e]] for e in self.ENGS}

            def run(eng_name, eng):
                seen = {}
                if eng_name == "sp":
                    for key, (ap, lo, hi) in getattr(self, "dyn_spec", {}).items():
                        reg = eng.alloc_register("dyn_" + key)
                        eng.reg_load(reg, ap)
                        self.dyn[key] = eng.snap(reg, min_val=lo, max_val=hi)
                for o in per_eng[eng_name]:
                    waits = {}
                    for d in o.deps:
                        a = ops[d]
                        if a.eng == o.eng and o.eng == "pe" and not a.dma and not o.dma:
                            continue
                        if a.ticket is None:
                            continue
                        if waits.get(a.semkey, 0) < a.ticket:
                            waits[a.semkey] = a.ticket
                    if o.dma:
                        prev = o.ticket - 16
                        if prev > 0 and waits.get(o.semkey, 0) < prev:
                            waits[o.semkey] = prev
                    for key, val in waits.items():
                        if seen.get(key, 0) < val:
                            eng.wait_ge(sems[key], val)
                            seen[key] = val
                    ins = o.fn(eng)
                    if o.dma:
                        ins.then_inc(sems[o.semkey], 16)
                    elif o.ticket is not None:
                        ins.then_inc(sems[o.semkey], 1)
                if eng_name in KD:
                    n = dcnt[eng_name]
                    for k in range(min(n, KD[eng_name])):
                        tot = 16 * ((n - 1 - k) // KD[eng_name] + 1)
                        if seen.get((eng_name, k), 0) < tot:
                            eng.wait_ge(sems[(eng_name, k)], tot)

            @block.tensor
            def _(e):
                run("pe", e)

            @block.scalar
            def _(e):
                run("act", e)

            @block.vector
            def _(e):
                run("dve", e)

            @block.gpsimd
            def _(e):
                run("pool", e)

            @block.sync
            def _(e):
                run("sp", e)


def make_consts():
    c = {}
    bf = ml_dtypes.bfloat16
    c["ident_bf"] = np.eye(128, dtype=np.float32).astype(bf)
    c["ident_f"] = np.eye(128, dtype=np.float32)
    bd = np.zeros((128, 128), np.float32)
    bd[:64, :64] = 1.0
    bd[64:, 64:] = 1.0
    c["ones_bd"] = bd.astype(bf)
    c["ones_all"] = np.ones((128, 128), np.float32).astype(bf)
    s = np.arange(128)[:, None]
    t = np.arange(128)[None, :]
    c["trif"] = np.where(s <= t, -1.0 / 16.0, 0.0).astype(np.float32).astype(bf)
    c["trib"] = np.where(s >= t, -1.0 / 16.0, 0.0).astype(np.float32).astype(bf)
    c["maskf"] = np.where(s <= t, 1.0, 0.0).astype(np.float32).astype(bf)
    c["maskb"] = np.where(s >= t, 1.0, 0.0).astype(np.float32).astype(bf)
    c["band_next"] = np.where(t <= s, 0.0, NEG).astype(np.float32).astype(bf)
    c["band_prev"] = np.where(s <= t, 0.0, NEG).astype(np.float32).astype(bf)
    nf = 16
    inv_freq = (10000.0 ** (-np.arange(nf, dtype=np.float32) / nf)).astype(np.float32)
    tt = np.arange(TS)
    row = (tt // 64).astype(np.float32)
    col = (tt % 64).astype(np.float32)
    ang = np.zeros((64, TS), np.float32)
    for d in range(64):
        pos = row if d < 32 else col
        ang[d] = pos * inv_freq[d % 16]
    cos = np.cos(ang).astype(np.float32)
    sin = np.sin(ang).astype(np.float32)
    c["rope_cos"] = np.concatenate([cos, cos], 0)
    c["rope_sin"] = np.concatenate([sin, sin], 0)
    Pm = np.zeros((128, 128), np.float32)
    for d in range(128):
        if (d % 32) < 16:
            Pm[d, d + 16] = -1.0
        else:
            Pm[d, d - 16] = 1.0
    c["rope_pT"] = np.ascontiguousarray(Pm.T).astype(bf)
    J = np.zeros((64, 64), np.float32)
    for i in range(64):
        J[i, 63 - i] = 1.0
    JJ = np.zeros((128, 128), np.float32)
    JJ[:64, :64] = J
    JJ[64:, 64:] = J
    c["jj"] = JJ.astype(bf)
    cq = np.arange(64)[None, :]
    ckp = np.arange(64)[:, None]
    ck = 63 - ckp
    cs = np.clip(cq - 8, 0, 48)
    ok = (ck >= cs) & (ck < cs + 16)
    nm_ = np.where(ok, 0.0, NEG).astype(np.float32)
    c["na_mask"] = np.concatenate([nm_, nm_], 0)
    return c


CONST_SPECS = [
    ("ident_bf", [128, 128], BF16), ("ident_f", [128, 128], F32), ("ones_bd", [128, 128], BF16),
    ("ones_all", [128, 128], BF16), ("trif", [128, 128], BF16), ("trib", [128, 128], BF16),
    ("maskf", [128, 128], BF16), ("maskb", [128, 128], BF16), ("band_next", [128, 128], BF16),
    ("band_prev", [128, 128], BF16), ("rope_cos", [128, TS], F32), ("rope_sin", [128, TS], F32),
    ("rope_pT", [128, 128], BF16), ("jj", [128, 128], BF16), ("na_mask", [128, 64], F32),
]


GRAN = 512


class V:
    __slots__ = ("ap", "bufs", "excl")

    def __init__(self, ap, bufs, excl=False):
        self.ap = ap
        self.bufs = bufs
        self.excl = excl


def DR(ap):
    return V(ap, [])


class Arena:
    def __init__(self, nc, nbytes):
        self.nc = nc
        self.nbytes = nbytes
        self.t = nc.alloc_sbuf_tensor("arena", [128, nbytes // 4], F32)
        self.g = [Buf("g%d" % i) for i in range((nbytes + GRAN - 1) // GRAN)]
        self.top = 0

    def bufs(self, lo, hi):
        return self.g[lo // GRAN:(hi - 1) // GRAN + 1]

    def alloc(self, shape, dtype, align=GRAN):
        esz = 4 if dtype == F32 else 2
        n = 1
        for d in shape[1:]:
            n *= d
        nb = (n * esz + 3) // 4 * 4
        off = (self.top + align - 1) // align * align
        assert off + nb <= self.nbytes, ("arena overflow", off, nb, self.nbytes)
        self.top = off + nb
        return Tile(self, off, shape, dtype)

    def mark(self):
        return self.top

    def release(self, m):
        self.top = m


class Tile:
    def __init__(self, arena, off, shape, dtype):
        self.arena = arena
        self.off = off
        self.shape = tuple(shape)
        self.dt = dtype
        self.esz = 4 if dtype == F32 else 2
        n = 1
        for d in shape[1:]:
            n *= d
        w0 = off // 4
        w1 = w0 + (n * self.esz + 3) // 4
        base = arena.t[:, w0:w1]
        if dtype != F32:
            base = base.bitcast(dtype)
        if len(shape) > 2:
            names = ["d%d" % i for i in range(len(shape) - 1)]
            pat = "p (" + " ".join(names) + ") -> p " + " ".join(names)
            base = base.rearrange(pat, **{nm: shape[i + 1] for i, nm in enumerate(names[:-1])})
        self.ap = base[0:shape[0]]
        st = []
        acc = 1
        for d in reversed(shape[1:]):
            st.append(acc)
            acc *= d
        self.strides = list(reversed(st))

    def __getitem__(self, key):
        if not isinstance(key, tuple):
            key = (key,)
        ap = self.ap[key]
        fk = list(key[1:]) + [slice(None)] * (len(self.shape) - len(key))
        dims = []
        for k, d, s in zip(fk, self.shape[1:], self.strides):
            if isinstance(k, slice):
                a, b, stp = k.indices(d)
                cnt = max(0, (b - a + stp - 1) // stp)
                dims.append((a, cnt, stp, s, d))
            else:
                dims.append((k, 1, 1, s, d))
        gset = {}
        esz = self.esz
        off = self.off
        arena = self.arena

        def rec(i, base):
            a, cnt, stp, st, d = dims[i]
            inner_full = all(dd[1] == dd[4] and dd[2] == 1 for dd in dims[i + 1:])
            if stp == 1 and inner_full:
                lo = base + a * st
                hi = base + (a + cnt) * st
                for b_ in arena.bufs(off + lo * esz, off + hi * esz):
                    gset[id(b_)] = b_
                return
            if i == len(dims) - 1:
                for j in range(cnt):
                    lo = base + (a + j * stp) * st
                    for b_ in arena.bufs(off + lo * esz, off + (lo + 1) * esz):
                        gset[id(b_)] = b_
                return
            for j in range(cnt):
                rec(i + 1, base + (a + j * stp) * st)

        rec(0, 0)
        return V(ap, list(gset.values()))

    def full(self):
        return self[tuple(slice(None) for _ in self.shape)]


class Bank:
    def __init__(self, nc, i):
        self.t = nc.alloc_psum_tensor("pb%d" % i, [128, 512], F32)
        self.buf = Buf("pb%d" % i)
        self.ap = self.t[:, :]

    def __getitem__(self, key):
        return V(self.ap[key], [self.buf], True)

    def v(self, ap):
        return V(ap, [self.buf], True)

    def bf(self):
        return self.ap.bitcast(BF16)


class K:
    def __init__(self, nc):
        self.nc = nc
        self.P = Prog(nc)
        self.A = Arena(nc, 205 * 1024)
        self.banks = [Bank(nc, i) for i in range(8)]
        self.bi = 0
        self.flip = 0
        self.held = []
        self.fence = V(None, [Buf('fence')])

    def bank(self):
        while True:
            b = self.banks[self.bi]
            self.bi = (self.bi + 1) % 8
            if b not in self.held:
                return b

    def hold(self):
        b = self.bank()
        self.held.append(b)
        return b

    def unhold(self, b):
        self.held.remove(b)

    def _rw(self, reads, writes):
        r, w = [], []
        for v in reads:
            if isinstance(v, V):
                (w if v.excl else r).extend(v.bufs)
        for v in writes:
            w.extend(v.bufs)
        return r, w

    def op(self, eng, fn, reads, writes, dma=False):
        r, w = self._rw(reads, writes)
        try:
            shp = writes[0].ap.shape
            nfree = 1
            for x in shp[1:]:
                nfree *= x
            npart = shp[0]
        except Exception:
            nfree, npart = 512, 128
        if dma:
            try:
                esz = 4 if reads[0].ap.dtype == F32 else 2
            except Exception:
                esz = 4
            cost = npart * nfree * esz / 180.0
        elif eng == "pe":
            cost = max(64.0, nfree) * 0.46 + (70.0 if nfree < 256 else 15.0)
        elif eng == "act":
            cost = 230.0 + nfree * 0.75
        elif eng == "dve":
            cost = 120.0 + nfree * 0.95
        else:
            cost = 250.0 + nfree * 1.9
        return self.P.op(eng, fn, r, w, dma, cost)

    def mm(self, out, lhsT, rhs, start=True, stop=True):
        self.op("pe", lambda e: e.matmul(out.ap, lhsT=lhsT.ap, rhs=rhs.ap, start=start, stop=stop,
                                         skip_group_check=True), [lhsT, rhs], [out])

    def tr(self, out, in_, ident):
        self.op("pe", lambda e: e.transpose(out.ap, in_.ap, ident.ap), [in_, ident], [out])

    def act(self, out, in_, func, bias=None, scale=None):
        kw = {}
        rd = [in_]
        if bias is not None:
            kw["bias"] = bias.ap if isinstance(bias, V) else bias
            rd.append(bias)
        if scale is not None:
            kw["scale"] = scale.ap if isinstance(scale, V) else scale
            rd.append(scale)
        o_ = self.op("act", lambda e: e.activation(out.ap, in_.ap, func, **kw), rd, [out])
        if func in (AF.Exp, AF.Ln):
            o_.aset = "exp"
        elif func in (AF.Sigmoid, AF.Silu):
            o_.aset = "sig"

    def tt(self, out, a, b, op, eng="dve"):
        self.op(eng, lambda e: e.tensor_tensor(out.ap, a.ap, b.ap, op), [a, b], [out])

    def ts(self, out, a, s1, op0, s2=None, op1=None, eng="dve"):
        rd = [a, s1, s2]
        s1a = s1.ap if isinstance(s1, V) else s1
        s2a = s2.ap if isinstance(s2, V) else s2
        if op1 is None:
            self.op(eng, lambda e: e.tensor_scalar(out.ap, a.ap, s1a, None, op0), rd, [out])
        else:
            self.op(eng, lambda e: e.tensor_scalar(out.ap, a.ap, s1a, s2a, op0, op1), rd, [out])

    def stt(self, out, in0, scalar, in1, op0, op1, eng="dve"):
        sa = scalar.ap if isinstance(scalar, V) else scalar
        self.op(eng, lambda e: e.scalar_tensor_tensor(out.ap, in0.ap, sa, in1.ap, op0, op1),
                [in0, scalar, in1], [out])

    def cp(self, out, in_, eng="dve"):
        if eng == "act":
            self.op("act", lambda e: e.copy(out.ap, in_.ap), [in_], [out])
        else:
            self.op(eng, lambda e: e.tensor_copy(out.ap, in_.ap), [in_], [out])

    def cpx(self, out, in_):
        self.flip ^= 1
        self.cp(out, in_, "act" if self.flip else "dve")

    def recip(self, out, in_):
        self.op("dve", lambda e: e.reciprocal(out.ap, in_.ap), [in_], [out])

    def rstd(self, out, ss, n):
        self.act(out, ss, AF.Ln, bias=EPS, scale=1.0 / n)
        self.act(out, out, AF.Exp, scale=-0.5)

    def recip_act(self, out, in_):
        self.act(out, in_, AF.Ln)
        self.act(out, out, AF.Exp, scale=-1.0)

    def memset(self, out, val, eng="dve"):
        self.op(eng, lambda e: e.memset(out.ap, val), [], [out])

    def dma_dyn(self, out, src_tile, key, width, track):
        P = self.P

        def fn(e):
            return e.dma_start(out=out.ap, in_=src_tile.ap[:, :, bass.ds(P.dyn[key], width)])
        self.op("sp", fn, [track], [out], dma=True)

    def allgather(self, out, in_, groups):
        o_ = self.op("pool", lambda e: e.collective_compute("AllGather", ALU.bypass, replica_groups=groups,
                                                            ins=[in_.ap.opt()], outs=[out.ap.opt()]),
                     [in_], [out, self.fence])
        o_.cost = 50000.0

    def dma(self, q, out, in_, **kw):
        rd = [in_, self.fence] if q == "pool" else [in_]
        self.op(q, lambda e: e.dma_start(out=out.ap, in_=in_.ap, **kw), rd, [out], dma=True)


def build_program(depth=DEPTH, do_sample=True, dbg_names=(), stop=None):
    nc = bass.Bass("TRN2", target_bir_lowering=False)

    def din(name, shape, dt=F32):
        return nc.dram_tensor(name, shape, dt, kind="ExternalInput").ap()

    def dout(name, shape):
        return nc.dram_tensor(name, shape, F32, kind="ExternalOutput").ap()

    d = {}
    d["xp"] = din("xp", [TP, D])
    d["xs"] = din("xs", [TS, D])
    d["cond"] = din("cond", [16, 128])
    d["rk"] = din("rk", [1, 2], mybir.dt.int32)
    d["st_gla"] = din("st_gla", [DEPTH, 2, 4, 64, 128])
    d["cswk"] = din("cswk", [DEPTH, 512, 128])
    d["cswv"] = din("cswv", [DEPTH, 512, 128])
    d["cnak"] = din("cnak", [DEPTH, 512, 512])
    d["cnav"] = din("cnav", [DEPTH, 512, 512])
    d["w_mod"] = din("w_mod", [DEPTH, D, 6 * D])
    d["b_mod"] = din("b_mod", [DEPTH, 48, 128])
    d["norm1"] = din("norm1", [DEPTH, 8, 128])
    d["norm2"] = din("norm2", [DEPTH, 8, 128])
    d["w_in"] = din("w_in", [DEPTH, D, IN_W])
    d["w_a2_f"] = din("w_a2_f", [DEPTH, 16, 256])
    d["b_a_f"] = din("b_a_f", [DEPTH, 256])
    d["w_a2_b"] = din("w_a2_b", [DEPTH, 16, 256])
    d["b_a_b"] = din("b_a_b", [DEPTH, 256])
    d["gla_onorm"] = din("gla_onorm", [DEPTH, 128])
    for nm in ("qn_swa", "kn_swa", "qn_na", "kn_na"):
        d[nm] = din(nm, [DEPTH, 64])
    d["sink_swa"] = din("sink_swa", [DEPTH, 8])
    d["rpb_pad"] = din("rpb_pad", [DEPTH, 8, 15, 127])
    for nm in ("w_pa", "w_pb", "w_pc"):
        d[nm] = din(nm, [DEPTH, 512, D])
    d["w_o"] = din("w_o", [DEPTH, D, D])
    d["w_fc1"] = din("w_fc1", [DEPTH, D, 4 * D])
    d["w_fc2"] = din("w_fc2", [DEPTH, 4 * D, D])
    for nm, shp, dt in CONST_SPECS:
        d[nm] = din(nm, shp, dt)
    o = {}
    o["yp"] = dout("yp", [TP, D])
    o["ysq"] = dout("ysq", [256, D])
    o["o_gla"] = dout("o_gla", [2, DEPTH, 2, 4, 64, 128])
    o["o_swk"] = dout("o_swk", [2, DEPTH, 256, 128])
    o["o_swv"] = dout("o_swv", [2, DEPTH, 256, 128])
    o["o_nak"] = dout("o_nak", [2, DEPTH, 256, 512])
    o["o_nav"] = dout("o_nav", [2, DEPTH, 256, 512])

    k = K(nc)
    A = k.A
    k.P.dyn_spec["cabs"] = (d["rk"][0:1, 0:1], TP, TP + 768)
    k.P.dyn_spec["crel"] = (d["rk"][0:1, 1:2], 0, 768)
    xq_in = nc.dram_tensor("xq_in", [1024, 256], F32)
    xq_all = nc.dram_tensor("xq_all", [4096, 256], F32)
    XQI = V(xq_in.ap(), [Buf("xq_in")])
    XQA = V(xq_all.ap(), [Buf("xq_all")])
    X = A.alloc([128, 8, TT], F32)
    H = A.alloc([128, 8, TT], BF16)
    MG = A.alloc([128, 8, TP], BF16)
    MGQ = A.alloc([128, 8, 256], BF16)
    HQ = A.alloc([128, 8, 256], BF16)
    WS = [A.alloc([128, 8, 512], BF16) for _ in range(4)]
    wsi = [0]

    def wslot():
        w = WS[wsi[0]]
        wsi[0] = (wsi[0] + 1) % len(WS)
        return w

    C = {}
    for nm, shp, dt in CONST_SPECS:
        C[nm] = A.alloc(shp, dt, align=64 if shp[1] <= 128 else GRAN)
        k.dma("sp", C[nm].full(), DR(d[nm]))
    SCT = A.alloc([128, 8, 2], BF16, align=64)
    PVt = A.alloc([128, DEPTH, 64], F32)
    MODVL = [A.alloc([128, 48, 2], F32, align=64) for _ in range(2)]
    SCLL = [A.alloc([128, 2, 2, 8], F32, align=64) for _ in range(2)]
    cur = [0]
    GVL = [A.alloc([128, 8], F32, align=64) for _ in range(2)]
    ESKL = [A.alloc([128, 8], F32, align=64) for _ in range(2)]
    ALR = A.alloc([64, TS], BF16)
    WA2L = [A.alloc([64, 512], BF16) for _ in range(2)]

    class _Cur:
        def __init__(self, lst):
            self.lst = lst

        def __getitem__(self, key):
            return self.lst[cur[0]][key]

        def full(self):
            return self.lst[cur[0]].full()

    GV = _Cur(GVL)
    ESK = _Cur(ESKL)
    WA2 = _Cur(WA2L)
    k.memset(ALR.full(), 1.0)

    dbg = {}

    def dbg_out(name, v, shape):
        if name in dbg_names:
            ap = dout("dbg_" + name, shape)
            k.dma("sp", DR(ap), v)

    m0 = A.mark()
    for l in range(DEPTH):
        stg = A.alloc([64, 128], F32)
        k.dma("sp", stg[0:48, :], DR(d["b_mod"][l]))
        k.dma("sp", stg[48:56, :], DR(d["norm1"][l]))
        k.dma("sp", stg[56:64, :], DR(d["norm2"][l]))
        pb = k.bank()
        k.tr(pb[:, 0:64], stg.full(), C["ident_f"][0:64, 0:64])
        k.cp(PVt[:, l, :], pb[:, 0:64])
    stg = A.alloc([16, 128], F32)
    k.dma("sp", stg.full(), DR(d["cond"]))
    pb = k.bank()
    k.tr(pb[:, 0:16], stg.full(), C["ident_f"][0:16, 0:16])
    for ci in range(2):
        k.act(SCT[:, :, ci], pb[:, ci * 8:(ci + 1) * 8], AF.Silu)
    A.release(m0)

    def load_tokens(src, n, col0):
        m = A.mark()
        st2 = [A.alloc([128, D], F32) for _ in range(2)]
        for tix in range(n // 128):
            s_ = st2[tix % 2]
            k.dma("sp", s_.full(), DR(src[tix * 128:(tix + 1) * 128, :]))
            for q4 in range(2):
                pb = k.bank()
                for j in range(4):
                    oc = q4 * 4 + j
                    k.tr(pb[:, j * 128:(j + 1) * 128], s_[:, oc * 128:(oc + 1) * 128], C["ident_f"].full())
                c0 = col0 + tix * 128
                k.cpx(X[:, q4 * 4:q4 * 4 + 4, c0:c0 + 128],
                      pb.v(pb.ap.rearrange("p (a b) -> p a b", a=4)))
        A.release(m)

    load_tokens(d["xp"], TP, 0)
    load_tokens(d["xs"], TS, TP)

    def wload(slot_view, src_ap):
        k.dma("pool", slot_view, DR(src_ap))

    wc_scr = {}
    wc_valid = set()

    def wtile(key, slot_view, loader):
        if os.environ.get("NO_WCACHE") == "1":
            loader()
            return
        shp = list(slot_view.ap.shape)
        if key not in wc_scr:
            t = nc.dram_tensor("wc%d" % len(wc_scr), shp, BF16)
            wc_scr[key] = V(t.ap(), [Buf("wc")])
        scr = wc_scr[key]
        if key in wc_valid:
            k.dma("sp", slot_view, scr)
        else:
            loader()
            k.dma("sp", scr, slot_view)
            wc_valid.add(key)

    def wrows(w2d, c0, n):
        return w2d[:, c0:c0 + n].rearrange("(kc p) n -> p kc n", p=128)

    GCI = [0, 1, 1]

    def mod_steps(l):
        MODV = MODVL[l % 2]
        SCL = SCLL[l % 2]
        st = {}

        def step(j):
            if j == 0:
                st["mb"] = k.hold()
            mb = st["mb"]
            ws = wslot()
            wload(ws.full(), wrows(d["w_mod"][l], j * 512, 512))
            for jj in range(4):
                ch = j * 4 + jj
                for kc in range(8):
                    k.mm(mb[:, ch * 2:ch * 2 + 2], ws[:, kc, jj * 128:(jj + 1) * 128], SCT[:, kc, :],
                         start=(kc == 0), stop=(kc == 7))

        def fin():
            mb = st["mb"]
            bm = PVt[:, l, 0:48]
            k.tt(MODV.full(), mb.v(mb.ap[:, 0:96].rearrange("p (a b) -> p a b", b=2)),
                 V(bm.ap.unsqueeze(2).to_broadcast([128, 48, 2]), bm.bufs), ALU.add)
            k.unhold(mb)
            for n in range(2):
                for ci in range(2):
                    sc = MODV[:, 8 + 24 * n:16 + 24 * n, ci]
                    k.ts(SCL[:, n, ci, :], sc, 1.0, ALU.add)
                    k.tt(SCL[:, n, ci, :], SCL[:, n, ci, :], PVt[:, l, 48 + 8 * n:56 + 8 * n], ALU.mult)

        return [(lambda j=j: step(j)) for j in range(12)] + [fin]

    def norm_group(l, n, g):
        ci = GCI[g]
        tok = slice(g * 512, (g + 1) * 512)
        m = A.mark()
        sq = [A.alloc([128, 512], BF16) for _ in range(2)]
        rs = A.alloc([128, 512], F32)
        tmp = [A.alloc([128, 512], F32) for _ in range(2)]
        sb = k.bank()
        for oc in range(8):
            s_ = sq[oc % 2]
            k.act(s_.full(), X[:, oc, tok], AF.Square)
            k.mm(sb[:, :], C["ones_all"].full(), s_.full(), start=(oc == 0), stop=(oc == 7))
        k.rstd(rs.full(), sb[:, :], 1024.0)
        for oc in range(8):
            t_ = tmp[oc % 2]
            k.stt(t_.full(), X[:, oc, tok], SCLL[cur[0]][:, n, ci, oc:oc + 1], rs.full(), ALU.mult, ALU.mult)
            k.act(H[:, oc, tok], t_.full(), AF.Identity, bias=MODVL[cur[0]][:, 24 * n + oc, ci:ci + 1])
        A.release(m)

    def proj_fm(ws, c0, mcols, tok, pb=None, col_off=0):
        if pb is None:
            pb = k.bank()
        if isinstance(tok, tuple):
            hf, n = tok
        else:
            hf, n = (lambda kc: H[:, kc, tok]), 512
        for kc in range(8):
            k.mm(pb[0:mcols, col_off:col_off + n], ws[:, kc, c0:c0 + mcols], hf(kc),
                 start=(kc == 0), stop=(kc == 7))
        return pb

    def proj_tm(ws, c0, ncols, t0, pb=None):
        if pb is None:
            pb = k.bank()
        for kc in range(8):
            k.mm(pb[:, 0:ncols], H[:, kc, t0:t0 + 128], ws[:, kc, c0:c0 + ncols],
                 start=(kc == 0), stop=(kc == 7))
        return pb

    nh_flip = [0]

    def normhead(pb, out_f32, out_bf, gain, out_pair=None):
        m = A.mark()
        nh_flip[0] ^= 1
        if nh_flip[0]:
            A.alloc([128, 512 + 256 + 512], F32)
        qf = A.alloc([128, 512], F32)
        sq = A.alloc([128, 512], BF16)
        rs = A.alloc([128, 512], F32)
        k.cp(qf.full(), pb[:, :], "dve")
        k.act(sq.full(), pb[:, :], AF.Square)
        sb = k.bank()
        k.mm(sb[:, :], C["ones_bd"].full(), sq.full())
        k.rstd(rs.full(), sb[:, :], 64.0)
        if out_pair is not None:
            for hf in range(2):
                rw = slice(hf * 64, hf * 64 + 64)
                k.stt(out_pair[hf], qf[rw, :], V(gain.ap[rw], gain.bufs), rs[rw, :], ALU.mult, ALU.mult)
        elif out_f32 is not None:
            k.stt(out_f32, qf.full(), gain, rs.full(), ALU.mult, ALU.mult)
            if out_bf is not None:
                k.cp(out_bf, out_f32, "act")
        else:
            k.stt(out_bf, qf.full(), gain, rs.full(), ALU.mult, ALU.mult)
        A.release(m)

    def layer_small(l):
        GV, ESK, WA2 = GVL[l % 2], ESKL[l % 2], WA2L[l % 2]
        for j, (nm, f) in enumerate((("qn_swa", 0.125), ("kn_swa", 1.0), ("qn_na", 0.125), ("kn_na", 1.0))):
            src = d[nm][l].rearrange("(p o) -> p o", o=1)
            k.dma("sp", GV[0:64, j:j + 1], DR(src))
            k.dma("sp", GV[64:128, j:j + 1], DR(src))
        k.dma("sp", GV[:, 4:5], DR(d["gla_onorm"][l].rearrange("(p o) -> p o", o=1)))
        k.ts(GV[:, 0:1], GV[:, 0:1], 0.125, ALU.mult)
        k.ts(GV[:, 2:3], GV[:, 2:3], 0.125, ALU.mult)
        k.dma("sp", ESK.full(), DR(d["sink_swa"][l].partition_broadcast(128)))
        k.act(ESK.full(), ESK.full(), AF.Exp)
        k.memset(WA2.full(), 0.0)
        k.dma("pool", WA2[0:16, 0:256], DR(d["w_a2_f"][l]))
        k.dma("pool", WA2[16:17, 0:256], DR(d["b_a_f"][l].rearrange("(o n) -> o n", o=1)))
        k.dma("pool", WA2[32:48, 256:512], DR(d["w_a2_b"][l]))
        k.dma("pool", WA2[48:49, 256:512], DR(d["b_a_b"][l].rearrange("(o n) -> o n", o=1)))

    def dense_attn(Q, Kt, kmap, VA, vmap, O, sink):
        m = A.mark()
        PT = [A.alloc([128, 512], BF16) for _ in range(3)]
        RC = [A.alloc([64, 256], F32) for _ in range(2)]
        it = 0
        ob = None
        for s in range(2):
            for h in range(8):
                rows = slice((h % 2) * 64, (h % 2) * 64 + 64)
                sb = k.bank()
                for kt in range(2):
                    k0 = s * 256 + kt * 128
                    k.mm(sb[:, kt * 256:(kt + 1) * 256], Kt[:, kmap(h), k0:k0 + 128],
                         Q[:, h, s * 256:(s + 1) * 256])
                pt = PT[it % 3]
                k.act(pt.full(), sb[:, :], AF.Exp)
                if it % 2 == 0:
                    ob = k.bank()
                c0 = (it % 2) * 256
                for kt in range(2):
                    k.mm(ob[:, c0:c0 + 256], VA[:, s * 2 + kt, vmap(h), :], pt[:, kt * 256:(kt + 1) * 256],
                         start=(kt == 0), stop=(kt == 1))
                rc = RC[it % 2]
                if sink is not None:
                    k.act(rc.full(), ob[64:128, c0:c0 + 256], AF.Ln, bias=ESK[0:64, h:h + 1])
                else:
                    k.act(rc.full(), ob[64:128, c0:c0 + 256], AF.Ln)
                k.act(rc.full(), rc.full(), AF.Exp, scale=-1.0)
                k.tt(O[rows, h // 2, s * 256:(s + 1) * 256], ob[0:64, c0:c0 + 256], rc.full(), ALU.mult)
                it += 1
        A.release(m)

    def merge(l, wp, gcol, groups, first):
        m = A.mark()
        SG = [A.alloc([128, 512], F32) for _ in range(2)]
        TM = [A.alloc([128, 512], F32) for _ in range(2)]
        for half in range(2):
            wsp = wslot()
            wtile(("mp", gcol, half), wsp[:, 0:4, :], lambda wsp=wsp, half=half: wload(
                wsp[:, 0:4, :], wp[:, half * 512:(half + 1) * 512].rearrange("(h p) n -> p h n", p=128)))
            wsg = wslot()
            wtile(("mg", gcol, half), wsg.full(), lambda wsg=wsg, half=half: wload(
                wsg.full(), wrows(d["w_in"][l], gcol + half * 512, 512)))
            it = 0
            for g in groups:
                n = g["n"]
                for j in range(4):
                    oc = half * 4 + j
                    gb = proj_fm(wsg, j * 128, 128, (g["h"], n))
                    sg = SG[it % 2]
                    k.act(sg[:, 0:n], gb[:, 0:n], AF.Sigmoid)
                    yb = k.bank()
                    for kk in range(4):
                        k.mm(yb[:, 0:n], wsp[:, kk, j * 128:(j + 1) * 128], g["o"](kk), start=(kk == 0), stop=(kk == 3))
                    mgv = g["mg"](oc)
                    if first:
                        k.tt(mgv, yb[:, 0:n], sg[:, 0:n], ALU.mult)
                    else:
                        tm = TM[it % 2]
                        k.tt(tm[:, 0:n], yb[:, 0:n], sg[:, 0:n], ALU.mult)
                        k.tt(mgv, mgv, tm[:, 0:n], ALU.add)
                    it += 1
        A.release(m)

    ctx = dict(nc=nc, k=k, A=A, d=d, o=o, X=X, H=H, MG=MG, MGQ=MGQ, HQ=HQ, XQI=XQI, XQA=XQA, C=C, wslot=wslot, wload=wload, wrows=wrows,
               wtile=wtile, wc_valid=wc_valid, proj_fm=proj_fm, proj_tm=proj_tm, normhead=normhead, dense_attn=dense_attn, merge=merge,
               GV=GV, ESK=ESK, ALR=ALR, WA2=WA2, MODVL=MODVL, SCLL=SCLL, cur=cur, dbg_out=dbg_out, depth=depth,
               do_sample=do_sample, stop=stop, mod_steps=mod_steps, norm_group=norm_group, layer_small=layer_small)
    build_layers(ctx)
    if os.environ.get('NO_SCHED') != '1':
        k.P.schedule()
    k.P.emit()
    return nc


def build_layers(c):
    k, A, d, o, X, H, MG, C = c["k"], c["A"], c["d"], c["o"], c["X"], c["H"], c["MG"], c["C"]
    MGQ, HQ, XQI, XQA = c["MGQ"], c["HQ"], c["XQI"], c["XQA"]

    def grp_P(O):
        return dict(h=lambda kc: H[:, kc, 0:512], o=lambda kk: O[:, kk, 0:512], mg=lambda oc: MG[:, oc, 0:512], n=512)

    def merge_quarter(l, wp, gcol, O, first):
        mq = A.mark()
        OQ = A.alloc([128, 4, 256], BF16)
        k.dma_dyn(OQ.full(), O, "crel", 256, O.full())
        merge(l, wp, gcol, [dict(h=lambda kc: HQ[:, kc, :], o=lambda kk: OQ[:, kk, :], mg=lambda oc: MGQ[:, oc, :], n=256)], first)
        A.release(mq)
    wslot, wload, wrows = c["wslot"], c["wload"], c["wrows"]
    wtile, wc_valid = c["wtile"], c["wc_valid"]
    proj_fm, proj_tm, normhead, dense_attn, merge = c["proj_fm"], c["proj_tm"], c["normhead"], c["dense_attn"], c["merge"]
    GV, ESK, ALR, WA2, MODVL, cur = c["GV"], c["ESK"], c["ALR"], c["WA2"], c["MODVL"], c["cur"]
    depth = c["depth"]
    dbg_out = c["dbg_out"]

    def phaseA(l, t0, nchunk, seqs, is_prompt):
        k.P.tag = "A.%s" % ("P" if is_prompt else "S")
        ngrp = nchunk // 4
        n = nchunk * 128
        m = A.mark()
        QP = A.alloc([128, 4, n], BF16)
        KP = A.alloc([128, 4, n], BF16)
        VT = A.alloc([128, nchunk, 512], BF16)
        SIN = A.alloc([128, nchunk, 2, 2, 128], BF16)
        MB = A.alloc([128, nchunk, 2, 128], F32)
        ALAST = A.alloc([128, 4, nchunk], F32, align=64)
        R = A.alloc([128, 2, 2, 128], F32)
        w1 = wslot()
        wtile("a1", w1.full(), lambda: wload(w1.full(), wrows(d["w_in"][l], O_AQ, 512)))
        w4 = wslot()
        wload(w4[:, :, 0:16], wrows(d["w_in"][l], O_ALF, 16))
        wload(w4[:, :, 32:48], wrows(d["w_in"][l], O_ALB, 16))
        w2 = wslot()
        wtile("a2", w2.full(), lambda: wload(w2.full(), wrows(d["w_in"][l], O_AV, 512)))
        while pending:
            pending.pop(0)()

        def seq_of(cidx):
            for si, (a, b) in enumerate(seqs):
                if a <= cidx < b:
                    return si, a, b
            raise AssertionError

        def init_state(si, dr):
            if is_prompt:
                k.memset(R[:, dr, :, :], 0.0)
            else:
                src = d["st_gla"][l, dr].rearrange("h k v -> (h k) v").rearrange("(pr q) v -> q pr v", q=128)
                k.dma("sp", R[:, dr, :, :], DR(src))

        def store_state(si, dr, a_c):
            if not is_prompt:
                return
            mm_ = A.mark()
            sf = A.alloc([128, 2, 128], F32)
            for pr in range(2):
                k.ts(sf[:, pr, :], R[:, dr, pr, :], ALAST[:, dr * 2 + pr, a_c:a_c + 1], ALU.mult)
            dst = o["o_gla"][si, l, dr].rearrange("h k v -> (h k) v").rearrange("(pr q) v -> q pr v", q=128)
            k.dma("sp", DR(dst), sf.full())
            A.release(mm_)

        def scan_step(cidx, dr, msrc):
            si, a, b = seq_of(cidx)
            first = (cidx == a) if dr == 0 else (cidx == b - 1)
            prev = cidx - 1 if dr == 0 else cidx + 1
            if first:
                init_state(si, dr)
            for pr in range(2):
                for hf in range(2):
                    rows = slice(hf * 64, hf * 64 + 64)
                    if first:
                        k.cp(SIN[rows, cidx, dr, pr, :], R[rows, dr, pr, :])
                        k.tt(R[rows, dr, pr, :], R[rows, dr, pr, :], msrc(pr, hf), ALU.add)
                    else:
                        al = ALAST[rows, dr * 2 + pr, prev:prev + 1]
                        k.ts(SIN[rows, cidx, dr, pr, :], R[rows, dr, pr, :], al, ALU.mult)
                        k.stt(R[rows, dr, pr, :], R[rows, dr, pr, :], al, msrc(pr, hf), ALU.mult, ALU.add)
            last = (cidx == b - 1) if dr == 0 else (cidx == a)
            if last:
                store_state(si, dr, cidx)

        m1 = A.mark()
        QK32 = A.alloc([128, 4, 512], F32)
        EP = A.alloc([128, 4, 256], F32)
        EN = A.alloc([128, 4, 256], F32)
        E1 = A.alloc([128, 512], F32)
        LA = A.alloc([128, 512], F32)
        LAH = A.alloc([128, 512], BF16)
        LAL = A.alloc([128, 512], BF16)
        KT = [A.alloc([128, 4, 128], BF16) for _ in range(2)]
        for gi in range(ngrp):
            htok = slice(t0 + gi * 512, t0 + gi * 512 + 512)
            pb = k.bank()
            for kc in range(8):
                k.mm(pb[0:64, :], w4[:, kc, 0:64], H[:, kc, htok], start=(kc == 0), stop=(kc == 7))
            k.cp(ALR[0:16, gi * 512:gi * 512 + 512], pb[0:16, :], "act")
            k.cp(ALR[32:48, gi * 512:gi * 512 + 512], pb[32:48, :], "dve")
            for j in range(4):
                pq = proj_fm(w1, j * 128, 128, htok)
                k.cpx(QK32[:, j, :], pq[:, :])
            if PA_STOP <= 1:
                continue
            for hf2 in range(2):
                cb = [k.hold(), k.hold()]
                for c2 in range(2):
                    cl = gi * 4 + hf2 * 2 + c2
                    tk = cl * 128
                    zb = k.bank()
                    k.mm(zb[:, :], ALR[0:64, tk:tk + 128], WA2.full())
                    if PA_STOP <= 1.2:
                        k.cp(E1.full(), zb[:, :])
                        continue
                    k.act(E1.full(), zb[:, :], AF.Exp, scale=-1.0)
                    k.act(LA.full(), E1.full(), AF.Ln, bias=1.0)
                    if PA_STOP <= 1.4:
                        continue
                    k.cp(LAH.full(), LA.full(), "act")
                    k.tt(LAL.full(), LA.full(), LAH.full(), ALU.subtract)
                    for dr in range(2):
                        tri = C["trif"] if dr == 0 else C["trib"]
                        for cc in range(2):
                            osl = cb[dr][:, cc * 256 + c2 * 128:cc * 256 + c2 * 128 + 128]
                            k.mm(osl, LAH[:, dr * 256 + cc * 128:dr * 256 + cc * 128 + 128], tri.full(), True, False)
                            k.mm(osl, LAL[:, dr * 256 + cc * 128:dr * 256 + cc * 128 + 128], tri.full(), False, True)
                if PA_STOP <= 1.6:
                    k.unhold(cb[0])
                    k.unhold(cb[1])
                    continue
                for dr in range(2):
                    cv = cb[dr].v(cb[dr].ap.rearrange("p (a b) -> p a b", a=2))
                    k.act(EP[:, dr * 2:dr * 2 + 2, :], cv, AF.Exp)
                    k.act(EN[:, dr * 2:dr * 2 + 2, :], cv, AF.Exp, scale=-1.0)
                k.unhold(cb[0])
                k.unhold(cb[1])
                if PA_STOP <= 2:
                    continue
                cbase = gi * 4 + hf2 * 2
                k.cp(ALAST[:, 0:2, cbase:cbase + 2], EP[:, 0:2, 127:256:128])
                k.cp(ALAST[:, 2:4, cbase:cbase + 2], EP[:, 2:4, 0:256:128])
                tcol = slice(gi * 512 + hf2 * 256, gi * 512 + hf2 * 256 + 256)
                hcol = slice(hf2 * 256, hf2 * 256 + 256)
                for dc in range(4):
                    cc = dc % 2
                    k.stt(QP[:, dc, tcol], QK32[:, cc, hcol], 0.125, EP[:, dc, :], ALU.mult, ALU.mult)
                    k.tt(KP[:, dc, tcol], QK32[:, 2 + cc, hcol], EN[:, dc, :], ALU.mult)
                if PA_STOP <= 3:
                    continue
                for c2 in range(2):
                    cl = gi * 4 + hf2 * 2 + c2
                    ctok = slice(cl * 128, cl * 128 + 128)
                    tb = k.bank()
                    tbv = tb.bf()
                    for dc in range(4):
                        k.tr(tb.v(tbv[:, dc * 128:(dc + 1) * 128]), KP[:, dc, ctok], C["ident_bf"].full())
                    kt = KT[cl % 2]
                    k.cp(kt.full(), tb.v(tbv[:, 0:512].rearrange("p (a b) -> p a b", a=4)), "act")
                    vb = proj_tm(w2, 0, 512, t0 + cl * 128)
                    k.cp(VT[:, cl, :], vb[:, :], "dve")
                    for dr in range(2):
                        mb = k.bank()
                        for pr in range(2):
                            k.mm(mb[:, pr * 256:(pr + 1) * 256], kt[:, dr * 2 + pr, :], VT[:, cl, pr * 256:(pr + 1) * 256])
                        mv = mb.ap.rearrange("p (a b) -> p a b", a=2)
                        if dr == 0:
                            scan_step(cl, 0, lambda pr, hf, mv=mv, mb=mb: mb.v(
                                mv[hf * 64:hf * 64 + 64, pr, hf * 128:hf * 128 + 128]))
                        else:
                            for hf in range(2):
                                k.cp(MB[hf * 64:hf * 64 + 64, cl, :, :],
                                     mb.v(mv[hf * 64:hf * 64 + 64, :, hf * 128:hf * 128 + 128]), "act")
        A.release(m1)
        if PA_STOP <= 4:
            A.release(m)
            return
        for cl in reversed(range(nchunk)):
            scan_step(cl, 1, lambda pr, hf, cl=cl: MB[hf * 64:hf * 64 + 64, cl, pr, :])
        if PA_STOP <= 5:
            A.release(m)
            return
        OA = A.alloc([128, 4, n], BF16)
        m3 = A.mark()
        w3 = wslot()
        wtile("a3", w3.full(), lambda: wload(w3.full(), wrows(d["w_in"][l], O_AR, 512)))
        SIL = A.alloc([128, 4, 512], BF16)
        ATM = [A.alloc([128, 4, 128], BF16) for _ in range(2)]
        OFs = [A.alloc([128, 512], F32) for _ in range(2)]
        SQs = [A.alloc([128, 512], BF16) for _ in range(2)]
        RSs = [A.alloc([128, 512], F32) for _ in range(2)]
        for gi in range(ngrp):
            htok = slice(t0 + gi * 512, t0 + gi * 512 + 512)
            for hd in range(4):
                pb = proj_fm(w3, hd * 128, 128, htok)
                k.act(SIL[:, hd, :], pb[:, :], AF.Silu)
            ob = [k.hold() for _ in range(4)]
            for c4 in range(4):
                cl = gi * 4 + c4
                ctok = slice(cl * 128, cl * 128 + 128)
                for dr in range(2):
                    mk = C["maskf"] if dr == 0 else C["maskb"]
                    mkf = mk.full()
                    for hf in range(2):
                        ab = k.bank()
                        rows = slice(hf * 64, hf * 64 + 64)
                        for pr in range(2):
                            k.mm(ab[:, pr * 128:(pr + 1) * 128], KP[rows, dr * 2 + pr, ctok], QP[rows, dr * 2 + pr, ctok])
                        k.tt(ATM[dr][:, hf:4:2, :], ab.v(ab.ap[:, 0:256].rearrange("p (a b) -> p a b", a=2)),
                             V(mkf.ap.unsqueeze(1).to_broadcast([128, 2, 128]), mkf.bufs), ALU.mult)
                for hd in range(4):
                    rows = slice((hd % 2) * 64, (hd % 2) * 64 + 64)
                    col = slice(c4 * 128, c4 * 128 + 128)
                    k.mm(ob[hd][:, col], VT[:, cl, hd * 128:(hd + 1) * 128], ATM[0][:, hd, :], True, False)
                    k.mm(ob[hd][:, col], VT[:, cl, hd * 128:(hd + 1) * 128], ATM[1][:, hd, :], False, False)
                    k.mm(ob[hd][:, col], SIN[rows, cl, 0, hd // 2, :], QP[rows, hd // 2, ctok], False, False)
                    k.mm(ob[hd][:, col], SIN[rows, cl, 1, hd // 2, :], QP[rows, 2 + hd // 2, ctok], False, True)
            gtok = slice(gi * 512, gi * 512 + 512)
            for hd in range(4):
                OF, SQ, RS = OFs[hd % 2], SQs[hd % 2], RSs[hd % 2]
                k.cp(OF.full(), ob[hd][:, :], "dve")
                k.act(SQ.full(), ob[hd][:, :], AF.Square)
                k.unhold(ob[hd])
                sb = k.bank()
                k.mm(sb[:, :], C["ones_all"].full(), SQ.full())
                k.rstd(RS.full(), sb[:, :], 128.0)
                k.tt(OF.full(), OF.full(), RS.full(), ALU.mult)
                k.stt(OA[:, hd, gtok], OF.full(), GV[:, 4:5], SIL[:, hd, :], ALU.mult, ALU.mult)
        A.release(m3)
        if PA_STOP <= 6:
            A.release(m)
            return
        if is_prompt:
            merge(l, d["w_pa"][l], O_GA, [grp_P(OA)], True)
        else:
            merge_quarter(l, d["w_pa"][l], O_GA, OA, True)
        A.release(m)

    def phaseB_P(l):
        k.P.tag = "B.P"
        tok = slice(0, 512)
        m = A.mark()
        QN = A.alloc([128, 8, 512], BF16)
        k.memset(QN.full(), 0.0)
        KN = A.alloc([128, 2, 512], BF16)
        KNF = A.alloc([128, 2, 512], F32)
        VA = A.alloc([128, 4, 2, 128], BF16)
        OB = A.alloc([128, 4, 512], BF16)
        KO = [A.alloc([128, 128], F32) for _ in range(2)]
        w1 = wslot()
        wtile("b1", w1.full(), lambda: wload(w1.full(), wrows(d["w_in"][l], O_BQ, 512)))
        w2 = wslot()

        def ld_b2():
            for kv in range(2):
                for dup in range(2):
                    wload(w2[:, :, kv * 128 + dup * 64:kv * 128 + dup * 64 + 64], wrows(d["w_in"][l], O_BK + kv * 64, 64))
            wload(w2[:, :, 256:384], wrows(d["w_in"][l], O_BV, 128))
        wtile("b2", w2[:, :, 0:384], ld_b2)
        for cc in range(4):
            pb = proj_fm(w1, cc * 128, 128, tok)
            normhead(pb, None, None, GV[:, 0:1], out_pair=(QN[0:64, 2 * cc, :], QN[64:128, 2 * cc + 1, :]))
        for kv in range(2):
            pb = proj_fm(w2, kv * 128, 128, tok)
            normhead(pb, KNF[:, kv, :], KN[:, kv, :], GV[:, 1:2])
        k.memset(VA[:, :, :, 64:128], 1.0)
        for tix in range(4):
            ttok = slice(tix * 128, tix * 128 + 128)
            pb = k.bank()
            for kv in range(2):
                k.tr(pb[:, kv * 128:(kv + 1) * 128], KNF[:, kv, ttok], C["ident_f"].full())
            ko = KO[0]
            kov = ko.full()
            k.cp(V(kov.ap.rearrange("p (a b) -> p a b", a=2), kov.bufs),
                 pb.v(pb.ap[:, 0:256].rearrange("p (a b) -> p a b", a=2)[:, :, 0:64]), "act")
            k.dma("sp", DR(o["o_swk"][tix // 2, l, (tix % 2) * 128:(tix % 2) * 128 + 128, :]), kov)
            pv = proj_tm(w2, 256, 128, tix * 128)
            vo = KO[1]
            k.cp(vo.full(), pv[:, 0:128], "dve")
            k.dma("sp", DR(o["o_swv"][tix // 2, l, (tix % 2) * 128:(tix % 2) * 128 + 128, :]), vo.full())
            k.cp(VA[:, tix, :, 0:64], pv.v(pv.ap[:, 0:128].rearrange("p (a b) -> p a b", a=2)), "act")
        dense_attn(QN, KN, lambda h: h // 4, VA, lambda h: h // 4, OB, True)
        merge(l, d["w_pb"][l], O_GB, [grp_P(OB)], False)
        A.release(m)

    def phaseC_P(l):
        k.P.tag = "C.P"
        tok = slice(0, 512)
        m = A.mark()
        QN = A.alloc([128, 8, 512], BF16)
        k.memset(QN.full(), 0.0)
        KN = A.alloc([128, 4, 512], BF16)
        KNF = A.alloc([128, 4, 512], F32)
        VA = A.alloc([128, 4, 8, 128], BF16)
        OC = A.alloc([128, 4, 512], BF16)
        KO = [A.alloc([128, 512], F32) for _ in range(2)]
        wqk = []
        wv = []
        for hh in range(2):
            wa = wslot()
            wtile(("c1", hh), wa.full(), lambda wa=wa, hh=hh: (
                wload(wa[:, :, 0:256], wrows(d["w_in"][l], O_CQ + hh * 256, 256)),
                wload(wa[:, :, 256:512], wrows(d["w_in"][l], O_CK + hh * 256, 256))))
            wqk.append(wa)
            for c2 in range(2):
                cc = hh * 2 + c2
                pb = proj_fm(wa, c2 * 128, 128, tok)
                normhead(pb, None, None, GV[:, 2:3], out_pair=(QN[0:64, 2 * cc, :], QN[64:128, 2 * cc + 1, :]))
                pb = proj_fm(wa, 256 + c2 * 128, 128, tok)
                normhead(pb, KNF[:, cc, :], KN[:, cc, :], GV[:, 3:4])
        for hh in range(2):
            wb = wslot()
            wtile(("c3", hh), wb[:, :, 0:256], lambda wb=wb, hh=hh: wload(
                wb[:, :, 0:256], wrows(d["w_in"][l], O_CV + hh * 256, 256)))
            wv.append(wb)
        k.memset(VA[:, :, :, 64:128], 1.0)
        for tix in range(4):
            ttok = slice(tix * 128, tix * 128 + 128)
            pb = k.bank()
            for cc in range(4):
                k.tr(pb[:, cc * 128:(cc + 1) * 128], KNF[:, cc, ttok], C["ident_f"].full())
            k.cp(KO[0].full(), pb[:, :], "act")
            k.dma("sp", DR(o["o_nak"][tix // 2, l, (tix % 2) * 128:(tix % 2) * 128 + 128, :]), KO[0].full())
            pv = k.bank()
            for hh in range(2):
                for kc in range(8):
                    k.mm(pv[:, hh * 256:(hh + 1) * 256], H[:, kc, tix * 128:tix * 128 + 128], wv[hh][:, kc, 0:256],
                         start=(kc == 0), stop=(kc == 7))
            k.cp(KO[1].full(), pv[:, :], "dve")
            k.dma("sp", DR(o["o_nav"][tix // 2, l, (tix % 2) * 128:(tix % 2) * 128 + 128, :]), KO[1].full())
            k.cp(VA[:, tix, :, 0:64], pv.v(pv.ap.rearrange("p (a b) -> p a b", a=8)), "act")
        dense_attn(QN, KN, lambda h: h // 2, VA, lambda h: h, OC, None)
        merge(l, d["w_pc"][l], O_GC, [grp_P(OC)], False)
        A.release(m)

    rp_flip = [0]

    def rope_apply(qf, qfb, out_bf, tsl):
        mr = A.mark()
        rp_flip[0] ^= 1
        if rp_flip[0]:
            A.alloc([128, 1024], F32)
        t1 = A.alloc([128, 512], F32)
        t2 = A.alloc([128, 512], F32)
        pr = k.bank()
        k.mm(pr[:, :], C["rope_pT"].full(), qfb)
        k.tt(t1.full(), qf, C["rope_cos"][:, tsl], ALU.mult)
        k.tt(t2.full(), pr[:, :], C["rope_sin"][:, tsl], ALU.mult)
        if isinstance(out_bf, tuple):
            k.tt(out_bf[0], t1[0:64, :], t2[0:64, :], ALU.add, eng="pool")
            k.tt(out_bf[1], t1[64:128, :], t2[64:128, :], ALU.add, eng="pool")
        else:
            k.tt(out_bf, t1.full(), t2.full(), ALU.add, eng="pool")
        A.release(mr)

    def phaseB_S(l):
        k.P.tag = "B.S"
        m = A.mark()
        QN = A.alloc([128, 8, TS], BF16)
        k.memset(QN.full(), 0.0, eng="pool")
        KN = A.alloc([128, 2, TS], BF16)
        VA = A.alloc([128, 8, 2, 128], BF16)
        KC = A.alloc([128, 2, 512], BF16)
        VC = A.alloc([128, 4, 2, 128], BF16)
        OB = A.alloc([128, 4, TS], BF16)
        w1 = wslot()
        wtile("b1", w1.full(), lambda: wload(w1.full(), wrows(d["w_in"][l], O_BQ, 512)))
        w2 = wslot()

        def ld_b2():
            for kv in range(2):
                for dup in range(2):
                    wload(w2[:, :, kv * 128 + dup * 64:kv * 128 + dup * 64 + 64], wrows(d["w_in"][l], O_BK + kv * 64, 64))
            wload(w2[:, :, 256:384], wrows(d["w_in"][l], O_BV, 128))
        wtile("b2", w2[:, :, 0:384], ld_b2)
        m2 = A.mark()
        QFs = [A.alloc([128, 512], F32) for _ in range(2)]
        QFBs = [A.alloc([128, 512], BF16) for _ in range(2)]
        qi = 0
        for g in range(2):
            htok = slice(TP + g * 512, TP + g * 512 + 512)
            tsl = slice(g * 512, g * 512 + 512)
            for cc in range(4):
                QF, QFB = QFs[qi % 2], QFBs[qi % 2]
                qi += 1
                pb = proj_fm(w1, cc * 128, 128, htok)
                normhead(pb, QF.full(), QFB.full(), GV[:, 0:1])
                rope_apply(QF.full(), QFB.full(), (QN[0:64, 2 * cc, tsl], QN[64:128, 2 * cc + 1, tsl]), tsl)
            for kv in range(2):
                QF, QFB = QFs[qi % 2], QFBs[qi % 2]
                qi += 1
                pb = proj_fm(w2, kv * 128, 128, htok)
                normhead(pb, QF.full(), QFB.full(), GV[:, 1:2])
                rope_apply(QF.full(), QFB.full(), KN[:, kv, tsl], tsl)
        A.release(m2)
        k.memset(VA[:, :, :, 64:128], 1.0, eng="pool")
        k.memset(VC[:, :, :, 64:128], 1.0, eng="pool")
        for tix in range(8):
            pv = proj_tm(w2, 256, 128, TP + tix * 128)
            k.cpx(VA[:, tix, :, 0:64], pv.v(pv.ap[:, 0:128].rearrange("p (a b) -> p a b", a=2)))
        m2 = A.mark()
        ST = [A.alloc([128, 256], F32) for _ in range(2)]
        SV = [A.alloc([128, 128], F32) for _ in range(2)]
        for i in range(4):
            st = ST[i % 2]
            src = d["cswk"][l, i * 128:(i + 1) * 128, :].rearrange("p (a b) -> p a b", a=2)
            stv = st.full()
            for dup in range(2):
                k.dma("sp", V(stv.ap.rearrange("p (a c b) -> p a c b", a=2, c=2)[:, :, dup, :], stv.bufs), DR(src))
            pb = k.bank()
            for kv in range(2):
                k.tr(pb[:, kv * 128:(kv + 1) * 128], st[:, kv * 128:(kv + 1) * 128], C["ident_f"].full())
            k.cpx(KC[:, :, i * 128:(i + 1) * 128], pb.v(pb.ap[:, 0:256].rearrange("p (a b) -> p a b", a=2)))
            sv = SV[i % 2]
            k.dma("sp", sv.full(), DR(d["cswv"][l, i * 128:(i + 1) * 128, :]))
            svv = sv.full()
            k.cpx(VC[:, i, :, 0:64], V(svv.ap.rearrange("p (a b) -> p a b", a=2), svv.bufs))
        A.release(m2)
        m2 = A.mark()
        PT = [A.alloc([128, 512], BF16) for _ in range(3)]
        RC = [A.alloc([64, 512], F32) for _ in range(2)]
        it = 0
        for h in range(8):
            rows = slice((h % 2) * 64, (h % 2) * 64 + 64)
            kv = h // 4
            ob = [k.hold(), k.hold()]
            for qh in range(2):
                qsl = slice(qh * 512, qh * 512 + 512)
                for i in range(4):
                    sa = k.bank()
                    k.mm(sa[:, :], KC[:, kv, i * 128:(i + 1) * 128], QN[:, h, qsl])
                    pt = PT[it % 3]
                    it += 1
                    k.act(pt.full(), sa[:, :], AF.Exp)
                    k.mm(ob[qh][:, :], VC[:, i, kv, :], pt.full(), i == 0, False)
            for kt in range(8):
                qb0, qb1 = max(0, kt - 1), min(7, kt + 1)
                nq = (qb1 - qb0 + 1) * 128
                q0 = qb0 * 128
                sbk = k.bank()
                k.mm(sbk[:, 0:nq], KN[:, kv, kt * 128:(kt + 1) * 128], QN[:, h, q0:q0 + nq], True, False)
                for qb in range(qb0, qb1 + 1):
                    if qb == kt:
                        continue
                    msk = C["band_next"] if qb == kt + 1 else C["band_prev"]
                    k.mm(sbk[:, (qb - qb0) * 128:(qb - qb0 + 1) * 128], C["ident_bf"].full(), msk.full(), False, False)
                pt = PT[it % 3]
                it += 1
                k.act(pt[:, 0:nq], sbk[:, 0:nq], AF.Exp)
                segs = []
                a_ = q0
                while a_ < q0 + nq:
                    b_ = min(q0 + nq, (a_ // 512 + 1) * 512)
                    segs.append((a_, b_))
                    a_ = b_
                for (a_, b_) in segs:
                    qh = a_ // 512
                    k.mm(ob[qh][:, a_ - qh * 512:b_ - qh * 512], VA[:, kt, kv, :], pt[:, a_ - q0:b_ - q0], False, False)
            for qh in range(2):
                rc = RC[qh]
                k.act(rc.full(), ob[qh][64:128, :], AF.Ln, bias=ESK[0:64, h:h + 1])
                k.act(rc.full(), rc.full(), AF.Exp, scale=-1.0)
                k.tt(OB[rows, h // 2, qh * 512:(qh + 1) * 512], ob[qh][0:64, :], rc.full(), ALU.mult)
                k.unhold(ob[qh])
        A.release(m2)
        merge_quarter(l, d["w_pb"][l], O_GB, OB, False)
        A.release(m)

    def na_runs(mt):
        runs = []
        if mt <= 3:
            runs.append((0, 4, False))
        lo, hi = max(5, 2 * mt - 3), min(11, 2 * mt + 5)
        if lo <= hi:
            if lo <= 7 and hi >= 8:
                runs.append((lo, 7, True))
                runs.append((8, hi, True))
            else:
                runs.append((lo, hi, True))
        if mt >= 4:
            runs.append((12, 15, False))
        return runs

    def phaseC_S(l):
        k.P.tag = "C.S"
        m = A.mark()
        OC = A.alloc([128, 4, TS], BF16)
        for hh in range(2):
            mh = A.mark()
            QN = A.alloc([128, 4, TS], BF16)
            k.memset(QN.full(), 0.0, eng="pool")
            KN = A.alloc([128, 2, TS], BF16)
            VA = A.alloc([128, 8, 4, 128], BF16)
            KC = A.alloc([128, 2, 512], BF16)
            VC = A.alloc([128, 4, 4, 128], BF16)
            w1 = wslot()
            wtile(("c1", hh), w1.full(), lambda w1=w1, hh=hh: (
                wload(w1[:, :, 0:256], wrows(d["w_in"][l], O_CQ + hh * 256, 256)),
                wload(w1[:, :, 256:512], wrows(d["w_in"][l], O_CK + hh * 256, 256))))
            w3 = wslot()
            wtile(("c3", hh), w3[:, :, 0:256], lambda w3=w3, hh=hh: wload(
                w3[:, :, 0:256], wrows(d["w_in"][l], O_CV + hh * 256, 256)))
            for g in range(2):
                htok = slice(TP + g * 512, TP + g * 512 + 512)
                tsl = slice(g * 512, g * 512 + 512)
                for cc in range(2):
                    pb = proj_fm(w1, cc * 128, 128, htok)
                    normhead(pb, None, None, GV[:, 2:3], out_pair=(QN[0:64, 2 * cc, tsl], QN[64:128, 2 * cc + 1, tsl]))
                    pb = proj_fm(w1, 256 + cc * 128, 128, htok)
                    normhead(pb, None, KN[:, cc, tsl], GV[:, 3:4])
            k.memset(VA[:, :, :, 64:128], 1.0, eng="pool")
            k.memset(VC[:, :, :, 64:128], 1.0, eng="pool")
            for tix in range(8):
                pv = proj_tm(w3, 0, 256, TP + tix * 128)
                k.cpx(VA[:, tix, :, 0:64], pv.v(pv.ap[:, 0:256].rearrange("p (a b) -> p a b", a=4)))
            m2 = A.mark()
            ST = [A.alloc([128, 256], F32) for _ in range(2)]
            SV = [A.alloc([128, 256], F32) for _ in range(2)]
            for i in range(4):
                st = ST[i % 2]
                k.dma("sp", st.full(), DR(d["cnak"][l, i * 128:(i + 1) * 128, hh * 256:hh * 256 + 256]))
                pb = k.bank()
                for cc in range(2):
                    k.tr(pb[:, cc * 128:(cc + 1) * 128], st[:, cc * 128:(cc + 1) * 128], C["ident_f"].full())
                k.cpx(KC[:, :, i * 128:(i + 1) * 128], pb.v(pb.ap[:, 0:256].rearrange("p (a b) -> p a b", a=2)))
                sv = SV[i % 2]
                k.dma("sp", sv.full(), DR(d["cnav"][l, i * 128:(i + 1) * 128, hh * 256:hh * 256 + 256]))
                svv = sv.full()
                k.cpx(VC[:, i, :, 0:64], V(svv.ap.rearrange("p (a b) -> p a b", a=4), svv.bufs))
            A.release(m2)
            m2 = A.mark()
            STG = A.alloc([128, 16, 64], F32)
            SFUL = A.alloc([128, 16, 64], BF16)
            SINT = A.alloc([128, 16, 64], BF16)
            PT = [A.alloc([128, 512], BF16) for _ in range(2)]
            RC = A.alloc([64, 512], F32)
            it = 0
            for hl in range(4):
                h = hh * 4 + hl
                rows = slice((hl % 2) * 64, (hl % 2) * 64 + 64)
                cc = hl // 2
                base = d["rpb_pad"][l, h]
                hk = bass.AP(base.tensor, base.offset, [[1, 64], [127, 15], [1, 64]])
                k.memset(STG.full(), 0.0)
                k.dma("sp", STG[0:64, 0:15, :], DR(hk))
                k.dma("sp", STG[64:128, 1:16, :], DR(hk))
                nmk = C["na_mask"].full()
                k.tt(SFUL.full(), STG.full(), V(nmk.ap.unsqueeze(1).to_broadcast([128, 16, 64]), nmk.bufs), ALU.add)
                k.cp(SINT.full(), SFUL.full(), "act")
                k.memset(SINT[0:64, 0:4, :], NEG, eng="pool")
                k.memset(SINT[0:64, 12:16, :], NEG, eng="pool")
                k.memset(SINT[64:128, 0:5, :], NEG, eng="pool")
                k.memset(SINT[64:128, 13:16, :], NEG, eng="pool")
                ob = [k.hold(), k.hold()]
                for qh in range(2):
                    qsl = slice(qh * 512, qh * 512 + 512)
                    for i in range(4):
                        sb = k.bank()
                        k.mm(sb[:, :], KC[:, cc, i * 128:(i + 1) * 128], QN[:, hl, qsl])
                        pt = PT[it % 2]
                        it += 1
                        k.act(pt.full(), sb[:, :], AF.Exp)
                        k.mm(ob[qh][:, :], VC[:, i, hl, :], pt.full(), i == 0, False)
                for mt in range(8):
                    for (r0, r1, interior) in na_runs(mt):
                        nq = (r1 - r0 + 1) * 64
                        q0 = r0 * 64
                        qh = q0 // 512
                        b0 = 7 + r0 - 2 * mt
                        strip = SINT if interior else SFUL
                        sb = k.bank()
                        k.mm(sb[:, 0:nq], KN[:, cc, mt * 128:(mt + 1) * 128], QN[:, hl, q0:q0 + nq], True, False)
                        sv_ = strip[:, b0:b0 + (r1 - r0 + 1), :]
                        k.mm(sb[:, 0:nq], C["jj"].full(), V(sv_.ap.rearrange("p a b -> p (a b)"), sv_.bufs), False, True)
                        pt = PT[it % 2]
                        it += 1
                        k.act(pt[:, 0:nq], sb[:, 0:nq], AF.Exp)
                        k.mm(ob[qh][:, q0 - qh * 512:q0 - qh * 512 + nq], VA[:, mt, hl, :], pt[:, 0:nq], False, False)
                for qh in range(2):
                    k.recip_act(RC.full(), ob[qh][64:128, :])
                    k.tt(OC[rows, h // 2, qh * 512:(qh + 1) * 512], ob[qh][0:64, :], RC.full(), ALU.mult)
                    k.unhold(ob[qh])
            A.release(m2)
            A.release(mh)
        merge_quarter(l, d["w_pc"][l], O_GC, OC, False)
        A.release(m)

    def wo_groups(l, groups):
        k.P.tag = "wo"
        for half in range(2):
            ws = wslot()
            wtile(("wo", half), ws.full(), lambda ws=ws, half=half: wload(ws.full(), wrows(d["w_o"][l], half * 512, 512)))
            for g in groups:
                n = g["n"]
                for j in range(4):
                    oc = half * 4 + j
                    pb = k.bank()
                    for kc in range(8):
                        k.mm(pb[:, 0:n], ws[:, kc, j * 128:(j + 1) * 128], g["mg"](kc), start=(kc == 0), stop=(kc == 7))
                    xv = g["x"](oc)
                    k.stt(xv, pb[:, 0:n], MODVL[cur[0]][:, 16 + oc, g["ci"]:g["ci"] + 1], xv, ALU.mult, ALU.add)

    def norm_q(l, XQ, HQ2):
        mn = A.mark()
        sq = [A.alloc([128, 256], BF16) for _ in range(2)]
        rs = A.alloc([128, 256], F32)
        tmp = [A.alloc([128, 256], F32) for _ in range(2)]
        sb = k.bank()
        for oc in range(8):
            s_ = sq[oc % 2]
            k.act(s_.full(), XQ[:, oc, :], AF.Square)
            k.mm(sb[:, 0:256], C["ones_all"].full(), s_.full(), start=(oc == 0), stop=(oc == 7))
        k.rstd(rs.full(), sb[:, 0:256], 1024.0)
        for oc in range(8):
            t_ = tmp[oc % 2]
            k.stt(t_.full(), XQ[:, oc, :], c["SCLL"][cur[0]][:, 1, 1, oc:oc + 1], rs.full(), ALU.mult, ALU.mult)
            k.act(HQ2[:, oc, :], t_.full(), AF.Identity, bias=MODVL[cur[0]][:, 24 + oc, 1:2])
        A.release(mn)

    pending = []

    def tail(l, extra):
        m = A.mark()
        XQ = A.alloc([128, 8, 256], F32)
        HQ2 = A.alloc([128, 8, 256], BF16)
        k.dma_dyn(XQ.full(), X, "cabs", 256, X[:, :, TP:TT])
        wo_groups(l, [dict(mg=lambda kc: MGQ[:, kc, :], x=lambda oc: XQ[:, oc, :], n=256, ci=1)])
        k.P.tag = "mlp"
        c["norm_group"](l, 1, 0)
        norm_q(l, XQ, HQ2)
        G = [dict(h=lambda kc: H[:, kc, 0:512], u=slice(0, 512), x=lambda oc: X[:, oc, 0:512], n=512, ci=0),
             dict(h=lambda kc: HQ2[:, kc, :], u=slice(512, 768), x=lambda oc: XQ[:, oc, :], n=256, ci=1)]
        U = A.alloc([128, 16, 768], BF16)
        RL = [A.alloc([128, 512], BF16) for _ in range(2)]
        ri = 0
        for hh in range(2):
            for t4 in range(4):
                ws = wslot()
                wload(ws.full(), wrows(d["w_fc1"][l], hh * 2048 + t4 * 512, 512))
                for g in G:
                    n = g["n"]
                    for j in range(4):
                        pb = proj_fm(ws, j * 128, 128, (g["h"], n))
                        rl = RL[ri % 2]
                        ri += 1
                        uv = U[:, t4 * 4 + j, g["u"]]
                        if j % 2 == 0:
                            k.act(rl[:, 0:n], pb[:, 0:n], AF.Relu)
                            k.tt(uv, rl[:, 0:n], rl[:, 0:n], ALU.mult)
                        else:
                            k.ts(rl[:, 0:n], pb[:, 0:n], 0.0, ALU.max)
                            k.tt(uv, rl[:, 0:n], rl[:, 0:n], ALU.mult, eng="pool")
                if extra:
                    extra.pop(0)()
            for oh in range(2):
                wsa = wslot()
                wsb = wslot()
                r0 = hh * 2048
                wload(wsa.full(), d["w_fc2"][l][r0:r0 + 1024, oh * 512:oh * 512 + 512].rearrange("(kc p) n -> p kc n", p=128))
                wload(wsb.full(), d["w_fc2"][l][r0 + 1024:r0 + 2048, oh * 512:oh * 512 + 512].rearrange("(kc p) n -> p kc n", p=128))
                for g in G:
                    n = g["n"]
                    for j in range(4):
                        oc = oh * 4 + j
                        pb = k.bank()
                        for kk in range(16):
                            wsx = wsa if kk < 8 else wsb
                            k.mm(pb[:, 0:n], wsx[:, kk % 8, j * 128:(j + 1) * 128], U[:, kk, g["u"]], start=(kk == 0), stop=(kk == 15))
                        xv = g["x"](oc)
                        k.stt(xv, pb[:, 0:n], MODVL[cur[0]][:, 40 + oc, g["ci"]:g["ci"] + 1], xv, ALU.mult, ALU.add)
                if extra:
                    extra.pop(0)()
        while extra:
            extra.pop(0)()
        if l + 1 < depth:
            k.dma("sp", V(XQI.ap.rearrange("(oc p) t -> p oc t", p=128), XQI.bufs), XQ.full())

            def gather():
                tg = k.P.tag
                k.P.tag = "gather"
                k.allgather(XQA, XQI, [[0, 1, 2, 3], [4, 5, 6, 7]])
                for r in range(4):
                    k.dma("sp", X[:, :, TP + r * 256:TP + (r + 1) * 256],
                          V(XQA.ap[r * 1024:(r + 1) * 1024, :].rearrange("(oc p) t -> p oc t", p=128), XQA.bufs))
                k.P.tag = tg
            pending.append(gather)
        else:
            st2 = [A.alloc([128, D], F32) for _ in range(2)]
            for tix in range(2):
                s_ = st2[tix]
                for q4 in range(2):
                    pb = k.bank()
                    for j in range(4):
                        oc = q4 * 4 + j
                        k.tr(pb[:, j * 128:(j + 1) * 128], XQ[:, oc, tix * 128:(tix + 1) * 128], C["ident_f"].full())
                    k.cpx(s_[:, q4 * 512:(q4 + 1) * 512], pb[:, :])
                k.dma("sp", DR(o["ysq"][tix * 128:(tix + 1) * 128, :]), s_.full())
        A.release(m)

    def store_tokens(dst, ntok, col0):
        m = A.mark()
        st2 = [A.alloc([128, D], F32) for _ in range(2)]
        for tix in range(ntok // 128):
            s_ = st2[tix % 2]
            c0 = col0 + tix * 128
            for q4 in range(2):
                pb = k.bank()
                for j in range(4):
                    oc = q4 * 4 + j
                    k.tr(pb[:, j * 128:(j + 1) * 128], X[:, oc, c0:c0 + 128], C["ident_f"].full())
                k.cpx(s_[:, q4 * 512:(q4 + 1) * 512], pb[:, :])
            k.dma("sp", DR(dst[tix * 128:(tix + 1) * 128, :]), s_.full())
        A.release(m)

    stop = c["stop"]
    for l in range(depth):
        if stop == "load":
            break
        cur[0] = l % 2
        k.P.tag = "pre"
        if l == 0:
            c["layer_small"](0)
            for f_ in c["mod_steps"](0):
                f_()
        c["norm_group"](l, 0, 0)
        wc_valid.clear()
        phaseA(l, 0, 4, [(0, 2), (2, 4)], True)
        phaseB_P(l)
        phaseC_P(l)
        wo_groups(l, [dict(mg=lambda kc: MG[:, kc, 0:512], x=lambda oc: X[:, oc, 0:512], n=512, ci=0)])
        k.P.tag = "pre"
        c["norm_group"](l, 0, 1)
        c["norm_group"](l, 0, 2)
        k.dma_dyn(HQ.full(), H, "cabs", 256, H[:, :, TP:TT])
        phaseA(l, TP, 8, [(0, 8)], False)
        phaseB_S(l)
        phaseC_S(l)
        ex = []
        if l + 1 < depth:
            ex = [lambda l=l: c["layer_small"](l + 1)] + c["mod_steps"](l + 1)
        tail(l, ex)
    store_tokens(o["yp"], TP, 0)


_CACHE = {}


def make_in_maps(inp):
    consts = make_consts()
    f = lambda a: np.ascontiguousarray(np.asarray(a), dtype=np.float32)
    shared = {}
    for nm in ("w_mod", "w_in", "w_a2_f", "b_a_f", "w_a2_b", "b_a_b", "gla_onorm", "qn_swa", "kn_swa",
               "qn_na", "kn_na", "sink_swa", "w_pa", "w_pb", "w_pc", "w_o", "w_fc1", "w_fc2"):
        shared[nm] = f(inp[nm])
    shared["b_mod"] = f(inp["b_mod"]).reshape(DEPTH, 48, 128)
    shared["norm1"] = f(inp["norm1"]).reshape(DEPTH, 8, 128)
    shared["norm2"] = f(inp["norm2"]).reshape(DEPTH, 8, 128)
    rp = f(inp["rpb_na"])[:, :, ::-1, ::-1]
    pad = np.zeros((DEPTH, 8, 15, 127), np.float32)
    pad[..., 48:79] = rp
    shared["rpb_pad"] = pad
    shared.update({nm: consts[nm] for nm, _, _ in CONST_SPECS})
    xp = f(inp["x_prompt"])
    xs = f(inp["x_sample"])
    maps = []
    for ci in range(NCORES):
        b = ci // 4
        m = dict(shared)
        m["xp"] = np.ascontiguousarray(xp[2 * ci:2 * ci + 2].reshape(TP, D))
        m["xs"] = np.ascontiguousarray(xs[b])
        cond = np.stack([f(inp["c_ctx"]), f(inp["c"])[b]], 0).reshape(16, 128)
        m["cond"] = np.ascontiguousarray(cond)
        m["rk"] = np.array([[TP + (ci % 4) * 256, (ci % 4) * 256]], np.int32)
        m["st_gla"] = np.ascontiguousarray(f(inp["state_gla"])[b])
        m["cswk"] = np.ascontiguousarray(f(inp["cache_swa_k"])[b].reshape(DEPTH, 512, 128))
        m["cswv"] = np.ascontiguousarray(f(inp["cache_swa_v"])[b].reshape(DEPTH, 512, 128))
        m["cnak"] = np.ascontiguousarray(f(inp["cache_na_k"])[b].reshape(DEPTH, 512, 512))
        m["cnav"] = np.ascontiguousarray(f(inp["cache_na_v"])[b].reshape(DEPTH, 512, 512))
        maps.append(m)
    return maps


def assemble(results):
    yp = np.stack([r["yp"].reshape(2, 256, D) for r in results], 0).reshape(16, 256, D)
    ys = np.stack([np.concatenate([results[4 * b + r]["ysq"] for r in range(4)], 0) for b in range(2)], 0)
    gla = np.concatenate([r["o_gla"] for r in results], 0)
    swk = np.concatenate([r["o_swk"].reshape(2, DEPTH, 256, 2, 64) for r in results], 0)
    swv = np.concatenate([r["o_swv"].reshape(2, DEPTH, 256, 2, 64) for r in results], 0)
    nak = np.concatenate([r["o_nak"].reshape(2, DEPTH, 256, 8, 64) for r in results], 0)
    nav = np.concatenate([r["o_nav"].reshape(2, DEPTH, 256, 8, 64) for r in results], 0)
    return tuple(np.ascontiguousarray(a, dtype=np.float32) for a in (yp, ys, gla, swk, swv, nak, nav))


def kernel(**inputs):
    if "nc" not in _CACHE:
        _CACHE["nc"] = build_program()
    nc = _CACHE["nc"]
    maps = make_in_maps(inputs)
    res = run_bass_kernel_spmd(nc, maps, core_ids=list(range(NCORES)))
    return assemble(res.results)
```

```python
import os
import numpy as np
import ml_dtypes
import concourse.bass as bass
import concourse.mybir as mybir
from concourse.bass_utils import run_bass_kernel_spmd

F32 = mybir.dt.float32
BF16 = mybir.dt.bfloat16
AF = mybir.ActivationFunctionType
ALU = mybir.AluOpType

D = 1024
DEPTH = 4
NCORES = 8
TP = 512
TS = 1024
TT = TP + TS
IN_W = 6944
NEG = -30000.0
PA_STOP = float(os.environ.get('PA_STOP', '99'))
EPS = 1e-6

O_AQ, O_AK, O_AV, O_AR, O_ALF, O_ALB = 0, 256, 512, 1024, 1536, 1552
O_BQ, O_BK, O_BV = 1568, 2080, 2208
O_CQ, O_CK, O_CV = 2336, 2848, 3360
O_GA, O_GB, O_GC = 3872, 4896, 5920


class Buf:
    __slots__ = ("lw", "rd", "name")

    def __init__(self, name=""):
        self.lw = None
        self.rd = []
        self.name = name


class Op:
    __slots__ = ("eng", "fn", "deps", "dma", "ticket", "semkey", "nsig", "idx", "cost", "tag", "st", "fi", "aset")


class Prog:
    ENGS = ("pe", "act", "dve", "pool", "sp")

    def __init__(self, nc):
        self.nc = nc
        self.ops = []
        self.dyn = {}
        self.dyn_spec = {}

    def op(self, eng, fn, reads=(), writes=(), dma=False, cost=500.0):
        o = Op()
        o.aset = None
        o.tag = getattr(self, "tag", "")
        o.cost = cost
        o.eng = eng
        o.fn = fn
        o.dma = dma
        o.idx = len(self.ops)
        deps = set()
        for b in reads:
            if b.lw is not None:
                deps.add(b.lw)
        for b in writes:
            if b.lw is not None:
                deps.add(b.lw)
            deps.update(b.rd)
        o.deps = deps
        self.ops.append(o)
        for b in reads:
            b.rd.append(o.idx)
        for b in writes:
            b.lw = o.idx
            b.rd = []
        return o


    def schedule(self, window=40):
        ops = self.ops
        n = len(ops)
        left = [len(o.deps) for o in ops]
        users = [[] for _ in ops]
        for o in ops:
            for dd in o.deps:
                users[dd].append(o.idx)
        ready = [0.0] * n
        fin = [0.0] * n
        done = [False] * n
        cpl = [0.0] * n
        use_cp = os.environ.get("NO_CP") != "1"
        for o in reversed(ops):
            tail_ = 0.0
            for u in users[o.idx]:
                if cpl[u] > tail_:
                    tail_ = cpl[u]
            cpl[o.idx] = tail_ + o.cost + (2000.0 if o.dma else 150.0)
        pend = {e: [o.idx for o in ops if o.eng == e] for e in self.ENGS}
        head = {e: 0 for e in self.ENGS}
        free = {e: 0.0 for e in self.ENGS}
        order = {e: [] for e in self.ENGS}
        pipe = 0.0
        remaining = n
        glob = []
        last_aset = [None]
        TSW = 1283.0 if os.environ.get("NO_TSW") != "1" else 0.0
        while remaining:
            best = None
            for e in self.ENGS:
                lst = pend[e]
                i = head[e]
                while i < len(lst) and done[lst[i]]:
                    i += 1
                head[e] = i
                cnt = 0
                fe = free[e]
                while i < len(lst) and cnt < window:
                    idx = lst[i]
                    i += 1
                    if done[idx]:
                        continue
                    cnt += 1
                    if left[idx] == 0:
                        st = ready[idx] if ready[idx] > fe else fe
                        if e == "act" and ops[idx].aset is not None and ops[idx].aset != last_aset[0]:
                            st += TSW
                        key = (st, -cpl[idx], idx) if use_cp else (st, 0.0, idx)
                        if best is None or key < best[0]:
                            best = (key, e, idx)
                        if st <= fe and not use_cp:
                            break
            assert best is not None, "scheduler deadlock"
            (st, _, _), e, idx = best
            o = ops[idx]
            if o.dma:
                issue = 1000.0 if e == "pool" else 100.0
                free[e] = st + issue
                p0 = max(pipe, st + issue)
                pipe = p0 + o.cost
                f = pipe + 1600.0
            else:
                if e == "act" and o.aset is not None:
                    last_aset[0] = o.aset
                free[e] = st + o.cost
                f = st + o.cost + 150.0
            fin[idx] = f
            o.st = st
            o.fi = f
            done[idx] = True
            order[e].append(idx)
            glob.append(idx)
            remaining -= 1
            for u in users[idx]:
                left[u] -= 1
                if ready[u] < f:
                    ready[u] = f
        self.order = order
        self.est_ns = max(fin) if fin else 0.0

    def emit(self, final_wait_eng="sp"):
        nc = self.nc
        ops = self.ops
        KD = {"sp": 14, "pool": 10, "act": 4}
        needed = [False] * len(ops)
        for o in ops:
            for d in o.deps:
                a = ops[d]
                if a.eng == o.eng and o.eng == "pe" and not a.dma and not o.dma:
                    continue
                needed[d] = True
        cnt = {e: 0 for e in self.ENGS}
        dcnt = {e: 0 for e in KD}
        order = getattr(self, "order", None)
        if order is None:
            order = {e: [o.idx for o in ops if o.eng == e] for e in self.ENGS}
        seq = [ops[i] for e in self.ENGS for i in order[e]]
        for o in seq:
            if o.dma:
                n = dcnt[o.eng]
                dcnt[o.eng] += 1
                k = n % KD[o.eng]
                o.semkey = (o.eng, k)
                o.ticket = 16 * (n // KD[o.eng] + 1)
            else:
                o.semkey = o.eng
                if needed[o.idx]:
                    cnt[o.eng] += 1
                    o.ticket = cnt[o.eng]
                else:
                    o.ticket = None
        sems = {}
        import contextlib
        with contextlib.ExitStack() as st:
            for e in self.ENGS:
                sems[e] = st.enter_context(nc.semaphore("s_" + e))
            for e, k in KD.items():
                for i in range(k):
                    sems[(e, i)] = st.enter_context(nc.semaphore("d_%s%d" % (e, i)))
            block = st.enter_context(nc.Block())
            per_eng = {e: [ops[i] for i in order[e]] for e in self.ENGS}

            def run(eng_name, eng):
                seen = {}
                if eng_name == "sp":
                    for key, (ap, lo, hi) in getattr(self, "dyn_spec", {}).items():
                        reg = eng.alloc_register("dyn_" + key)
                        eng.reg_load(reg, ap)
                        self.dyn[key] = eng.snap(reg, min_val=lo, max_val=hi)
                for o in per_eng[eng_name]:
                    waits = {}
                    for d in o.deps:
                        a = ops[d]
                        if a.eng == o.eng and o.eng == "pe" and not a.dma and not o.dma:
                            continue
                        if a.ticket is None:
                            continue
                        if waits.get(a.semkey, 0) < a.ticket:
                            waits[a.semkey] = a.ticket
                    if o.dma:
                        prev = o.ticket - 16
                        if prev > 0 and waits.get(o.semkey, 0) < prev:
                            waits[o.semkey] = prev
                    for key, val in waits.items():
                        if seen.get(key, 0) < val:
                            eng.wait_ge(sems[key], val)
                            seen[key] = val
                    ins = o.fn(eng)
                    if o.dma:
                        ins.then_inc(sems[o.semkey], 16)
                    elif o.ticket is not None:
                        ins.then_inc(sems[o.semkey], 1)
                if eng_name in KD:
                    n = dcnt[eng_name]
                    for k in range(min(n, KD[eng_name])):
                        tot = 16 * ((n - 1 - k) // KD[eng_name] + 1)
                        if seen.get((eng_name, k), 0) < tot:
                            eng.wait_ge(sems[(eng_name, k)], tot)

            @block.tensor
            def _(e):
                run("pe", e)

            @block.scalar
            def _(e):
                run("act", e)

            @block.vector
            def _(e):
                run("dve", e)

            @block.gpsimd
            def _(e):
                run("pool", e)

            @block.sync
            def _(e):
                run("sp", e)


def make_consts():
    c = {}
    bf = ml_dtypes.bfloat16
    c["ident_bf"] = np.eye(128, dtype=np.float32).astype(bf)
    c["ident_f"] = np.eye(128, dtype=np.float32)
    bd = np.zeros((128, 128), np.float32)
    bd[:64, :64] = 1.0
    bd[64:, 64:] = 1.0
    c["ones_bd"] = bd.astype(bf)
    c["ones_all"] = np.ones((128, 128), np.float32).astype(bf)
    s = np.arange(128)[:, None]
    t = np.arange(128)[None, :]
    c["trif"] = np.where(s <= t, -1.0 / 16.0, 0.0).astype(np.float32).astype(bf)
    c["trib"] = np.where(s >= t, -1.0 / 16.0, 0.0).astype(np.float32).astype(bf)
    c["maskf"] = np.where(s <= t, 1.0, 0.0).astype(np.float32).astype(bf)
    c["maskb"] = np.where(s >= t, 1.0, 0.0).astype(np.float32).astype(bf)
    c["band_next"] = np.where(t <= s, 0.0, NEG).astype(np.float32).astype(bf)
    c["band_prev"] = np.where(s <= t, 0.0, NEG).astype(np.float32).astype(bf)
    nf = 16
    inv_freq = (10000.0 ** (-np.arange(nf, dtype=np.float32) / nf)).astype(np.float32)
    tt = np.arange(TS)
    row = (tt // 64).astype(np.float32)
    col = (tt % 64).astype(np.float32)
    ang = np.zeros((64, TS), np.float32)
    for d in range(64):
        pos = row if d < 32 else col
        ang[d] = pos * inv_freq[d % 16]
    cos = np.cos(ang).astype(np.float32)
    sin = np.sin(ang).astype(np.float32)
    c["rope_cos"] = np.concatenate([cos, cos], 0)
    c["rope_sin"] = np.concatenate([sin, sin], 0)
    Pm = np.zeros((128, 128), np.float32)
    for d in range(128):
        if (d % 32) < 16:
            Pm[d, d + 16] = -1.0
        else:
            Pm[d, d - 16] = 1.0
    c["rope_pT"] = np.ascontiguousarray(Pm.T).astype(bf)
    J = np.zeros((64, 64), np.float32)
    for i in range(64):
        J[i, 63 - i] = 1.0
    JJ = np.zeros((128, 128), np.float32)
    JJ[:64, :64] = J
    JJ[64:, 64:] = J
    c["jj"] = JJ.astype(bf)
    cq = np.arange(64)[None, :]
    ckp = np.arange(64)[:, None]
    ck = 63 - ckp
    cs = np.clip(cq - 8, 0, 48)
    ok = (ck >= cs) & (ck < cs + 16)
    nm_ = np.where(ok, 0.0, NEG).astype(np.float32)
    c["na_mask"] = np.concatenate([nm_, nm_], 0)
    return c


CONST_SPECS = [
    ("ident_bf", [128, 128], BF16), ("ident_f", [128, 128], F32), ("ones_bd", [128, 128], BF16),
    ("ones_all", [128, 128], BF16), ("trif", [128, 128], BF16), ("trib", [128, 128], BF16),
    ("maskf", [128, 128], BF16), ("maskb", [128, 128], BF16), ("band_next", [128, 128], BF16),
    ("band_prev", [128, 128], BF16), ("rope_cos", [128, TS], F32), ("rope_sin", [128, TS], F32),
    ("rope_pT", [128, 128], BF16), ("jj", [128, 128], BF16), ("na_mask", [128, 64], F32),
]


GRAN = 512


class V:
    __slots__ = ("ap", "bufs", "excl")

    def __init__(self, ap, bufs, excl=False):
        self.ap = ap
        self.bufs = bufs
        self.excl = excl


def DR(ap):
    return V(ap, [])


class Arena:
    def __init__(self, nc, nbytes):
        self.nc = nc
        self.nbytes = nbytes
        self.t = nc.alloc_sbuf_tensor("arena", [128, nbytes // 4], F32)
        self.g = [Buf("g%d" % i) for i in range((nbytes + GRAN - 1) // GRAN)]
        self.top = 0

    def bufs(self, lo, hi):
        return self.g[lo // GRAN:(hi - 1) // GRAN + 1]

    def alloc(self, shape, dtype, align=GRAN):
        esz = 4 if dtype == F32 else 2
        n = 1
        for d in shape[1:]:
            n *= d
        nb = (n * esz + 3) // 4 * 4
        off = (self.top + align - 1) // align * align
        assert off + nb <= self.nbytes, ("arena overflow", off, nb, self.nbytes)
        self.top = off + nb
        return Tile(self, off, shape, dtype)

    def mark(self):
        return self.top

    def release(self, m):
        self.top = m


class Tile:
    def __init__(self, arena, off, shape, dtype):
        self.arena = arena
        self.off = off
        self.shape = tuple(shape)
        self.dt = dtype
        self.esz = 4 if dtype == F32 else 2
        n = 1
        for d in shape[1:]:
            n *= d
        w0 = off // 4
        w1 = w0 + (n * self.esz + 3) // 4
        base = arena.t[:, w0:w1]
        if dtype != F32:
            base = base.bitcast(dtype)
        if len(shape) > 2:
            names = ["d%d" % i for i in range(len(shape) - 1)]
            pat = "p (" + " ".join(names) + ") -> p " + " ".join(names)
            base = base.rearrange(pat, **{nm: shape[i + 1] for i, nm in enumerate(names[:-1])})
        self.ap = base[0:shape[0]]
        st = []
        acc = 1
        for d in reversed(shape[1:]):
            st.append(acc)
            acc *= d
        self.strides = list(reversed(st))

    def __getitem__(self, key):
        if not isinstance(key, tuple):
            key = (key,)
        ap = self.ap[key]
        fk = list(key[1:]) + [slice(None)] * (len(self.shape) - len(key))
        dims = []
        for k, d, s in zip(fk, self.shape[1:], self.strides):
            if isinstance(k, slice):
                a, b, stp = k.indices(d)
                cnt = max(0, (b - a + stp - 1) // stp)
                dims.append((a, cnt, stp, s, d))
            else:
                dims.append((k, 1, 1, s, d))
        gset = {}
        esz = self.esz
        off = self.off
        arena = self.arena

        def rec(i, base):
            a, cnt, stp, st, d = dims[i]
            inner_full = all(dd[1] == dd[4] and dd[2] == 1 for dd in dims[i + 1:])
            if stp == 1 and inner_full:
                lo = base + a * st
                hi = base + (a + cnt) * st
                for b_ in arena.bufs(off + lo * esz, off + hi * esz):
                    gset[id(b_)] = b_
                return
            if i == len(dims) - 1:
                for j in range(cnt):
                    lo = base + (a + j * stp) * st
                    for b_ in arena.bufs(off + lo * esz, off + (lo + 1) * esz):
                        gset[id(b_)] = b_
                return
            for j in range(cnt):
                rec(i + 1, base + (a + j * stp) * st)

        rec(0, 0)
        return V(ap, list(gset.values()))

    def full(self):
        return self[tuple(slice(None) for _ in self.shape)]


class Bank:
    def __init__(self, nc, i):
        self.t = nc.alloc_psum_tensor("pb%d" % i, [128, 512], F32)
        self.buf = Buf("pb%d" % i)
        self.ap = self.t[:, :]

    def __getitem__(self, key):
        return V(self.ap[key], [self.buf], True)

    def v(self, ap):
        return V(ap, [self.buf], True)

    def bf(self):
        return self.ap.bitcast(BF16)


class K:
    def __init__(self, nc):
        self.nc = nc
        self.P = Prog(nc)
        self.A = Arena(nc, 205 * 1024)
        self.banks = [Bank(nc, i) for i in range(8)]
        self.bi = 0
        self.flip = 0
        self.held = []
        self.fence = V(None, [Buf('fence')])

    def bank(self):
        while True:
            b = self.banks[self.bi]
            self.bi = (self.bi + 1) % 8
            if b not in self.held:
                return b

    def hold(self):
        b = self.bank()
        self.held.append(b)
        return b

    def unhold(self, b):
        self.held.remove(b)

    def _rw(self, reads, writes):
        r, w = [], []
        for v in reads:
            if isinstance(v, V):
                (w if v.excl else r).extend(v.bufs)
        for v in writes:
            w.extend(v.bufs)
        return r, w

    def op(self, eng, fn, reads, writes, dma=False):
        r, w = self._rw(reads, writes)
        try:
            shp = writes[0].ap.shape
            nfree = 1
            for x in shp[1:]:
                nfree *= x
            npart = shp[0]
        except Exception:
            nfree, npart = 512, 128
        if dma:
            try:
                esz = 4 if reads[0].ap.dtype == F32 else 2
            except Exception:
                esz = 4
            cost = npart * nfree * esz / 240.0
        elif eng == "pe":
            cost = max(64.0, nfree) * 0.46 + (70.0 if nfree < 256 else 15.0)
        elif eng == "act":
            cost = 230.0 + nfree * 0.75
        elif eng == "dve":
            cost = 120.0 + nfree * 0.95
        else:
            cost = 250.0 + nfree * 1.9
        return self.P.op(eng, fn, r, w, dma, cost)

    def mm(self, out, lhsT, rhs, start=True, stop=True):
        self.op("pe", lambda e: e.matmul(out.ap, lhsT=lhsT.ap, rhs=rhs.ap, start=start, stop=stop,
                                         skip_group_check=True), [lhsT, rhs], [out])

    def tr(self, out, in_, ident):
        self.op("pe", lambda e: e.transpose(out.ap, in_.ap, ident.ap), [in_, ident], [out])

    def act(self, out, in_, func, bias=None, scale=None):
        kw = {}
        rd = [in_]
        if bias is not None:
            kw["bias"] = bias.ap if isinstance(bias, V) else bias
            rd.append(bias)
        if scale is not None:
            kw["scale"] = scale.ap if isinstance(scale, V) else scale
            rd.append(scale)
        o_ = self.op("act", lambda e: e.activation(out.ap, in_.ap, func, **kw), rd, [out])
        if func in (AF.Exp, AF.Ln):
            o_.aset = "exp"
        elif func in (AF.Sigmoid, AF.Silu):
            o_.aset = "sig"

    def tt(self, out, a, b, op, eng="dve"):
        self.op(eng, lambda e: e.tensor_tensor(out.ap, a.ap, b.ap, op), [a, b], [out])

    def ts(self, out, a, s1, op0, s2=None, op1=None, eng="dve"):
        rd = [a, s1, s2]
        s1a = s1.ap if isinstance(s1, V) else s1
        s2a = s2.ap if isinstance(s2, V) else s2
        if op1 is None:
            self.op(eng, lambda e: e.tensor_scalar(out.ap, a.ap, s1a, None, op0), rd, [out])
        else:
            self.op(eng, lambda e: e.tensor_scalar(out.ap, a.ap, s1a, s2a, op0, op1), rd, [out])

    def stt(self, out, in0, scalar, in1, op0, op1, eng="dve"):
        sa = scalar.ap if isinstance(scalar, V) else scalar
        self.op(eng, lambda e: e.scalar_tensor_tensor(out.ap, in0.ap, sa, in1.ap, op0, op1),
                [in0, scalar, in1], [out])

    def cp(self, out, in_, eng="dve"):
        if eng == "act":
            self.op("act", lambda e: e.copy(out.ap, in_.ap), [in_], [out])
        else:
            self.op(eng, lambda e: e.tensor_copy(out.ap, in_.ap), [in_], [out])

    def cpx(self, out, in_):
        self.flip ^= 1
        self.cp(out, in_, "act" if self.flip else "dve")

    def recip(self, out, in_):
        self.op("dve", lambda e: e.reciprocal(out.ap, in_.ap), [in_], [out])

    def rstd(self, out, ss, n):
        self.act(out, ss, AF.Ln, bias=EPS, scale=1.0 / n)
        self.act(out, out, AF.Exp, scale=-0.5)

    def recip_act(self, out, in_):
        self.act(out, in_, AF.Ln)
        self.act(out, out, AF.Exp, scale=-1.0)

    def memset(self, out, val, eng="dve"):
        self.op(eng, lambda e: e.memset(out.ap, val), [], [out])

    def dma_dyn(self, out, src_tile, key, width, track):
        P = self.P

        def fn(e):
            return e.dma_start(out=out.ap, in_=src_tile.ap[:, :, bass.ds(P.dyn[key], width)])
        self.op("sp", fn, [track], [out], dma=True)

    def allgather(self, out, in_, groups):
        o_ = self.op("pool", lambda e: e.collective_compute("AllGather", ALU.bypass, replica_groups=groups,
                                                            ins=[in_.ap.opt()], outs=[out.ap.opt()]),
                     [in_], [out, self.fence])
        o_.cost = 50000.0

    def dma(self, q, out, in_, **kw):
        rd = [in_, self.fence] if q == "pool" else [in_]
        self.op(q, lambda e: e.dma_start(out=out.ap, in_=in_.ap, **kw), rd, [out], dma=True)


def build_program(depth=DEPTH, do_sample=True, dbg_names=(), stop=None):
    nc = bass.Bass("TRN2", target_bir_lowering=False)

    def din(name, shape, dt=F32):
        return nc.dram_tensor(name, shape, dt, kind="ExternalInput").ap()

    def dout(name, shape):
        return nc.dram_tensor(name, shape, F32, kind="ExternalOutput").ap()

    d = {}
    d["xp"] = din("xp", [TP, D])
    d["xs"] = din("xs", [TS, D])
    d["cond"] = din("cond", [16, 128])
    d["rk"] = din("rk", [1, 2], mybir.dt.int32)
    d["st_gla"] = din("st_gla", [DEPTH, 2, 4, 64, 128])
    d["cswk"] = din("cswk", [DEPTH, 512, 128])
    d["cswv"] = din("cswv", [DEPTH, 512, 128])
    d["cnak"] = din("cnak", [DEPTH, 512, 512])
    d["cnav"] = din("cnav", [DEPTH, 512, 512])
    d["w_mod"] = din("w_mod", [DEPTH, D, 6 * D])
    d["b_mod"] = din("b_mod", [DEPTH, 48, 128])
    d["norm1"] = din("norm1", [DEPTH, 8, 128])
    d["norm2"] = din("norm2", [DEPTH, 8, 128])
    d["w_in"] = din("w_in", [DEPTH, D, IN_W])
    d["w_a2_f"] = din("w_a2_f", [DEPTH, 16, 256])
    d["b_a_f"] = din("b_a_f", [DEPTH, 256])
    d["w_a2_b"] = din("w_a2_b", [DEPTH, 16, 256])
    d["b_a_b"] = din("b_a_b", [DEPTH, 256])
    d["gla_onorm"] = din("gla_onorm", [DEPTH, 128])
    for nm in ("qn_swa", "kn_swa", "qn_na", "kn_na"):
        d[nm] = din(nm, [DEPTH, 64])
    d["sink_swa"] = din("sink_swa", [DEPTH, 8])
    d["rpb_pad"] = din("rpb_pad", [DEPTH, 8, 15, 127])
    for nm in ("w_pa", "w_pb", "w_pc"):
        d[nm] = din(nm, [DEPTH, 512, D])
    d["w_o"] = din("w_o", [DEPTH, D, D])
    d["w_fc1"] = din("w_fc1", [DEPTH, D, 4 * D])
    d["w_fc2"] = din("w_fc2", [DEPTH, 4 * D, D])
    for nm, shp, dt in CONST_SPECS:
        d[nm] = din(nm, shp, dt)
    o = {}
    o["yp"] = dout("yp", [TP, D])
    o["ysq"] = dout("ysq", [256, D])
    o["o_gla"] = dout("o_gla", [2, DEPTH, 2, 4, 64, 128])
    o["o_swk"] = dout("o_swk", [2, DEPTH, 256, 128])
    o["o_swv"] = dout("o_swv", [2, DEPTH, 256, 128])
    o["o_nak"] = dout("o_nak", [2, DEPTH, 256, 512])
    o["o_nav"] = dout("o_nav", [2, DEPTH, 256, 512])

    k = K(nc)
    A = k.A
    k.P.dyn_spec["cabs"] = (d["rk"][0:1, 0:1], TP, TP + 768)
    k.P.dyn_spec["crel"] = (d["rk"][0:1, 1:2], 0, 768)
    xq_in = nc.dram_tensor("xq_in", [1024, 256], F32)
    xq_all = nc.dram_tensor("xq_all", [4096, 256], F32)
    XQI = V(xq_in.ap(), [Buf("xq_in")])
    XQA = V(xq_all.ap(), [Buf("xq_all")])
    X = A.alloc([128, 8, TT], F32)
    H = A.alloc([128, 8, TT], BF16)
    MG = A.alloc([128, 8, TP], BF16)
    MGQ = A.alloc([128, 8, 256], BF16)
    HQ = A.alloc([128, 8, 256], BF16)
    WS = [A.alloc([128, 8, 512], BF16) for _ in range(4)]
    wsi = [0]

    def wslot():
        w = WS[wsi[0]]
        wsi[0] = (wsi[0] + 1) % len(WS)
        return w

    C = {}
    for nm, shp, dt in CONST_SPECS:
        C[nm] = A.alloc(shp, dt, align=64 if shp[1] <= 128 else GRAN)
        k.dma("sp", C[nm].full(), DR(d[nm]))
    SCT = A.alloc([128, 8, 2], BF16, align=64)
    PVt = A.alloc([128, DEPTH, 64], F32)
    MODVL = [A.alloc([128, 48, 2], F32, align=64) for _ in range(2)]
    SCLL = [A.alloc([128, 2, 2, 8], F32, align=64) for _ in range(2)]
    cur = [0]
    GVL = [A.alloc([128, 8], F32, align=64) for _ in range(2)]
    ESKL = [A.alloc([128, 8], F32, align=64) for _ in range(2)]
    ALR = A.alloc([64, TS], BF16)
    WA2L = [A.alloc([64, 512], BF16) for _ in range(2)]

    class _Cur:
        def __init__(self, lst):
            self.lst = lst

        def __getitem__(self, key):
            return self.lst[cur[0]][key]

        def full(self):
            return self.lst[cur[0]].full()

    GV = _Cur(GVL)
    ESK = _Cur(ESKL)
    WA2 = _Cur(WA2L)
    k.memset(ALR.full(), 1.0)

    dbg = {}

    def dbg_out(name, v, shape):
        if name in dbg_names:
            ap = dout("dbg_" + name, shape)
            k.dma("sp", DR(ap), v)

    m0 = A.mark()
    for l in range(DEPTH):
        stg = A.alloc([64, 128], F32)
        k.dma("sp", stg[0:48, :], DR(d["b_mod"][l]))
        k.dma("sp", stg[48:56, :], DR(d["norm1"][l]))
        k.dma("sp", stg[56:64, :], DR(d["norm2"][l]))
        pb = k.bank()
        k.tr(pb[:, 0:64], stg.full(), C["ident_f"][0:64, 0:64])
        k.cp(PVt[:, l, :], pb[:, 0:64])
    stg = A.alloc([16, 128], F32)
    k.dma("sp", stg.full(), DR(d["cond"]))
    pb = k.bank()
    k.tr(pb[:, 0:16], stg.full(), C["ident_f"][0:16, 0:16])
    for ci in range(2):
        k.act(SCT[:, :, ci], pb[:, ci * 8:(ci + 1) * 8], AF.Silu)
    A.release(m0)

    def load_tokens(src, n, col0):
        m = A.mark()
        st2 = [A.alloc([128, D], F32) for _ in range(2)]
        for tix in range(n // 128):
            s_ = st2[tix % 2]
            k.dma("sp", s_.full(), DR(src[tix * 128:(tix + 1) * 128, :]))
            for q4 in range(2):
                pb = k.bank()
                for j in range(4):
                    oc = q4 * 4 + j
                    k.tr(pb[:, j * 128:(j + 1) * 128], s_[:, oc * 128:(oc + 1) * 128], C["ident_f"].full())
                c0 = col0 + tix * 128
                k.cpx(X[:, q4 * 4:q4 * 4 + 4, c0:c0 + 128],
                      pb.v(pb.ap.rearrange("p (a b) -> p a b", a=4)))
        A.release(m)

    load_tokens(d["xp"], TP, 0)
    load_tokens(d["xs"], TS, TP)

    def wload(slot_view, src_ap):
        k.dma("pool", slot_view, DR(src_ap))

    wc_scr = {}
    wc_valid = set()

    def wtile(key, slot_view, loader):
        if os.environ.get("NO_WCACHE") == "1":
            loader()
            return
        shp = list(slot_view.ap.shape)
        if key not in wc_scr:
            t = nc.dram_tensor("wc%d" % len(wc_scr), shp, BF16)
            wc_scr[key] = V(t.ap(), [Buf("wc")])
        scr = wc_scr[key]
        if key in wc_valid:
            k.dma("sp", slot_view, scr)
        else:
            loader()
            k.dma("sp", scr, slot_view)
            wc_valid.add(key)

    def wrows(w2d, c0, n):
        return w2d[:, c0:c0 + n].rearrange("(kc p) n -> p kc n", p=128)

    GCI = [0, 1, 1]

    def mod_steps(l):
        MODV = MODVL[l % 2]
        SCL = SCLL[l % 2]
        st = {}

        def step(j):
            if j == 0:
                st["mb"] = k.hold()
            mb = st["mb"]
            ws = wslot()
            wload(ws.full(), wrows(d["w_mod"][l], j * 512, 512))
            for jj in range(4):
                ch = j * 4 + jj
                for kc in range(8):
                    k.mm(mb[:, ch * 2:ch * 2 + 2], ws[:, kc, jj * 128:(jj + 1) * 128], SCT[:, kc, :],
                         start=(kc == 0), stop=(kc == 7))

        def fin():
            mb = st["mb"]
            bm = PVt[:, l, 0:48]
            k.tt(MODV.full(), mb.v(mb.ap[:, 0:96].rearrange("p (a b) -> p a b", b=2)),
                 V(bm.ap.unsqueeze(2).to_broadcast([128, 48, 2]), bm.bufs), ALU.add)
            k.unhold(mb)
            for n in range(2):
                for ci in range(2):
                    sc = MODV[:, 8 + 24 * n:16 + 24 * n, ci]
                    k.ts(SCL[:, n, ci, :], sc, 1.0, ALU.add)
                    k.tt(SCL[:, n, ci, :], SCL[:, n, ci, :], PVt[:, l, 48 + 8 * n:56 + 8 * n], ALU.mult)

        return [(lambda j=j: step(j)) for j in range(12)] + [fin]

    def norm_group(l, n, g):
        ci = GCI[g]
        tok = slice(g * 512, (g + 1) * 512)
        m = A.mark()
        sq = [A.alloc([128, 512], BF16) for _ in range(2)]
        rs = A.alloc([128, 512], F32)
        tmp = [A.alloc([128, 512], F32) for _ in range(2)]
        sb = k.bank()
        for oc in range(8):
            s_ = sq[oc % 2]
            k.act(s_.full(), X[:, oc, tok], AF.Square)
            k.mm(sb[:, :], C["ones_all"].full(), s_.full(), start=(oc == 0), stop=(oc == 7))
        k.rstd(rs.full(), sb[:, :], 1024.0)
        for oc in range(8):
            t_ = tmp[oc % 2]
            k.stt(t_.full(), X[:, oc, tok], SCLL[cur[0]][:, n, ci, oc:oc + 1], rs.full(), ALU.mult, ALU.mult)
            k.act(H[:, oc, tok], t_.full(), AF.Identity, bias=MODVL[cur[0]][:, 24 * n + oc, ci:ci + 1])
        A.release(m)

    def proj_fm(ws, c0, mcols, tok, pb=None, col_off=0):
        if pb is None:
            pb = k.bank()
        if isinstance(tok, tuple):
            hf, n = tok
        else:
            hf, n = (lambda kc: H[:, kc, tok]), 512
        for kc in range(8):
            k.mm(pb[0:mcols, col_off:col_off + n], ws[:, kc, c0:c0 + mcols], hf(kc),
                 start=(kc == 0), stop=(kc == 7))
        return pb

    def proj_tm(ws, c0, ncols, t0, pb=None):
        if pb is None:
            pb = k.bank()
        for kc in range(8):
            k.mm(pb[:, 0:ncols], H[:, kc, t0:t0 + 128], ws[:, kc, c0:c0 + ncols],
                 start=(kc == 0), stop=(kc == 7))
        return pb

    nh_flip = [0]

    def normhead(pb, out_f32, out_bf, gain, out_pair=None):
        m = A.mark()
        nh_flip[0] ^= 1
        if nh_flip[0]:
            A.alloc([128, 512 + 256 + 512], F32)
        qf = A.alloc([128, 512], F32)
        sq = A.alloc([128, 512], BF16)
        rs = A.alloc([128, 512], F32)
        k.cp(qf.full(), pb[:, :], "dve")
        k.act(sq.full(), pb[:, :], AF.Square)
        sb = k.bank()
        k.mm(sb[:, :], C["ones_bd"].full(), sq.full())
        k.rstd(rs.full(), sb[:, :], 64.0)
        if out_pair is not None:
            for hf in range(2):
                rw = slice(hf * 64, hf * 64 + 64)
                k.stt(out_pair[hf], qf[rw, :], V(gain.ap[rw], gain.bufs), rs[rw, :], ALU.mult, ALU.mult)
        elif out_f32 is not None:
            k.stt(out_f32, qf.full(), gain, rs.full(), ALU.mult, ALU.mult)
            if out_bf is not None:
                k.cp(out_bf, out_f32, "act")
        else:
            k.stt(out_bf, qf.full(), gain, rs.full(), ALU.mult, ALU.mult)
        A.release(m)

    def layer_small(l):
        GV, ESK, WA2 = GVL[l % 2], ESKL[l % 2], WA2L[l % 2]
        for j, (nm, f) in enumerate((("qn_swa", 0.125), ("kn_swa", 1.0), ("qn_na", 0.125), ("kn_na", 1.0))):
            src = d[nm][l].rearrange("(p o) -> p o", o=1)
            k.dma("sp", GV[0:64, j:j + 1], DR(src))
            k.dma("sp", GV[64:128, j:j + 1], DR(src))
        k.dma("sp", GV[:, 4:5], DR(d["gla_onorm"][l].rearrange("(p o) -> p o", o=1)))
        k.ts(GV[:, 0:1], GV[:, 0:1], 0.125, ALU.mult)
        k.ts(GV[:, 2:3], GV[:, 2:3], 0.125, ALU.mult)
        k.dma("sp", ESK.full(), DR(d["sink_swa"][l].partition_broadcast(128)))
        k.act(ESK.full(), ESK.full(), AF.Exp)
        k.memset(WA2.full(), 0.0)
        k.dma("pool", WA2[0:16, 0:256], DR(d["w_a2_f"][l]))
        k.dma("pool", WA2[16:17, 0:256], DR(d["b_a_f"][l].rearrange("(o n) -> o n", o=1)))
        k.dma("pool", WA2[32:48, 256:512], DR(d["w_a2_b"][l]))
        k.dma("pool", WA2[48:49, 256:512], DR(d["b_a_b"][l].rearrange("(o n) -> o n", o=1)))

    def dense_attn(Q, Kt, kmap, VA, vmap, O, sink):
        m = A.mark()
        PT = [A.alloc([128, 512], BF16) for _ in range(3)]
        RC = [A.alloc([64, 256], F32) for _ in range(2)]
        it = 0
        ob = None
        for s in range(2):
            for h in range(8):
                rows = slice((h % 2) * 64, (h % 2) * 64 + 64)
                sb = k.bank()
                for kt in range(2):
                    k0 = s * 256 + kt * 128
                    k.mm(sb[:, kt * 256:(kt + 1) * 256], Kt[:, kmap(h), k0:k0 + 128],
                         Q[:, h, s * 256:(s + 1) * 256])
                pt = PT[it % 3]
                k.act(pt.full(), sb[:, :], AF.Exp)
                if it % 2 == 0:
                    ob = k.bank()
                c0 = (it % 2) * 256
                for kt in range(2):
                    k.mm(ob[:, c0:c0 + 256], VA[:, s * 2 + kt, vmap(h), :], pt[:, kt * 256:(kt + 1) * 256],
                         start=(kt == 0), stop=(kt == 1))
                rc = RC[it % 2]
                if sink is not None:
                    k.act(rc.full(), ob[64:128, c0:c0 + 256], AF.Ln, bias=ESK[0:64, h:h + 1])
                else:
                    k.act(rc.full(), ob[64:128, c0:c0 + 256], AF.Ln)
                k.act(rc.full(), rc.full(), AF.Exp, scale=-1.0)
                k.tt(O[rows, h // 2, s * 256:(s + 1) * 256], ob[0:64, c0:c0 + 256], rc.full(), ALU.mult)
                it += 1
        A.release(m)

    def merge(l, wp, gcol, groups, first):
        m = A.mark()
        SG = [A.alloc([128, 512], F32) for _ in range(2)]
        TM = [A.alloc([128, 512], F32) for _ in range(2)]
        for half in range(2):
            wsp = wslot()
            wtile(("mp", gcol, half), wsp[:, 0:4, :], lambda wsp=wsp, half=half: wload(
                wsp[:, 0:4, :], wp[:, half * 512:(half + 1) * 512].rearrange("(h p) n -> p h n", p=128)))
            wsg = wslot()
            wtile(("mg", gcol, half), wsg.full(), lambda wsg=wsg, half=half: wload(
                wsg.full(), wrows(d["w_in"][l], gcol + half * 512, 512)))
            it = 0
            for g in groups:
                n = g["n"]
                for j in range(4):
                    oc = half * 4 + j
                    gb = proj_fm(wsg, j * 128, 128, (g["h"], n))
                    sg = SG[it % 2]
                    k.act(sg[:, 0:n], gb[:, 0:n], AF.Sigmoid)
                    yb = k.bank()
                    for kk in range(4):
                        k.mm(yb[:, 0:n], wsp[:, kk, j * 128:(j + 1) * 128], g["o"](kk), start=(kk == 0), stop=(kk == 3))
                    mgv = g["mg"](oc)
                    if first:
                        k.tt(mgv, yb[:, 0:n], sg[:, 0:n], ALU.mult)
                    else:
                        tm = TM[it % 2]
                        k.tt(tm[:, 0:n], yb[:, 0:n], sg[:, 0:n], ALU.mult)
                        k.tt(mgv, mgv, tm[:, 0:n], ALU.add)
                    it += 1
        A.release(m)

    ctx = dict(nc=nc, k=k, A=A, d=d, o=o, X=X, H=H, MG=MG, MGQ=MGQ, HQ=HQ, XQI=XQI, XQA=XQA, C=C, wslot=wslot, wload=wload, wrows=wrows,
               wtile=wtile, wc_valid=wc_valid, proj_fm=proj_fm, proj_tm=proj_tm, normhead=normhead, dense_attn=dense_attn, merge=merge,
               GV=GV, ESK=ESK, ALR=ALR, WA2=WA2, MODVL=MODVL, SCLL=SCLL, cur=cur, dbg_out=dbg_out, depth=depth,
               do_sample=do_sample, stop=stop, mod_steps=mod_steps, norm_group=norm_group, layer_small=layer_small)
    build_layers(ctx)
    if os.environ.get('NO_SCHED') != '1':
        k.P.schedule()
    k.P.emit()
    return nc


def build_layers(c):
    k, A, d, o, X, H, MG, C = c["k"], c["A"], c["d"], c["o"], c["X"], c["H"], c["MG"], c["C"]
    MGQ, HQ, XQI, XQA = c["MGQ"], c["HQ"], c["XQI"], c["XQA"]

    def grp_P(O):
        return dict(h=lambda kc: H[:, kc, 0:512], o=lambda kk: O[:, kk, 0:512], mg=lambda oc: MG[:, oc, 0:512], n=512)

    def merge_quarter(l, wp, gcol, O, first):
        mq = A.mark()
        OQ = A.alloc([128, 4, 256], BF16)
        k.dma_dyn(OQ.full(), O, "crel", 256, O.full())
        merge(l, wp, gcol, [dict(h=lambda kc: HQ[:, kc, :], o=lambda kk: OQ[:, kk, :], mg=lambda oc: MGQ[:, oc, :], n=256)], first)
        A.release(mq)
    wslot, wload, wrows = c["wslot"], c["wload"], c["wrows"]
    wtile, wc_valid = c["wtile"], c["wc_valid"]
    proj_fm, proj_tm, normhead, dense_attn, merge = c["proj_fm"], c["proj_tm"], c["normhead"], c["dense_attn"], c["merge"]
    GV, ESK, ALR, WA2, MODVL, cur = c["GV"], c["ESK"], c["ALR"], c["WA2"], c["MODVL"], c["cur"]
    depth = c["depth"]
    dbg_out = c["dbg_out"]

    def phaseA(l, t0, nchunk, seqs, is_prompt):
        k.P.tag = "A.%s" % ("P" if is_prompt else "S")
        ngrp = nchunk // 4
        n = nchunk * 128
        m = A.mark()
        QP = A.alloc([128, 4, n], BF16)
        KP = A.alloc([128, 4, n], BF16)
        VT = A.alloc([128, nchunk, 512], BF16)
        SIN = A.alloc([128, nchunk, 2, 2, 128], BF16)
        MB = A.alloc([128, nchunk, 2, 128], F32)
        ALAST = A.alloc([128, 4, nchunk], F32, align=64)
        R = A.alloc([128, 2, 2, 128], F32)
        w1 = wslot()
        wtile("a1", w1.full(), lambda: wload(w1.full(), wrows(d["w_in"][l], O_AQ, 512)))
        w4 = wslot()
        wload(w4[:, :, 0:16], wrows(d["w_in"][l], O_ALF, 16))
        wload(w4[:, :, 32:48], wrows(d["w_in"][l], O_ALB, 16))
        w2 = wslot()
        wtile("a2", w2.full(), lambda: wload(w2.full(), wrows(d["w_in"][l], O_AV, 512)))
        while pending:
            pending.pop(0)()

        def seq_of(cidx):
            for si, (a, b) in enumerate(seqs):
                if a <= cidx < b:
                    return si, a, b
            raise AssertionError

        def init_state(si, dr):
            if is_prompt:
                k.memset(R[:, dr, :, :], 0.0)
            else:
                src = d["st_gla"][l, dr].rearrange("h k v -> (h k) v").rearrange("(pr q) v -> q pr v", q=128)
                k.dma("sp", R[:, dr, :, :], DR(src))

        def store_state(si, dr, a_c):
            if not is_prompt:
                return
            mm_ = A.mark()
            sf = A.alloc([128, 2, 128], F32)
            for pr in range(2):
                k.ts(sf[:, pr, :], R[:, dr, pr, :], ALAST[:, dr * 2 + pr, a_c:a_c + 1], ALU.mult)
            dst = o["o_gla"][si, l, dr].rearrange("h k v -> (h k) v").rearrange("(pr q) v -> q pr v", q=128)
            k.dma("sp", DR(dst), sf.full())
            A.release(mm_)

        def scan_step(cidx, dr, msrc):
            si, a, b = seq_of(cidx)
            first = (cidx == a) if dr == 0 else (cidx == b - 1)
            prev = cidx - 1 if dr == 0 else cidx + 1
            if first:
                init_state(si, dr)
            for pr in range(2):
                for hf in range(2):
                    rows = slice(hf * 64, hf * 64 + 64)
                    if first:
                        k.cp(SIN[rows, cidx, dr, pr, :], R[rows, dr, pr, :])
                        k.tt(R[rows, dr, pr, :], R[rows, dr, pr, :], msrc(pr, hf), ALU.add)
                    else:
                        al = ALAST[rows, dr * 2 + pr, prev:prev + 1]
                        k.ts(SIN[rows, cidx, dr, pr, :], R[rows, dr, pr, :], al, ALU.mult)
                        k.stt(R[rows, dr, pr, :], R[rows, dr, pr, :], al, msrc(pr, hf), ALU.mult, ALU.add)
            last = (cidx == b - 1) if dr == 0 else (cidx == a)
            if last:
                store_state(si, dr, cidx)

        m1 = A.mark()
        QK32 = A.alloc([128, 4, 512], F32)
        EP = A.alloc([128, 4, 256], F32)
        EN = A.alloc([128, 4, 256], F32)
        E1 = A.alloc([128, 512], F32)
        LA = A.alloc([128, 512], F32)
        LAH = A.alloc([128, 512], BF16)
        LAL = A.alloc([128, 512], BF16)
        KT = [A.alloc([128, 4, 128], BF16) for _ in range(2)]
        for gi in range(ngrp):
            htok = slice(t0 + gi * 512, t0 + gi * 512 + 512)
            pb = k.bank()
            for kc in range(8):
                k.mm(pb[0:64, :], w4[:, kc, 0:64], H[:, kc, htok], start=(kc == 0), stop=(kc == 7))
            k.cp(ALR[0:16, gi * 512:gi * 512 + 512], pb[0:16, :], "act")
            k.cp(ALR[32:48, gi * 512:gi * 512 + 512], pb[32:48, :], "dve")
            for j in range(4):
                pq = proj_fm(w1, j * 128, 128, htok)
                k.cpx(QK32[:, j, :], pq[:, :])
            if PA_STOP <= 1:
                continue
            for hf2 in range(2):
                cb = [k.hold(), k.hold()]
                for c2 in range(2):
                    cl = gi * 4 + hf2 * 2 + c2
                    tk = cl * 128
                    zb = k.bank()
                    k.mm(zb[:, :], ALR[0:64, tk:tk + 128], WA2.full())
                    if PA_STOP <= 1.2:
                        k.cp(E1.full(), zb[:, :])
                        continue
                    k.act(E1.full(), zb[:, :], AF.Exp, scale=-1.0)
                    k.act(LA.full(), E1.full(), AF.Ln, bias=1.0)
                    if PA_STOP <= 1.4:
                        continue
                    k.cp(LAH.full(), LA.full(), "act")
                    k.tt(LAL.full(), LA.full(), LAH.full(), ALU.subtract)
                    for dr in range(2):
                        tri = C["trif"] if dr == 0 else C["trib"]
                        for cc in range(2):
                            osl = cb[dr][:, cc * 256 + c2 * 128:cc * 256 + c2 * 128 + 128]
                            k.mm(osl, LAH[:, dr * 256 + cc * 128:dr * 256 + cc * 128 + 128], tri.full(), True, False)
                            k.mm(osl, LAL[:, dr * 256 + cc * 128:dr * 256 + cc * 128 + 128], tri.full(), False, True)
                if PA_STOP <= 1.6:
                    k.unhold(cb[0])
                    k.unhold(cb[1])
                    continue
                for dr in range(2):
                    cv = cb[dr].v(cb[dr].ap.rearrange("p (a b) -> p a b", a=2))
                    k.act(EP[:, dr * 2:dr * 2 + 2, :], cv, AF.Exp)
                    k.act(EN[:, dr * 2:dr * 2 + 2, :], cv, AF.Exp, scale=-1.0)
                k.unhold(cb[0])
                k.unhold(cb[1])
                if PA_STOP <= 2:
                    continue
                cbase = gi * 4 + hf2 * 2
                k.cp(ALAST[:, 0:2, cbase:cbase + 2], EP[:, 0:2, 127:256:128])
                k.cp(ALAST[:, 2:4, cbase:cbase + 2], EP[:, 2:4, 0:256:128])
                tcol = slice(gi * 512 + hf2 * 256, gi * 512 + hf2 * 256 + 256)
                hcol = slice(hf2 * 256, hf2 * 256 + 256)
                for dc in range(4):
                    cc = dc % 2
                    k.stt(QP[:, dc, tcol], QK32[:, cc, hcol], 0.125, EP[:, dc, :], ALU.mult, ALU.mult)
                    k.tt(KP[:, dc, tcol], QK32[:, 2 + cc, hcol], EN[:, dc, :], ALU.mult)
                if PA_STOP <= 3:
                    continue
                for c2 in range(2):
                    cl = gi * 4 + hf2 * 2 + c2
                    ctok = slice(cl * 128, cl * 128 + 128)
                    tb = k.bank()
                    tbv = tb.bf()
                    for dc in range(4):
                        k.tr(tb.v(tbv[:, dc * 128:(dc + 1) * 128]), KP[:, dc, ctok], C["ident_bf"].full())
                    kt = KT[cl % 2]
                    k.cp(kt.full(), tb.v(tbv[:, 0:512].rearrange("p (a b) -> p a b", a=4)), "act")
                    vb = proj_tm(w2, 0, 512, t0 + cl * 128)
                    k.cp(VT[:, cl, :], vb[:, :], "dve")
                    for dr in range(2):
                        mb = k.bank()
                        for pr in range(2):
                            k.mm(mb[:, pr * 256:(pr + 1) * 256], kt[:, dr * 2 + pr, :], VT[:, cl, pr * 256:(pr + 1) * 256])
                        mv = mb.ap.rearrange("p (a b) -> p a b", a=2)
                        if dr == 0:
                            scan_step(cl, 0, lambda pr, hf, mv=mv, mb=mb: mb.v(
                                mv[hf * 64:hf * 64 + 64, pr, hf * 128:hf * 128 + 128]))
                        else:
                            for hf in range(2):
                                k.cp(MB[hf * 64:hf * 64 + 64, cl, :, :],
                                     mb.v(mv[hf * 64:hf * 64 + 64, :, hf * 128:hf * 128 + 128]), "act")
        A.release(m1)
        if PA_STOP <= 4:
            A.release(m)
            return
        for cl in reversed(range(nchunk)):
            scan_step(cl, 1, lambda pr, hf, cl=cl: MB[hf * 64:hf * 64 + 64, cl, pr, :])
        if PA_STOP <= 5:
            A.release(m)
            return
        OA = A.alloc([128, 4, n], BF16)
        m3 = A.mark()
        w3 = wslot()
        wtile("a3", w3.full(), lambda: wload(w3.full(), wrows(d["w_in"][l], O_AR, 512)))
        SIL = A.alloc([128, 4, 512], BF16)
        ATM = [A.alloc([128, 4, 128], BF16) for _ in range(2)]
        OFs = [A.alloc([128, 512], F32) for _ in range(2)]
        SQs = [A.alloc([128, 512], BF16) for _ in range(2)]
        RSs = [A.alloc([128, 512], F32) for _ in range(2)]
        for gi in range(ngrp):
            htok = slice(t0 + gi * 512, t0 + gi * 512 + 512)
            for hd in range(4):
                pb = proj_fm(w3, hd * 128, 128, htok)
                k.act(SIL[:, hd, :], pb[:, :], AF.Silu)
            ob = [k.hold() for _ in range(4)]
            for c4 in range(4):
                cl = gi * 4 + c4
                ctok = slice(cl * 128, cl * 128 + 128)
                for dr in range(2):
                    mk = C["maskf"] if dr == 0 else C["maskb"]
                    mkf = mk.full()
                    for hf in range(2):
                        ab = k.bank()
                        rows = slice(hf * 64, hf * 64 + 64)
                        for pr in range(2):
                            k.mm(ab[:, pr * 128:(pr + 1) * 128], KP[rows, dr * 2 + pr, ctok], QP[rows, dr * 2 + pr, ctok])
                        k.tt(ATM[dr][:, hf:4:2, :], ab.v(ab.ap[:, 0:256].rearrange("p (a b) -> p a b", a=2)),
                             V(mkf.ap.unsqueeze(1).to_broadcast([128, 2, 128]), mkf.bufs), ALU.mult)
                for hd in range(4):
                    rows = slice((hd % 2) * 64, (hd % 2) * 64 + 64)
                    col = slice(c4 * 128, c4 * 128 + 128)
                    k.mm(ob[hd][:, col], VT[:, cl, hd * 128:(hd + 1) * 128], ATM[0][:, hd, :], True, False)
                    k.mm(ob[hd][:, col], VT[:, cl, hd * 128:(hd + 1) * 128], ATM[1][:, hd, :], False, False)
                    k.mm(ob[hd][:, col], SIN[rows, cl, 0, hd // 2, :], QP[rows, hd // 2, ctok], False, False)
                    k.mm(ob[hd][:, col], SIN[rows, cl, 1, hd // 2, :], QP[rows, 2 + hd // 2, ctok], False, True)
            gtok = slice(gi * 512, gi * 512 + 512)
            for hd in range(4):
                OF, SQ, RS = OFs[hd % 2], SQs[hd % 2], RSs[hd % 2]
                k.cp(OF.full(), ob[hd][:, :], "dve")
                k.act(SQ.full(), ob[hd][:, :], AF.Square)
                k.unhold(ob[hd])
                sb = k.bank()
                k.mm(sb[:, :], C["ones_all"].full(), SQ.full())
                k.rstd(RS.full(), sb[:, :], 128.0)
                k.tt(OF.full(), OF.full(), RS.full(), ALU.mult)
                k.stt(OA[:, hd, gtok], OF.full(), GV[:, 4:5], SIL[:, hd, :], ALU.mult, ALU.mult)
        A.release(m3)
        if PA_STOP <= 6:
            A.release(m)
            return
        if is_prompt:
            merge(l, d["w_pa"][l], O_GA, [grp_P(OA)], True)
        else:
            merge_quarter(l, d["w_pa"][l], O_GA, OA, True)
        A.release(m)

    def phaseB_P(l):
        k.P.tag = "B.P"
        tok = slice(0, 512)
        m = A.mark()
        QN = A.alloc([128, 8, 512], BF16)
        k.memset(QN.full(), 0.0)
        KN = A.alloc([128, 2, 512], BF16)
        KNF = A.alloc([128, 2, 512], F32)
        VA = A.alloc([128, 4, 2, 128], BF16)
        OB = A.alloc([128, 4, 512], BF16)
        KO = [A.alloc([128, 128], F32) for _ in range(2)]
        w1 = wslot()
        wtile("b1", w1.full(), lambda: wload(w1.full(), wrows(d["w_in"][l], O_BQ, 512)))
        w2 = wslot()

        def ld_b2():
            for kv in range(2):
                for dup in range(2):
                    wload(w2[:, :, kv * 128 + dup * 64:kv * 128 + dup * 64 + 64], wrows(d["w_in"][l], O_BK + kv * 64, 64))
            wload(w2[:, :, 256:384], wrows(d["w_in"][l], O_BV, 128))
        wtile("b2", w2[:, :, 0:384], ld_b2)
        for cc in range(4):
            pb = proj_fm(w1, cc * 128, 128, tok)
            normhead(pb, None, None, GV[:, 0:1], out_pair=(QN[0:64, 2 * cc, :], QN[64:128, 2 * cc + 1, :]))
        for kv in range(2):
            pb = proj_fm(w2, kv * 128, 128, tok)
            normhead(pb, KNF[:, kv, :], KN[:, kv, :], GV[:, 1:2])
        k.memset(VA[:, :, :, 64:128], 1.0)
        for tix in range(4):
            ttok = slice(tix * 128, tix * 128 + 128)
            pb = k.bank()
            for kv in range(2):
                k.tr(pb[:, kv * 128:(kv + 1) * 128], KNF[:, kv, ttok], C["ident_f"].full())
            ko = KO[0]
            kov = ko.full()
            k.cp(V(kov.ap.rearrange("p (a b) -> p a b", a=2), kov.bufs),
                 pb.v(pb.ap[:, 0:256].rearrange("p (a b) -> p a b", a=2)[:, :, 0:64]), "act")
            k.dma("sp", DR(o["o_swk"][tix // 2, l, (tix % 2) * 128:(tix % 2) * 128 + 128, :]), kov)
            pv = proj_tm(w2, 256, 128, tix * 128)
            vo = KO[1]
            k.cp(vo.full(), pv[:, 0:128], "dve")
            k.dma("sp", DR(o["o_swv"][tix // 2, l, (tix % 2) * 128:(tix % 2) * 128 + 128, :]), vo.full())
            k.cp(VA[:, tix, :, 0:64], pv.v(pv.ap[:, 0:128].rearrange("p (a b) -> p a b", a=2)), "act")
        dense_attn(QN, KN, lambda h: h // 4, VA, lambda h: h // 4, OB, True)
        merge(l, d["w_pb"][l], O_GB, [grp_P(OB)], False)
        A.release(m)

    def phaseC_P(l):
        k.P.tag = "C.P"
        tok = slice(0, 512)
        m = A.mark()
        QN = A.alloc([128, 8, 512], BF16)
        k.memset(QN.full(), 0.0)
        KN = A.alloc([128, 4, 512], BF16)
        KNF = A.alloc([128, 4, 512], F32)
        VA = A.alloc([128, 4, 8, 128], BF16)
        OC = A.alloc([128, 4, 512], BF16)
        KO = [A.alloc([128, 512], F32) for _ in range(2)]
        wqk = []
        wv = []
        for hh in range(2):
            wa = wslot()
            wtile(("c1", hh), wa.full(), lambda wa=wa, hh=hh: (
                wload(wa[:, :, 0:256], wrows(d["w_in"][l], O_CQ + hh * 256, 256)),
                wload(wa[:, :, 256:512], wrows(d["w_in"][l], O_CK + hh * 256, 256))))
            wqk.append(wa)
            for c2 in range(2):
                cc = hh * 2 + c2
                pb = proj_fm(wa, c2 * 128, 128, tok)
                normhead(pb, None, None, GV[:, 2:3], out_pair=(QN[0:64, 2 * cc, :], QN[64:128, 2 * cc + 1, :]))
                pb = proj_fm(wa, 256 + c2 * 128, 128, tok)
                normhead(pb, KNF[:, cc, :], KN[:, cc, :], GV[:, 3:4])
        for hh in range(2):
            wb = wslot()
            wtile(("c3", hh), wb[:, :, 0:256], lambda wb=wb, hh=hh: wload(
                wb[:, :, 0:256], wrows(d["w_in"][l], O_CV + hh * 256, 256)))
            wv.append(wb)
        k.memset(VA[:, :, :, 64:128], 1.0)
        for tix in range(4):
            ttok = slice(tix * 128, tix * 128 + 128)
            pb = k.bank()
            for cc in range(4):
                k.tr(pb[:, cc * 128:(cc + 1) * 128], KNF[:, cc, ttok], C["ident_f"].full())
            k.cp(KO[0].full(), pb[:, :], "act")
            k.dma("sp", DR(o["o_nak"][tix // 2, l, (tix % 2) * 128:(tix % 2) * 128 + 128, :]), KO[0].full())
            pv = k.bank()
            for hh in range(2):
                for kc in range(8):
                    k.mm(pv[:, hh * 256:(hh + 1) * 256], H[:, kc, tix * 128:tix * 128 + 128], wv[hh][:, kc, 0:256],
                         start=(kc == 0), stop=(kc == 7))
            k.cp(KO[1].full(), pv[:, :], "dve")
            k.dma("sp", DR(o["o_nav"][tix // 2, l, (tix % 2) * 128:(tix % 2) * 128 + 128, :]), KO[1].full())
            k.cp(VA[:, tix, :, 0:64], pv.v(pv.ap.rearrange("p (a b) -> p a b", a=8)), "act")
        dense_attn(QN, KN, lambda h: h // 2, VA, lambda h: h, OC, None)
        merge(l, d["w_pc"][l], O_GC, [grp_P(OC)], False)
        A.release(m)

    rp_flip = [0]

    def rope_apply(qf, qfb, out_bf, tsl):
        mr = A.mark()
        rp_flip[0] ^= 1
        if rp_flip[0]:
            A.alloc([128, 1024], F32)
        t1 = A.alloc([128, 512], F32)
        t2 = A.alloc([128, 512], F32)
        pr = k.bank()
        k.mm(pr[:, :], C["rope_pT"].full(), qfb)
        k.tt(t1.full(), qf, C["rope_cos"][:, tsl], ALU.mult)
        k.tt(t2.full(), pr[:, :], C["rope_sin"][:, tsl], ALU.mult)
        if isinstance(out_bf, tuple):
            k.tt(out_bf[0], t1[0:64, :], t2[0:64, :], ALU.add, eng="pool")
            k.tt(out_bf[1], t1[64:128, :], t2[64:128, :], ALU.add, eng="pool")
        else:
            k.tt(out_bf, t1.full(), t2.full(), ALU.add, eng="pool")
        A.release(mr)

    def phaseB_S(l):
        k.P.tag = "B.S"
        m = A.mark()
        QN = A.alloc([128, 8, TS], BF16)
        k.memset(QN.full(), 0.0, eng="pool")
        KN = A.alloc([128, 2, TS], BF16)
        VA = A.alloc([128, 8, 2, 128], BF16)
        KC = A.alloc([128, 2, 512], BF16)
        VC = A.alloc([128, 4, 2, 128], BF16)
        OB = A.alloc([128, 4, TS], BF16)
        w1 = wslot()
        wtile("b1", w1.full(), lambda: wload(w1.full(), wrows(d["w_in"][l], O_BQ, 512)))
        w2 = wslot()

        def ld_b2():
            for kv in range(2):
                for dup in range(2):
                    wload(w2[:, :, kv * 128 + dup * 64:kv * 128 + dup * 64 + 64], wrows(d["w_in"][l], O_BK + kv * 64, 64))
            wload(w2[:, :, 256:384], wrows(d["w_in"][l], O_BV, 128))
        wtile("b2", w2[:, :, 0:384], ld_b2)
        m2 = A.mark()
        QFs = [A.alloc([128, 512], F32) for _ in range(2)]
        QFBs = [A.alloc([128, 512], BF16) for _ in range(2)]
        qi = 0
        for g in range(2):
            htok = slice(TP + g * 512, TP + g * 512 + 512)
            tsl = slice(g * 512, g * 512 + 512)
            for cc in range(4):
                QF, QFB = QFs[qi % 2], QFBs[qi % 2]
                qi += 1
                pb = proj_fm(w1, cc * 128, 128, htok)
                normhead(pb, QF.full(), QFB.full(), GV[:, 0:1])
                rope_apply(QF.full(), QFB.full(), (QN[0:64, 2 * cc, tsl], QN[64:128, 2 * cc + 1, tsl]), tsl)
            for kv in range(2):
                QF, QFB = QFs[qi % 2], QFBs[qi % 2]
                qi += 1
                pb = proj_fm(w2, kv * 128, 128, htok)
                normhead(pb, QF.full(), QFB.full(), GV[:, 1:2])
                rope_apply(QF.full(), QFB.full(), KN[:, kv, tsl], tsl)
        A.release(m2)
        k.memset(VA[:, :, :, 64:128], 1.0, eng="pool")
        k.memset(VC[:, :, :, 64:128], 1.0, eng="pool")
        for tix in range(8):
            pv = proj_tm(w2, 256, 128, TP + tix * 128)
            k.cpx(VA[:, tix, :, 0:64], pv.v(pv.ap[:, 0:128].rearrange("p (a b) -> p a b", a=2)))
        m2 = A.mark()
        ST = [A.alloc([128, 256], F32) for _ in range(2)]
        SV = [A.alloc([128, 128], F32) for _ in range(2)]
        for i in range(4):
            st = ST[i % 2]
            src = d["cswk"][l, i * 128:(i + 1) * 128, :].rearrange("p (a b) -> p a b", a=2)
            stv = st.full()
            for dup in range(2):
                k.dma("sp", V(stv.ap.rearrange("p (a c b) -> p a c b", a=2, c=2)[:, :, dup, :], stv.bufs), DR(src))
            pb = k.bank()
            for kv in range(2):
                k.tr(pb[:, kv * 128:(kv + 1) * 128], st[:, kv * 128:(kv + 1) * 128], C["ident_f"].full())
            k.cpx(KC[:, :, i * 128:(i + 1) * 128], pb.v(pb.ap[:, 0:256].rearrange("p (a b) -> p a b", a=2)))
            sv = SV[i % 2]
            k.dma("sp", sv.full(), DR(d["cswv"][l, i * 128:(i + 1) * 128, :]))
            svv = sv.full()
            k.cpx(VC[:, i, :, 0:64], V(svv.ap.rearrange("p (a b) -> p a b", a=2), svv.bufs))
        A.release(m2)
        m2 = A.mark()
        PT = [A.alloc([128, 512], BF16) for _ in range(3)]
        RC = [A.alloc([64, 512], F32) for _ in range(2)]
        it = 0
        for h in range(8):
            rows = slice((h % 2) * 64, (h % 2) * 64 + 64)
            kv = h // 4
            ob = [k.hold(), k.hold()]
            for qh in range(2):
                qsl = slice(qh * 512, qh * 512 + 512)
                for i in range(4):
                    sa = k.bank()
                    k.mm(sa[:, :], KC[:, kv, i * 128:(i + 1) * 128], QN[:, h, qsl])
                    pt = PT[it % 3]
                    it += 1
                    k.act(pt.full(), sa[:, :], AF.Exp)
                    k.mm(ob[qh][:, :], VC[:, i, kv, :], pt.full(), i == 0, False)
            for kt in range(8):
                qb0, qb1 = max(0, kt - 1), min(7, kt + 1)
                nq = (qb1 - qb0 + 1) * 128
                q0 = qb0 * 128
                sbk = k.bank()
                k.mm(sbk[:, 0:nq], KN[:, kv, kt * 128:(kt + 1) * 128], QN[:, h, q0:q0 + nq], True, False)
                for qb in range(qb0, qb1 + 1):
                    if qb == kt:
                        continue
                    msk = C["band_next"] if qb == kt + 1 else C["band_prev"]
                    k.mm(sbk[:, (qb - qb0) * 128:(qb - qb0 + 1) * 128], C["ident_bf"].full(), msk.full(), False, False)
                pt = PT[it % 3]
                it += 1
                k.act(pt[:, 0:nq], sbk[:, 0:nq], AF.Exp)
                segs = []
                a_ = q0
                while a_ < q0 + nq:
                    b_ = min(q0 + nq, (a_ // 512 + 1) * 512)
                    segs.append((a_, b_))
                    a_ = b_
                for (a_, b_) in segs:
                    qh = a_ // 512
                    k.mm(ob[qh][:, a_ - qh * 512:b_ - qh * 512], VA[:, kt, kv, :], pt[:, a_ - q0:b_ - q0], False, False)
            for qh in range(2):
                rc = RC[qh]
                k.act(rc.full(), ob[qh][64:128, :], AF.Ln, bias=ESK[0:64, h:h + 1])
                k.act(rc.full(), rc.full(), AF.Exp, scale=-1.0)
                k.tt(OB[rows, h // 2, qh * 512:(qh + 1) * 512], ob[qh][0:64, :], rc.full(), ALU.mult)
                k.unhold(ob[qh])
        A.release(m2)
        merge_quarter(l, d["w_pb"][l], O_GB, OB, False)
        A.release(m)

    def na_runs(mt):
        runs = []
        if mt <= 3:
            runs.append((0, 4, False))
        lo, hi = max(5, 2 * mt - 3), min(11, 2 * mt + 5)
        if lo <= hi:
            if lo <= 7 and hi >= 8:
                runs.append((lo, 7, True))
                runs.append((8, hi, True))
            else:
                runs.append((lo, hi, True))
        if mt >= 4:
            runs.append((12, 15, False))
        return runs

    def phaseC_S(l):
        k.P.tag = "C.S"
        m = A.mark()
        OC = A.alloc([128, 4, TS], BF16)
        for hh in range(2):
            mh = A.mark()
            QN = A.alloc([128, 4, TS], BF16)
            k.memset(QN.full(), 0.0, eng="pool")
            KN = A.alloc([128, 2, TS], BF16)
            VA = A.alloc([128, 8, 4, 128], BF16)
            KC = A.alloc([128, 2, 512], BF16)
            VC = A.alloc([128, 4, 4, 128], BF16)
            w1 = wslot()
            wtile(("c1", hh), w1.full(), lambda w1=w1, hh=hh: (
                wload(w1[:, :, 0:256], wrows(d["w_in"][l], O_CQ + hh * 256, 256)),
                wload(w1[:, :, 256:512], wrows(d["w_in"][l], O_CK + hh * 256, 256))))
            w3 = wslot()
            wtile(("c3", hh), w3[:, :, 0:256], lambda w3=w3, hh=hh: wload(
                w3[:, :, 0:256], wrows(d["w_in"][l], O_CV + hh * 256, 256)))
            for g in range(2):
                htok = slice(TP + g * 512, TP + g * 512 + 512)
                tsl = slice(g * 512, g * 512 + 512)
                for cc in range(2):
                    pb = proj_fm(w1, cc * 128, 128, htok)
                    normhead(pb, None, None, GV[:, 2:3], out_pair=(QN[0:64, 2 * cc, tsl], QN[64:128, 2 * cc + 1, tsl]))
                    pb = proj_fm(w1, 256 + cc * 128, 128, htok)
                    normhead(pb, None, KN[:, cc, tsl], GV[:, 3:4])
            k.memset(VA[:, :, :, 64:128], 1.0, eng="pool")
            k.memset(VC[:, :, :, 64:128], 1.0, eng="pool")
            for tix in range(8):
                pv = proj_tm(w3, 0, 256, TP + tix * 128)
                k.cpx(VA[:, tix, :, 0:64], pv.v(pv.ap[:, 0:256].rearrange("p (a b) -> p a b", a=4)))
            m2 = A.mark()
            ST = [A.alloc([128, 256], F32) for _ in range(2)]
            SV = [A.alloc([128, 256], F32) for _ in range(2)]
            for i in range(4):
                st = ST[i % 2]
                k.dma("sp", st.full(), DR(d["cnak"][l, i * 128:(i + 1) * 128, hh * 256:hh * 256 + 256]))
                pb = k.bank()
                for cc in range(2):
                    k.tr(pb[:, cc * 128:(cc + 1) * 128], st[:, cc * 128:(cc + 1) * 128], C["ident_f"].full())
                k.cpx(KC[:, :, i * 128:(i + 1) * 128], pb.v(pb.ap[:, 0:256].rearrange("p (a b) -> p a b", a=2)))
                sv = SV[i % 2]
                k.dma("sp", sv.full(), DR(d["cnav"][l, i * 128:(i + 1) * 128, hh * 256:hh * 256 + 256]))
                svv = sv.full()
                k.cpx(VC[:, i, :, 0:64], V(svv.ap.rearrange("p (a b) -> p a b", a=4), svv.bufs))
            A.release(m2)
            m2 = A.mark()
            STG = A.alloc([128, 16, 64], F32)
            SFUL = A.alloc([128, 16, 64], BF16)
            SINT = A.alloc([128, 16, 64], BF16)
            PT = [A.alloc([128, 512], BF16) for _ in range(2)]
            RC = A.alloc([64, 512], F32)
            it = 0
            for hl in range(4):
                h = hh * 4 + hl
                rows = slice((hl % 2) * 64, (hl % 2) * 64 + 64)
                cc = hl // 2
                base = d["rpb_pad"][l, h]
                hk = bass.AP(base.tensor, base.offset, [[1, 64], [127, 15], [1, 64]])
                k.memset(STG.full(), 0.0)
                k.dma("sp", STG[0:64, 0:15, :], DR(hk))
                k.dma("sp", STG[64:128, 1:16, :], DR(hk))
                nmk = C["na_mask"].full()
                k.tt(SFUL.full(), STG.full(), V(nmk.ap.unsqueeze(1).to_broadcast([128, 16, 64]), nmk.bufs), ALU.add)
                k.cp(SINT.full(), SFUL.full(), "act")
                k.memset(SINT[0:64, 0:4, :], NEG, eng="pool")
                k.memset(SINT[0:64, 12:16, :], NEG, eng="pool")
                k.memset(SINT[64:128, 0:5, :], NEG, eng="pool")
                k.memset(SINT[64:128, 13:16, :], NEG, eng="pool")
                ob = [k.hold(), k.hold()]
                for qh in range(2):
                    qsl = slice(qh * 512, qh * 512 + 512)
                    for i in range(4):
                        sb = k.bank()
                        k.mm(sb[:, :], KC[:, cc, i * 128:(i + 1) * 128], QN[:, hl, qsl])
                        pt = PT[it % 2]
                        it += 1
                        k.act(pt.full(), sb[:, :], AF.Exp)
                        k.mm(ob[qh][:, :], VC[:, i, hl, :], pt.full(), i == 0, False)
                for mt in range(8):
                    for (r0, r1, interior) in na_runs(mt):
                        nq = (r1 - r0 + 1) * 64
                        q0 = r0 * 64
                        qh = q0 // 512
                        b0 = 7 + r0 - 2 * mt
                        strip = SINT if interior else SFUL
                        sb = k.bank()
                        k.mm(sb[:, 0:nq], KN[:, cc, mt * 128:(mt + 1) * 128], QN[:, hl, q0:q0 + nq], True, False)
                        sv_ = strip[:, b0:b0 + (r1 - r0 + 1), :]
                        k.mm(sb[:, 0:nq], C["jj"].full(), V(sv_.ap.rearrange("p a b -> p (a b)"), sv_.bufs), False, True)
                        pt = PT[it % 2]
                        it += 1
                        k.act(pt[:, 0:nq], sb[:, 0:nq], AF.Exp)
                        k.mm(ob[qh][:, q0 - qh * 512:q0 - qh * 512 + nq], VA[:, mt, hl, :], pt[:, 0:nq], False, False)
                for qh in range(2):
                    k.recip_act(RC.full(), ob[qh][64:128, :])
                    k.tt(OC[rows, h // 2, qh * 512:(qh + 1) * 512], ob[qh][0:64, :], RC.full(), ALU.mult)
                    k.unhold(ob[qh])
            A.release(m2)
            A.release(mh)
        merge_quarter(l, d["w_pc"][l], O_GC, OC, False)
        A.release(m)

    def wo_groups(l, groups):
        k.P.tag = "wo"
        for half in range(2):
            ws = wslot()
            wtile(("wo", half), ws.full(), lambda ws=ws, half=half: wload(ws.full(), wrows(d["w_o"][l], half * 512, 512)))
            for g in groups:
                n = g["n"]
                for j in range(4):
                    oc = half * 4 + j
                    pb = k.bank()
                    for kc in range(8):
                        k.mm(pb[:, 0:n], ws[:, kc, j * 128:(j + 1) * 128], g["mg"](kc), start=(kc == 0), stop=(kc == 7))
                    xv = g["x"](oc)
                    k.stt(xv, pb[:, 0:n], MODVL[cur[0]][:, 16 + oc, g["ci"]:g["ci"] + 1], xv, ALU.mult, ALU.add)

    def norm_q(l, XQ, HQ2):
        mn = A.mark()
        sq = [A.alloc([128, 256], BF16) for _ in range(2)]
        rs = A.alloc([128, 256], F32)
        tmp = [A.alloc([128, 256], F32) for _ in range(2)]
        sb = k.bank()
        for oc in range(8):
            s_ = sq[oc % 2]
            k.act(s_.full(), XQ[:, oc, :], AF.Square)
            k.mm(sb[:, 0:256], C["ones_all"].full(), s_.full(), start=(oc == 0), stop=(oc == 7))
        k.rstd(rs.full(), sb[:, 0:256], 1024.0)
        for oc in range(8):
            t_ = tmp[oc % 2]
            k.stt(t_.full(), XQ[:, oc, :], c["SCLL"][cur[0]][:, 1, 1, oc:oc + 1], rs.full(), ALU.mult, ALU.mult)
            k.act(HQ2[:, oc, :], t_.full(), AF.Identity, bias=MODVL[cur[0]][:, 24 + oc, 1:2])
        A.release(mn)

    pending = []

    def tail(l, extra):
        m = A.mark()
        XQ = A.alloc([128, 8, 256], F32)
        HQ2 = A.alloc([128, 8, 256], BF16)
        k.dma_dyn(XQ.full(), X, "cabs", 256, X[:, :, TP:TT])
        wo_groups(l, [dict(mg=lambda kc: MGQ[:, kc, :], x=lambda oc: XQ[:, oc, :], n=256, ci=1)])
        k.P.tag = "mlp"
        c["norm_group"](l, 1, 0)
        norm_q(l, XQ, HQ2)
        G = [dict(h=lambda kc: H[:, kc, 0:512], u=slice(0, 512), x=lambda oc: X[:, oc, 0:512], n=512, ci=0),
             dict(h=lambda kc: HQ2[:, kc, :], u=slice(512, 768), x=lambda oc: XQ[:, oc, :], n=256, ci=1)]
        U = A.alloc([128, 16, 768], BF16)
        RL = [A.alloc([128, 512], BF16) for _ in range(2)]
        ri = 0
        for hh in range(2):
            for t4 in range(4):
                ws = wslot()
                wload(ws.full(), wrows(d["w_fc1"][l], hh * 2048 + t4 * 512, 512))
                for g in G:
                    n = g["n"]
                    for j in range(4):
                        pb = proj_fm(ws, j * 128, 128, (g["h"], n))
                        rl = RL[ri % 2]
                        ri += 1
                        uv = U[:, t4 * 4 + j, g["u"]]
                        if j % 2 == 0:
                            k.act(rl[:, 0:n], pb[:, 0:n], AF.Relu)
                            k.tt(uv, rl[:, 0:n], rl[:, 0:n], ALU.mult)
                        else:
                            k.ts(rl[:, 0:n], pb[:, 0:n], 0.0, ALU.max)
                            k.tt(uv, rl[:, 0:n], rl[:, 0:n], ALU.mult, eng="pool")
                if extra:
                    extra.pop(0)()
            for oh in range(2):
                wsa = wslot()
                wsb = wslot()
                r0 = hh * 2048
                wload(wsa.full(), d["w_fc2"][l][r0:r0 + 1024, oh * 512:oh * 512 + 512].rearrange("(kc p) n -> p kc n", p=128))
                wload(wsb.full(), d["w_fc2"][l][r0 + 1024:r0 + 2048, oh * 512:oh * 512 + 512].rearrange("(kc p) n -> p kc n", p=128))
                for g in G:
                    n = g["n"]
                    for j in range(4):
                        oc = oh * 4 + j
                        pb = k.bank()
                        for kk in range(16):
                            wsx = wsa if kk < 8 else wsb
                            k.mm(pb[:, 0:n], wsx[:, kk % 8, j * 128:(j + 1) * 128], U[:, kk, g["u"]], start=(kk == 0), stop=(kk == 15))
                        xv = g["x"](oc)
                        k.stt(xv, pb[:, 0:n], MODVL[cur[0]][:, 40 + oc, g["ci"]:g["ci"] + 1], xv, ALU.mult, ALU.add)
                if extra:
                    extra.pop(0)()
        while extra:
            extra.pop(0)()
        if l + 1 < depth:
            k.dma("sp", V(XQI.ap.rearrange("(oc p) t -> p oc t", p=128), XQI.bufs), XQ.full())

            def gather():
                tg = k.P.tag
                k.P.tag = "gather"
                k.allgather(XQA, XQI, [[0, 1, 2, 3], [4, 5, 6, 7]])
                for r in range(4):
                    k.dma("sp", X[:, :, TP + r * 256:TP + (r + 1) * 256],
                          V(XQA.ap[r * 1024:(r + 1) * 1024, :].rearrange("(oc p) t -> p oc t", p=128), XQA.bufs))
                k.P.tag = tg
            pending.append(gather)
        else:
            st2 = [A.alloc([128, D], F32) for _ in range(2)]
            for tix in range(2):
                s_ = st2[tix]
                for q4 in range(2):
                    pb = k.bank()
                    for j in range(4):
                        oc = q4 * 4 + j
                        k.tr(pb[:, j * 128:(j + 1) * 128], XQ[:, oc, tix * 128:(tix + 1) * 128], C["ident_f"].full())
                    k.cpx(s_[:, q4 * 512:(q4 + 1) * 512], pb[:, :])
                k.dma("sp", DR(o["ysq"][tix * 128:(tix + 1) * 128, :]), s_.full())
        A.release(m)

    def store_tokens(dst, ntok, col0):
        m = A.mark()
        st2 = [A.alloc([128, D], F32) for _ in range(2)]
        for tix in range(ntok // 128):
            s_ = st2[tix % 2]
            c0 = col0 + tix * 128
            for q4 in range(2):
                pb = k.bank()
                for j in range(4):
                    oc = q4 * 4 + j
                    k.tr(pb[:, j * 128:(j + 1) * 128], X[:, oc, c0:c0 + 128], C["ident_f"].full())
                k.cpx(s_[:, q4 * 512:(q4 + 1) * 512], pb[:, :])
            k.dma("sp", DR(dst[tix * 128:(tix + 1) * 128, :]), s_.full())
        A.release(m)

    stop = c["stop"]
    for l in range(depth):
        if stop == "load":
            break
        cur[0] = l % 2
        k.P.tag = "pre"
        if l == 0:
            c["layer_small"](0)
            for f_ in c["mod_steps"](0):
                f_()
        c["norm_group"](l, 0, 0)
        wc_valid.clear()
        phaseA(l, 0, 4, [(0, 2), (2, 4)], True)
        phaseB_P(l)
        phaseC_P(l)
        wo_groups(l, [dict(mg=lambda kc: MG[:, kc, 0:512], x=lambda oc: X[:, oc, 0:512], n=512, ci=0)])
        k.P.tag = "pre"
        c["norm_group"](l, 0, 1)
        c["norm_group"](l, 0, 2)
        k.dma_dyn(HQ.full(), H, "cabs", 256, H[:, :, TP:TT])
        phaseA(l, TP, 8, [(0, 8)], False)
        phaseB_S(l)
        phaseC_S(l)
        ex = []
        if l + 1 < depth:
            ex = [lambda l=l: c["layer_small"](l + 1)] + c["mod_steps"](l + 1)
        tail(l, ex)
    store_tokens(o["yp"], TP, 0)


_CACHE = {}


def make_in_maps(inp):
    consts = make_consts()
    f = lambda a: np.ascontiguousarray(np.asarray(a), dtype=np.float32)
    shared = {}
    for nm in ("w_mod", "w_in", "w_a2_f", "b_a_f", "w_a2_b", "b_a_b", "gla_onorm", "qn_swa", "kn_swa",
               "qn_na", "kn_na", "sink_swa", "w_pa", "w_pb", "w_pc", "w_o", "w_fc1", "w_fc2"):
        shared[nm] = f(inp[nm])
    shared["b_mod"] = f(inp["b_mod"]).reshape(DEPTH, 48, 128)
    shared["norm1"] = f(inp["norm1"]).reshape(DEPTH, 8, 128)
    shared["norm2"] = f(inp["norm2"]).reshape(DEPTH, 8, 128)
    rp = f(inp["rpb_na"])[:, :, ::-1, ::-1]
    pad = np.zeros((DEPTH, 8, 15, 127), np.float32)
    pad[..., 48:79] = rp
    shared["rpb_pad"] = pad
    shared.update({nm: consts[nm] for nm, _, _ in CONST_SPECS})
    xp = f(inp["x_prompt"])
    xs = f(inp["x_sample"])
    maps = []
    for ci in range(NCORES):
        b = ci // 4
        m = dict(shared)
        m["xp"] = np.ascontiguousarray(xp[2 * ci:2 * ci + 2].reshape(TP, D))
        m["xs"] = np.ascontiguousarray(xs[b])
        cond = np.stack([f(inp["c_ctx"]), f(inp["c"])[b]], 0).reshape(16, 128)
        m["cond"] = np.ascontiguousarray(cond)
        m["rk"] = np.array([[TP + (ci % 4) * 256, (ci % 4) * 256]], np.int32)
        m["st_gla"] = np.ascontiguousarray(f(inp["state_gla"])[b])
        m["cswk"] = np.ascontiguousarray(f(inp["cache_swa_k"])[b].reshape(DEPTH, 512, 128))
        m["cswv"] = np.ascontiguousarray(f(inp["cache_swa_v"])[b].reshape(DEPTH, 512, 128))
        m["cnak"] = np.ascontiguousarray(f(inp["cache_na_k"])[b].reshape(DEPTH, 512, 512))
        m["cnav"] = np.ascontiguousarray(f(inp["cache_na_v"])[b].reshape(DEPTH, 512, 512))
        maps.append(m)
    return maps


def assemble(results):
    yp = np.stack([r["yp"].reshape(2, 256, D) for r in results], 0).reshape(16, 256, D)
    ys = np.stack([np.concatenate([results[4 * b + r]["ysq"] for r in range(4)], 0) for b in range(2)], 0)
    gla = np.concatenate([r["o_gla"] for r in results], 0)
    swk = np.concatenate([r["o_swk"].reshape(2, DEPTH, 256, 2, 64) for r in results], 0)
    swv = np.concatenate([r["o_swv"].reshape(2, DEPTH, 256, 2, 64) for r in results], 0)
    nak = np.concatenate([r["o_nak"].reshape(2, DEPTH, 256, 8, 64) for r in results], 0)
    nav = np.concatenate([r["o_nav"].reshape(2, DEPTH, 256, 8, 64) for r in results], 0)
    return tuple(np.ascontiguousarray(a, dtype=np.float32) for a in (yp, ys, gla, swk, swv, nak, nav))


def kernel(**inputs):
    if "nc" not in _CACHE:
        _CACHE["nc"] = build_program()
    nc = _CACHE["nc"]
    maps = make_in_maps(inputs)
    res = run_bass_kernel_spmd(nc, maps, core_ids=list(range(NCORES)))
    return assemble(res.results)
```
